# Optimizing a Trainium2 kernel written in Bass

```python
import math
import jax, jax.numpy as jnp
from jax import lax
import numpy as np

D_MODEL = 2048
BATCH = 4
SEQ = 2048
DEPTH = 4
DEC_BATCH = 128
DEC_SEQ = 4
PAST_LEN = 16384
PAGE_SIZE = 128

N_EVEN = (DEPTH + 1) // 2
N_ODD = DEPTH // 2
MIX_HALF = D_MODEL // 2
RET_HEADS = 4
RET_DK = MIX_HALF // RET_HEADS
RET_DV = MIX_HALF // RET_HEADS
RET_CHUNK = 128
ROPE_BASE = 10000.0
LRU_WIDTH = MIX_HALF
LRU_HEADS = 8
LRU_BLOCK = LRU_WIDTH // LRU_HEADS
LRU_C = 8.0
CONV_W = 4
IN_WIDTH = 6 * MIX_HALF
SSM_GROUP = 16
SSM_GROUPS = D_MODEL // SSM_GROUP
SSM_P = 64
SSM_CHUNK = 128
FFN_HIDDEN = ((8 * D_MODEL + 3 * 256 - 1) // (3 * 256)) * 256
EPS = 1e-6

kernel_name = 'hybrid_retention_rglru_s5_decode_step'


def rmsnorm(x, g):
    xf = x.astype(jnp.float32)
    y = xf * lax.rsqrt(jnp.mean(xf * xf, axis=-1, keepdims=True) + EPS)
    return (y * g.astype(jnp.float32)).astype(x.dtype)


def rotary(x, pos):
    half = x.shape[-1] // 2
    inv = 1.0 / jnp.power(ROPE_BASE, jnp.linspace(0.0, 1.0, half, dtype=jnp.float32))
    ang = pos.astype(jnp.float32)[:, None] * inv[None, :]
    cos = jnp.cos(ang)[None, :, None, :]
    sin = jnp.sin(ang)[None, :, None, :]
    x1, x2 = x[..., :half], x[..., half:]
    return jnp.concatenate([x1 * cos - x2 * sin, x2 * cos + x1 * sin], axis=-1)


def retention(q, k, v, s0, pos0):
    B, T, H, dk = q.shape
    C = math.gcd(T, RET_CHUNK)
    NC = T // C
    pos = pos0 + jnp.arange(T)
    qf = rotary(q.astype(jnp.float32), pos)
    kf = rotary(k.astype(jnp.float32), pos) * (dk ** -0.5)
    vf = v.astype(jnp.float32)
    log_g = jnp.log1p(-jnp.exp2(-5.0 - jnp.arange(H, dtype=jnp.float32)))
    idx = jnp.arange(C, dtype=jnp.float32)
    diff = idx[:, None] - idx[None, :]
    mask = jnp.where(diff[None] >= 0, jnp.exp(jnp.maximum(diff, 0.0)[None] * log_g[:, None, None]), 0.0)
    q_dec = jnp.exp((idx[:, None] + 1.0) * log_g[None, :])
    k_dec = jnp.exp((C - 1.0 - idx)[:, None] * log_g[None, :])
    chunk_dec = jnp.exp(C * log_g)

    def to_chunks(a):
        return a.reshape(B, NC, C, H, a.shape[-1]).swapaxes(0, 1)

    def step(s, inp):
        qc, kc, vc = inp
        scores = jnp.einsum('bnhd,bmhd->bhnm', qc, kc) * mask
        o = (jnp.einsum('bhnm,bmhe->bnhe', scores, vc)
             + jnp.einsum('bnhd,bhde->bnhe', qc, s) * q_dec[None, :, :, None])
        s = (s * chunk_dec[None, :, None, None]
             + jnp.einsum('bmhd,bmhe->bhde', kc * k_dec[None, :, :, None], vc))
        return s, o

    s, o = lax.scan(step, s0.astype(jnp.float32), (to_chunks(qf), to_chunks(kf), to_chunks(vf)))
    o = o.swapaxes(0, 1).reshape(B, T, H, vf.shape[-1])
    return o, s


def causal_conv(x, buf, w, b):
    T = x.shape[1]
    xp = jnp.concatenate([buf.astype(jnp.float32), x.astype(jnp.float32)], axis=1)
    wf = w.astype(jnp.float32)
    y = b.astype(jnp.float32) + sum(xp[:, i:i + T] * wf[i] for i in range(CONV_W))
    new_buf = xp[:, xp.shape[1] - (CONV_W - 1):]
    return y, new_buf


def _lin_combine(e1, e2):
    a1, b1 = e1
    a2, b2 = e2
    return a1 * a2, a2 * b1 + b2


def rglru(x, h0, wa, ba, wx, bx, lam):
    B, T, W = x.shape
    xb = x.reshape(B, T, LRU_HEADS, LRU_BLOCK)
    r = jax.nn.sigmoid(jnp.einsum('btnk,nkj->btnj', xb, wa.astype(jnp.float32)).reshape(B, T, W) + ba.astype(jnp.float32))
    i = jax.nn.sigmoid(jnp.einsum('btnk,nkj->btnj', xb, wx.astype(jnp.float32)).reshape(B, T, W) + bx.astype(jnp.float32))
    log_a = -LRU_C * r * jax.nn.softplus(-lam.astype(jnp.float32))
    a = jnp.exp(log_a)
    mult = jnp.sqrt(jnp.maximum(-jnp.expm1(2.0 * log_a), 0.0))
    bterm = mult * i * x
    bterm = bterm.at[:, 0].add(a[:, 0] * h0.astype(jnp.float32))
    _, h = lax.associative_scan(_lin_combine, (a, bterm), axis=1)
    return h, h[:, -1]


def _cplx_combine(e1, e2):
    ar1, ai1, br1, bi1 = e1
    ar2, ai2, br2, bi2 = e2
    return (ar2 * ar1 - ai2 * ai1, ar2 * ai1 + ai2 * ar1,
            ar2 * br1 - ai2 * bi1 + br2, ar2 * bi1 + ai2 * br1 + bi2)


def s5(u, h0_re, h0_im, a_re, a_im, b_re, b_im, c_re, c_im, d, log_dt):
    B, T, Dm = u.shape
    f32 = jnp.float32
    uf = u.astype(f32)
    a_re, a_im = a_re.astype(f32), a_im.astype(f32)
    dt = jnp.exp(log_dt.astype(f32))[:, None]
    mag = jnp.exp(a_re * dt)
    abr = mag * jnp.cos(a_im * dt)
    abi = mag * jnp.sin(a_im * dt)
    den = a_re * a_re + a_im * a_im
    nr, ni = abr - 1.0, abi
    fr = (nr * a_re + ni * a_im) / den
    fi = (ni * a_re - nr * a_im) / den
    b_re, b_im = b_re.astype(f32), b_im.astype(f32)
    bbr = fr[..., None] * b_re - fi[..., None] * b_im
    bbi = fr[..., None] * b_im + fi[..., None] * b_re
    c_re, c_im = c_re.astype(f32), c_im.astype(f32)
    C = math.gcd(T, SSM_CHUNK)
    NC = T // C
    abr_c = jnp.broadcast_to(abr, (B, C, SSM_GROUPS, SSM_P))
    abi_c = jnp.broadcast_to(abi, (B, C, SSM_GROUPS, SSM_P))

    def step(carry, uc):
        hr, hi = carry
        bu_r = jnp.einsum('bcgk,gpk->bcgp', uc, bbr)
        bu_i = jnp.einsum('bcgk,gpk->bcgp', uc, bbi)
        bu_r = bu_r.at[:, 0].add(abr * hr - abi * hi)
        bu_i = bu_i.at[:, 0].add(abr * hi + abi * hr)
        _, _, sr, si = lax.associative_scan(_cplx_combine, (abr_c, abi_c, bu_r, bu_i), axis=1)
        y = jnp.einsum('bcgp,gkp->bcgk', sr, c_re) - jnp.einsum('bcgp,gkp->bcgk', si, c_im)
        return (sr[:, -1], si[:, -1]), y

    uc_all = uf.reshape(B, NC, C, SSM_GROUPS, SSM_GROUP).swapaxes(0, 1)
    (hr, hi), y = lax.scan(step, (h0_re.astype(f32), h0_im.astype(f32)), uc_all)
    y = y.swapaxes(0, 1).reshape(B, T, Dm) + d.astype(f32) * uf
    return y, hr, hi


def swiglu(x, w_gu, w_down):
    h = x @ w_gu
    g, u = jnp.split(h, 2, axis=-1)
    return (jax.nn.silu(g) * u) @ w_down


def even_mixer(x, s_ret0, h0, conv0, pos0, g, w_in, conv_w, conv_b, wa, ba, wx, bx, lam, w_out):
    B, T, _ = x.shape
    proj = rmsnorm(x, g) @ w_in
    q, k, v, gr, xl, yl = jnp.split(proj, 6, axis=-1)
    q = q.reshape(B, T, RET_HEADS, RET_DK)
    k = k.reshape(B, T, RET_HEADS, RET_DK)
    v = v.reshape(B, T, RET_HEADS, RET_DV)
    o, s_ret = retention(q, k, v, s0=s_ret0, pos0=pos0)
    o = o * lax.rsqrt(jnp.mean(o * o, axis=-1, keepdims=True) + EPS)
    o = o.reshape(B, T, RET_HEADS * RET_DV) * jax.nn.silu(gr.astype(jnp.float32))
    xc, conv_new = causal_conv(xl, conv0, conv_w, conv_b)
    hs, h_last = rglru(xc, h0, wa, ba, wx, bx, lam)
    lo = jax.nn.gelu(yl.astype(jnp.float32)) * hs
    out = jnp.concatenate([o, lo], axis=-1).astype(x.dtype) @ w_out
    return out, s_ret, h_last, conv_new


def odd_mixer(x, h0_re, h0_im, g, a_re, a_im, b_re, b_im, c_re, c_im, d, log_dt, w_glu, b_glu):
    u = rmsnorm(x, g)
    y, hr, hi = s5(u, h0_re, h0_im, a_re, a_im, b_re, b_im, c_re, c_im, d, log_dt)
    y = jax.nn.gelu(y).astype(x.dtype)
    z = y @ w_glu + b_glu
    z1, z2 = jnp.split(z, 2, axis=-1)
    return z1 * jax.nn.sigmoid(z2), hr, hi


def setup_inputs(seed: int = 0) -> dict:
    key = jax.random.key(seed)
    ks = iter(jax.random.split(key, 40))
    f32 = jnp.float32

    def nrm(shape, scale):
        return jax.random.normal(next(ks), shape, f32) * scale

    def gain(shape):
        return 1.0 + nrm(shape, 0.02)

    x_prompt = nrm((BATCH, SEQ, D_MODEL), 1.0)
    x_sample = nrm((DEC_BATCH, DEC_SEQ, D_MODEL), 1.0)
    state_ret = nrm((N_EVEN, DEC_BATCH, RET_HEADS, RET_DK, RET_DV), 0.1)
    state_lru = nrm((N_EVEN, DEC_BATCH, LRU_WIDTH), 0.5)
    state_conv = nrm((N_EVEN, DEC_BATCH, CONV_W - 1, LRU_WIDTH), 1.0)
    state_ssm_re = nrm((N_ODD, DEC_BATCH, SSM_GROUPS, SSM_P), 0.1)
    state_ssm_im = nrm((N_ODD, DEC_BATCH, SSM_GROUPS, SSM_P), 0.1)
    norm_mix_even = gain((N_EVEN, D_MODEL))
    w_in_even = nrm((N_EVEN, D_MODEL, IN_WIDTH), D_MODEL ** -0.5)
    lru_conv_w = nrm((N_EVEN, CONV_W, LRU_WIDTH), CONV_W ** -0.5)
    lru_conv_b = nrm((N_EVEN, LRU_WIDTH), 0.01)
    lru_wa = nrm((N_EVEN, LRU_HEADS, LRU_BLOCK, LRU_BLOCK), LRU_BLOCK ** -0.5)
    lru_ba = nrm((N_EVEN, LRU_WIDTH), 0.01)
    lru_wx = nrm((N_EVEN, LRU_HEADS, LRU_BLOCK, LRU_BLOCK), LRU_BLOCK ** -0.5)
    lru_bx = nrm((N_EVEN, LRU_WIDTH), 0.01)
    a_pow = jax.random.uniform(next(ks), (N_EVEN, LRU_WIDTH), f32, 0.9, 0.999)
    s = a_pow ** (1.0 / LRU_C)
    lru_lambda = jnp.log(s) - jnp.log1p(-s)
    w_out_even = nrm((N_EVEN, 2 * MIX_HALF, D_MODEL), (2 * MIX_HALF) ** -0.5)
    norm_mix_odd = gain((N_ODD, D_MODEL))
    ssm_a_re = -0.5 + nrm((N_ODD, SSM_GROUPS, SSM_P), 0.01)
    ssm_a_im = math.pi * jnp.arange(SSM_P, dtype=f32) + nrm((N_ODD, SSM_GROUPS, SSM_P), 0.01)
    ssm_b_re = nrm((N_ODD, SSM_GROUPS, SSM_P, SSM_GROUP), (2 * SSM_GROUP) ** -0.5)
    ssm_b_im = nrm((N_ODD, SSM_GROUPS, SSM_P, SSM_GROUP), (2 * SSM_GROUP) ** -0.5)
    ssm_c_re = nrm((N_ODD, SSM_GROUPS, SSM_GROUP, SSM_P), SSM_P ** -0.5)
    ssm_c_im = nrm((N_ODD, SSM_GROUPS, SSM_GROUP, SSM_P), SSM_P ** -0.5)
    ssm_d = nrm((N_ODD, D_MODEL), 0.5)
    ssm_log_dt = jax.random.uniform(next(ks), (N_ODD, SSM_GROUPS), f32, math.log(1e-3), math.log(1e-1))
    w_glu = nrm((N_ODD, D_MODEL, 2 * D_MODEL), D_MODEL ** -0.5)
    b_glu = nrm((N_ODD, 2 * D_MODEL), 0.01)
    norm_ffn = gain((DEPTH, D_MODEL))
    w_ffn_gu = nrm((DEPTH, D_MODEL, 2 * FFN_HIDDEN), D_MODEL ** -0.5)
    w_ffn_down = nrm((DEPTH, FFN_HIDDEN, D_MODEL), FFN_HIDDEN ** -0.5)
    norm_final = gain((D_MODEL,))
    return {
        'x_prompt': x_prompt, 'x_sample': x_sample,
        'state_ret': state_ret, 'state_lru': state_lru, 'state_conv': state_conv,
        'state_ssm_re': state_ssm_re, 'state_ssm_im': state_ssm_im,
        'norm_mix_even': norm_mix_even, 'w_in_even': w_in_even,
        'lru_conv_w': lru_conv_w, 'lru_conv_b': lru_conv_b,
        'lru_wa': lru_wa, 'lru_ba': lru_ba, 'lru_wx': lru_wx, 'lru_bx': lru_bx,
        'lru_lambda': lru_lambda, 'w_out_even': w_out_even,
        'norm_mix_odd': norm_mix_odd, 'ssm_a_re': ssm_a_re, 'ssm_a_im': ssm_a_im,
        'ssm_b_re': ssm_b_re, 'ssm_b_im': ssm_b_im, 'ssm_c_re': ssm_c_re, 'ssm_c_im': ssm_c_im,
        'ssm_d': ssm_d, 'ssm_log_dt': ssm_log_dt, 'w_glu': w_glu, 'b_glu': b_glu,
        'norm_ffn': norm_ffn, 'w_ffn_gu': w_ffn_gu, 'w_ffn_down': w_ffn_down,
        'norm_final': norm_final,
    }


def reference(x_prompt, x_sample, state_ret, state_lru, state_conv, state_ssm_re, state_ssm_im,
              norm_mix_even, w_in_even, lru_conv_w, lru_conv_b, lru_wa, lru_ba, lru_wx, lru_bx,
              lru_lambda, w_out_even, norm_mix_odd, ssm_a_re, ssm_a_im, ssm_b_re, ssm_b_im,
              ssm_c_re, ssm_c_im, ssm_d, ssm_log_dt, w_glu, b_glu, norm_ffn, w_ffn_gu, w_ffn_down,
              norm_final):
    def trunk(x, ret0, lru0, conv0, sre0, sim0, pos0):
        rets, lrus, convs, sres, sims = [], [], [], [], []
        for layer in range(DEPTH):
            if layer % 2 == 0:
                e = layer // 2
                out, s_r, s_l, s_c = even_mixer(
                    x, ret0[e], lru0[e], conv0[e], pos0, norm_mix_even[e], w_in_even[e],
                    lru_conv_w[e], lru_conv_b[e], lru_wa[e], lru_ba[e], lru_wx[e], lru_bx[e],
                    lru_lambda[e], w_out_even[e])
                rets.append(s_r)
                lrus.append(s_l)
                convs.append(s_c)
            else:
                o = layer // 2
                out, s_re, s_im = odd_mixer(
                    x, sre0[o], sim0[o], norm_mix_odd[o], ssm_a_re[o], ssm_a_im[o], ssm_b_re[o],
                    ssm_b_im[o], ssm_c_re[o], ssm_c_im[o], ssm_d[o], ssm_log_dt[o], w_glu[o], b_glu[o])
                sres.append(s_re)
                sims.append(s_im)
            x = x + out.astype(x.dtype)
            x = x + swiglu(rmsnorm(x, norm_ffn[layer]), w_ffn_gu[layer], w_ffn_down[layer])
        y = rmsnorm(x, norm_final)
        return (y,
                jnp.stack(rets).astype(state_ret.dtype),
                jnp.stack(lrus).astype(state_lru.dtype),
                jnp.stack(convs).astype(state_conv.dtype),
                jnp.stack(sres).astype(state_ssm_re.dtype),
                jnp.stack(sims).astype(state_ssm_im.dtype))

    bp = x_prompt.shape[0]
    ret0_p = jnp.zeros((N_EVEN, bp, RET_HEADS, RET_DK, RET_DV), state_ret.dtype)
    lru0_p = jnp.zeros((N_EVEN, bp, LRU_WIDTH), state_lru.dtype)
    conv0_p = jnp.zeros((N_EVEN, bp, CONV_W - 1, LRU_WIDTH), state_conv.dtype)
    sre0_p = jnp.zeros((N_ODD, bp, SSM_GROUPS, SSM_P), state_ssm_re.dtype)
    sim0_p = jnp.zeros((N_ODD, bp, SSM_GROUPS, SSM_P), state_ssm_im.dtype)

    y_prompt, ret_p, lru_p, conv_p, ssm_re_p, ssm_im_p = trunk(
        x_prompt, ret0_p, lru0_p, conv0_p, sre0_p, sim0_p, 0)
    y_sample, ret_s, lru_s, conv_s, ssm_re_s, ssm_im_s = trunk(
        x_sample, state_ret, state_lru, state_conv, state_ssm_re, state_ssm_im, PAST_LEN)
    return (y_prompt, y_sample, ret_p, ret_s, lru_p, lru_s, conv_p, conv_s,
            ssm_re_p, ssm_re_s, ssm_im_p, ssm_im_s)
```

```python
import math
import numpy as np
import concourse.bass as bass
import concourse.mybir as mybir
from concourse.bass_utils import run_bass_kernel_spmd

F32 = mybir.dt.float32
BF16 = mybir.dt.bfloat16
AF = mybir.ActivationFunctionType
ALU = mybir.AluOpType
SEM_MAX = 12000

D = 2048
KC = 16
FH = 5632
EPS = 1e-6
GAM = [1.0 - 2.0 ** (-5 - h) for h in range(4)]
GELU_C = 2.0 * math.sqrt(2.0 / math.pi)


class Dep:
    __slots__ = ("w", "r")

    def __init__(self):
        self.w = None
        self.r = []


class Eng:
    def __init__(self, fw, name, is_pe=False):
        self.fw = fw
        self.name = name
        self.is_pe = is_pe
        self.ops = []
        self.count = 0
        self.sems = []
        self.seen = {}

    def token(self):
        i, v = divmod(self.count - 1, SEM_MAX)
        while len(self.sems) <= i:
            self.sems.append(self.fw.nc.alloc_semaphore(f"s_{self.name}_{len(self.sems)}"))
        return (self.sems[i], v + 1, self)


class Slot:
    def __init__(self, fw, name):
        self.sem = fw.nc.alloc_semaphore(name)
        self.val = 0


class FW:
    def __init__(self, nc):
        self.nc = nc
        self.pe = Eng(self, "pe", True)
        self.act = Eng(self, "act")
        self.dve = Eng(self, "dve")
        self.pool = Eng(self, "pool")
        self.sp = Eng(self, "sp")
        self.nslot = 0

    def slot(self):
        self.nslot += 1
        return Slot(self, f"dq{self.nslot}")

    @staticmethod
    def _flat(lst):
        out = []
        for d in lst:
            if isinstance(d, (list, tuple)):
                out.extend(FW._flat(d))
            else:
                out.append(d)
        return out

    def _waits(self, eng, reads, writes):
        deps = {}
        for d in reads:
            if d.w is not None:
                deps[id(d.w)] = d.w
        for d in writes:
            if d.w is not None:
                deps[id(d.w)] = d.w
            for t in d.r:
                deps[id(t)] = t
        waits = []
        for t in deps.values():
            sem, val, src = t
            if src is eng and eng.is_pe:
                continue
            k = id(sem)
            if eng.seen.get(k, 0) >= val:
                continue
            eng.seen[k] = val
            waits.append((sem, val))
        return waits

    def op(self, eng, fn, reads=(), writes=()):
        reads = self._flat(reads)
        writes = self._flat(writes)
        waits = self._waits(eng, reads, writes)
        eng.count += 1
        tok = eng.token()
        for d in writes:
            d.w = tok
            d.r = []
        for d in reads:
            if d.w is not tok:
                d.r.append(tok)
        eng.ops.append((waits, fn, (tok[0], 1)))
        return tok

    def dma(self, eng, fn, slot, reads=(), writes=(), n=1):
        reads = self._flat(reads)
        writes = self._flat(writes)
        waits = self._waits(eng, reads, writes)
        slot.val += 16 * n
        tok = (slot.sem, slot.val, slot)
        for d in writes:
            d.w = tok
            d.r = []
        for d in reads:
            d.r.append(tok)
        eng.ops.append((waits, fn, (slot.sem, 16)))
        return tok

    def wait_tokens(self, eng, toks):
        waits = []
        for (sem, val, src) in toks:
            if eng.seen.get(id(sem), 0) >= val:
                continue
            eng.seen[id(sem)] = val
            waits.append((sem, val))
        eng.ops.append((waits, None, None))

    def emit(self):
        nc = self.nc

        def run(eng, e):
            for waits, fn, inc in eng.ops:
                for sem, val in waits:
                    e.wait_ge(sem, val)
                if fn is None:
                    continue
                r = fn(e)
                if isinstance(r, (list, tuple)):
                    for x in r:
                        x.then_inc(inc[0], inc[1])
                else:
                    r.then_inc(inc[0], inc[1])

        with nc.Block() as block:
            @block.tensor
            def _(e):
                run(self.pe, e)

            @block.scalar
            def _(e):
                run(self.act, e)

            @block.vector
            def _(e):
                run(self.dve, e)

            @block.gpsimd
            def _(e):
                run(self.pool, e)

            @block.sync
            def _(e):
                run(self.sp, e)


class T:
    def __init__(self, h):
        self.h = h
        self.d = Dep()

    def __getitem__(self, k):
        return self.h[k]


def sap(t, off, dims, parts=128, p0=0):
    row = 1
    for s in t.shape[1:]:
        row *= s
    return bass.AP(t, p0 * row + off, [[row, parts]] + [list(d) for d in dims])


def host_consts():
    f32 = np.float32
    c = {}
    inv = (1.0 / np.power(f32(10000.0), np.linspace(0.0, 1.0, 128, dtype=f32))).astype(f32)
    posP = np.arange(2048, dtype=f32)
    angP = (posP[None, :] * inv[:, None]).astype(f32).astype(np.float64)
    posS = (16384 + (np.arange(64) % 4)).astype(f32)
    angS = (posS[None, :] * inv[:, None]).astype(f32).astype(np.float64)
    c["cosP"] = np.cos(angP).astype(f32)
    c["sinP"] = np.sin(angP).astype(f32)
    c["cosS"] = np.cos(angS).astype(f32)
    c["sinS"] = np.sin(angS).astype(f32)
    g = np.array(GAM, dtype=np.float64)
    nP = np.arange(512) % 128
    nS = np.arange(64) % 4
    qdP = np.power(g[:, None], nP[None, :] + 1.0)
    qdS = np.power(g[:, None], nS[None, :] + 1.0)
    c["qdecP"] = np.broadcast_to(qdP[None, :, 0:128], (128, 4, 128)).astype(f32).copy()
    c["qdecS"] = np.broadcast_to(qdS[None], (128, 4, 64)).astype(f32).copy()
    m = np.arange(128)
    kv = np.zeros((128, 16), f32)
    kv[:, 0:4] = (np.power(g[None, :], -(m[:, None] + 1.0)) / 16.0)
    kv[:, 4:8] = (np.power(g[None, :], 127.0 - m[:, None]) / 16.0)
    kv[:, 8:12] = (np.power(g[None, :], -((m[:, None] % 4) + 1.0)) / 16.0)
    kv[:, 12:16] = (np.power(g[None, :], 3.0 - (m[:, None] % 4)) / 16.0)
    c["kvec"] = kv
    mk = np.zeros((128, 192), f32)
    mk[:, 0:128] = (m[None, :] >= m[:, None]).astype(f32)
    ms = np.arange(64)
    mk[0:64, 128:192] = ((ms[None, :] >= ms[:, None]) & (ms[None, :] // 4 == ms[:, None] // 4)).astype(f32)
    c["masks"] = mk
    oh = np.zeros((128, 16), f32)
    oh[0:64] = (ms[:, None] // 4 == np.arange(16)[None, :]).astype(f32)
    c["onehot"] = oh
    io = np.zeros((128, 256), f32)
    io[:, 0:128] = np.eye(128, dtype=f32)
    io[:, 128:256] = 1.0
    c["identones"] = io
    st = np.zeros((128, 8, 240), f32)
    for a in range(8):
        for j in range(16):
            st[a * 16 + j, a, 112 + j] = 1.0
    c["strips"] = st
    bm = np.zeros((128, 128), f32)
    for s_ in range(8):
        for t_ in range(s_, 8):
            bm[s_ * 16:(s_ + 1) * 16, t_ * 16:(t_ + 1) * 16] = 1.0
    c["bmask"] = bm
    return c


def prep_weights(inp):
    f32 = np.float32
    w = {}
    gains = [inp["norm_mix_even"][0], inp["norm_mix_even"][1], inp["norm_mix_odd"][0], inp["norm_mix_odd"][1],
             inp["norm_ffn"][0], inp["norm_ffn"][1], inp["norm_ffn"][2], inp["norm_ffn"][3], inp["norm_final"]]
    w["gains"] = np.ascontiguousarray(np.stack([np.asarray(g).reshape(16, 128).T for g in gains], axis=1)).astype(f32)
    lv = np.zeros((128, 2, 8, 8), f32)
    for e in range(2):
        for i in range(4):
            lv[:, e, :, i] = np.asarray(inp["lru_conv_w"][e, i]).reshape(8, 128).T
        lv[:, e, :, 4] = np.asarray(inp["lru_conv_b"][e]).reshape(8, 128).T
        lv[:, e, :, 5] = np.asarray(inp["lru_ba"][e]).reshape(8, 128).T
        lv[:, e, :, 6] = np.asarray(inp["lru_bx"][e]).reshape(8, 128).T
        lv[:, e, :, 7] = np.asarray(inp["lru_lambda"][e]).reshape(8, 128).T
    w["lruvec"] = lv
    w["ssmd"] = np.ascontiguousarray(np.stack([np.asarray(inp["ssm_d"][o]).reshape(16, 128).T for o in range(2)], axis=1)).astype(f32)
    w["bglu"] = np.ascontiguousarray(np.stack([np.asarray(inp["b_glu"][o]).reshape(32, 128).T for o in range(2)], axis=1)).astype(f32)
    for nm in ("ssm_a_re", "ssm_a_im"):
        a = np.asarray(inp[nm]).reshape(2, 2, 64, 64).transpose(0, 1, 3, 2).reshape(2, 128, 64)
        w[nm] = np.ascontiguousarray(a)
    w["ssm_log_dt"] = np.ascontiguousarray(np.asarray(inp["ssm_log_dt"]).reshape(2, 128))
    for nm in ("ssm_b_re", "ssm_b_im"):
        a = np.asarray(inp[nm]).reshape(2, 2, 64, 64, 16).transpose(0, 1, 3, 2, 4).reshape(2, 128, 64, 16)
        w[nm] = np.ascontiguousarray(a)
    for nm in ("ssm_c_re", "ssm_c_im"):
        a = np.asarray(inp[nm]).reshape(2, 2, 64, 16, 64).transpose(0, 1, 4, 2, 3).reshape(2, 128, 64, 16)
        w[nm] = np.ascontiguousarray(a)
    for nm in ("w_in_even", "w_out_even", "w_glu", "w_ffn_gu", "w_ffn_down", "lru_wa", "lru_wx"):
        w[nm] = np.ascontiguousarray(np.asarray(inp[nm], dtype=f32))
    return w


def build(cfg):
    tiles = cfg.get("tiles", [("p", 0), ("p", 1), ("p", 2), ("p", 3), ("s", 0)])
    nlayers = cfg.get("nlayers", 4)
    dbg = cfg.get("dbg", False)
    do_odd = cfg.get("do_odd", True)

    nc = bass.Bass("TRN2", target_bir_lowering=False)
    fw = FW(nc)
    PE, ACT, DVE, POOL, SP = fw.pe, fw.act, fw.dve, fw.pool, fw.sp

    def din(name, shape):
        return nc.dram_tensor(name, list(shape), F32, kind="ExternalInput")

    def dout(name, shape):
        return nc.dram_tensor(name, list(shape), F32, kind="ExternalOutput")

    xpT = din("xpT", [D, 2048])
    xsT = din("xsT", [D, 64])
    sret = din("sret", [2, 16, 4, 256, 256])
    slru = din("slru", [2, 1024, 16])
    sconv = din("sconv", [2, 1024, 16, 3])
    sssm = [din("sssm_re", [2, 128, 64, 16]), din("sssm_im", [2, 128, 64, 16])]
    w_in = din("w_in_even", [2, D, 6144])
    w_out = din("w_out_even", [2, D, D])
    w_glu = din("w_glu", [2, D, 2 * D])
    w_gu = din("w_ffn_gu", [4, D, 2 * FH])
    w_dn = din("w_ffn_down", [4, FH, D])
    lru_wa = din("lru_wa", [2, 8, 128, 128])
    lru_wx = din("lru_wx", [2, 8, 128, 128])
    gains_d = din("gains", [128, 9, 16])
    lruvec_d = din("lruvec", [128, 2, 8, 8])
    ssmd_d = din("ssmd", [128, 2, 16])
    bglu_d = din("bglu", [128, 2, 32])
    a_d = [din("ssm_a_re", [2, 128, 64]), din("ssm_a_im", [2, 128, 64])]
    ldt_d = din("ssm_log_dt", [2, 128])
    b_d = [din("ssm_b_re", [2, 128, 64, 16]), din("ssm_b_im", [2, 128, 64, 16])]
    c_d = [din("ssm_c_re", [2, 128, 64, 16]), din("ssm_c_im", [2, 128, 64, 16])]
    cst = {k: din("c_" + k, v.shape) for k, v in host_consts().items()}

    ypT = dout("ypT", [D, 2048])
    ysT = dout("ysT", [D, 64])
    o_retp = dout("o_retp", [2, 4, 256, 256])
    o_rets = dout("o_rets", [2, 16, 4, 256, 256])
    o_lrup = dout("o_lrup", [2, 128, 8])
    o_lrus = dout("o_lrus", [2, 1024, 16])
    o_convp = dout("o_convp", [2, 1024, 3])
    o_convs = dout("o_convs", [2, 1024, 16, 3])
    o_ssmp = [dout("o_ssmp_re", [2, 128, 64]), dout("o_ssmp_im", [2, 128, 64])]
    o_ssms = [dout("o_ssms_re", [2, 128, 64, 16]), dout("o_ssms_im", [2, 128, 64, 16])]
    if dbg:
        o_dbg = dout("o_dbg", [8, 128, 16, 512])

    def sb(name, shape, dt=F32):
        return T(nc.alloc_sbuf_tensor("sb_" + name, list(shape), dt))

    x = sb("x", [128, KC, 512])
    xn = sb("xn", [128, KC, 512], BF16)
    cat = sb("cat", [128, KC, 512], BF16)
    NW = 2
    wb = [sb(f"wb{i}", [128, KC, 512], BF16) for i in range(NW)]
    wslot = [fw.slot() for _ in range(NW)]
    wctr = [0]
    gains = sb("gains", [128, 9, 16])
    lruvec = sb("lruvec", [128, 2, 8, 8])
    ssmd = sb("ssmd", [128, 2, 16])
    bglu = sb("bglu", [128, 2, 32])
    wa_sb = sb("wa_sb", [128, 2, 8, 128], BF16)
    wx_sb = sb("wx_sb", [128, 2, 8, 128], BF16)
    kvec = sb("kvec", [128, 16])
    masks = sb("masks", [128, 192])
    onehot = sb("onehot", [128, 16])
    identones = sb("identones", [128, 256], BF16)
    qdecP = sb("qdecP", [128, 4, 128])
    qdecS = sb("qdecS", [128, 4, 64])
    strips = sb("strips", [128, 8, 240], BF16)
    bmask = sb("bmask", [128, 128])
    Pst = sb("Pst", [128, 2, 2, 64])
    apar = sb("apar", [128, 2, 2, 64])
    dtb = sb("dtb", [128, 2, 64])
    lsp = sb("lsp", [128, 2, 8, 2])
    Sst = sb("Sst", [128, 2, 4, 2, 256])
    Sbf = sb("Sbf", [128, 4, 2, 256], BF16)
    hst = sb("hst", [128, 2, 8])
    convst = sb("convst", [128, 2, 8, 3])
    ident = identones.h[:, 0:128]
    ones = identones.h[:, 128:256]

    cslot = fw.slot()
    cslot2 = fw.slot()
    cdeps = []
    cdeps2 = []

    def cload(t, src, cast=False):
        if cast:
            fw.dma(POOL, lambda e: e.dma_start(out=t.h[:], in_=src), cslot2, writes=[t.d])
            cdeps2.append(t.d)
        else:
            fw.dma(SP, lambda e: e.dma_start(out=t.h[:], in_=src), cslot, writes=[t.d])
            cdeps.append(t.d)

    cload(gains, gains_d.ap())
    cload(lruvec, lruvec_d.ap())
    cload(ssmd, ssmd_d.ap())
    cload(bglu, bglu_d.ap())
    cload(wa_sb, lru_wa.ap().rearrange("e n k j -> k e n j"), cast=True)
    cload(wx_sb, lru_wx.ap().rearrange("e n k j -> k e n j"), cast=True)
    cload(kvec, cst["kvec"].ap())
    cload(masks, cst["masks"].ap())
    cload(onehot, cst["onehot"].ap())
    cload(identones, cst["identones"].ap(), cast=True)
    cload(qdecP, cst["qdecP"].ap())
    cload(qdecS, cst["qdecS"].ap())
    cload(strips, cst["strips"].ap(), cast=True)
    cload(bmask, cst["bmask"].ap())
    for ri in range(2):
        fw.dma(SP, lambda e, ri=ri: e.dma_start(out=apar.h[:, ri, :, :], in_=a_d[ri].ap().rearrange("o p g -> p o g")), cslot, writes=[apar.d])
    for gh in range(2):
        for o in range(2):
            fw.dma(SP, lambda e, gh=gh, o=o: e.dma_start(out=dtb.h[gh * 64:(gh + 1) * 64, o, :], in_=bass.AP(ldt_d, o * 128 + gh * 64, [[0, 64], [1, 64]])), cslot, writes=[dtb.d])
    cdeps.append(apar.d)
    cdeps.append(dtb.d)
    final_tok = (cslot.sem, cslot.val, cslot)
    for d_ in cdeps:
        d_.w = final_tok
    final_tok2 = (cslot2.sem, cslot2.val, cslot2)
    for d_ in cdeps2:
        d_.w = final_tok2

    fw.op(ACT, lambda e: e.activation(out=dtb.h[:], in_=dtb.h[:], func=AF.Exp), reads=[dtb.d], writes=[dtb.d])
    sp_t = sb("sp_t", [128, 2, 8])
    fw.op(ACT, lambda e: e.activation(out=sp_t.h[:], in_=lruvec.h[:, :, :, 7], func=AF.Exp, scale=-1.0), reads=[lruvec.d], writes=[sp_t.d])
    fw.op(ACT, lambda e: e.activation(out=sp_t.h[:], in_=sp_t.h[:], func=AF.Ln, bias=1.0), reads=[sp_t.d], writes=[sp_t.d])
    fw.op(DVE, lambda e: e.tensor_scalar(out=lsp.h[:, :, :, 0], in0=sp_t.h[:], scalar1=-8.0, scalar2=None, op0=ALU.mult), reads=[sp_t.d], writes=[lsp.d])
    fw.op(DVE, lambda e: e.tensor_scalar(out=lsp.h[:, :, :, 1], in0=sp_t.h[:], scalar1=-16.0, scalar2=None, op0=ALU.mult), reads=[sp_t.d], writes=[lsp.d])

    banks = [T(nc.alloc_psum_tensor(f"pb{i}", [128, 512], F32)) for i in range(8)]
    bctr = [0]

    def bank():
        b = banks[2 + bctr[0] % 6]
        bctr[0] += 1
        return b

    NPG = 116
    arena = nc.alloc_sbuf_tensor("arena", [128, NPG * 128], F32)
    pdeps = [Dep() for _ in range(NPG)]

    class Ph:
        def __init__(self, p0=0):
            self.p = p0

        def al(self, shape, dt=F32):
            nel = 1
            for s_ in shape[1:]:
                nel *= s_
            nb = nel * (4 if dt == F32 else 2)
            npg = (nb + 511) // 512
            assert self.p + npg <= NPG, (self.p, npg)
            v = arena[:, self.p * 128:(self.p + npg) * 128]
            if dt != F32:
                v = v.bitcast(dt)
            v = v[:, 0:nel]
            if len(shape) == 3:
                v = v.rearrange("p (a b) -> p a b", a=shape[1])
            elif len(shape) == 4:
                v = v.rearrange("p (a b c) -> p a b c", a=shape[1], b=shape[2])
            t = T(v)
            t.d = pdeps[self.p:self.p + npg]
            self.p += npg
            return t

    ph = Ph()
    qraw = ph.al([128, 2, 512])
    kraw = ph.al([128, 2, 512])
    qr = ph.al([128, 2, 512], BF16)
    kr = ph.al([128, 2, 512], BF16)
    qdd = ph.al([128, 2, 512], BF16)
    t1 = ph.al([128, 2, 512])
    t2 = ph.al([128, 2, 512])
    gate = ph.al([128, 2, 512], BF16)
    vtok = ph.al([128, 4, 1024], BF16)
    kdtok = ph.al([128, 4, 256], BF16)
    kdm = [ph.al([128, 256], BF16) for i in range(2)]
    PT = [ph.al([128, 128], BF16) for i in range(2)]
    oT = ph.al([128, 2, 512])
    sq = ph.al([128, 2, 512], BF16)
    S0f = [ph.al([128, 2, 256]) for i in range(3)]
    S0b = [ph.al([128, 2, 256], BF16) for i in range(3)]
    Sout = [ph.al([128, 2, 256]) for i in range(3)]
    p_ret_end = ph.p
    ph = Ph()
    xp_ = ph.al([128, 3 + 512 + 1])
    xps = ph.al([128, 16, 7])
    xc = ph.al([128, 512])
    xcb = ph.al([128, 512], BF16)
    rg = ph.al([128, 512])
    ig = ph.al([128, 512])
    av = ph.al([128, 512])
    mv = ph.al([128, 512])
    hT = ph.al([128, 512])
    h0s = ph.al([128, 8, 16])
    cv0s = ph.al([128, 8, 16, 3])
    lrus_o = ph.al([128, 8, 16])
    convs_o = ph.al([128, 8, 16, 3])
    y32 = ph.al([128, 512])
    g1 = ph.al([128, 512])
    g2 = ph.al([128, 512])
    ph = Ph()
    hact = [ph.al([128, 4, 512], BF16) for i in range(2)]
    sgt = [ph.al([128, 512]) for i in range(2)]
    ph = Ph()
    Wneg = ph.al([128, 8, 2, 64])
    Wpos = ph.al([128, 9, 2, 64])
    frfi = ph.al([128, 2, 64])
    wtmp = [ph.al([128, 64]) for i in range(8)]
    braw = [ph.al([128, 8, 16]) for i in range(2)]
    craw = [ph.al([128, 8, 16]) for i in range(2)]
    bbar = [ph.al([128, 8, 16]) for i in range(2)]
    tA = ph.al([128, 1024])
    tB = ph.al([128, 1024])
    Rp = [ph.al([128, 8, 8, 16], BF16) for i in range(2)]
    Op = [ph.al([128, 8, 8, 16], BF16) for i in range(2)]
    MTs = ph.al([128, 16, 128], BF16)
    QTs = [ph.al([128, 128], BF16) for i in range(2)]
    Ub = ph.al([128, 16, 64], BF16)
    Yb = ph.al([128, 16, 64], BF16)
    Aar = ph.al([128, 2, 8, 65])
    Abf = ph.al([128, 2, 8, 64], BF16)
    sS = ph.al([128, 2, 8])
    sT1 = ph.al([128, 2, 8])
    sT2 = ph.al([128, 2, 8])
    h0s5 = ph.al([128, 2, 8, 16])
    Pp = ph.al([128, 2, 8, 16])
    Pf = ph.al([128, 2, 8, 16])
    y32b = ph.al([128, 512])
    g1b = ph.al([128, 512])
    g2b = ph.al([128, 512])
    rstd = sb("rstd", [128, 512])
    cosT = sb("cosT", [128, 512])
    sinT = sb("sinT", [128, 512])
    tslot = fw.slot()
    tslot2 = fw.slot()
    stslot2 = fw.slot()
    xslot = fw.slot()
    S0slot = [fw.slot() for _ in range(3)]
    S0bslot = [fw.slot() for _ in range(3)]
    Soslot = [fw.slot() for _ in range(3)]
    stslot = fw.slot()
    oslots = {}

    def oslot(k):
        if k not in oslots:
            oslots[k] = fw.slot()
        return oslots[k]

    out_toks = []

    def load_w(dram_ap_list):
        i = wctr[0] % NW
        wctr[0] += 1
        t = wb[i]

        def fn(e, t=t, lst=dram_ap_list):
            r = []
            for src, dst in lst:
                r.append(e.dma_start(out=dst(t.h), in_=src))
            return r
        fw.dma(POOL, fn, wslot[i], writes=[t.d], n=len(dram_ap_list))
        return t

    def wsrc(wd, c0, ncol):
        return wd[:, c0:c0 + ncol].rearrange("(kc p) j -> p kc j", p=128)

    def rmsnorm(gi, Tn):
        fw.op(ACT, lambda e: e.activation(out=cat.h[:, :, 0:Tn], in_=x.h[:, :, 0:Tn], func=AF.Square), reads=[x.d], writes=[cat.d])
        b = bank()

        def mm(e):
            r = None
            for kc in range(KC):
                r = e.matmul(b.h[:, 0:Tn], lhsT=ones, rhs=cat.h[:, kc, 0:Tn], start=(kc == 0), stop=(kc == KC - 1))
            return r
        fw.op(PE, mm, reads=[cat.d, identones.d], writes=[b.d])
        fw.op(ACT, lambda e: e.activation(out=rstd.h[:, 0:Tn], in_=b.h[:, 0:Tn], func=AF.Sqrt, scale=1.0 / D, bias=EPS), reads=[b.d], writes=[rstd.d])
        fw.op(DVE, lambda e: e.reciprocal(out=rstd.h[:, 0:Tn], in_=rstd.h[:, 0:Tn]), reads=[rstd.d], writes=[rstd.d])
        for kc in range(KC):
            eng = DVE
            fw.op(eng, lambda e, kc=kc: e.scalar_tensor_tensor(out=xn.h[:, kc, 0:Tn], in0=x.h[:, kc, 0:Tn], scalar=gains.h[:, gi, kc:kc + 1],
                                                               in1=rstd.h[:, 0:Tn], op0=ALU.mult, op1=ALU.mult),
                  reads=[x.d, gains.d, rstd.d], writes=[xn.d])

    def proj_fm(wt, col0, src, Tn, nk=KC):
        b = bank()

        def mm(e):
            r = None
            for kc in range(nk):
                r = e.matmul(b.h[:, 0:Tn], lhsT=wt.h[:, kc, col0:col0 + 128], rhs=src.h[:, kc, 0:Tn], start=(kc == 0), stop=(kc == nk - 1))
            return r
        fw.op(PE, mm, reads=[wt.d, src.d], writes=[b.d])
        return b

    def add_to_x(b, oc, Tn):
        fw.op(DVE, lambda e: e.tensor_tensor(out=x.h[:, oc, 0:Tn], in0=b.h[:, 0:Tn], in1=x.h[:, oc, 0:Tn], op=ALU.add), reads=[b.d, x.d], writes=[x.d])

    def ffn(layer, Tn):
        rmsnorm(4 + layer, Tn)
        wg = w_gu.ap()[layer]
        wd = w_dn.ap()[layer]
        for hb in range(FH // 512):
            tg = load_w([(wsrc(wg, hb * 512, 512), lambda h: h[:])])
            tu = load_w([(wsrc(wg, FH + hb * 512, 512), lambda h: h[:])])
            ha = hact[hb % 2]
            for m in range(4):
                bg = proj_fm(tg, m * 128, xn, Tn)
                bu = proj_fm(tu, m * 128, xn, Tn)
                sg = sgt[m % 2]
                fw.op(ACT, lambda e, bg=bg, sg=sg: e.activation(out=sg.h[:, 0:Tn], in_=bg.h[:, 0:Tn], func=AF.Silu), reads=[bg.d], writes=[sg.d])
                fw.op(DVE, lambda e, bu=bu, sg=sg, ha=ha, m=m: e.tensor_tensor(out=ha.h[:, m, 0:Tn], in0=bu.h[:, 0:Tn], in1=sg.h[:, 0:Tn], op=ALU.mult),
                      reads=[bu.d, sg.d], writes=[ha.d])
            td = load_w([(wd[hb * 512:(hb + 1) * 512, :].rearrange("(kc p) j -> p kc j", p=128), lambda h: h[:].rearrange("p a b -> p (a b)").rearrange("p (kc j) -> p kc j", kc=4))])
            tdv = td.h[:].rearrange("p a b -> p (a b)").rearrange("p (kc j) -> p kc j", kc=4)
            for oc in range(KC):
                b = bank()

                def mm(e, b=b, oc=oc, ha=ha, tdv=tdv):
                    r = None
                    for kc in range(4):
                        r = e.matmul(b.h[:, 0:Tn], lhsT=tdv[:, kc, oc * 128:(oc + 1) * 128], rhs=ha.h[:, kc, 0:Tn], start=(kc == 0), stop=(kc == 3))
                    return r
                fw.op(PE, mm, reads=[td.d, ha.d], writes=[b.d])
                add_to_x(b, oc, Tn)

    def even_layer(e_, kind, ti, Tn):
        samp = kind == "s"
        NTC = 1 if samp else 4
        TR = 64 if samp else 128
        rmsnorm(e_, Tn)
        wi = w_in.ap()[e_]
        qdec = qdecS if samp else qdecP
        kv0 = 8 if samp else 0
        gd = [GAM[h] ** (4 if samp else 128) for h in range(4)]
        for half in range(2):
            tv = load_w([(wsrc(wi, 2048 + half * 512, 512), lambda h: h[:])])
            for tc in range(NTC):
                b = bank()

                def mm(e, b=b, tc=tc, tv=tv):
                    r = None
                    for kc in range(KC):
                        r = e.matmul(b.h[0:TR, :], lhsT=xn.h[:, kc, tc * 128:tc * 128 + TR], rhs=tv.h[:, kc, :], start=(kc == 0), stop=(kc == KC - 1))
                    return r
                fw.op(PE, mm, reads=[tv.d, xn.d], writes=[b.d])
                fw.op(ACT, lambda e, b=b, tc=tc, half=half: e.activation(out=vtok.h[0:TR, tc, half * 512:(half + 1) * 512], in_=b.h[0:TR, :], func=AF.Copy),
                      reads=[b.d], writes=[vtok.d])
        def _head(h):
            tqk = load_w([(wsrc(wi, h * 256, 256), lambda hh: hh[:, :, 0:256]), (wsrc(wi, 1024 + h * 256, 256), lambda hh: hh[:, :, 256:512])])
            for dc in range(2):
                bq = proj_fm(tqk, dc * 128, xn, Tn)
                fw.op(ACT, lambda e, bq=bq, dc=dc: e.activation(out=qraw.h[:, dc, 0:Tn], in_=bq.h[:, 0:Tn], func=AF.Copy), reads=[bq.d], writes=[qraw.d])
                bk = proj_fm(tqk, 256 + dc * 128, xn, Tn)
                fw.op(ACT, lambda e, bk=bk, dc=dc: e.activation(out=kraw.h[:, dc, 0:Tn], in_=bk.h[:, 0:Tn], func=AF.Copy), reads=[bk.d], writes=[kraw.d])
            tg = load_w([(wsrc(wi, 3072 + h * 256, 256), lambda hh: hh[:, :, 0:256])])
            for dc in range(2):
                bg = proj_fm(tg, dc * 128, xn, Tn)
                fw.op(ACT, lambda e, bg=bg, dc=dc: e.activation(out=gate.h[:, dc, 0:Tn], in_=bg.h[:, 0:Tn], func=AF.Silu), reads=[bg.d], writes=[gate.d])
            for raw, outt, eng in ((qraw, qr, DVE), (kraw, kr, POOL)):
                for dc in range(2):
                    fw.op(eng, lambda e, raw=raw, dc=dc: e.tensor_tensor(out=t1.h[:, dc, 0:Tn], in0=raw.h[:, dc, 0:Tn], in1=cosT.h[:, 0:Tn], op=ALU.mult), reads=[raw.d, cosT.d], writes=[t1.d])
                    fw.op(eng, lambda e, raw=raw, dc=dc: e.tensor_tensor(out=t2.h[:, dc, 0:Tn], in0=raw.h[:, dc, 0:Tn], in1=sinT.h[:, 0:Tn], op=ALU.mult), reads=[raw.d, sinT.d], writes=[t2.d])
                fw.op(eng, lambda e, outt=outt: e.tensor_tensor(out=outt.h[:, 0, 0:Tn], in0=t1.h[:, 0, 0:Tn], in1=t2.h[:, 1, 0:Tn], op=ALU.subtract), reads=[t1.d, t2.d], writes=[outt.d])
                fw.op(eng, lambda e, outt=outt: e.tensor_tensor(out=outt.h[:, 1, 0:Tn], in0=t1.h[:, 1, 0:Tn], in1=t2.h[:, 0, 0:Tn], op=ALU.add), reads=[t1.d, t2.d], writes=[outt.d])
            if samp:
                qdb = sap(qdecS.h, h * 64, [[0, 2], [1, 64]])
                fw.op(DVE, lambda e, qdb=qdb: e.tensor_tensor(out=qdd.h[:, :, 0:64], in0=qr.h[:, :, 0:64], in1=qdb, op=ALU.mult), reads=[qr.d, qdecS.d], writes=[qdd.d])
            else:
                qdb = sap(qdecP.h, h * 128, [[0, 2], [0, 4], [1, 128]])
                fw.op(DVE, lambda e, qdb=qdb: e.tensor_tensor(out=qdd.h[:, :, :].rearrange("p a (c n) -> p a c n", c=4), in0=qr.h[:, :, :].rearrange("p a (c n) -> p a c n", c=4), in1=qdb, op=ALU.mult), reads=[qr.d, qdecP.d], writes=[qdd.d])
            for tc in range(NTC):
                b = bank()
                bb = b.h[:, 0:128].bitcast(BF16)

                def tr(e, tc=tc, bb=bb):
                    r = None
                    for dc in range(2):
                        r = e.transpose(out=bb[0:TR, dc * 128:(dc + 1) * 128], in_=kr.h[:, dc, tc * 128:tc * 128 + TR], identity=ident)
                    return r
                fw.op(PE, tr, reads=[kr.d, identones.d], writes=[b.d])
                fw.op(DVE, lambda e, tc=tc, bb=bb: e.tensor_scalar(out=kdtok.h[0:TR, tc, :], in0=bb[0:TR, :], scalar1=kvec.h[0:TR, kv0 + 4 + h:kv0 + 5 + h], scalar2=None, op0=ALU.mult),
                      reads=[b.d, kvec.d], writes=[kdtok.d])
            po = [banks[0], banks[1]]
            for c in range(NTC):
                bs = bank()

                def sc(e, bs=bs, c=c):
                    r = None
                    for dc in range(2):
                        r = e.matmul(bs.h[0:TR, 0:TR], lhsT=kr.h[:, dc, c * 128:c * 128 + TR], rhs=qdd.h[:, dc, c * 128:c * 128 + TR], start=(dc == 0), stop=(dc == 1))
                    return r
                fw.op(PE, sc, reads=[kr.d, qdd.d], writes=[bs.d])
                pt = PT[c % 2]
                moff = 128 if samp else 0
                fw.op(DVE, lambda e, bs=bs, pt=pt: e.scalar_tensor_tensor(out=pt.h[0:TR, 0:TR], in0=bs.h[0:TR, 0:TR], scalar=kvec.h[0:TR, kv0 + h:kv0 + h + 1],
                                                                             in1=masks.h[0:TR, moff:moff + TR], op0=ALU.mult, op1=ALU.mult),
                      reads=[bs.d, kvec.d, masks.d], writes=[pt.d])
                if not samp:
                    for ec in range(2):
                        def om(e, ec=ec, c=c, pt=pt):
                            e.matmul(po[ec].h[:, c * 128:(c + 1) * 128], lhsT=vtok.h[:, c, h * 256 + ec * 128:h * 256 + (ec + 1) * 128], rhs=pt.h[:, :], start=True, stop=False)
                            r = None
                            for dc in range(2):
                                r = e.matmul(po[ec].h[:, c * 128:(c + 1) * 128], lhsT=Sbf.h[:, h, dc, ec * 128:(ec + 1) * 128], rhs=qdd.h[:, dc, c * 128:(c + 1) * 128], start=False, stop=(dc == 1))
                            return r
                        fw.op(PE, om, reads=[vtok.d, pt.d, Sbf.d, qdd.d], writes=[po[ec].d])
                    bS = bank()

                    def su(e, bS=bS, c=c):
                        r = None
                        for dc in range(2):
                            r = e.matmul(bS.h[:, dc * 256:(dc + 1) * 256], lhsT=kdtok.h[:, c, dc * 128:(dc + 1) * 128], rhs=vtok.h[:, c, h * 256:(h + 1) * 256], start=True, stop=True)
                        return r
                    fw.op(PE, su, reads=[kdtok.d, vtok.d], writes=[bS.d])
                    fw.op(DVE, lambda e, bS=bS: e.scalar_tensor_tensor(out=Sst.h[:, e_, h, :, :], in0=Sst.h[:, e_, h, :, :], scalar=gd[h], in1=bS.h[:, :].rearrange("p (a b) -> p a b", a=2), op0=ALU.mult, op1=ALU.add),
                          reads=[bS.d, Sst.d], writes=[Sst.d])
                    fw.op(ACT, lambda e: e.activation(out=Sbf.h[:, h, :, :], in_=Sst.h[:, e_, h, :, :], func=AF.Copy), reads=[Sst.d], writes=[Sbf.d])
                else:
                    for ec in range(2):
                        fw.op(PE, lambda e, ec=ec, pt=pt: e.matmul(po[ec].h[:, 0:64], lhsT=vtok.h[0:64, 0, h * 256 + ec * 128:h * 256 + (ec + 1) * 128], rhs=pt.h[0:64, 0:64], start=True, stop=True),
                              reads=[vtok.d, pt.d], writes=[po[ec].d])
                    for j in range(16):
                        si = (h * 16 + j) % 3
                        s0f, s0b, so_ = S0f[si], S0b[si], Sout[si]
                        src = sret.ap()[e_, j, h].rearrange("(dc p) e -> p dc e", p=128)
                        fw.dma(SP, lambda e, s0f=s0f, src=src: e.dma_start(out=s0f.h[:], in_=src), S0slot[si], writes=[s0f.d])
                        fw.dma(POOL, lambda e, s0b=s0b, src=src: e.dma_start(out=s0b.h[:], in_=src), S0bslot[si], writes=[s0b.d])
                        for ec in range(2):
                            def im(e, ec=ec, j=j, s0b=s0b):
                                r = None
                                for dc in range(2):
                                    r = e.matmul(po[ec].h[:, 4 * j:4 * j + 4], lhsT=s0b.h[:, dc, ec * 128:(ec + 1) * 128], rhs=qdd.h[:, dc, 4 * j:4 * j + 4], start=False, stop=(dc == 1), skip_group_check=True)
                                return r
                            fw.op(PE, im, reads=[s0b.d, qdd.d], writes=[po[ec].d])
                        km = kdm[j % 2]
                        fw.op(DVE, lambda e, km=km, j=j: e.tensor_scalar(out=km.h[0:64, :], in0=kdtok.h[0:64, 0, :], scalar1=onehot.h[0:64, j:j + 1], scalar2=None, op0=ALU.mult),
                              reads=[kdtok.d, onehot.d], writes=[km.d])
                        bS = bank()

                        def su(e, bS=bS, km=km):
                            r = None
                            for dc in range(2):
                                r = e.matmul(bS.h[:, dc * 256:(dc + 1) * 256], lhsT=km.h[0:64, dc * 128:(dc + 1) * 128], rhs=vtok.h[0:64, 0, h * 256:(h + 1) * 256], start=True, stop=True)
                            return r
                        fw.op(PE, su, reads=[km.d, vtok.d], writes=[bS.d])
                        fw.op(DVE, lambda e, bS=bS, s0f=s0f, so_=so_: e.scalar_tensor_tensor(out=so_.h[:], in0=s0f.h[:], scalar=gd[h], in1=bS.h[:, :].rearrange("p (a b) -> p a b", a=2), op0=ALU.mult, op1=ALU.add),
                              reads=[bS.d, s0f.d], writes=[so_.d])
                        dst = o_rets.ap()[e_, j, h].rearrange("(dc p) e -> p dc e", p=128)
                        out_toks.append(fw.dma(SP, lambda e, so_=so_, dst=dst: e.dma_start(out=dst, in_=so_.h[:]), Soslot[si], reads=[so_.d]))
            for ec in range(2):
                fw.op(ACT, lambda e, ec=ec: e.activation(out=oT.h[:, ec, 0:Tn], in_=po[ec].h[:, 0:Tn], func=AF.Copy), reads=[po[ec].d], writes=[oT.d])
            fw.op(ACT, lambda e: e.activation(out=sq.h[:, :, 0:Tn], in_=oT.h[:, :, 0:Tn], func=AF.Square), reads=[oT.d], writes=[sq.d])
            bn = bank()

            def nm(e, bn=bn):
                r = None
                for ec in range(2):
                    r = e.matmul(bn.h[:, 0:Tn], lhsT=ones, rhs=sq.h[:, ec, 0:Tn], start=(ec == 0), stop=(ec == 1))
                return r
            fw.op(PE, nm, reads=[sq.d, identones.d], writes=[bn.d])
            fw.op(ACT, lambda e, bn=bn: e.activation(out=rstd.h[:, 0:Tn], in_=bn.h[:, 0:Tn], func=AF.Sqrt, scale=1.0 / 256, bias=EPS), reads=[bn.d], writes=[rstd.d])
            fw.op(DVE, lambda e: e.reciprocal(out=rstd.h[:, 0:Tn], in_=rstd.h[:, 0:Tn]), reads=[rstd.d], writes=[rstd.d])
            for ec in range(2):
                fw.op(DVE, lambda e, ec=ec: e.tensor_tensor(out=t1.h[:, ec, 0:Tn], in0=oT.h[:, ec, 0:Tn], in1=rstd.h[:, 0:Tn], op=ALU.mult), reads=[oT.d, rstd.d], writes=[t1.d])
                fw.op(DVE, lambda e, ec=ec: e.tensor_tensor(out=cat.h[:, 2 * h + ec, 0:Tn], in0=t1.h[:, ec, 0:Tn], in1=gate.h[:, ec, 0:Tn], op=ALU.mult), reads=[t1.d, gate.d], writes=[cat.d])
            if (not samp) and ti == 3:
                dst = o_retp.ap()[e_, h].rearrange("(dc p) e -> p dc e", p=128)
                out_toks.append(fw.dma(SP, lambda e, dst=dst: e.dma_start(out=dst, in_=Sst.h[:, e_, h, :, :]), oslot(("retp", e_, h)), reads=[Sst.d]))
        for h_ in range(4):
            _head(h_)
        if samp:
            fw.dma(SP, lambda e: e.dma_start(out=h0s.h[:], in_=slru.ap()[e_].rearrange("(n p) j -> p n j", p=128)), stslot, writes=[h0s.d])
            fw.dma(SP, lambda e: e.dma_start(out=cv0s.h[:], in_=sconv.ap()[e_].rearrange("(n p) j i -> p n j i", p=128)), stslot2, writes=[cv0s.d])
        def _blk(nb):
            txy = load_w([(wsrc(wi, 4096 + nb * 128, 128), lambda hh: hh[:, :, 0:128]), (wsrc(wi, 5120 + nb * 128, 128), lambda hh: hh[:, :, 128:256])])
            bx_ = proj_fm(txy, 0, xn, Tn)
            by_ = proj_fm(txy, 128, xn, Tn)
            lv = lambda k: lruvec.h[:, e_, nb, k:k + 1]
            if not samp:
                fw.op(ACT, lambda e, bx_=bx_: e.activation(out=xp_.h[:, 3:3 + Tn], in_=bx_.h[:, 0:Tn], func=AF.Copy), reads=[bx_.d], writes=[xp_.d])
                if ti == 0:
                    fw.op(DVE, lambda e: e.memset(xp_.h[:, 0:3], 0.0), writes=[xp_.d])
                else:
                    fw.op(DVE, lambda e, nb=nb: e.tensor_copy(out=xp_.h[:, 0:3], in_=convst.h[:, e_, nb, :]), reads=[convst.d], writes=[xp_.d])
                xin = lambda i: xp_.h[:, i:i + Tn]
                xco = xc.h[:, 0:Tn]
            else:
                fw.op(ACT, lambda e, bx_=bx_: e.activation(out=xps.h[:, :, 3:7], in_=bx_.h[:, 0:64].rearrange("p (j t) -> p j t", t=4), func=AF.Copy), reads=[bx_.d], writes=[xps.d])
                fw.op(DVE, lambda e, nb=nb: e.tensor_copy(out=xps.h[:, :, 0:3], in_=cv0s.h[:, nb, :, :]), reads=[cv0s.d], writes=[xps.d])
                xin = lambda i: xps.h[:, :, i:i + 4]
                xco = xc.h[:, 0:64].rearrange("p (j t) -> p j t", t=4)
            fw.op(DVE, lambda e, xin=xin, xco=xco, nb=nb: e.tensor_scalar(out=xco, in0=xin(0), scalar1=lruvec.h[:, e_, nb, 0:1], scalar2=lruvec.h[:, e_, nb, 4:5], op0=ALU.mult, op1=ALU.add),
                  reads=[xp_.d, xps.d, lruvec.d], writes=[xc.d])
            for i in range(1, 4):
                fw.op(DVE, lambda e, xin=xin, xco=xco, nb=nb, i=i: e.scalar_tensor_tensor(out=xco, in0=xin(i), scalar=lruvec.h[:, e_, nb, i:i + 1], in1=xco, op0=ALU.mult, op1=ALU.add),
                      reads=[xp_.d, xps.d, lruvec.d, xc.d], writes=[xc.d])
            if not samp:
                fw.op(POOL, lambda e, nb=nb: e.tensor_copy(out=convst.h[:, e_, nb, :], in_=xp_.h[:, Tn:Tn + 3]), reads=[xp_.d], writes=[convst.d])
            else:
                fw.op(POOL, lambda e, nb=nb: e.tensor_copy(out=convs_o.h[:, nb, :, :], in_=xps.h[:, :, 4:7]), reads=[xps.d], writes=[convs_o.d])
            fw.op(ACT, lambda e: e.activation(out=xcb.h[:, 0:Tn], in_=xc.h[:, 0:Tn], func=AF.Copy), reads=[xc.d], writes=[xcb.d])
            br = bank()
            fw.op(PE, lambda e, br=br, nb=nb: e.matmul(br.h[:, 0:Tn], lhsT=wa_sb.h[:, e_, nb, :], rhs=xcb.h[:, 0:Tn], start=True, stop=True), reads=[wa_sb.d, xcb.d], writes=[br.d])
            bi = bank()
            fw.op(PE, lambda e, bi=bi, nb=nb: e.matmul(bi.h[:, 0:Tn], lhsT=wx_sb.h[:, e_, nb, :], rhs=xcb.h[:, 0:Tn], start=True, stop=True), reads=[wx_sb.d, xcb.d], writes=[bi.d])
            fw.op(ACT, lambda e, br=br, nb=nb: e.activation(out=rg.h[:, 0:Tn], in_=br.h[:, 0:Tn], func=AF.Sigmoid, bias=lruvec.h[:, e_, nb, 5:6]), reads=[br.d, lruvec.d], writes=[rg.d])
            fw.op(ACT, lambda e, bi=bi, nb=nb: e.activation(out=ig.h[:, 0:Tn], in_=bi.h[:, 0:Tn], func=AF.Sigmoid, bias=lruvec.h[:, e_, nb, 6:7]), reads=[bi.d, lruvec.d], writes=[ig.d])
            fw.op(ACT, lambda e, nb=nb: e.activation(out=av.h[:, 0:Tn], in_=rg.h[:, 0:Tn], func=AF.Exp, scale=lsp.h[:, e_, nb, 0:1]), reads=[rg.d, lsp.d], writes=[av.d])
            fw.op(ACT, lambda e, nb=nb: e.activation(out=mv.h[:, 0:Tn], in_=rg.h[:, 0:Tn], func=AF.Exp, scale=lsp.h[:, e_, nb, 1:2]), reads=[rg.d, lsp.d], writes=[mv.d])
            fw.op(ACT, lambda e: e.activation(out=mv.h[:, 0:Tn], in_=mv.h[:, 0:Tn], func=AF.Sqrt, scale=-1.0, bias=1.0), reads=[mv.d], writes=[mv.d])
            fw.op(DVE, lambda e: e.tensor_tensor(out=mv.h[:, 0:Tn], in0=mv.h[:, 0:Tn], in1=ig.h[:, 0:Tn], op=ALU.mult), reads=[mv.d, ig.d], writes=[mv.d])
            fw.op(DVE, lambda e: e.tensor_tensor(out=mv.h[:, 0:Tn], in0=mv.h[:, 0:Tn], in1=xc.h[:, 0:Tn], op=ALU.mult), reads=[mv.d, xc.d], writes=[mv.d])
            if not samp:
                init = 0.0 if ti == 0 else hst.h[:, e_, nb:nb + 1]
                fw.op(DVE, lambda e, init=init: e.tensor_tensor_scan(out=hT.h[:, 0:Tn], data0=av.h[:, 0:Tn], data1=mv.h[:, 0:Tn], initial=init, op0=ALU.mult, op1=ALU.add),
                      reads=[av.d, mv.d, hst.d], writes=[hT.d])
                fw.op(POOL, lambda e, nb=nb: e.tensor_copy(out=hst.h[:, e_, nb:nb + 1], in_=hT.h[:, Tn - 1:Tn]), reads=[hT.d], writes=[hst.d])
            else:
                av3 = av.h[:, 0:64].rearrange("p (j t) -> p j t", t=4)
                mv3 = mv.h[:, 0:64].rearrange("p (j t) -> p j t", t=4)
                fw.op(DVE, lambda e, nb=nb: e.tensor_tensor(out=g1.h[:, 0:16], in0=av3[:, :, 0], in1=h0s.h[:, nb, :], op=ALU.mult), reads=[av.d, h0s.d], writes=[g1.d])
                fw.op(DVE, lambda e: e.tensor_tensor(out=mv3[:, :, 0], in0=mv3[:, :, 0], in1=g1.h[:, 0:16], op=ALU.add), reads=[mv.d, g1.d], writes=[mv.d])
                fw.op(DVE, lambda e: e.memset(av3[:, :, 0], 0.0), reads=[g1.d], writes=[av.d])
                fw.op(DVE, lambda e: e.tensor_tensor_scan(out=hT.h[:, 0:64], data0=av.h[:, 0:64], data1=mv.h[:, 0:64], initial=0.0, op0=ALU.mult, op1=ALU.add),
                      reads=[av.d, mv.d], writes=[hT.d])
                fw.op(POOL, lambda e, nb=nb: e.tensor_copy(out=lrus_o.h[:, nb, :], in_=hT.h[:, 0:64].rearrange("p (j t) -> p j t", t=4)[:, :, 3]), reads=[hT.d], writes=[lrus_o.d])
            fw.op(ACT, lambda e, by_=by_: e.activation(out=y32.h[:, 0:Tn], in_=by_.h[:, 0:Tn], func=AF.Copy), reads=[by_.d], writes=[y32.d])
            gelu_mul(y32, hT, cat.h[:, 8 + nb, 0:Tn], cat, Tn, g1, g2)
        for nb_ in range(8):
            _blk(nb_)
        if (not samp) and ti == 3:
            out_toks.append(fw.dma(SP, lambda e: e.dma_start(out=o_lrup.ap()[e_], in_=hst.h[:, e_, :]), oslot(("lrup", e_)), reads=[hst.d]))
            out_toks.append(fw.dma(SP, lambda e: e.dma_start(out=o_convp.ap()[e_].rearrange("(n p) i -> p n i", p=128), in_=convst.h[:, e_, :, :]), oslot(("convp", e_)), reads=[convst.d]))
        if samp:
            out_toks.append(fw.dma(SP, lambda e: e.dma_start(out=o_lrus.ap()[e_].rearrange("(n p) j -> p n j", p=128), in_=lrus_o.h[:]), oslot(("lrus", e_)), reads=[lrus_o.d]))
            out_toks.append(fw.dma(SP, lambda e: e.dma_start(out=o_convs.ap()[e_].rearrange("(n p) j i -> p n j i", p=128), in_=convs_o.h[:]), oslot(("convs", e_)), reads=[convs_o.d]))
        wo = w_out.ap()[e_]
        for og in range(4):
            two = load_w([(wsrc(wo, og * 512, 512), lambda hh: hh[:])])
            for m in range(4):
                b = proj_fm(two, m * 128, cat, Tn)
                add_to_x(b, og * 4 + m, Tn)

    def gelu_mul(src, mul, out_ap, out_t, Tn, g1, g2):
        s = src.h[:, 0:Tn]
        fw.op(DVE, lambda e: e.tensor_tensor(out=g1.h[:, 0:Tn], in0=s, in1=s, op=ALU.mult), reads=[src.d], writes=[g1.d])
        fw.op(DVE, lambda e: e.tensor_scalar(out=g1.h[:, 0:Tn], in0=g1.h[:, 0:Tn], scalar1=0.044715, scalar2=1.0, op0=ALU.mult, op1=ALU.add), reads=[g1.d], writes=[g1.d])
        fw.op(DVE, lambda e: e.tensor_tensor(out=g1.h[:, 0:Tn], in0=g1.h[:, 0:Tn], in1=s, op=ALU.mult), reads=[g1.d, src.d], writes=[g1.d])
        fw.op(ACT, lambda e: e.activation(out=g2.h[:, 0:Tn], in_=g1.h[:, 0:Tn], func=AF.Sigmoid, scale=GELU_C), reads=[g1.d], writes=[g2.d])
        if mul is not None:
            fw.op(DVE, lambda e: e.tensor_tensor(out=g2.h[:, 0:Tn], in0=g2.h[:, 0:Tn], in1=s, op=ALU.mult), reads=[g2.d, src.d], writes=[g2.d])
            fw.op(DVE, lambda e: e.tensor_tensor(out=out_ap, in0=g2.h[:, 0:Tn], in1=mul.h[:, 0:Tn], op=ALU.mult), reads=[g2.d, mul.d], writes=[out_t.d])
        else:
            fw.op(DVE, lambda e: e.tensor_tensor(out=out_ap, in0=g2.h[:, 0:Tn], in1=s, op=ALU.mult), reads=[g2.d, src.d], writes=[out_t.d])

    def odd_layer(o_, kind, ti, Tn):
        rmsnorm(2 + o_, Tn)
        if not do_odd:
            return
        s5_layer(o_, kind, ti, Tn)
        if cfg.get("stop_s5"):
            raise StopIteration
        wg = w_glu.ap()[o_]
        for og in range(4):
            t1w = load_w([(wsrc(wg, og * 512, 512), lambda hh: hh[:])])
            t2w = load_w([(wsrc(wg, D + og * 512, 512), lambda hh: hh[:])])
            for m in range(4):
                oc = og * 4 + m
                b1 = proj_fm(t1w, m * 128, cat, Tn)
                b2 = proj_fm(t2w, m * 128, cat, Tn)
                sg = sgt[m % 2]
                fw.op(ACT, lambda e, b2=b2, sg=sg, oc=oc: e.activation(out=sg.h[:, 0:Tn], in_=b2.h[:, 0:Tn], func=AF.Sigmoid, bias=bglu.h[:, o_, 16 + oc:17 + oc]), reads=[b2.d, bglu.d], writes=[sg.d])
                fw.op(DVE, lambda e, b1=b1, sg=sg, oc=oc: e.scalar_tensor_tensor(out=sg.h[:, 0:Tn], in0=b1.h[:, 0:Tn], scalar=bglu.h[:, o_, oc:oc + 1], in1=sg.h[:, 0:Tn], op0=ALU.add, op1=ALU.mult),
                      reads=[b1.d, sg.d, bglu.d], writes=[sg.d])
                fw.op(DVE, lambda e, sg=sg, oc=oc: e.tensor_tensor(out=x.h[:, oc, 0:Tn], in0=sg.h[:, 0:Tn], in1=x.h[:, oc, 0:Tn], op=ALU.add), reads=[sg.d, x.d], writes=[x.d])

    def vap(t, off, dims, parts=128, p0=0):
        v = t.h
        ps = v.ap[0][0]
        return bass.AP(arena, v.offset + p0 * ps + off, [[ps, parts]] + [list(d_) for d_ in dims])

    bslots = [fw.slot() for _ in range(4)]
    h0slots = [fw.slot() for _ in range(2)]
    TWO_PI = 2.0 * math.pi

    def s5_layer(o_, kind, ti, Tn):
        samp = kind == "s"
        n = 16 if samp else 64
        SL = 4 if samp else 8
        s_list = list(range(4, 8)) if samp else list(range(8))
        W = wtmp

        def tt(eng, out, i0, i1, op, rd, wr):
            fw.op(eng, lambda e: e.tensor_tensor(out=out, in0=i0, in1=i1, op=op), reads=rd, writes=wr)

        def ts(eng, out, i0, s1, s2, op0, op1, rd, wr):
            if op1 is None:
                fw.op(eng, lambda e: e.tensor_scalar(out=out, in0=i0, scalar1=s1, scalar2=None, op0=op0), reads=rd, writes=wr)
            else:
                fw.op(eng, lambda e: e.tensor_scalar(out=out, in0=i0, scalar1=s1, scalar2=s2, op0=op0, op1=op1), reads=rd, writes=wr)

        are = apar.h[:, 0, o_, :]
        aim = apar.h[:, 1, o_, :]
        dt_ = dtb.h[:, o_, :]
        wd = [w_.d for w_ in W]
        tt(DVE, W[0].h[:], are, dt_, ALU.mult, [apar.d, dtb.d], [W[0].d])
        tt(DVE, W[1].h[:], aim, dt_, ALU.mult, [apar.d, dtb.d], [W[1].d])
        fw.op(ACT, lambda e: e.activation(out=W[2].h[:], in_=W[0].h[:], func=AF.Exp), reads=[W[0].d], writes=[W[2].d])
        fw.op(ACT, lambda e: e.activation(out=W[3].h[:], in_=W[0].h[:], func=AF.Exp, scale=-1.0), reads=[W[0].d], writes=[W[3].d])
        for dst, shift in ((W[5], 0.0), (W[6], math.pi / 2)):
            ts(DVE, dst.h[:], W[1].h[:], shift, None, ALU.add, None, [W[1].d], [dst.d])
            ts(DVE, W[7].h[:], W[1].h[:], shift, None, ALU.add, None, [W[1].d], [W[7].d])
            for kthr in range(5):
                thr = (2 * kthr + 1) * math.pi
                ts(DVE, W[4].h[:], W[7].h[:], thr, -TWO_PI, ALU.is_gt, ALU.mult, [W[7].d], [W[4].d])
                tt(DVE, dst.h[:], dst.h[:], W[4].h[:], ALU.add, [dst.d, W[4].d], [dst.d])
        fw.op(ACT, lambda e: e.activation(out=W[5].h[:], in_=W[5].h[:], func=AF.Sin), reads=[W[5].d], writes=[W[5].d])
        fw.op(ACT, lambda e: e.activation(out=W[6].h[:], in_=W[6].h[:], func=AF.Sin), reads=[W[6].d], writes=[W[6].d])
        wp = lambda t_, ri: Wpos.h[:, t_, ri, :]
        wn = lambda s_, ri: Wneg.h[:, s_, ri, :]
        tt(DVE, wp(1, 0), W[2].h[:], W[6].h[:], ALU.mult, [W[2].d, W[6].d], [Wpos.d])
        tt(DVE, wp(1, 1), W[2].h[:], W[5].h[:], ALU.mult, [W[2].d, W[5].d], [Wpos.d])
        tt(DVE, wn(1, 0), W[3].h[:], W[6].h[:], ALU.mult, [W[3].d, W[6].d], [Wneg.d])
        fw.op(DVE, lambda e: e.scalar_tensor_tensor(out=wn(1, 1), in0=W[3].h[:], scalar=-1.0, in1=W[5].h[:], op0=ALU.mult, op1=ALU.mult), reads=[W[3].d, W[5].d], writes=[Wneg.d])
        for arr in (Wpos, Wneg):
            fw.op(DVE, lambda e, arr=arr: e.memset(arr.h[:, 0, 0, :], 1.0), writes=[arr.d])
            fw.op(DVE, lambda e, arr=arr: e.memset(arr.h[:, 0, 1, :], 0.0), writes=[arr.d])

        def cmul(outr, outi, ar, ai, br, bi, rd, wr):
            tt(DVE, W[0].h[:], ar, br, ALU.mult, rd, [W[0].d])
            tt(DVE, W[4].h[:], ai, bi, ALU.mult, rd, [W[4].d])
            tt(DVE, outr, W[0].h[:], W[4].h[:], ALU.subtract, [W[0].d, W[4].d], wr)
            tt(DVE, W[0].h[:], ar, bi, ALU.mult, rd + wr, [W[0].d])
            tt(DVE, W[4].h[:], ai, br, ALU.mult, rd + wr, [W[4].d])
            tt(DVE, outi, W[0].h[:], W[4].h[:], ALU.add, [W[0].d, W[4].d], wr)
        for t_ in range(2, 9):
            cmul(wp(t_, 0), wp(t_, 1), wp(t_ - 1, 0), wp(t_ - 1, 1), wp(1, 0), wp(1, 1), [Wpos.d], [Wpos.d])
        for s_ in range(2, 8):
            cmul(wn(s_, 0), wn(s_, 1), wn(s_ - 1, 0), wn(s_ - 1, 1), wn(1, 0), wn(1, 1), [Wneg.d], [Wneg.d])
        fr_ = frfi.h[:, 0, :]
        fi_ = frfi.h[:, 1, :]
        tt(DVE, W[2].h[:], are, are, ALU.mult, [apar.d], [W[2].d])
        tt(DVE, W[3].h[:], aim, aim, ALU.mult, [apar.d], [W[3].d])
        tt(DVE, W[2].h[:], W[2].h[:], W[3].h[:], ALU.add, [W[2].d, W[3].d], [W[2].d])
        fw.op(DVE, lambda e: e.reciprocal(out=W[2].h[:], in_=W[2].h[:]), reads=[W[2].d], writes=[W[2].d])
        ts(DVE, W[3].h[:], wp(1, 0), -1.0, None, ALU.add, None, [Wpos.d], [W[3].d])
        tt(DVE, W[5].h[:], W[3].h[:], are, ALU.mult, [W[3].d, apar.d], [W[5].d])
        tt(DVE, W[6].h[:], wp(1, 1), aim, ALU.mult, [Wpos.d, apar.d], [W[6].d])
        tt(DVE, W[5].h[:], W[5].h[:], W[6].h[:], ALU.add, [W[5].d, W[6].d], [W[5].d])
        tt(DVE, fr_, W[5].h[:], W[2].h[:], ALU.mult, [W[5].d, W[2].d], [frfi.d])
        tt(DVE, W[5].h[:], wp(1, 1), are, ALU.mult, [Wpos.d, apar.d], [W[5].d])
        tt(DVE, W[6].h[:], W[3].h[:], aim, ALU.mult, [W[3].d, apar.d], [W[6].d])
        tt(DVE, W[5].h[:], W[5].h[:], W[6].h[:], ALU.subtract, [W[5].d, W[6].d], [W[5].d])
        tt(DVE, fi_, W[5].h[:], W[2].h[:], ALU.mult, [W[5].d, W[2].d], [frfi.d])

        bHr, bHi = banks[0], banks[1]
        def _batch(bt):
            g0 = bt * 8
            for ri in range(2):
                fw.dma(SP, lambda e, ri=ri: e.dma_start(out=braw[ri].h[:], in_=b_d[ri].ap()[o_, :, g0:g0 + 8, :]), bslots[ri], writes=[braw[ri].d])
                fw.dma(SP, lambda e, ri=ri: e.dma_start(out=craw[ri].h[:], in_=c_d[ri].ap()[o_, :, g0:g0 + 8, :]), bslots[2 + ri], writes=[craw[ri].d])
            frb = vap(frfi, g0, [[1, 8], [0, 16]])
            fib = vap(frfi, 64 + g0, [[1, 8], [0, 16]])
            v8 = lambda t_: t_.h[:, :, :]
            tA3 = tA.h[:, 0:128].rearrange("p (a b) -> p a b", a=8)
            tB3 = tB.h[:, 0:128].rearrange("p (a b) -> p a b", a=8)
            tt(DVE, tA3, v8(braw[0]), frb, ALU.mult, [braw[0].d, frfi.d], [tA.d])
            tt(DVE, tB3, v8(braw[1]), fib, ALU.mult, [braw[1].d, frfi.d], [tB.d])
            tt(DVE, v8(bbar[0]), tA3, tB3, ALU.subtract, [tA.d, tB.d], [bbar[0].d])
            tt(DVE, tA3, v8(braw[1]), frb, ALU.mult, [braw[1].d, frfi.d], [tA.d])
            tt(DVE, tB3, v8(braw[0]), fib, ALU.mult, [braw[0].d, frfi.d], [tB.d])
            tt(DVE, v8(bbar[1]), tA3, tB3, ALU.add, [tA.d, tB.d], [bbar[1].d])
            tA4 = tA.h[:, :].rearrange("p (a b c) -> p a b c", a=8, b=8)
            tB4 = tB.h[:, :].rearrange("p (a b c) -> p a b c", a=8, b=8)
            wnr = vap(Wneg, g0, [[1, 8], [128, 8], [0, 16]])
            wni = vap(Wneg, 64 + g0, [[1, 8], [128, 8], [0, 16]])
            wpr = vap(Wpos, g0, [[1, 8], [128, 8], [0, 16]])
            wpi = vap(Wpos, 64 + g0, [[1, 8], [128, 8], [0, 16]])
            bb4 = [vap(bbar[ri], 0, [[16, 8], [0, 8], [1, 16]]) for ri in range(2)]
            cc4 = [vap(craw[ri], 0, [[16, 8], [0, 8], [1, 16]]) for ri in range(2)]
            e1, e2 = DVE, POOL
            tt(e1, tA4, bb4[0], wnr, ALU.mult, [bbar[0].d, Wneg.d], [tA.d])
            tt(e1, tB4, bb4[1], wni, ALU.mult, [bbar[1].d, Wneg.d], [tB.d])
            tt(e1, Rp[0].h[:], tA4, tB4, ALU.subtract, [tA.d, tB.d], [Rp[0].d])
            tt(e1, tA4, bb4[1], wnr, ALU.mult, [bbar[1].d, Wneg.d], [tA.d])
            tt(e1, tB4, bb4[0], wni, ALU.mult, [bbar[0].d, Wneg.d], [tB.d])
            tt(e1, Rp[1].h[:], tA4, tB4, ALU.add, [tA.d, tB.d], [Rp[1].d])
            tt(e1, tA4, cc4[0], wpr, ALU.mult, [craw[0].d, Wpos.d], [tA.d])
            tt(e1, tB4, cc4[1], wpi, ALU.mult, [craw[1].d, Wpos.d], [tB.d])
            tt(e1, Op[0].h[:], tA4, tB4, ALU.subtract, [tA.d, tB.d], [Op[0].d])
            tt(e1, tA4, cc4[0], wpi, ALU.mult, [craw[0].d, Wpos.d], [tA.d])
            tt(e1, tB4, cc4[1], wpr, ALU.mult, [craw[1].d, Wpos.d], [tB.d])
            fw.op(e1, lambda e: e.scalar_tensor_tensor(out=Op[1].h[:].rearrange("p a b c -> p (a b c)"), in0=tA.h[:, :], scalar=-1.0, in1=tB.h[:, :], op0=ALU.mult, op1=ALU.subtract),
                  reads=[tA.d, tB.d], writes=[Op[1].d])

            def grp_views(gh, g8):
                P0 = 64 * gh
                r0 = Rp[0].h[P0:P0 + 64, g8, :, :].rearrange("p s j -> p (s j)")
                r1 = Rp[1].h[P0:P0 + 64, g8, :, :].rearrange("p s j -> p (s j)")
                o0 = Op[0].h[P0:P0 + 64, g8, :, :].rearrange("p s j -> p (s j)")
                o1 = Op[1].h[P0:P0 + 64, g8, :, :].rearrange("p s j -> p (s j)")
                return P0, r0, r1, o0, o1

            for gh in range(2):
                ft = bt + 8 * gh
                for g8 in range(8):
                    gi = gh * 8 + g8
                    P0, r0, r1, o0, o1 = grp_views(gh, g8)
                    b = bank()

                    def mt(e, b=b, r0=r0, r1=r1, o0=o0, o1=o1):
                        e.matmul(b.h[:, 0:128], lhsT=r0, rhs=o0, start=True, stop=False)
                        return e.matmul(b.h[:, 0:128], lhsT=r1, rhs=o1, start=False, stop=True)
                    fw.op(PE, mt, reads=[Rp[0].d, Rp[1].d, Op[0].d, Op[1].d], writes=[b.d])
                    fw.op(DVE, lambda e, b=b, gi=gi: e.tensor_tensor(out=MTs.h[:, gi, :], in0=b.h[:, 0:128], in1=bmask.h[:], op=ALU.mult), reads=[b.d, bmask.d], writes=[MTs.d])
                    b2 = bank()
                    bb2 = b2.h[:, 0:64].bitcast(BF16)

                    def qt(e, bb2=bb2, r0=r0, r1=r1, P0=P0):
                        idn = ident[P0:P0 + 64, P0:P0 + 64]
                        e.transpose(out=bb2[:, 0:64], in_=r0, identity=idn)
                        return e.transpose(out=bb2[:, 64:128], in_=r1, identity=idn)
                    fw.op(PE, qt, reads=[Rp[0].d, Rp[1].d, identones.d], writes=[b2.d])
                    q_ = QTs[gi % 2]
                    fw.op(ACT, lambda e, bb2=bb2, q_=q_: e.activation(out=q_.h[:], in_=bb2, func=AF.Copy), reads=[b2.d], writes=[q_.d])
                    b3 = bank()

                    def um(e, b3=b3, g8=g8, ft=ft):
                        r = None
                        for k_, s_ in enumerate(s_list):
                            rhs = xn.h[:, ft, 0:Tn].rearrange("p (c s) -> p s c", s=SL)[:, s_ - (8 - SL), :]
                            r = e.matmul(b3.h[:, 0:n], lhsT=strips.h[:, g8, 112 - 16 * s_:240 - 16 * s_], rhs=rhs, start=(k_ == 0), stop=(k_ == len(s_list) - 1))
                        return r
                    fw.op(PE, um, reads=[strips.d, xn.d], writes=[b3.d])
                    fw.op(ACT, lambda e, b3=b3, gi=gi: e.activation(out=Ub.h[:, gi, 0:n], in_=b3.h[:, 0:n], func=AF.Copy), reads=[b3.d], writes=[Ub.d])
                    fw.op(PE, lambda e, q_=q_, gi=gi, g8=g8, P0=P0: e.matmul(bHr.h[P0:P0 + 64, g8 * 64:g8 * 64 + n], lhsT=q_.h[:, 0:64], rhs=Ub.h[:, gi, 0:n], start=True, stop=True),
                          reads=[q_.d, Ub.d], writes=[bHr.d])
                    fw.op(PE, lambda e, q_=q_, gi=gi, g8=g8, P0=P0: e.matmul(bHi.h[P0:P0 + 64, g8 * 64:g8 * 64 + n], lhsT=q_.h[:, 64:128], rhs=Ub.h[:, gi, 0:n], start=True, stop=True),
                          reads=[q_.d, Ub.d], writes=[bHi.d])
            Hv = [bHr.h[:, :].rearrange("p (g c) -> p g c", g=8)[:, :, 0:n], bHi.h[:, :].rearrange("p (g c) -> p g c", g=8)[:, :, 0:n]]
            if not samp:
                for ri in range(2):
                    fw.op(ACT, lambda e, ri=ri: e.activation(out=Aar.h[:, ri, :, 1:n + 1], in_=Hv[ri], func=AF.Copy), reads=[(bHr, bHi)[ri].d], writes=[Aar.d])
                if ti == 0:
                    fw.op(DVE, lambda e: e.memset(Aar.h[:, :, :, 0], 0.0), writes=[Aar.d])
                else:
                    fw.op(DVE, lambda e: e.tensor_copy(out=Aar.h[:, :, :, 0], in_=Pst.h[:, o_, :, g0:g0 + 8]), reads=[Pst.d], writes=[Aar.d])
                w8r = vap(Wpos, 8 * 128 + g0, [[0, 2], [1, 8]])
                w8i = vap(Wpos, 8 * 128 + 64 + g0, [[0, 2], [1, 8]])
                for c in range(n):
                    tt(DVE, sS.h[:], Aar.h[:, :, :, c], Aar.h[:, :, :, c + 1], ALU.add, [Aar.d], [sS.d])
                    tt(DVE, sT1.h[:], sS.h[:], w8r, ALU.mult, [sS.d, Wpos.d], [sT1.d])
                    tt(DVE, sT2.h[:], sS.h[:], w8i, ALU.mult, [sS.d, Wpos.d], [sT2.d])
                    tt(DVE, Aar.h[:, 0, :, c + 1], sT1.h[:, 0, :], sT2.h[:, 1, :], ALU.subtract, [sT1.d, sT2.d], [Aar.d])
                    tt(DVE, Aar.h[:, 1, :, c + 1], sT1.h[:, 1, :], sT2.h[:, 0, :], ALU.add, [sT1.d, sT2.d], [Aar.d])
                fw.op(DVE, lambda e: e.tensor_copy(out=Pst.h[:, o_, :, g0:g0 + 8], in_=Aar.h[:, :, :, n]), reads=[Aar.d], writes=[Pst.d])
                fw.op(ACT, lambda e: e.activation(out=Abf.h[:, :, :, 0:n], in_=Aar.h[:, :, :, 0:n], func=AF.Copy), reads=[Aar.d], writes=[Abf.d])
            else:
                for ri in range(2):
                    fw.dma(SP, lambda e, ri=ri: e.dma_start(out=h0s5.h[:, ri, :, :], in_=sssm[ri].ap()[o_, :, g0:g0 + 8, :]), h0slots[ri], writes=[h0s5.d])
                wm3r = vap(Wneg, 3 * 128 + g0, [[1, 8], [0, 16]])
                wm3i = vap(Wneg, 3 * 128 + 64 + g0, [[1, 8], [0, 16]])
                w7r = vap(Wpos, 7 * 128 + g0, [[1, 8], [0, 16]])
                w7i = vap(Wpos, 7 * 128 + 64 + g0, [[1, 8], [0, 16]])
                tA3 = tA.h[:, 0:128].rearrange("p (a b) -> p a b", a=8)
                tB3 = tB.h[:, 0:128].rearrange("p (a b) -> p a b", a=8)

                def cm3(outr, outi, xr, xi, wr_, wi_, rd, wrd):
                    tt(DVE, tA3, xr, wr_, ALU.mult, rd, [tA.d])
                    tt(DVE, tB3, xi, wi_, ALU.mult, rd, [tB.d])
                    tt(DVE, outr, tA3, tB3, ALU.subtract, [tA.d, tB.d], wrd)
                    tt(DVE, tA3, xi, wr_, ALU.mult, rd, [tA.d])
                    tt(DVE, tB3, xr, wi_, ALU.mult, rd, [tB.d])
                    tt(DVE, outi, tA3, tB3, ALU.add, [tA.d, tB.d], wrd)
                cm3(Pp.h[:, 0, :, :], Pp.h[:, 1, :, :], h0s5.h[:, 0, :, :], h0s5.h[:, 1, :, :], wm3r, wm3i, [h0s5.d, Wneg.d], [Pp.d])
                fw.op(ACT, lambda e: e.activation(out=Abf.h[:, :, :, 0:16], in_=Pp.h[:, :, :, :], func=AF.Copy), reads=[Pp.d], writes=[Abf.d])
                for ri in range(2):
                    tt(DVE, Aar.h[:, ri, :, 0:16], Hv[ri], Pp.h[:, ri, :, :], ALU.add, [(bHr, bHi)[ri].d, Pp.d], [Aar.d])
                cm3(Pf.h[:, 0, :, :], Pf.h[:, 1, :, :], Aar.h[:, 0, :, 0:16], Aar.h[:, 1, :, 0:16], w7r, w7i, [Aar.d, Wpos.d], [Pf.d])
                for ri in range(2):
                    out_toks.append(fw.dma(SP, lambda e, ri=ri: e.dma_start(out=o_ssms[ri].ap()[o_, :, g0:g0 + 8, :], in_=Pf.h[:, ri, :, :]), oslot(("ssms", ri)), reads=[Pf.d]))
            for gh in range(2):
                for g8 in range(8):
                    gi = gh * 8 + g8
                    P0, r0, r1, o0, o1 = grp_views(gh, g8)
                    b = bank()

                    def ym(e, b=b, gi=gi, g8=g8, P0=P0, o0=o0, o1=o1):
                        e.matmul(b.h[:, 0:n], lhsT=MTs.h[:, gi, :], rhs=Ub.h[:, gi, 0:n], start=True, stop=False)
                        e.matmul(b.h[:, 0:n], lhsT=o0, rhs=Abf.h[P0:P0 + 64, 0, g8, 0:n], start=False, stop=False)
                        return e.matmul(b.h[:, 0:n], lhsT=o1, rhs=Abf.h[P0:P0 + 64, 1, g8, 0:n], start=False, stop=True)
                    fw.op(PE, ym, reads=[MTs.d, Ub.d, Op[0].d, Op[1].d, Abf.d], writes=[b.d])
                    fw.op(ACT, lambda e, b=b, gi=gi: e.activation(out=Yb.h[:, gi, 0:n], in_=b.h[:, 0:n], func=AF.Copy), reads=[b.d], writes=[Yb.d])
            for gh in range(2):
                ft = bt + 8 * gh
                bY = bank()

                def bc(e, bY=bY, gh=gh):
                    r = None
                    for t_ in s_list:
                        for g8 in range(8):
                            r = e.matmul(bY.h[:, t_ * 64:t_ * 64 + n], lhsT=strips.h[:, t_, 112 - 16 * g8:240 - 16 * g8], rhs=Yb.h[:, gh * 8 + g8, 0:n], start=(g8 == 0), stop=(g8 == 7))
                    return r
                fw.op(PE, bc, reads=[strips.d, Yb.d], writes=[bY.d])
                src = bY.h[:, :].rearrange("p (t c) -> p t c", t=8)[:, 8 - SL:8, 0:n]
                dst = y32b.h[:, 0:Tn].rearrange("p (c t) -> p t c", t=SL)
                fw.op(ACT, lambda e, src=src, dst=dst: e.activation(out=dst, in_=src, func=AF.Copy), reads=[bY.d], writes=[y32b.d])
                fw.op(DVE, lambda e, ft=ft: e.scalar_tensor_tensor(out=y32b.h[:, 0:Tn], in0=xn.h[:, ft, 0:Tn], scalar=ssmd.h[:, o_, ft:ft + 1], in1=y32b.h[:, 0:Tn], op0=ALU.mult, op1=ALU.add),
                      reads=[xn.d, ssmd.d, y32b.d], writes=[y32b.d])
                gelu_mul(y32b, None, cat.h[:, ft, 0:Tn], cat, Tn, g1b, g2b)
        for bt_ in range(8):
            _batch(bt_)
        if (not samp) and ti == 3:
            w1r = Wneg.h[:, 1, 0, :]
            w1i = Wneg.h[:, 1, 1, :]
            pr = Pst.h[:, o_, 0, :]
            pi_ = Pst.h[:, o_, 1, :]
            tt(DVE, W[0].h[:], pr, w1r, ALU.mult, [Pst.d, Wneg.d], [W[0].d])
            tt(DVE, W[4].h[:], pi_, w1i, ALU.mult, [Pst.d, Wneg.d], [W[4].d])
            tt(DVE, W[2].h[:], W[0].h[:], W[4].h[:], ALU.subtract, [W[0].d, W[4].d], [W[2].d])
            tt(DVE, W[0].h[:], pi_, w1r, ALU.mult, [Pst.d, Wneg.d], [W[0].d])
            tt(DVE, W[4].h[:], pr, w1i, ALU.mult, [Pst.d, Wneg.d], [W[4].d])
            tt(DVE, W[3].h[:], W[0].h[:], W[4].h[:], ALU.add, [W[0].d, W[4].d], [W[3].d])
            out_toks.append(fw.dma(SP, lambda e: e.dma_start(out=o_ssmp[0].ap()[o_], in_=W[2].h[:]), oslot(("ssmp", 0, o_)), reads=[W[2].d]))
            out_toks.append(fw.dma(SP, lambda e: e.dma_start(out=o_ssmp[1].ap()[o_], in_=W[3].h[:]), oslot(("ssmp", 1, o_)), reads=[W[3].d]))

    dctr = [0]

    def run_tile(kind, ti):
        samp = kind == "s"
        Tn = 64 if samp else 512
        if samp:
            fw.dma(SP, lambda e: e.dma_start(out=x.h[:, :, 0:64], in_=xsT.ap().rearrange("(kc p) t -> p kc t", p=128)), xslot, writes=[x.d])
            fw.dma(SP, lambda e: e.dma_start(out=cosT.h[:, 0:64], in_=cst["cosS"].ap()), tslot, writes=[cosT.d])
            fw.dma(SP, lambda e: e.dma_start(out=sinT.h[:, 0:64], in_=cst["sinS"].ap()), tslot2, writes=[sinT.d])
        else:
            fw.dma(SP, lambda e, ti=ti: e.dma_start(out=x.h[:], in_=xpT.ap()[:, ti * 512:(ti + 1) * 512].rearrange("(kc p) t -> p kc t", p=128)), xslot, writes=[x.d])
            fw.dma(SP, lambda e, ti=ti: e.dma_start(out=cosT.h[:], in_=cst["cosP"].ap()[:, ti * 512:(ti + 1) * 512]), tslot, writes=[cosT.d])
            fw.dma(SP, lambda e, ti=ti: e.dma_start(out=sinT.h[:], in_=cst["sinP"].ap()[:, ti * 512:(ti + 1) * 512]), tslot2, writes=[sinT.d])
            if ti == 0:
                fw.op(DVE, lambda e: e.memset(Sst.h[:], 0.0), writes=[Sst.d])
                fw.op(POOL, lambda e: e.memset(Sbf.h[:], 0.0), writes=[Sbf.d])
        for layer in range(nlayers):
            if layer % 2 == 0:
                if not samp:
                    fw.op(ACT, lambda e, layer=layer: e.activation(out=Sbf.h[:], in_=Sst.h[:, layer // 2, :, :, :], func=AF.Copy), reads=[Sst.d], writes=[Sbf.d])
                even_layer(layer // 2, kind, ti, Tn)
            else:
                odd_layer(layer // 2, kind, ti, Tn)
            ffn(layer, Tn)
            if dbg and dctr[0] < 8:
                di = dctr[0]
                out_toks.append(fw.dma(SP, lambda e, di=di: e.dma_start(out=o_dbg.ap()[di], in_=x.h[:]), oslot(("dbg", di)), reads=[x.d]))
                dctr[0] += 1
        if (not samp) and nlayers > 0:
            pass
        fw.op(ACT, lambda e: e.activation(out=cat.h[:, :, 0:Tn], in_=x.h[:, :, 0:Tn], func=AF.Square), reads=[x.d], writes=[cat.d])
        b = bank()

        def mmf(e, b=b, Tn=Tn):
            r = None
            for kc in range(KC):
                r = e.matmul(b.h[:, 0:Tn], lhsT=ones, rhs=cat.h[:, kc, 0:Tn], start=(kc == 0), stop=(kc == KC - 1))
            return r
        fw.op(PE, mmf, reads=[cat.d, identones.d], writes=[b.d])
        fw.op(ACT, lambda e, b=b, Tn=Tn: e.activation(out=rstd.h[:, 0:Tn], in_=b.h[:, 0:Tn], func=AF.Sqrt, scale=1.0 / D, bias=EPS), reads=[b.d], writes=[rstd.d])
        fw.op(DVE, lambda e, Tn=Tn: e.reciprocal(out=rstd.h[:, 0:Tn], in_=rstd.h[:, 0:Tn]), reads=[rstd.d], writes=[rstd.d])
        for kc in range(KC):
            fw.op(DVE, lambda e, kc=kc, Tn=Tn: e.scalar_tensor_tensor(out=x.h[:, kc, 0:Tn], in0=x.h[:, kc, 0:Tn], scalar=gains.h[:, 8, kc:kc + 1], in1=rstd.h[:, 0:Tn], op0=ALU.mult, op1=ALU.mult),
                  reads=[x.d, gains.d, rstd.d], writes=[x.d])
        if samp:
            out_toks.append(fw.dma(SP, lambda e: e.dma_start(out=ysT.ap().rearrange("(kc p) t -> p kc t", p=128), in_=x.h[:, :, 0:64]), oslot("ys"), reads=[x.d]))
        else:
            out_toks.append(fw.dma(SP, lambda e, ti=ti: e.dma_start(out=ypT.ap()[:, ti * 512:(ti + 1) * 512].rearrange("(kc p) t -> p kc t", p=128), in_=x.h[:]), oslot("yp"), reads=[x.d]))
    for (kind_, ti_) in tiles:
        try:
            run_tile(kind_, ti_)
        except StopIteration:
            break
    last = {}
    for t in out_toks:
        last[id(t[0])] = t
    fw.wait_tokens(SP, list(last.values()))
    fw.emit()
    return nc


_CACHE = {}


def make_in_maps(inp, ncores=8):
    W = prep_weights(inp)
    C = host_consts()
    maps = []
    f32 = np.float32
    for c in range(ncores):
        b = c // 2
        m = dict(W)
        for k, v in C.items():
            m["c_" + k] = v
        m["xpT"] = np.ascontiguousarray(np.asarray(inp["x_prompt"][b]).T)
        m["xsT"] = np.ascontiguousarray(np.asarray(inp["x_sample"][16 * c:16 * c + 16]).reshape(64, D).T)
        m["sret"] = np.ascontiguousarray(np.asarray(inp["state_ret"][:, 16 * c:16 * c + 16]))
        m["slru"] = np.ascontiguousarray(np.asarray(inp["state_lru"][:, 16 * c:16 * c + 16]).transpose(0, 2, 1))
        m["sconv"] = np.ascontiguousarray(np.asarray(inp["state_conv"][:, 16 * c:16 * c + 16]).transpose(0, 3, 1, 2))
        for nm, key in (("sssm_re", "state_ssm_re"), ("sssm_im", "state_ssm_im")):
            a = np.asarray(inp[key][:, 16 * c:16 * c + 16]).reshape(2, 16, 2, 64, 64).transpose(0, 2, 4, 3, 1).reshape(2, 128, 64, 16)
            m[nm] = np.ascontiguousarray(a)
        maps.append(m)
    return maps


def assemble(results):
    f32 = np.float32
    y_p = np.zeros((4, 2048, D), f32)
    y_s = np.zeros((128, 4, D), f32)
    ret_p = np.zeros((2, 4, 4, 256, 256), f32)
    ret_s = np.zeros((2, 128, 4, 256, 256), f32)
    lru_p = np.zeros((2, 4, 1024), f32)
    lru_s = np.zeros((2, 128, 1024), f32)
    conv_p = np.zeros((2, 4, 3, 1024), f32)
    conv_s = np.zeros((2, 128, 3, 1024), f32)
    ssm_p = [np.zeros((2, 4, 128, 64), f32), np.zeros((2, 4, 128, 64), f32)]
    ssm_s = [np.zeros((2, 128, 128, 64), f32), np.zeros((2, 128, 128, 64), f32)]
    for c, r in enumerate(results):
        sl = slice(16 * c, 16 * c + 16)
        y_s[sl] = r["ysT"].T.reshape(16, 4, D)
        ret_s[:, sl] = r["o_rets"]
        lru_s[:, sl] = r["o_lrus"].transpose(0, 2, 1)
        conv_s[:, sl] = r["o_convs"].transpose(0, 2, 3, 1)
        for ri, nm in enumerate(("o_ssms_re", "o_ssms_im")):
            a = r[nm].reshape(2, 2, 64, 64, 16).transpose(0, 4, 1, 3, 2).reshape(2, 16, 128, 64)
            ssm_s[ri][:, sl] = a
        if c % 2 == 0:
            b = c // 2
            y_p[b] = r["ypT"].T
            ret_p[:, b] = r["o_retp"]
            lru_p[:, b] = r["o_lrup"].transpose(0, 2, 1).reshape(2, 1024)
            conv_p[:, b] = r["o_convp"].transpose(0, 2, 1)
            for ri, nm in enumerate(("o_ssmp_re", "o_ssmp_im")):
                a = r[nm].reshape(2, 2, 64, 64).transpose(0, 1, 3, 2).reshape(2, 128, 64)
                ssm_p[ri][:, b] = a
    return (y_p, y_s, ret_p, ret_s, lru_p, lru_s, conv_p, conv_s, ssm_p[0], ssm_s[0], ssm_p[1], ssm_s[1])


def kernel(**inputs):
    nc = build({})
    maps = make_in_maps(inputs)
    res = run_bass_kernel_spmd(nc, maps, core_ids=list(range(8)))
    return assemble(res.results)
```

```python
import math
import numpy as np
import concourse.bass as bass
import concourse.mybir as mybir
from concourse.bass_utils import run_bass_kernel_spmd

F32 = mybir.dt.float32
BF16 = mybir.dt.bfloat16
AF = mybir.ActivationFunctionType
ALU = mybir.AluOpType
SEM_MAX = 12000

D = 2048
KC = 16
FH = 5632
EPS = 1e-6
GAM = [1.0 - 2.0 ** (-5 - h) for h in range(4)]
GELU_C = 2.0 * math.sqrt(2.0 / math.pi)


class Dep:
    __slots__ = ("w", "r")

    def __init__(self):
        self.w = None
        self.r = []


class Eng:
    def __init__(self, fw, name, is_pe=False):
        self.fw = fw
        self.name = name
        self.is_pe = is_pe
        self.ops = []
        self.count = 0
        self.sems = []
        self.seen = {}

    def token(self):
        i, v = divmod(self.count - 1, SEM_MAX)
        while len(self.sems) <= i:
            self.sems.append(self.fw.nc.alloc_semaphore(f"s_{self.name}_{len(self.sems)}"))
        return (self.sems[i], v + 1, self)


class Slot:
    def __init__(self, fw, name):
        self.sem = fw.nc.alloc_semaphore(name)
        self.val = 0


class FW:
    def __init__(self, nc):
        self.nc = nc
        self.pe = Eng(self, "pe", True)
        self.act = Eng(self, "act")
        self.dve = Eng(self, "dve")
        self.pool = Eng(self, "pool")
        self.sp = Eng(self, "sp")
        self.nslot = 0

    def slot(self):
        self.nslot += 1
        return Slot(self, f"dq{self.nslot}")

    @staticmethod
    def _flat(lst):
        out = []
        for d in lst:
            if isinstance(d, (list, tuple)):
                out.extend(FW._flat(d))
            else:
                out.append(d)
        return out

    def _waits(self, eng, reads, writes):
        deps = {}
        for d in reads:
            if d.w is not None:
                deps[id(d.w)] = d.w
        for d in writes:
            if d.w is not None:
                deps[id(d.w)] = d.w
            for t in d.r:
                deps[id(t)] = t
        waits = []
        for t in deps.values():
            sem, val, src = t
            if src is eng and eng.is_pe:
                continue
            k = id(sem)
            if eng.seen.get(k, 0) >= val:
                continue
            eng.seen[k] = val
            waits.append((sem, val))
        return waits

    def op(self, eng, fn, reads=(), writes=()):
        reads = self._flat(reads)
        writes = self._flat(writes)
        waits = self._waits(eng, reads, writes)
        eng.count += 1
        tok = eng.token()
        for d in writes:
            d.w = tok
            d.r = []
        for d in reads:
            if d.w is not tok:
                d.r.append(tok)
        eng.ops.append((waits, fn, (tok[0], 1)))
        return tok

    def dma(self, eng, fn, slot, reads=(), writes=(), n=1):
        reads = self._flat(reads)
        writes = self._flat(writes)
        waits = self._waits(eng, reads, writes)
        slot.val += 16 * n
        tok = (slot.sem, slot.val, slot)
        for d in writes:
            d.w = tok
            d.r = []
        for d in reads:
            d.r.append(tok)
        eng.ops.append((waits, fn, (slot.sem, 16)))
        return tok

    def wait_tokens(self, eng, toks):
        waits = []
        for (sem, val, src) in toks:
            if eng.seen.get(id(sem), 0) >= val:
                continue
            eng.seen[id(sem)] = val
            waits.append((sem, val))
        eng.ops.append((waits, None, None))

    def emit(self):
        nc = self.nc

        def run(eng, e):
            for waits, fn, inc in eng.ops:
                for sem, val in waits:
                    e.wait_ge(sem, val)
                if fn is None:
                    continue
                r = fn(e)
                if isinstance(r, (list, tuple)):
                    for x in r:
                        x.then_inc(inc[0], inc[1])
                else:
                    r.then_inc(inc[0], inc[1])

        with nc.Block() as block:
            @block.tensor
            def _(e):
                run(self.pe, e)

            @block.scalar
            def _(e):
                run(self.act, e)

            @block.vector
            def _(e):
                run(self.dve, e)

            @block.gpsimd
            def _(e):
                run(self.pool, e)

            @block.sync
            def _(e):
                run(self.sp, e)


class T:
    def __init__(self, h):
        self.h = h
        self.d = Dep()

    def __getitem__(self, k):
        return self.h[k]


def sap(t, off, dims, parts=128, p0=0):
    row = 1
    for s in t.shape[1:]:
        row *= s
    return bass.AP(t, p0 * row + off, [[row, parts]] + [list(d) for d in dims])


def host_consts():
    f32 = np.float32
    c = {}
    inv = (1.0 / np.power(f32(10000.0), np.linspace(0.0, 1.0, 128, dtype=f32))).astype(f32)
    posP = np.arange(2048, dtype=f32)
    angP = (posP[None, :] * inv[:, None]).astype(f32).astype(np.float64)
    posS = (16384 + (np.arange(64) % 4)).astype(f32)
    angS = (posS[None, :] * inv[:, None]).astype(f32).astype(np.float64)
    c["cosP"] = np.cos(angP).astype(f32)
    c["sinP"] = np.sin(angP).astype(f32)
    c["cosS"] = np.cos(angS).astype(f32)
    c["sinS"] = np.sin(angS).astype(f32)
    g = np.array(GAM, dtype=np.float64)
    nP = np.arange(512) % 128
    nS = np.arange(64) % 4
    qdP = np.power(g[:, None], nP[None, :] + 1.0)
    qdS = np.power(g[:, None], nS[None, :] + 1.0)
    c["qdecP"] = np.broadcast_to(qdP[None, :, 0:128], (128, 4, 128)).astype(f32).copy()
    c["qdecS"] = np.broadcast_to(qdS[None], (128, 4, 64)).astype(f32).copy()
    m = np.arange(128)
    kv = np.zeros((128, 16), f32)
    kv[:, 0:4] = (np.power(g[None, :], -(m[:, None] + 1.0)) / 16.0)
    kv[:, 4:8] = (np.power(g[None, :], 127.0 - m[:, None]) / 16.0)
    kv[:, 8:12] = (np.power(g[None, :], -((m[:, None] % 4) + 1.0)) / 16.0)
    kv[:, 12:16] = (np.power(g[None, :], 3.0 - (m[:, None] % 4)) / 16.0)
    c["kvec"] = kv
    mk = np.zeros((128, 192), f32)
    mk[:, 0:128] = (m[None, :] >= m[:, None]).astype(f32)
    ms = np.arange(64)
    mk[0:64, 128:192] = ((ms[None, :] >= ms[:, None]) & (ms[None, :] // 4 == ms[:, None] // 4)).astype(f32)
    c["masks"] = mk
    oh = np.zeros((128, 16), f32)
    oh[0:64] = (ms[:, None] // 4 == np.arange(16)[None, :]).astype(f32)
    c["onehot"] = oh
    io = np.zeros((128, 256), f32)
    io[:, 0:128] = np.eye(128, dtype=f32)
    io[:, 128:256] = 1.0
    c["identones"] = io
    st = np.zeros((128, 8, 240), f32)
    for a in range(8):
        for j in range(16):
            st[a * 16 + j, a, 112 + j] = 1.0
    c["strips"] = st
    bm = np.zeros((128, 128), f32)
    for s_ in range(8):
        for t_ in range(s_, 8):
            bm[s_ * 16:(s_ + 1) * 16, t_ * 16:(t_ + 1) * 16] = 1.0
    c["bmask"] = bm
    return c


def prep_weights(inp):
    f32 = np.float32
    w = {}
    gains = [inp["norm_mix_even"][0], inp["norm_mix_even"][1], inp["norm_mix_odd"][0], inp["norm_mix_odd"][1],
             inp["norm_ffn"][0], inp["norm_ffn"][1], inp["norm_ffn"][2], inp["norm_ffn"][3], inp["norm_final"]]
    w["gains"] = np.ascontiguousarray(np.stack([np.asarray(g).reshape(16, 128).T for g in gains], axis=1)).astype(f32)
    lv = np.zeros((128, 2, 8, 8), f32)
    for e in range(2):
        for i in range(4):
            lv[:, e, :, i] = np.asarray(inp["lru_conv_w"][e, i]).reshape(8, 128).T
        lv[:, e, :, 4] = np.asarray(inp["lru_conv_b"][e]).reshape(8, 128).T
        lv[:, e, :, 5] = np.asarray(inp["lru_ba"][e]).reshape(8, 128).T
        lv[:, e, :, 6] = np.asarray(inp["lru_bx"][e]).reshape(8, 128).T
        lv[:, e, :, 7] = np.asarray(inp["lru_lambda"][e]).reshape(8, 128).T
    w["lruvec"] = lv
    w["ssmd"] = np.ascontiguousarray(np.stack([np.asarray(inp["ssm_d"][o]).reshape(16, 128).T for o in range(2)], axis=1)).astype(f32)
    w["bglu"] = np.ascontiguousarray(np.stack([np.asarray(inp["b_glu"][o]).reshape(32, 128).T for o in range(2)], axis=1)).astype(f32)
    for nm in ("ssm_a_re", "ssm_a_im"):
        a = np.asarray(inp[nm]).reshape(2, 2, 64, 64).transpose(0, 1, 3, 2).reshape(2, 128, 64)
        w[nm] = np.ascontiguousarray(a)
    w["ssm_log_dt"] = np.ascontiguousarray(np.asarray(inp["ssm_log_dt"]).reshape(2, 128))
    for nm in ("ssm_b_re", "ssm_b_im"):
        a = np.asarray(inp[nm]).reshape(2, 2, 64, 64, 16).transpose(0, 1, 3, 2, 4).reshape(2, 128, 64, 16)
        w[nm] = np.ascontiguousarray(a)
    for nm in ("ssm_c_re", "ssm_c_im"):
        a = np.asarray(inp[nm]).reshape(2, 2, 64, 16, 64).transpose(0, 1, 4, 2, 3).reshape(2, 128, 64, 16)
        w[nm] = np.ascontiguousarray(a)
    for nm in ("w_in_even", "w_out_even", "w_glu", "w_ffn_gu", "w_ffn_down", "lru_wa", "lru_wx"):
        w[nm] = np.ascontiguousarray(np.asarray(inp[nm], dtype=f32))
    return w


def build(cfg):
    tiles = cfg.get("tiles", [("p", 0), ("p", 1), ("p", 2), ("p", 3), ("s", 0)])
    nlayers = cfg.get("nlayers", 4)
    dbg = cfg.get("dbg", False)
    do_odd = cfg.get("do_odd", True)

    nc = bass.Bass("TRN2", target_bir_lowering=False)
    fw = FW(nc)
    PE, ACT, DVE, POOL, SP = fw.pe, fw.act, fw.dve, fw.pool, fw.sp

    def din(name, shape):
        return nc.dram_tensor(name, list(shape), F32, kind="ExternalInput")

    def dout(name, shape):
        return nc.dram_tensor(name, list(shape), F32, kind="ExternalOutput")

    xpT = din("xpT", [D, 2048])
    xsT = din("xsT", [D, 64])
    sret = din("sret", [2, 16, 4, 256, 256])
    slru = din("slru", [2, 1024, 16])
    sconv = din("sconv", [2, 1024, 16, 3])
    sssm = [din("sssm_re", [2, 128, 64, 16]), din("sssm_im", [2, 128, 64, 16])]
    w_in = din("w_in_even", [2, D, 6144])
    w_out = din("w_out_even", [2, D, D])
    w_glu = din("w_glu", [2, D, 2 * D])
    w_gu = din("w_ffn_gu", [4, D, 2 * FH])
    w_dn = din("w_ffn_down", [4, FH, D])
    lru_wa = din("lru_wa", [2, 8, 128, 128])
    lru_wx = din("lru_wx", [2, 8, 128, 128])
    gains_d = din("gains", [128, 9, 16])
    lruvec_d = din("lruvec", [128, 2, 8, 8])
    ssmd_d = din("ssmd", [128, 2, 16])
    bglu_d = din("bglu", [128, 2, 32])
    a_d = [din("ssm_a_re", [2, 128, 64]), din("ssm_a_im", [2, 128, 64])]
    ldt_d = din("ssm_log_dt", [2, 128])
    b_d = [din("ssm_b_re", [2, 128, 64, 16]), din("ssm_b_im", [2, 128, 64, 16])]
    c_d = [din("ssm_c_re", [2, 128, 64, 16]), din("ssm_c_im", [2, 128, 64, 16])]
    cst = {k: din("c_" + k, v.shape) for k, v in host_consts().items()}

    ypT = dout("ypT", [D, 2048])
    ysT = dout("ysT", [D, 64])
    o_retp = dout("o_retp", [2, 4, 256, 256])
    o_rets = dout("o_rets", [2, 16, 4, 256, 256])
    o_lrup = dout("o_lrup", [2, 128, 8])
    o_lrus = dout("o_lrus", [2, 1024, 16])
    o_convp = dout("o_convp", [2, 1024, 3])
    o_convs = dout("o_convs", [2, 1024, 16, 3])
    o_ssmp = [dout("o_ssmp_re", [2, 128, 64]), dout("o_ssmp_im", [2, 128, 64])]
    o_ssms = [dout("o_ssms_re", [2, 128, 64, 16]), dout("o_ssms_im", [2, 128, 64, 16])]
    if dbg:
        o_dbg = dout("o_dbg", [8, 128, 16, 512])

    def sb(name, shape, dt=F32):
        return T(nc.alloc_sbuf_tensor("sb_" + name, list(shape), dt))

    x = sb("x", [128, KC, 512])
    xn = sb("xn", [128, KC, 512], BF16)
    cat = sb("cat", [128, KC, 512], BF16)
    NW = 2
    wb = [sb(f"wb{i}", [128, KC, 512], BF16) for i in range(NW)]
    wslot = [fw.slot() for _ in range(NW)]
    wctr = [0]
    gains = sb("gains", [128, 9, 16])
    lruvec = sb("lruvec", [128, 2, 8, 8])
    ssmd = sb("ssmd", [128, 2, 16])
    bglu = sb("bglu", [128, 2, 32])
    wa_sb = sb("wa_sb", [128, 2, 8, 128], BF16)
    wx_sb = sb("wx_sb", [128, 2, 8, 128], BF16)
    kvec = sb("kvec", [128, 16])
    masks = sb("masks", [128, 192])
    onehot = sb("onehot", [128, 16])
    identones = sb("identones", [128, 256], BF16)
    qdecP = sb("qdecP", [128, 4, 128])
    qdecS = sb("qdecS", [128, 4, 64])
    strips = sb("strips", [128, 8, 240], BF16)
    bmask = sb("bmask", [128, 128])
    Pst = sb("Pst", [128, 2, 2, 64])
    apar = sb("apar", [128, 2, 2, 64])
    dtb = sb("dtb", [128, 2, 64])
    lsp = sb("lsp", [128, 2, 8, 2])
    Sst = sb("Sst", [128, 2, 4, 2, 256])
    Sbf = sb("Sbf", [128, 4, 2, 256], BF16)
    hst = sb("hst", [128, 2, 8])
    convst = sb("convst", [128, 2, 8, 3])
    ident = identones.h[:, 0:128]
    ones = identones.h[:, 128:256]

    cslot = fw.slot()
    cslot2 = fw.slot()
    cdeps = []
    cdeps2 = []

    def cload(t, src, cast=False):
        if cast:
            fw.dma(POOL, lambda e: e.dma_start(out=t.h[:], in_=src), cslot2, writes=[t.d])
            cdeps2.append(t.d)
        else:
            fw.dma(SP, lambda e: e.dma_start(out=t.h[:], in_=src), cslot, writes=[t.d])
            cdeps.append(t.d)

    cload(gains, gains_d.ap())
    cload(lruvec, lruvec_d.ap())
    cload(ssmd, ssmd_d.ap())
    cload(bglu, bglu_d.ap())
    cload(wa_sb, lru_wa.ap().rearrange("e n k j -> k e n j"), cast=True)
    cload(wx_sb, lru_wx.ap().rearrange("e n k j -> k e n j"), cast=True)
    cload(kvec, cst["kvec"].ap())
    cload(masks, cst["masks"].ap())
    cload(onehot, cst["onehot"].ap())
    cload(identones, cst["identones"].ap(), cast=True)
    cload(qdecP, cst["qdecP"].ap())
    cload(qdecS, cst["qdecS"].ap())
    cload(strips, cst["strips"].ap(), cast=True)
    cload(bmask, cst["bmask"].ap())
    for ri in range(2):
        fw.dma(SP, lambda e, ri=ri: e.dma_start(out=apar.h[:, ri, :, :], in_=a_d[ri].ap().rearrange("o p g -> p o g")), cslot, writes=[apar.d])
    for gh in range(2):
        for o in range(2):
            fw.dma(SP, lambda e, gh=gh, o=o: e.dma_start(out=dtb.h[gh * 64:(gh + 1) * 64, o, :], in_=bass.AP(ldt_d, o * 128 + gh * 64, [[0, 64], [1, 64]])), cslot, writes=[dtb.d])
    cdeps.append(apar.d)
    cdeps.append(dtb.d)
    final_tok = (cslot.sem, cslot.val, cslot)
    for d_ in cdeps:
        d_.w = final_tok
    final_tok2 = (cslot2.sem, cslot2.val, cslot2)
    for d_ in cdeps2:
        d_.w = final_tok2

    fw.op(ACT, lambda e: e.activation(out=dtb.h[:], in_=dtb.h[:], func=AF.Exp), reads=[dtb.d], writes=[dtb.d])
    sp_t = sb("sp_t", [128, 2, 8])
    fw.op(ACT, lambda e: e.activation(out=sp_t.h[:], in_=lruvec.h[:, :, :, 7], func=AF.Exp, scale=-1.0), reads=[lruvec.d], writes=[sp_t.d])
    fw.op(ACT, lambda e: e.activation(out=sp_t.h[:], in_=sp_t.h[:], func=AF.Ln, bias=1.0), reads=[sp_t.d], writes=[sp_t.d])
    fw.op(DVE, lambda e: e.tensor_scalar(out=lsp.h[:, :, :, 0], in0=sp_t.h[:], scalar1=-8.0, scalar2=None, op0=ALU.mult), reads=[sp_t.d], writes=[lsp.d])
    fw.op(DVE, lambda e: e.tensor_scalar(out=lsp.h[:, :, :, 1], in0=sp_t.h[:], scalar1=-16.0, scalar2=None, op0=ALU.mult), reads=[sp_t.d], writes=[lsp.d])

    banks = [T(nc.alloc_psum_tensor(f"pb{i}", [128, 512], F32)) for i in range(8)]
    bctr = [0]

    def bank():
        b = banks[2 + bctr[0] % 6]
        bctr[0] += 1
        return b

    NPG = 123
    arena = nc.alloc_sbuf_tensor("arena", [128, NPG * 128], F32)
    pdeps = [Dep() for _ in range(NPG)]

    class Ph:
        def __init__(self, p0=0):
            self.p = p0

        def al(self, shape, dt=F32):
            nel = 1
            for s_ in shape[1:]:
                nel *= s_
            nb = nel * (4 if dt == F32 else 2)
            npg = (nb + 511) // 512
            assert self.p + npg <= NPG, (self.p, npg)
            v = arena[:, self.p * 128:(self.p + npg) * 128]
            if dt != F32:
                v = v.bitcast(dt)
            v = v[:, 0:nel]
            if len(shape) == 3:
                v = v.rearrange("p (a b) -> p a b", a=shape[1])
            elif len(shape) == 4:
                v = v.rearrange("p (a b c) -> p a b c", a=shape[1], b=shape[2])
            t = T(v)
            t.d = pdeps[self.p:self.p + npg]
            self.p += npg
            return t

    ph = Ph()
    qraw = ph.al([128, 2, 512])
    kraw = ph.al([128, 2, 512])
    qr = ph.al([128, 2, 512], BF16)
    kr = ph.al([128, 2, 512], BF16)
    qdd = ph.al([128, 2, 512], BF16)
    t1 = ph.al([128, 2, 512])
    t2 = ph.al([128, 2, 512])
    gate = ph.al([128, 2, 512], BF16)
    vtok = ph.al([128, 4, 1024], BF16)
    kdtok = ph.al([128, 4, 256], BF16)
    kdm = [ph.al([128, 256], BF16) for i in range(2)]
    PT = [ph.al([128, 128], BF16) for i in range(2)]
    oT = ph.al([128, 2, 512])
    sq = ph.al([128, 2, 512], BF16)
    S0f = [ph.al([128, 2, 256]) for i in range(3)]
    S0b = [ph.al([128, 2, 256], BF16) for i in range(3)]
    Sout = [ph.al([128, 2, 256]) for i in range(3)]
    p_ret_end = ph.p
    ph = Ph()
    xp_ = ph.al([128, 3 + 512 + 1])
    xps = ph.al([128, 16, 7])
    xc = ph.al([128, 512])
    xcb = ph.al([128, 512], BF16)
    rg = ph.al([128, 512])
    ig = ph.al([128, 512])
    av = ph.al([128, 512])
    mv = ph.al([128, 512])
    hT = ph.al([128, 512])
    h0s = ph.al([128, 8, 16])
    cv0s = ph.al([128, 8, 16, 3])
    lrus_o = ph.al([128, 8, 16])
    convs_o = ph.al([128, 8, 16, 3])
    y32 = ph.al([128, 512])
    g1 = ph.al([128, 512])
    g2 = ph.al([128, 512])
    ph = Ph()
    hact = [ph.al([128, 4, 512], BF16) for i in range(2)]
    sgt = [ph.al([128, 512]) for i in range(2)]
    ph = Ph()
    Wneg = ph.al([128, 8, 2, 64])
    Wpos = ph.al([128, 9, 2, 64])
    frfi = ph.al([128, 2, 64])
    wtmp = [ph.al([128, 64]) for i in range(8)]
    braw = [ph.al([128, 8, 16]) for i in range(2)]
    craw = [ph.al([128, 8, 16]) for i in range(2)]
    bbar = [ph.al([128, 8, 16]) for i in range(2)]
    tA = ph.al([128, 1024])
    tB = ph.al([128, 1024])
    Rp = [ph.al([128, 8, 8, 16], BF16) for i in range(2)]
    Op = [ph.al([128, 8, 8, 16], BF16) for i in range(2)]
    MTs = ph.al([128, 16, 128], BF16)
    QTs = [ph.al([128, 128], BF16) for i in range(2)]
    Ub = ph.al([128, 16, 64], BF16)
    Yb = ph.al([128, 16, 64], BF16)
    Aar = ph.al([128, 2, 8, 65])
    Abf = ph.al([128, 2, 8, 64], BF16)
    sS = ph.al([128, 2, 8])
    sT1 = ph.al([128, 2, 8])
    sT2 = ph.al([128, 2, 8])
    h0s5 = ph.al([128, 2, 8, 16])
    Pp = ph.al([128, 2, 8, 16])
    Pf = ph.al([128, 2, 8, 16])
    y32b = ph.al([128, 512])
    g1b = ph.al([128, 512])
    g2b = ph.al([128, 512])
    Wd = ph.al([128, 7, 2, 64])
    rstd = sb("rstd", [128, 512])
    cosT = sb("cosT", [128, 512])
    sinT = sb("sinT", [128, 512])
    tslot = fw.slot()
    tslot2 = fw.slot()
    stslot2 = fw.slot()
    xslot = fw.slot()
    S0slot = [fw.slot() for _ in range(3)]
    S0bslot = [fw.slot() for _ in range(3)]
    Soslot = [fw.slot() for _ in range(3)]
    stslot = fw.slot()
    oslots = {}

    def oslot(k):
        if k not in oslots:
            oslots[k] = fw.slot()
        return oslots[k]

    out_toks = []

    NLOADS = 192
    wscr_l = [nc.dram_tensor(f"wscr{q}", [NLOADS // 2, 128, KC * 512], BF16, kind="Internal") for q in range(2)]
    wscr_ap = lambda idx: wscr_l[idx // (NLOADS // 2)].ap()[idx % (NLOADS // 2)]
    wsd = [Dep() for _ in range(NLOADS)]
    wslot_hw = [fw.slot() for _ in range(NW)]
    sslot = [fw.slot() for _ in range(NW)]
    lctr = [0]
    passno = [0]
    use_scr = cfg.get("use_scr", True)

    def load_w(dram_ap_list):
        i = wctr[0] % NW
        wctr[0] += 1
        t = wb[i]
        idx = lctr[0]
        lctr[0] += 1
        flat = t.h[:].rearrange("p a b -> p (a b)")
        if use_scr and passno[0] > 0:
            fw.dma(SP, lambda e: e.dma_start(out=flat, in_=wscr_ap(idx)), wslot_hw[i], reads=[wsd[idx]], writes=[t.d])
            return t

        def fn(e, t=t, lst=dram_ap_list):
            r = []
            for src, dst in lst:
                r.append(e.dma_start(out=dst(t.h), in_=src))
            return r
        fw.dma(POOL, fn, wslot[i], writes=[t.d], n=len(dram_ap_list))
        if use_scr and len(tiles) > 1:
            fw.dma(SP, lambda e: e.dma_start(out=wscr_ap(idx), in_=flat), sslot[i], reads=[t.d], writes=[wsd[idx]])
        return t

    def wsrc(wd, c0, ncol):
        return wd[:, c0:c0 + ncol].rearrange("(kc p) j -> p kc j", p=128)

    def rmsnorm(gi, Tn):
        fw.op(ACT, lambda e: e.activation(out=cat.h[:, :, 0:Tn], in_=x.h[:, :, 0:Tn], func=AF.Square), reads=[x.d], writes=[cat.d])
        b = bank()

        def mm(e):
            r = None
            for kc in range(KC):
                r = e.matmul(b.h[:, 0:Tn], lhsT=ones, rhs=cat.h[:, kc, 0:Tn], start=(kc == 0), stop=(kc == KC - 1))
            return r
        fw.op(PE, mm, reads=[cat.d, identones.d], writes=[b.d])
        fw.op(ACT, lambda e: e.activation(out=rstd.h[:, 0:Tn], in_=b.h[:, 0:Tn], func=AF.Sqrt, scale=1.0 / D, bias=EPS), reads=[b.d], writes=[rstd.d])
        fw.op(DVE, lambda e: e.reciprocal(out=rstd.h[:, 0:Tn], in_=rstd.h[:, 0:Tn]), reads=[rstd.d], writes=[rstd.d])
        for kc in range(KC):
            eng = DVE
            fw.op(eng, lambda e, kc=kc: e.scalar_tensor_tensor(out=xn.h[:, kc, 0:Tn], in0=x.h[:, kc, 0:Tn], scalar=gains.h[:, gi, kc:kc + 1],
                                                               in1=rstd.h[:, 0:Tn], op0=ALU.mult, op1=ALU.mult),
                  reads=[x.d, gains.d, rstd.d], writes=[xn.d])

    def proj_fm(wt, col0, src, Tn, nk=KC):
        b = bank()

        def mm(e):
            r = None
            for kc in range(nk):
                r = e.matmul(b.h[:, 0:Tn], lhsT=wt.h[:, kc, col0:col0 + 128], rhs=src.h[:, kc, 0:Tn], start=(kc == 0), stop=(kc == nk - 1))
            return r
        fw.op(PE, mm, reads=[wt.d, src.d], writes=[b.d])
        return b

    def add_to_x(b, oc, Tn):
        fw.op(DVE, lambda e: e.tensor_tensor(out=x.h[:, oc, 0:Tn], in0=b.h[:, 0:Tn], in1=x.h[:, oc, 0:Tn], op=ALU.add), reads=[b.d, x.d], writes=[x.d])

    def ffn(layer, Tn):
        rmsnorm(4 + layer, Tn)
        wg = w_gu.ap()[layer]
        wd = w_dn.ap()[layer]
        for hb in range(FH // 512):
            tg = load_w([(wsrc(wg, hb * 512, 512), lambda h: h[:])])
            tu = load_w([(wsrc(wg, FH + hb * 512, 512), lambda h: h[:])])
            ha = hact[hb % 2]
            for m in range(4):
                bg = proj_fm(tg, m * 128, xn, Tn)
                bu = proj_fm(tu, m * 128, xn, Tn)
                sg = sgt[m % 2]
                fw.op(ACT, lambda e, bg=bg, sg=sg: e.activation(out=sg.h[:, 0:Tn], in_=bg.h[:, 0:Tn], func=AF.Silu), reads=[bg.d], writes=[sg.d])
                fw.op(DVE, lambda e, bu=bu, sg=sg, ha=ha, m=m: e.tensor_tensor(out=ha.h[:, m, 0:Tn], in0=bu.h[:, 0:Tn], in1=sg.h[:, 0:Tn], op=ALU.mult),
                      reads=[bu.d, sg.d], writes=[ha.d])
            td = load_w([(wd[hb * 512:(hb + 1) * 512, :].rearrange("(kc p) j -> p kc j", p=128), lambda h: h[:].rearrange("p a b -> p (a b)").rearrange("p (kc j) -> p kc j", kc=4))])
            tdv = td.h[:].rearrange("p a b -> p (a b)").rearrange("p (kc j) -> p kc j", kc=4)
            for oc in range(KC):
                b = bank()

                def mm(e, b=b, oc=oc, ha=ha, tdv=tdv):
                    r = None
                    for kc in range(4):
                        r = e.matmul(b.h[:, 0:Tn], lhsT=tdv[:, kc, oc * 128:(oc + 1) * 128], rhs=ha.h[:, kc, 0:Tn], start=(kc == 0), stop=(kc == 3))
                    return r
                fw.op(PE, mm, reads=[td.d, ha.d], writes=[b.d])
                add_to_x(b, oc, Tn)

    def even_layer(e_, kind, ti, Tn):
        samp = kind == "s"
        NTC = 1 if samp else 4
        TR = 64 if samp else 128
        rmsnorm(e_, Tn)
        wi = w_in.ap()[e_]
        qdec = qdecS if samp else qdecP
        kv0 = 8 if samp else 0
        gd = [GAM[h] ** (4 if samp else 128) for h in range(4)]
        for half in range(2):
            tv = load_w([(wsrc(wi, 2048 + half * 512, 512), lambda h: h[:])])
            for tc in range(NTC):
                b = bank()

                def mm(e, b=b, tc=tc, tv=tv):
                    r = None
                    for kc in range(KC):
                        r = e.matmul(b.h[0:TR, :], lhsT=xn.h[:, kc, tc * 128:tc * 128 + TR], rhs=tv.h[:, kc, :], start=(kc == 0), stop=(kc == KC - 1))
                    return r
                fw.op(PE, mm, reads=[tv.d, xn.d], writes=[b.d])
                fw.op(ACT, lambda e, b=b, tc=tc, half=half: e.activation(out=vtok.h[0:TR, tc, half * 512:(half + 1) * 512], in_=b.h[0:TR, :], func=AF.Copy),
                      reads=[b.d], writes=[vtok.d])
        def _head(h):
            tqk = load_w([(wsrc(wi, h * 256, 256), lambda hh: hh[:, :, 0:256]), (wsrc(wi, 1024 + h * 256, 256), lambda hh: hh[:, :, 256:512])])
            for dc in range(2):
                bq = proj_fm(tqk, dc * 128, xn, Tn)
                fw.op(ACT, lambda e, bq=bq, dc=dc: e.activation(out=qraw.h[:, dc, 0:Tn], in_=bq.h[:, 0:Tn], func=AF.Copy), reads=[bq.d], writes=[qraw.d])
                bk = proj_fm(tqk, 256 + dc * 128, xn, Tn)
                fw.op(ACT, lambda e, bk=bk, dc=dc: e.activation(out=kraw.h[:, dc, 0:Tn], in_=bk.h[:, 0:Tn], func=AF.Copy), reads=[bk.d], writes=[kraw.d])
            tg = load_w([(wsrc(wi, 3072 + h * 256, 256), lambda hh: hh[:, :, 0:256])])
            for dc in range(2):
                bg = proj_fm(tg, dc * 128, xn, Tn)
                fw.op(ACT, lambda e, bg=bg, dc=dc: e.activation(out=gate.h[:, dc, 0:Tn], in_=bg.h[:, 0:Tn], func=AF.Silu), reads=[bg.d], writes=[gate.d])
            for raw, outt, eng in ((qraw, qr, DVE), (kraw, kr, POOL)):
                for dc in range(2):
                    fw.op(eng, lambda e, raw=raw, dc=dc: e.tensor_tensor(out=t1.h[:, dc, 0:Tn], in0=raw.h[:, dc, 0:Tn], in1=cosT.h[:, 0:Tn], op=ALU.mult), reads=[raw.d, cosT.d], writes=[t1.d])
                    fw.op(eng, lambda e, raw=raw, dc=dc: e.tensor_tensor(out=t2.h[:, dc, 0:Tn], in0=raw.h[:, dc, 0:Tn], in1=sinT.h[:, 0:Tn], op=ALU.mult), reads=[raw.d, sinT.d], writes=[t2.d])
                fw.op(eng, lambda e, outt=outt: e.tensor_tensor(out=outt.h[:, 0, 0:Tn], in0=t1.h[:, 0, 0:Tn], in1=t2.h[:, 1, 0:Tn], op=ALU.subtract), reads=[t1.d, t2.d], writes=[outt.d])
                fw.op(eng, lambda e, outt=outt: e.tensor_tensor(out=outt.h[:, 1, 0:Tn], in0=t1.h[:, 1, 0:Tn], in1=t2.h[:, 0, 0:Tn], op=ALU.add), reads=[t1.d, t2.d], writes=[outt.d])
            if samp:
                qdb = sap(qdecS.h, h * 64, [[0, 2], [1, 64]])
                fw.op(DVE, lambda e, qdb=qdb: e.tensor_tensor(out=qdd.h[:, :, 0:64], in0=qr.h[:, :, 0:64], in1=qdb, op=ALU.mult), reads=[qr.d, qdecS.d], writes=[qdd.d])
            else:
                qdb = sap(qdecP.h, h * 128, [[0, 2], [0, 4], [1, 128]])
                fw.op(DVE, lambda e, qdb=qdb: e.tensor_tensor(out=qdd.h[:, :, :].rearrange("p a (c n) -> p a c n", c=4), in0=qr.h[:, :, :].rearrange("p a (c n) -> p a c n", c=4), in1=qdb, op=ALU.mult), reads=[qr.d, qdecP.d], writes=[qdd.d])
            for tc in range(NTC):
                b = bank()
                bb = b.h[:, 0:128].bitcast(BF16)

                def tr(e, tc=tc, bb=bb):
                    r = None
                    for dc in range(2):
                        r = e.transpose(out=bb[0:TR, dc * 128:(dc + 1) * 128], in_=kr.h[:, dc, tc * 128:tc * 128 + TR], identity=ident)
                    return r
                fw.op(PE, tr, reads=[kr.d, identones.d], writes=[b.d])
                fw.op(DVE, lambda e, tc=tc, bb=bb: e.tensor_scalar(out=kdtok.h[0:TR, tc, :], in0=bb[0:TR, :], scalar1=kvec.h[0:TR, kv0 + 4 + h:kv0 + 5 + h], scalar2=None, op0=ALU.mult),
                      reads=[b.d, kvec.d], writes=[kdtok.d])
            po = [banks[0], banks[1]]
            for c in range(NTC):
                bs = bank()

                def sc(e, bs=bs, c=c):
                    r = None
                    for dc in range(2):
                        r = e.matmul(bs.h[0:TR, 0:TR], lhsT=kr.h[:, dc, c * 128:c * 128 + TR], rhs=qdd.h[:, dc, c * 128:c * 128 + TR], start=(dc == 0), stop=(dc == 1))
                    return r
                fw.op(PE, sc, reads=[kr.d, qdd.d], writes=[bs.d])
                pt = PT[c % 2]
                moff = 128 if samp else 0
                fw.op(DVE, lambda e, bs=bs, pt=pt: e.scalar_tensor_tensor(out=pt.h[0:TR, 0:TR], in0=bs.h[0:TR, 0:TR], scalar=kvec.h[0:TR, kv0 + h:kv0 + h + 1],
                                                                             in1=masks.h[0:TR, moff:moff + TR], op0=ALU.mult, op1=ALU.mult),
                      reads=[bs.d, kvec.d, masks.d], writes=[pt.d])
                if not samp:
                    for ec in range(2):
                        def om(e, ec=ec, c=c, pt=pt):
                            e.matmul(po[ec].h[:, c * 128:(c + 1) * 128], lhsT=vtok.h[:, c, h * 256 + ec * 128:h * 256 + (ec + 1) * 128], rhs=pt.h[:, :], start=True, stop=False)
                            r = None
                            for dc in range(2):
                                r = e.matmul(po[ec].h[:, c * 128:(c + 1) * 128], lhsT=Sbf.h[:, h, dc, ec * 128:(ec + 1) * 128], rhs=qdd.h[:, dc, c * 128:(c + 1) * 128], start=False, stop=(dc == 1))
                            return r
                        fw.op(PE, om, reads=[vtok.d, pt.d, Sbf.d, qdd.d], writes=[po[ec].d])
                    bS = bank()

                    def su(e, bS=bS, c=c):
                        r = None
                        for dc in range(2):
                            r = e.matmul(bS.h[:, dc * 256:(dc + 1) * 256], lhsT=kdtok.h[:, c, dc * 128:(dc + 1) * 128], rhs=vtok.h[:, c, h * 256:(h + 1) * 256], start=True, stop=True)
                        return r
                    fw.op(PE, su, reads=[kdtok.d, vtok.d], writes=[bS.d])
                    fw.op(DVE, lambda e, bS=bS: e.scalar_tensor_tensor(out=Sst.h[:, e_, h, :, :], in0=Sst.h[:, e_, h, :, :], scalar=gd[h], in1=bS.h[:, :].rearrange("p (a b) -> p a b", a=2), op0=ALU.mult, op1=ALU.add),
                          reads=[bS.d, Sst.d], writes=[Sst.d])
                    fw.op(ACT, lambda e: e.activation(out=Sbf.h[:, h, :, :], in_=Sst.h[:, e_, h, :, :], func=AF.Copy), reads=[Sst.d], writes=[Sbf.d])
                else:
                    for ec in range(2):
                        fw.op(PE, lambda e, ec=ec, pt=pt: e.matmul(po[ec].h[:, 0:64], lhsT=vtok.h[0:64, 0, h * 256 + ec * 128:h * 256 + (ec + 1) * 128], rhs=pt.h[0:64, 0:64], start=True, stop=True),
                              reads=[vtok.d, pt.d], writes=[po[ec].d])
                    for j in range(16):
                        si = (h * 16 + j) % 3
                        s0f, s0b, so_ = S0f[si], S0b[si], Sout[si]
                        src = sret.ap()[e_, j, h].rearrange("(dc p) e -> p dc e", p=128)
                        fw.dma(SP, lambda e, s0f=s0f, src=src: e.dma_start(out=s0f.h[:], in_=src), S0slot[si], writes=[s0f.d])
                        fw.dma(POOL, lambda e, s0b=s0b, src=src: e.dma_start(out=s0b.h[:], in_=src), S0bslot[si], writes=[s0b.d])
                        for ec in range(2):
                            def im(e, ec=ec, j=j, s0b=s0b):
                                r = None
                                for dc in range(2):
                                    r = e.matmul(po[ec].h[:, 4 * j:4 * j + 4], lhsT=s0b.h[:, dc, ec * 128:(ec + 1) * 128], rhs=qdd.h[:, dc, 4 * j:4 * j + 4], start=False, stop=(dc == 1), skip_group_check=True)
                                return r
                            fw.op(PE, im, reads=[s0b.d, qdd.d], writes=[po[ec].d])
                        km = kdm[j % 2]
                        fw.op(DVE, lambda e, km=km, j=j: e.tensor_scalar(out=km.h[0:64, :], in0=kdtok.h[0:64, 0, :], scalar1=onehot.h[0:64, j:j + 1], scalar2=None, op0=ALU.mult),
                              reads=[kdtok.d, onehot.d], writes=[km.d])
                        bS = bank()

                        def su(e, bS=bS, km=km):
                            r = None
                            for dc in range(2):
                                r = e.matmul(bS.h[:, dc * 256:(dc + 1) * 256], lhsT=km.h[0:64, dc * 128:(dc + 1) * 128], rhs=vtok.h[0:64, 0, h * 256:(h + 1) * 256], start=True, stop=True)
                            return r
                        fw.op(PE, su, reads=[km.d, vtok.d], writes=[bS.d])
                        fw.op(DVE, lambda e, bS=bS, s0f=s0f, so_=so_: e.scalar_tensor_tensor(out=so_.h[:], in0=s0f.h[:], scalar=gd[h], in1=bS.h[:, :].rearrange("p (a b) -> p a b", a=2), op0=ALU.mult, op1=ALU.add),
                              reads=[bS.d, s0f.d], writes=[so_.d])
                        dst = o_rets.ap()[e_, j, h].rearrange("(dc p) e -> p dc e", p=128)
                        out_toks.append(fw.dma(SP, lambda e, so_=so_, dst=dst: e.dma_start(out=dst, in_=so_.h[:]), Soslot[si], reads=[so_.d]))
            for ec in range(2):
                fw.op(ACT, lambda e, ec=ec: e.activation(out=oT.h[:, ec, 0:Tn], in_=po[ec].h[:, 0:Tn], func=AF.Copy), reads=[po[ec].d], writes=[oT.d])
            fw.op(ACT, lambda e: e.activation(out=sq.h[:, :, 0:Tn], in_=oT.h[:, :, 0:Tn], func=AF.Square), reads=[oT.d], writes=[sq.d])
            bn = bank()

            def nm(e, bn=bn):
                r = None
                for ec in range(2):
                    r = e.matmul(bn.h[:, 0:Tn], lhsT=ones, rhs=sq.h[:, ec, 0:Tn], start=(ec == 0), stop=(ec == 1))
                return r
            fw.op(PE, nm, reads=[sq.d, identones.d], writes=[bn.d])
            fw.op(ACT, lambda e, bn=bn: e.activation(out=rstd.h[:, 0:Tn], in_=bn.h[:, 0:Tn], func=AF.Sqrt, scale=1.0 / 256, bias=EPS), reads=[bn.d], writes=[rstd.d])
            fw.op(DVE, lambda e: e.reciprocal(out=rstd.h[:, 0:Tn], in_=rstd.h[:, 0:Tn]), reads=[rstd.d], writes=[rstd.d])
            for ec in range(2):
                fw.op(DVE, lambda e, ec=ec: e.tensor_tensor(out=t1.h[:, ec, 0:Tn], in0=oT.h[:, ec, 0:Tn], in1=rstd.h[:, 0:Tn], op=ALU.mult), reads=[oT.d, rstd.d], writes=[t1.d])
                fw.op(DVE, lambda e, ec=ec: e.tensor_tensor(out=cat.h[:, 2 * h + ec, 0:Tn], in0=t1.h[:, ec, 0:Tn], in1=gate.h[:, ec, 0:Tn], op=ALU.mult), reads=[t1.d, gate.d], writes=[cat.d])
            if (not samp) and ti == 3:
                dst = o_retp.ap()[e_, h].rearrange("(dc p) e -> p dc e", p=128)
                out_toks.append(fw.dma(SP, lambda e, dst=dst: e.dma_start(out=dst, in_=Sst.h[:, e_, h, :, :]), oslot(("retp", e_, h)), reads=[Sst.d]))
        for h_ in range(4):
            _head(h_)
        if samp:
            fw.dma(SP, lambda e: e.dma_start(out=h0s.h[:], in_=slru.ap()[e_].rearrange("(n p) j -> p n j", p=128)), stslot, writes=[h0s.d])
            fw.dma(SP, lambda e: e.dma_start(out=cv0s.h[:], in_=sconv.ap()[e_].rearrange("(n p) j i -> p n j i", p=128)), stslot2, writes=[cv0s.d])
        def _blk(nb):
            txy = load_w([(wsrc(wi, 4096 + nb * 128, 128), lambda hh: hh[:, :, 0:128]), (wsrc(wi, 5120 + nb * 128, 128), lambda hh: hh[:, :, 128:256])])
            bx_ = proj_fm(txy, 0, xn, Tn)
            by_ = proj_fm(txy, 128, xn, Tn)
            lv = lambda k: lruvec.h[:, e_, nb, k:k + 1]
            if not samp:
                fw.op(ACT, lambda e, bx_=bx_: e.activation(out=xp_.h[:, 3:3 + Tn], in_=bx_.h[:, 0:Tn], func=AF.Copy), reads=[bx_.d], writes=[xp_.d])
                if ti == 0:
                    fw.op(DVE, lambda e: e.memset(xp_.h[:, 0:3], 0.0), writes=[xp_.d])
                else:
                    fw.op(DVE, lambda e, nb=nb: e.tensor_copy(out=xp_.h[:, 0:3], in_=convst.h[:, e_, nb, :]), reads=[convst.d], writes=[xp_.d])
                xin = lambda i: xp_.h[:, i:i + Tn]
                xco = xc.h[:, 0:Tn]
            else:
                fw.op(ACT, lambda e, bx_=bx_: e.activation(out=xps.h[:, :, 3:7], in_=bx_.h[:, 0:64].rearrange("p (j t) -> p j t", t=4), func=AF.Copy), reads=[bx_.d], writes=[xps.d])
                fw.op(DVE, lambda e, nb=nb: e.tensor_copy(out=xps.h[:, :, 0:3], in_=cv0s.h[:, nb, :, :]), reads=[cv0s.d], writes=[xps.d])
                xin = lambda i: xps.h[:, :, i:i + 4]
                xco = xc.h[:, 0:64].rearrange("p (j t) -> p j t", t=4)
            fw.op(DVE, lambda e, xin=xin, xco=xco, nb=nb: e.tensor_scalar(out=xco, in0=xin(0), scalar1=lruvec.h[:, e_, nb, 0:1], scalar2=lruvec.h[:, e_, nb, 4:5], op0=ALU.mult, op1=ALU.add),
                  reads=[xp_.d, xps.d, lruvec.d], writes=[xc.d])
            for i in range(1, 4):
                fw.op(DVE, lambda e, xin=xin, xco=xco, nb=nb, i=i: e.scalar_tensor_tensor(out=xco, in0=xin(i), scalar=lruvec.h[:, e_, nb, i:i + 1], in1=xco, op0=ALU.mult, op1=ALU.add),
                      reads=[xp_.d, xps.d, lruvec.d, xc.d], writes=[xc.d])
            if not samp:
                fw.op(POOL, lambda e, nb=nb: e.tensor_copy(out=convst.h[:, e_, nb, :], in_=xp_.h[:, Tn:Tn + 3]), reads=[xp_.d], writes=[convst.d])
            else:
                fw.op(POOL, lambda e, nb=nb: e.tensor_copy(out=convs_o.h[:, nb, :, :], in_=xps.h[:, :, 4:7]), reads=[xps.d], writes=[convs_o.d])
            fw.op(ACT, lambda e: e.activation(out=xcb.h[:, 0:Tn], in_=xc.h[:, 0:Tn], func=AF.Copy), reads=[xc.d], writes=[xcb.d])
            br = bank()
            fw.op(PE, lambda e, br=br, nb=nb: e.matmul(br.h[:, 0:Tn], lhsT=wa_sb.h[:, e_, nb, :], rhs=xcb.h[:, 0:Tn], start=True, stop=True), reads=[wa_sb.d, xcb.d], writes=[br.d])
            bi = bank()
            fw.op(PE, lambda e, bi=bi, nb=nb: e.matmul(bi.h[:, 0:Tn], lhsT=wx_sb.h[:, e_, nb, :], rhs=xcb.h[:, 0:Tn], start=True, stop=True), reads=[wx_sb.d, xcb.d], writes=[bi.d])
            fw.op(ACT, lambda e, br=br, nb=nb: e.activation(out=rg.h[:, 0:Tn], in_=br.h[:, 0:Tn], func=AF.Sigmoid, bias=lruvec.h[:, e_, nb, 5:6]), reads=[br.d, lruvec.d], writes=[rg.d])
            fw.op(ACT, lambda e, bi=bi, nb=nb: e.activation(out=ig.h[:, 0:Tn], in_=bi.h[:, 0:Tn], func=AF.Sigmoid, bias=lruvec.h[:, e_, nb, 6:7]), reads=[bi.d, lruvec.d], writes=[ig.d])
            fw.op(ACT, lambda e, nb=nb: e.activation(out=av.h[:, 0:Tn], in_=rg.h[:, 0:Tn], func=AF.Exp, scale=lsp.h[:, e_, nb, 0:1]), reads=[rg.d, lsp.d], writes=[av.d])
            fw.op(ACT, lambda e, nb=nb: e.activation(out=mv.h[:, 0:Tn], in_=rg.h[:, 0:Tn], func=AF.Exp, scale=lsp.h[:, e_, nb, 1:2]), reads=[rg.d, lsp.d], writes=[mv.d])
            fw.op(ACT, lambda e: e.activation(out=mv.h[:, 0:Tn], in_=mv.h[:, 0:Tn], func=AF.Sqrt, scale=-1.0, bias=1.0), reads=[mv.d], writes=[mv.d])
            fw.op(DVE, lambda e: e.tensor_tensor(out=mv.h[:, 0:Tn], in0=mv.h[:, 0:Tn], in1=ig.h[:, 0:Tn], op=ALU.mult), reads=[mv.d, ig.d], writes=[mv.d])
            fw.op(DVE, lambda e: e.tensor_tensor(out=mv.h[:, 0:Tn], in0=mv.h[:, 0:Tn], in1=xc.h[:, 0:Tn], op=ALU.mult), reads=[mv.d, xc.d], writes=[mv.d])
            if not samp:
                init = 0.0 if ti == 0 else hst.h[:, e_, nb:nb + 1]
                fw.op(DVE, lambda e, init=init: e.tensor_tensor_scan(out=hT.h[:, 0:Tn], data0=av.h[:, 0:Tn], data1=mv.h[:, 0:Tn], initial=init, op0=ALU.mult, op1=ALU.add),
                      reads=[av.d, mv.d, hst.d], writes=[hT.d])
                fw.op(POOL, lambda e, nb=nb: e.tensor_copy(out=hst.h[:, e_, nb:nb + 1], in_=hT.h[:, Tn - 1:Tn]), reads=[hT.d], writes=[hst.d])
            else:
                av3 = av.h[:, 0:64].rearrange("p (j t) -> p j t", t=4)
                mv3 = mv.h[:, 0:64].rearrange("p (j t) -> p j t", t=4)
                fw.op(DVE, lambda e, nb=nb: e.tensor_tensor(out=g1.h[:, 0:16], in0=av3[:, :, 0], in1=h0s.h[:, nb, :], op=ALU.mult), reads=[av.d, h0s.d], writes=[g1.d])
                fw.op(DVE, lambda e: e.tensor_tensor(out=mv3[:, :, 0], in0=mv3[:, :, 0], in1=g1.h[:, 0:16], op=ALU.add), reads=[mv.d, g1.d], writes=[mv.d])
                fw.op(DVE, lambda e: e.memset(av3[:, :, 0], 0.0), reads=[g1.d], writes=[av.d])
                fw.op(DVE, lambda e: e.tensor_tensor_scan(out=hT.h[:, 0:64], data0=av.h[:, 0:64], data1=mv.h[:, 0:64], initial=0.0, op0=ALU.mult, op1=ALU.add),
                      reads=[av.d, mv.d], writes=[hT.d])
                fw.op(POOL, lambda e, nb=nb: e.tensor_copy(out=lrus_o.h[:, nb, :], in_=hT.h[:, 0:64].rearrange("p (j t) -> p j t", t=4)[:, :, 3]), reads=[hT.d], writes=[lrus_o.d])
            fw.op(ACT, lambda e, by_=by_: e.activation(out=y32.h[:, 0:Tn], in_=by_.h[:, 0:Tn], func=AF.Copy), reads=[by_.d], writes=[y32.d])
            gelu_mul(y32, hT, cat.h[:, 8 + nb, 0:Tn], cat, Tn, g1, g2)
        for nb_ in range(8):
            _blk(nb_)
        if (not samp) and ti == 3:
            out_toks.append(fw.dma(SP, lambda e: e.dma_start(out=o_lrup.ap()[e_], in_=hst.h[:, e_, :]), oslot(("lrup", e_)), reads=[hst.d]))
            out_toks.append(fw.dma(SP, lambda e: e.dma_start(out=o_convp.ap()[e_].rearrange("(n p) i -> p n i", p=128), in_=convst.h[:, e_, :, :]), oslot(("convp", e_)), reads=[convst.d]))
        if samp:
            out_toks.append(fw.dma(SP, lambda e: e.dma_start(out=o_lrus.ap()[e_].rearrange("(n p) j -> p n j", p=128), in_=lrus_o.h[:]), oslot(("lrus", e_)), reads=[lrus_o.d]))
            out_toks.append(fw.dma(SP, lambda e: e.dma_start(out=o_convs.ap()[e_].rearrange("(n p) j i -> p n j i", p=128), in_=convs_o.h[:]), oslot(("convs", e_)), reads=[convs_o.d]))
        wo = w_out.ap()[e_]
        for og in range(4):
            two = load_w([(wsrc(wo, og * 512, 512), lambda hh: hh[:])])
            for m in range(4):
                b = proj_fm(two, m * 128, cat, Tn)
                add_to_x(b, og * 4 + m, Tn)

    def gelu_mul(src, mul, out_ap, out_t, Tn, g1, g2):
        s = src.h[:, 0:Tn]
        fw.op(DVE, lambda e: e.tensor_tensor(out=g1.h[:, 0:Tn], in0=s, in1=s, op=ALU.mult), reads=[src.d], writes=[g1.d])
        fw.op(DVE, lambda e: e.tensor_scalar(out=g1.h[:, 0:Tn], in0=g1.h[:, 0:Tn], scalar1=0.044715, scalar2=1.0, op0=ALU.mult, op1=ALU.add), reads=[g1.d], writes=[g1.d])
        fw.op(DVE, lambda e: e.tensor_tensor(out=g1.h[:, 0:Tn], in0=g1.h[:, 0:Tn], in1=s, op=ALU.mult), reads=[g1.d, src.d], writes=[g1.d])
        fw.op(ACT, lambda e: e.activation(out=g2.h[:, 0:Tn], in_=g1.h[:, 0:Tn], func=AF.Sigmoid, scale=GELU_C), reads=[g1.d], writes=[g2.d])
        if mul is not None:
            fw.op(DVE, lambda e: e.tensor_tensor(out=g2.h[:, 0:Tn], in0=g2.h[:, 0:Tn], in1=s, op=ALU.mult), reads=[g2.d, src.d], writes=[g2.d])
            fw.op(DVE, lambda e: e.tensor_tensor(out=out_ap, in0=g2.h[:, 0:Tn], in1=mul.h[:, 0:Tn], op=ALU.mult), reads=[g2.d, mul.d], writes=[out_t.d])
        else:
            fw.op(DVE, lambda e: e.tensor_tensor(out=out_ap, in0=g2.h[:, 0:Tn], in1=s, op=ALU.mult), reads=[g2.d, src.d], writes=[out_t.d])

    def odd_layer(o_, kind, ti, Tn):
        rmsnorm(2 + o_, Tn)
        if not do_odd:
            return
        s5_layer(o_, kind, ti, Tn)
        if cfg.get("stop_s5"):
            raise StopIteration
        wg = w_glu.ap()[o_]
        for og in range(4):
            t1w = load_w([(wsrc(wg, og * 512, 512), lambda hh: hh[:])])
            t2w = load_w([(wsrc(wg, D + og * 512, 512), lambda hh: hh[:])])
            for m in range(4):
                oc = og * 4 + m
                b1 = proj_fm(t1w, m * 128, cat, Tn)
                b2 = proj_fm(t2w, m * 128, cat, Tn)
                sg = sgt[m % 2]
                fw.op(ACT, lambda e, b2=b2, sg=sg, oc=oc: e.activation(out=sg.h[:, 0:Tn], in_=b2.h[:, 0:Tn], func=AF.Sigmoid, bias=bglu.h[:, o_, 16 + oc:17 + oc]), reads=[b2.d, bglu.d], writes=[sg.d])
                fw.op(DVE, lambda e, b1=b1, sg=sg, oc=oc: e.scalar_tensor_tensor(out=sg.h[:, 0:Tn], in0=b1.h[:, 0:Tn], scalar=bglu.h[:, o_, oc:oc + 1], in1=sg.h[:, 0:Tn], op0=ALU.add, op1=ALU.mult),
                      reads=[b1.d, sg.d, bglu.d], writes=[sg.d])
                fw.op(DVE, lambda e, sg=sg, oc=oc: e.tensor_tensor(out=x.h[:, oc, 0:Tn], in0=sg.h[:, 0:Tn], in1=x.h[:, oc, 0:Tn], op=ALU.add), reads=[sg.d, x.d], writes=[x.d])

    def vap(t, off, dims, parts=128, p0=0):
        v = t.h
        ps = v.ap[0][0]
        return bass.AP(arena, v.offset + p0 * ps + off, [[ps, parts]] + [list(d_) for d_ in dims])

    bslots = [fw.slot() for _ in range(4)]
    h0slots = [fw.slot() for _ in range(2)]
    TWO_PI = 2.0 * math.pi

    def s5_layer(o_, kind, ti, Tn):
        samp = kind == "s"
        n = 16 if samp else 64
        SL = 4 if samp else 8
        s_list = list(range(4, 8)) if samp else list(range(8))
        W = wtmp

        def tt(eng, out, i0, i1, op, rd, wr):
            fw.op(eng, lambda e: e.tensor_tensor(out=out, in0=i0, in1=i1, op=op), reads=rd, writes=wr)

        def ts(eng, out, i0, s1, s2, op0, op1, rd, wr):
            if op1 is None:
                fw.op(eng, lambda e: e.tensor_scalar(out=out, in0=i0, scalar1=s1, scalar2=None, op0=op0), reads=rd, writes=wr)
            else:
                fw.op(eng, lambda e: e.tensor_scalar(out=out, in0=i0, scalar1=s1, scalar2=s2, op0=op0, op1=op1), reads=rd, writes=wr)

        are = apar.h[:, 0, o_, :]
        aim = apar.h[:, 1, o_, :]
        dt_ = dtb.h[:, o_, :]
        wd = [w_.d for w_ in W]
        tt(DVE, W[0].h[:], are, dt_, ALU.mult, [apar.d, dtb.d], [W[0].d])
        tt(DVE, W[1].h[:], aim, dt_, ALU.mult, [apar.d, dtb.d], [W[1].d])
        fw.op(ACT, lambda e: e.activation(out=W[2].h[:], in_=W[0].h[:], func=AF.Exp), reads=[W[0].d], writes=[W[2].d])
        fw.op(ACT, lambda e: e.activation(out=W[3].h[:], in_=W[0].h[:], func=AF.Exp, scale=-1.0), reads=[W[0].d], writes=[W[3].d])
        for dst, shift in ((W[5], 0.0), (W[6], math.pi / 2)):
            ts(DVE, dst.h[:], W[1].h[:], shift, None, ALU.add, None, [W[1].d], [dst.d])
            ts(DVE, W[7].h[:], W[1].h[:], shift, None, ALU.add, None, [W[1].d], [W[7].d])
            for kthr in range(5):
                thr = (2 * kthr + 1) * math.pi
                ts(DVE, W[4].h[:], W[7].h[:], thr, -TWO_PI, ALU.is_gt, ALU.mult, [W[7].d], [W[4].d])
                tt(DVE, dst.h[:], dst.h[:], W[4].h[:], ALU.add, [dst.d, W[4].d], [dst.d])
        fw.op(ACT, lambda e: e.activation(out=W[5].h[:], in_=W[5].h[:], func=AF.Sin), reads=[W[5].d], writes=[W[5].d])
        fw.op(ACT, lambda e: e.activation(out=W[6].h[:], in_=W[6].h[:], func=AF.Sin), reads=[W[6].d], writes=[W[6].d])
        wp = lambda t_, ri: Wpos.h[:, t_, ri, :]
        wn = lambda s_, ri: Wneg.h[:, s_, ri, :]
        tt(DVE, wp(1, 0), W[2].h[:], W[6].h[:], ALU.mult, [W[2].d, W[6].d], [Wpos.d])
        tt(DVE, wp(1, 1), W[2].h[:], W[5].h[:], ALU.mult, [W[2].d, W[5].d], [Wpos.d])
        tt(DVE, wn(1, 0), W[3].h[:], W[6].h[:], ALU.mult, [W[3].d, W[6].d], [Wneg.d])
        fw.op(DVE, lambda e: e.scalar_tensor_tensor(out=wn(1, 1), in0=W[3].h[:], scalar=-1.0, in1=W[5].h[:], op0=ALU.mult, op1=ALU.mult), reads=[W[3].d, W[5].d], writes=[Wneg.d])
        for arr in (Wpos, Wneg):
            fw.op(DVE, lambda e, arr=arr: e.memset(arr.h[:, 0, 0, :], 1.0), writes=[arr.d])
            fw.op(DVE, lambda e, arr=arr: e.memset(arr.h[:, 0, 1, :], 0.0), writes=[arr.d])

        def cmul(outr, outi, ar, ai, br, bi, rd, wr):
            tt(DVE, W[0].h[:], ar, br, ALU.mult, rd, [W[0].d])
            tt(DVE, W[4].h[:], ai, bi, ALU.mult, rd, [W[4].d])
            tt(DVE, outr, W[0].h[:], W[4].h[:], ALU.subtract, [W[0].d, W[4].d], wr)
            tt(DVE, W[0].h[:], ar, bi, ALU.mult, rd + wr, [W[0].d])
            tt(DVE, W[4].h[:], ai, br, ALU.mult, rd + wr, [W[4].d])
            tt(DVE, outi, W[0].h[:], W[4].h[:], ALU.add, [W[0].d, W[4].d], wr)
        for t_ in range(2, 9):
            cmul(wp(t_, 0), wp(t_, 1), wp(t_ - 1, 0), wp(t_ - 1, 1), wp(1, 0), wp(1, 1), [Wpos.d], [Wpos.d])
        for s_ in range(2, 8):
            cmul(wn(s_, 0), wn(s_, 1), wn(s_ - 1, 0), wn(s_ - 1, 1), wn(1, 0), wn(1, 1), [Wneg.d], [Wneg.d])
        if not samp:
            wdv = lambda k_, ri: Wd.h[:, k_, ri, :]
            fw.op(DVE, lambda e: e.tensor_copy(out=Wd.h[:, 0, :, :], in_=Wpos.h[:, 8, :, :]), reads=[Wpos.d], writes=[Wd.d])
            for k_ in range(1, 7):
                tt(DVE, W[0].h[:], wdv(k_ - 1, 0), wdv(k_ - 1, 0), ALU.mult, [Wd.d], [W[0].d])
                tt(DVE, W[4].h[:], wdv(k_ - 1, 1), wdv(k_ - 1, 1), ALU.mult, [Wd.d], [W[4].d])
                tt(DVE, wdv(k_, 0), W[0].h[:], W[4].h[:], ALU.subtract, [W[0].d, W[4].d], [Wd.d])
                fw.op(DVE, lambda e, k_=k_: e.scalar_tensor_tensor(out=wdv(k_, 1), in0=wdv(k_ - 1, 0), scalar=2.0, in1=wdv(k_ - 1, 1), op0=ALU.mult, op1=ALU.mult), reads=[Wd.d], writes=[Wd.d])
        fr_ = frfi.h[:, 0, :]
        fi_ = frfi.h[:, 1, :]
        tt(DVE, W[2].h[:], are, are, ALU.mult, [apar.d], [W[2].d])
        tt(DVE, W[3].h[:], aim, aim, ALU.mult, [apar.d], [W[3].d])
        tt(DVE, W[2].h[:], W[2].h[:], W[3].h[:], ALU.add, [W[2].d, W[3].d], [W[2].d])
        fw.op(DVE, lambda e: e.reciprocal(out=W[2].h[:], in_=W[2].h[:]), reads=[W[2].d], writes=[W[2].d])
        ts(DVE, W[3].h[:], wp(1, 0), -1.0, None, ALU.add, None, [Wpos.d], [W[3].d])
        tt(DVE, W[5].h[:], W[3].h[:], are, ALU.mult, [W[3].d, apar.d], [W[5].d])
        tt(DVE, W[6].h[:], wp(1, 1), aim, ALU.mult, [Wpos.d, apar.d], [W[6].d])
        tt(DVE, W[5].h[:], W[5].h[:], W[6].h[:], ALU.add, [W[5].d, W[6].d], [W[5].d])
        tt(DVE, fr_, W[5].h[:], W[2].h[:], ALU.mult, [W[5].d, W[2].d], [frfi.d])
        tt(DVE, W[5].h[:], wp(1, 1), are, ALU.mult, [Wpos.d, apar.d], [W[5].d])
        tt(DVE, W[6].h[:], W[3].h[:], aim, ALU.mult, [W[3].d, apar.d], [W[6].d])
        tt(DVE, W[5].h[:], W[5].h[:], W[6].h[:], ALU.subtract, [W[5].d, W[6].d], [W[5].d])
        tt(DVE, fi_, W[5].h[:], W[2].h[:], ALU.mult, [W[5].d, W[2].d], [frfi.d])

        bHr, bHi = banks[0], banks[1]
        def _batch(bt):
            g0 = bt * 8
            for ri in range(2):
                fw.dma(SP, lambda e, ri=ri: e.dma_start(out=braw[ri].h[:], in_=b_d[ri].ap()[o_, :, g0:g0 + 8, :]), bslots[ri], writes=[braw[ri].d])
                fw.dma(SP, lambda e, ri=ri: e.dma_start(out=craw[ri].h[:], in_=c_d[ri].ap()[o_, :, g0:g0 + 8, :]), bslots[2 + ri], writes=[craw[ri].d])
            frb = vap(frfi, g0, [[1, 8], [0, 16]])
            fib = vap(frfi, 64 + g0, [[1, 8], [0, 16]])
            v8 = lambda t_: t_.h[:, :, :]
            tA3 = tA.h[:, 0:128].rearrange("p (a b) -> p a b", a=8)
            tB3 = tB.h[:, 0:128].rearrange("p (a b) -> p a b", a=8)
            tt(DVE, tA3, v8(braw[0]), frb, ALU.mult, [braw[0].d, frfi.d], [tA.d])
            tt(DVE, tB3, v8(braw[1]), fib, ALU.mult, [braw[1].d, frfi.d], [tB.d])
            tt(DVE, v8(bbar[0]), tA3, tB3, ALU.subtract, [tA.d, tB.d], [bbar[0].d])
            tt(DVE, tA3, v8(braw[1]), frb, ALU.mult, [braw[1].d, frfi.d], [tA.d])
            tt(DVE, tB3, v8(braw[0]), fib, ALU.mult, [braw[0].d, frfi.d], [tB.d])
            tt(DVE, v8(bbar[1]), tA3, tB3, ALU.add, [tA.d, tB.d], [bbar[1].d])
            tA4 = tA.h[:, :].rearrange("p (a b c) -> p a b c", a=8, b=8)
            tB4 = tB.h[:, :].rearrange("p (a b c) -> p a b c", a=8, b=8)
            wnr = vap(Wneg, g0, [[1, 8], [128, 8], [0, 16]])
            wni = vap(Wneg, 64 + g0, [[1, 8], [128, 8], [0, 16]])
            wpr = vap(Wpos, g0, [[1, 8], [128, 8], [0, 16]])
            wpi = vap(Wpos, 64 + g0, [[1, 8], [128, 8], [0, 16]])
            bb4 = [vap(bbar[ri], 0, [[16, 8], [0, 8], [1, 16]]) for ri in range(2)]
            cc4 = [vap(craw[ri], 0, [[16, 8], [0, 8], [1, 16]]) for ri in range(2)]
            e1, e2 = DVE, POOL
            tt(e1, tA4, bb4[0], wnr, ALU.mult, [bbar[0].d, Wneg.d], [tA.d])
            tt(e1, tB4, bb4[1], wni, ALU.mult, [bbar[1].d, Wneg.d], [tB.d])
            tt(e1, Rp[0].h[:], tA4, tB4, ALU.subtract, [tA.d, tB.d], [Rp[0].d])
            tt(e1, tA4, bb4[1], wnr, ALU.mult, [bbar[1].d, Wneg.d], [tA.d])
            tt(e1, tB4, bb4[0], wni, ALU.mult, [bbar[0].d, Wneg.d], [tB.d])
            tt(e1, Rp[1].h[:], tA4, tB4, ALU.add, [tA.d, tB.d], [Rp[1].d])
            tt(e1, tA4, cc4[0], wpr, ALU.mult, [craw[0].d, Wpos.d], [tA.d])
            tt(e1, tB4, cc4[1], wpi, ALU.mult, [craw[1].d, Wpos.d], [tB.d])
            tt(e1, Op[0].h[:], tA4, tB4, ALU.subtract, [tA.d, tB.d], [Op[0].d])
            tt(e1, tA4, cc4[0], wpi, ALU.mult, [craw[0].d, Wpos.d], [tA.d])
            tt(e1, tB4, cc4[1], wpr, ALU.mult, [craw[1].d, Wpos.d], [tB.d])
            fw.op(e1, lambda e: e.scalar_tensor_tensor(out=Op[1].h[:].rearrange("p a b c -> p (a b c)"), in0=tA.h[:, :], scalar=-1.0, in1=tB.h[:, :], op0=ALU.mult, op1=ALU.subtract),
                  reads=[tA.d, tB.d], writes=[Op[1].d])

            def grp_views(gh, g8):
                P0 = 64 * gh
                r0 = Rp[0].h[P0:P0 + 64, g8, :, :].rearrange("p s j -> p (s j)")
                r1 = Rp[1].h[P0:P0 + 64, g8, :, :].rearrange("p s j -> p (s j)")
                o0 = Op[0].h[P0:P0 + 64, g8, :, :].rearrange("p s j -> p (s j)")
                o1 = Op[1].h[P0:P0 + 64, g8, :, :].rearrange("p s j -> p (s j)")
                return P0, r0, r1, o0, o1

            for gh in range(2):
                ft = bt + 8 * gh
                for g8 in range(8):
                    gi = gh * 8 + g8
                    P0, r0, r1, o0, o1 = grp_views(gh, g8)
                    b = bank()

                    def mt(e, b=b, r0=r0, r1=r1, o0=o0, o1=o1):
                        e.matmul(b.h[:, 0:128], lhsT=r0, rhs=o0, start=True, stop=False)
                        return e.matmul(b.h[:, 0:128], lhsT=r1, rhs=o1, start=False, stop=True)
                    fw.op(PE, mt, reads=[Rp[0].d, Rp[1].d, Op[0].d, Op[1].d], writes=[b.d])
                    fw.op(DVE, lambda e, b=b, gi=gi: e.tensor_tensor(out=MTs.h[:, gi, :], in0=b.h[:, 0:128], in1=bmask.h[:], op=ALU.mult), reads=[b.d, bmask.d], writes=[MTs.d])
                    b2 = bank()
                    bb2 = b2.h[:, 0:64].bitcast(BF16)

                    def qt(e, bb2=bb2, r0=r0, r1=r1, P0=P0):
                        idn = ident[P0:P0 + 64, P0:P0 + 64]
                        e.transpose(out=bb2[:, 0:64], in_=r0, identity=idn)
                        return e.transpose(out=bb2[:, 64:128], in_=r1, identity=idn)
                    fw.op(PE, qt, reads=[Rp[0].d, Rp[1].d, identones.d], writes=[b2.d])
                    q_ = QTs[gi % 2]
                    fw.op(ACT, lambda e, bb2=bb2, q_=q_: e.activation(out=q_.h[:], in_=bb2, func=AF.Copy), reads=[b2.d], writes=[q_.d])
                    b3 = bank()

                    def um(e, b3=b3, g8=g8, ft=ft):
                        r = None
                        for k_, s_ in enumerate(s_list):
                            rhs = xn.h[:, ft, 0:Tn].rearrange("p (c s) -> p s c", s=SL)[:, s_ - (8 - SL), :]
                            r = e.matmul(b3.h[:, 0:n], lhsT=strips.h[:, g8, 112 - 16 * s_:240 - 16 * s_], rhs=rhs, start=(k_ == 0), stop=(k_ == len(s_list) - 1))
                        return r
                    fw.op(PE, um, reads=[strips.d, xn.d], writes=[b3.d])
                    fw.op(ACT, lambda e, b3=b3, gi=gi: e.activation(out=Ub.h[:, gi, 0:n], in_=b3.h[:, 0:n], func=AF.Copy), reads=[b3.d], writes=[Ub.d])
                    fw.op(PE, lambda e, q_=q_, gi=gi, g8=g8, P0=P0: e.matmul(bHr.h[P0:P0 + 64, g8 * 64:g8 * 64 + n], lhsT=q_.h[:, 0:64], rhs=Ub.h[:, gi, 0:n], start=True, stop=True),
                          reads=[q_.d, Ub.d], writes=[bHr.d])
                    fw.op(PE, lambda e, q_=q_, gi=gi, g8=g8, P0=P0: e.matmul(bHi.h[P0:P0 + 64, g8 * 64:g8 * 64 + n], lhsT=q_.h[:, 64:128], rhs=Ub.h[:, gi, 0:n], start=True, stop=True),
                          reads=[q_.d, Ub.d], writes=[bHi.d])
            Hv = [bHr.h[:, :].rearrange("p (g c) -> p g c", g=8)[:, :, 0:n], bHi.h[:, :].rearrange("p (g c) -> p g c", g=8)[:, :, 0:n]]
            if not samp:
                for ri in range(2):
                    fw.op(ACT, lambda e, ri=ri: e.activation(out=Aar.h[:, ri, :, 1:n + 1], in_=Hv[ri], func=AF.Copy), reads=[(bHr, bHi)[ri].d], writes=[Aar.d])
                if ti == 0:
                    fw.op(DVE, lambda e: e.memset(Aar.h[:, :, :, 0], 0.0), writes=[Aar.d])
                else:
                    fw.op(DVE, lambda e: e.tensor_copy(out=Aar.h[:, :, :, 0], in_=Pst.h[:, o_, :, g0:g0 + 8]), reads=[Pst.d], writes=[Aar.d])
                T1f = tA.h[:, :].rearrange("p (r g c) -> p r g c", r=2, g=8)
                T2f = tB.h[:, :].rearrange("p (r g c) -> p r g c", r=2, g=8)
                L = n + 1
                wr0 = vap(Wd, g0, [[0, 2], [1, 8], [0, n]])
                wi0 = vap(Wd, 64 + g0, [[0, 2], [1, 8], [0, n]])
                tt(DVE, T1f[:, :, :, 0:n], Aar.h[:, :, :, 1:L], wr0, ALU.mult, [Aar.d, Wd.d], [tA.d])
                tt(DVE, T2f[:, :, :, 0:n], Aar.h[:, :, :, 1:L], wi0, ALU.mult, [Aar.d, Wd.d], [tB.d])
                tt(DVE, Aar.h[:, 0, :, 1:L], T1f[:, 0, :, 0:n], T2f[:, 1, :, 0:n], ALU.subtract, [tA.d, tB.d], [Aar.d])
                tt(DVE, Aar.h[:, 1, :, 1:L], T1f[:, 1, :, 0:n], T2f[:, 0, :, 0:n], ALU.add, [tA.d, tB.d], [Aar.d])
                for k_ in range(7):
                    d_ = 1 << k_
                    Lc = L - d_
                    wr_ = vap(Wd, k_ * 128 + g0, [[0, 2], [1, 8], [0, Lc]])
                    wi_ = vap(Wd, k_ * 128 + 64 + g0, [[0, 2], [1, 8], [0, Lc]])
                    tt(DVE, T1f[:, :, :, 0:Lc], Aar.h[:, :, :, 0:Lc], wr_, ALU.mult, [Aar.d, Wd.d], [tA.d])
                    tt(DVE, T2f[:, :, :, 0:Lc], Aar.h[:, :, :, 0:Lc], wi_, ALU.mult, [Aar.d, Wd.d], [tB.d])
                    tt(DVE, Aar.h[:, 0, :, d_:L], Aar.h[:, 0, :, d_:L], T1f[:, 0, :, 0:Lc], ALU.add, [Aar.d, tA.d], [Aar.d])
                    tt(DVE, Aar.h[:, 0, :, d_:L], Aar.h[:, 0, :, d_:L], T2f[:, 1, :, 0:Lc], ALU.subtract, [Aar.d, tB.d], [Aar.d])
                    tt(DVE, Aar.h[:, 1, :, d_:L], Aar.h[:, 1, :, d_:L], T1f[:, 1, :, 0:Lc], ALU.add, [Aar.d, tA.d], [Aar.d])
                    tt(DVE, Aar.h[:, 1, :, d_:L], Aar.h[:, 1, :, d_:L], T2f[:, 0, :, 0:Lc], ALU.add, [Aar.d, tB.d], [Aar.d])
                fw.op(DVE, lambda e: e.tensor_copy(out=Pst.h[:, o_, :, g0:g0 + 8], in_=Aar.h[:, :, :, n]), reads=[Aar.d], writes=[Pst.d])
                fw.op(ACT, lambda e: e.activation(out=Abf.h[:, :, :, 0:n], in_=Aar.h[:, :, :, 0:n], func=AF.Copy), reads=[Aar.d], writes=[Abf.d])
            else:
                for ri in range(2):
                    fw.dma(SP, lambda e, ri=ri: e.dma_start(out=h0s5.h[:, ri, :, :], in_=sssm[ri].ap()[o_, :, g0:g0 + 8, :]), h0slots[ri], writes=[h0s5.d])
                wm3r = vap(Wneg, 3 * 128 + g0, [[1, 8], [0, 16]])
                wm3i = vap(Wneg, 3 * 128 + 64 + g0, [[1, 8], [0, 16]])
                w7r = vap(Wpos, 7 * 128 + g0, [[1, 8], [0, 16]])
                w7i = vap(Wpos, 7 * 128 + 64 + g0, [[1, 8], [0, 16]])
                tA3 = tA.h[:, 0:128].rearrange("p (a b) -> p a b", a=8)
                tB3 = tB.h[:, 0:128].rearrange("p (a b) -> p a b", a=8)

                def cm3(outr, outi, xr, xi, wr_, wi_, rd, wrd):
                    tt(DVE, tA3, xr, wr_, ALU.mult, rd, [tA.d])
                    tt(DVE, tB3, xi, wi_, ALU.mult, rd, [tB.d])
                    tt(DVE, outr, tA3, tB3, ALU.subtract, [tA.d, tB.d], wrd)
                    tt(DVE, tA3, xi, wr_, ALU.mult, rd, [tA.d])
                    tt(DVE, tB3, xr, wi_, ALU.mult, rd, [tB.d])
                    tt(DVE, outi, tA3, tB3, ALU.add, [tA.d, tB.d], wrd)
                cm3(Pp.h[:, 0, :, :], Pp.h[:, 1, :, :], h0s5.h[:, 0, :, :], h0s5.h[:, 1, :, :], wm3r, wm3i, [h0s5.d, Wneg.d], [Pp.d])
                fw.op(ACT, lambda e: e.activation(out=Abf.h[:, :, :, 0:16], in_=Pp.h[:, :, :, :], func=AF.Copy), reads=[Pp.d], writes=[Abf.d])
                for ri in range(2):
                    tt(DVE, Aar.h[:, ri, :, 0:16], Hv[ri], Pp.h[:, ri, :, :], ALU.add, [(bHr, bHi)[ri].d, Pp.d], [Aar.d])
                cm3(Pf.h[:, 0, :, :], Pf.h[:, 1, :, :], Aar.h[:, 0, :, 0:16], Aar.h[:, 1, :, 0:16], w7r, w7i, [Aar.d, Wpos.d], [Pf.d])
                for ri in range(2):
                    out_toks.append(fw.dma(SP, lambda e, ri=ri: e.dma_start(out=o_ssms[ri].ap()[o_, :, g0:g0 + 8, :], in_=Pf.h[:, ri, :, :]), oslot(("ssms", ri)), reads=[Pf.d]))
            for gh in range(2):
                for g8 in range(8):
                    gi = gh * 8 + g8
                    P0, r0, r1, o0, o1 = grp_views(gh, g8)
                    b = bank()

                    def ym(e, b=b, gi=gi, g8=g8, P0=P0, o0=o0, o1=o1):
                        e.matmul(b.h[:, 0:n], lhsT=MTs.h[:, gi, :], rhs=Ub.h[:, gi, 0:n], start=True, stop=False)
                        e.matmul(b.h[:, 0:n], lhsT=o0, rhs=Abf.h[P0:P0 + 64, 0, g8, 0:n], start=False, stop=False)
                        return e.matmul(b.h[:, 0:n], lhsT=o1, rhs=Abf.h[P0:P0 + 64, 1, g8, 0:n], start=False, stop=True)
                    fw.op(PE, ym, reads=[MTs.d, Ub.d, Op[0].d, Op[1].d, Abf.d], writes=[b.d])
                    fw.op(ACT, lambda e, b=b, gi=gi: e.activation(out=Yb.h[:, gi, 0:n], in_=b.h[:, 0:n], func=AF.Copy), reads=[b.d], writes=[Yb.d])
            for gh in range(2):
                ft = bt + 8 * gh
                bY = bank()

                def bc(e, bY=bY, gh=gh):
                    r = None
                    for t_ in s_list:
                        for g8 in range(8):
                            r = e.matmul(bY.h[:, t_ * 64:t_ * 64 + n], lhsT=strips.h[:, t_, 112 - 16 * g8:240 - 16 * g8], rhs=Yb.h[:, gh * 8 + g8, 0:n], start=(g8 == 0), stop=(g8 == 7))
                    return r
                fw.op(PE, bc, reads=[strips.d, Yb.d], writes=[bY.d])
                src = bY.h[:, :].rearrange("p (t c) -> p t c", t=8)[:, 8 - SL:8, 0:n]
                dst = y32b.h[:, 0:Tn].rearrange("p (c t) -> p t c", t=SL)
                fw.op(ACT, lambda e, src=src, dst=dst: e.activation(out=dst, in_=src, func=AF.Copy), reads=[bY.d], writes=[y32b.d])
                fw.op(DVE, lambda e, ft=ft: e.scalar_tensor_tensor(out=y32b.h[:, 0:Tn], in0=xn.h[:, ft, 0:Tn], scalar=ssmd.h[:, o_, ft:ft + 1], in1=y32b.h[:, 0:Tn], op0=ALU.mult, op1=ALU.add),
                      reads=[xn.d, ssmd.d, y32b.d], writes=[y32b.d])
                gelu_mul(y32b, None, cat.h[:, ft, 0:Tn], cat, Tn, g1b, g2b)
        for bt_ in range(8):
            _batch(bt_)
        if (not samp) and ti == 3:
            w1r = Wneg.h[:, 1, 0, :]
            w1i = Wneg.h[:, 1, 1, :]
            pr = Pst.h[:, o_, 0, :]
            pi_ = Pst.h[:, o_, 1, :]
            tt(DVE, W[0].h[:], pr, w1r, ALU.mult, [Pst.d, Wneg.d], [W[0].d])
            tt(DVE, W[4].h[:], pi_, w1i, ALU.mult, [Pst.d, Wneg.d], [W[4].d])
            tt(DVE, W[2].h[:], W[0].h[:], W[4].h[:], ALU.subtract, [W[0].d, W[4].d], [W[2].d])
            tt(DVE, W[0].h[:], pi_, w1r, ALU.mult, [Pst.d, Wneg.d], [W[0].d])
            tt(DVE, W[4].h[:], pr, w1i, ALU.mult, [Pst.d, Wneg.d], [W[4].d])
            tt(DVE, W[3].h[:], W[0].h[:], W[4].h[:], ALU.add, [W[0].d, W[4].d], [W[3].d])
            out_toks.append(fw.dma(SP, lambda e: e.dma_start(out=o_ssmp[0].ap()[o_], in_=W[2].h[:]), oslot(("ssmp", 0, o_)), reads=[W[2].d]))
            out_toks.append(fw.dma(SP, lambda e: e.dma_start(out=o_ssmp[1].ap()[o_], in_=W[3].h[:]), oslot(("ssmp", 1, o_)), reads=[W[3].d]))

    dctr = [0]

    def run_tile(kind, ti):
        samp = kind == "s"
        Tn = 64 if samp else 512
        if samp:
            fw.dma(SP, lambda e: e.dma_start(out=x.h[:, :, 0:64], in_=xsT.ap().rearrange("(kc p) t -> p kc t", p=128)), xslot, writes=[x.d])
            fw.dma(SP, lambda e: e.dma_start(out=cosT.h[:, 0:64], in_=cst["cosS"].ap()), tslot, writes=[cosT.d])
            fw.dma(SP, lambda e: e.dma_start(out=sinT.h[:, 0:64], in_=cst["sinS"].ap()), tslot2, writes=[sinT.d])
        else:
            fw.dma(SP, lambda e, ti=ti: e.dma_start(out=x.h[:], in_=xpT.ap()[:, ti * 512:(ti + 1) * 512].rearrange("(kc p) t -> p kc t", p=128)), xslot, writes=[x.d])
            fw.dma(SP, lambda e, ti=ti: e.dma_start(out=cosT.h[:], in_=cst["cosP"].ap()[:, ti * 512:(ti + 1) * 512]), tslot, writes=[cosT.d])
            fw.dma(SP, lambda e, ti=ti: e.dma_start(out=sinT.h[:], in_=cst["sinP"].ap()[:, ti * 512:(ti + 1) * 512]), tslot2, writes=[sinT.d])
            if ti == 0:
                fw.op(DVE, lambda e: e.memset(Sst.h[:], 0.0), writes=[Sst.d])
                fw.op(POOL, lambda e: e.memset(Sbf.h[:], 0.0), writes=[Sbf.d])
        for layer in range(nlayers):
            if layer % 2 == 0:
                if not samp:
                    fw.op(ACT, lambda e, layer=layer: e.activation(out=Sbf.h[:], in_=Sst.h[:, layer // 2, :, :, :], func=AF.Copy), reads=[Sst.d], writes=[Sbf.d])
                even_layer(layer // 2, kind, ti, Tn)
            else:
                odd_layer(layer // 2, kind, ti, Tn)
            ffn(layer, Tn)
            if dbg and dctr[0] < 8:
                di = dctr[0]
                out_toks.append(fw.dma(SP, lambda e, di=di: e.dma_start(out=o_dbg.ap()[di], in_=x.h[:]), oslot(("dbg", di)), reads=[x.d]))
                dctr[0] += 1
        if (not samp) and nlayers > 0:
            pass
        fw.op(ACT, lambda e: e.activation(out=cat.h[:, :, 0:Tn], in_=x.h[:, :, 0:Tn], func=AF.Square), reads=[x.d], writes=[cat.d])
        b = bank()

        def mmf(e, b=b, Tn=Tn):
            r = None
            for kc in range(KC):
                r = e.matmul(b.h[:, 0:Tn], lhsT=ones, rhs=cat.h[:, kc, 0:Tn], start=(kc == 0), stop=(kc == KC - 1))
            return r
        fw.op(PE, mmf, reads=[cat.d, identones.d], writes=[b.d])
        fw.op(ACT, lambda e, b=b, Tn=Tn: e.activation(out=rstd.h[:, 0:Tn], in_=b.h[:, 0:Tn], func=AF.Sqrt, scale=1.0 / D, bias=EPS), reads=[b.d], writes=[rstd.d])
        fw.op(DVE, lambda e, Tn=Tn: e.reciprocal(out=rstd.h[:, 0:Tn], in_=rstd.h[:, 0:Tn]), reads=[rstd.d], writes=[rstd.d])
        for kc in range(KC):
            fw.op(DVE, lambda e, kc=kc, Tn=Tn: e.scalar_tensor_tensor(out=x.h[:, kc, 0:Tn], in0=x.h[:, kc, 0:Tn], scalar=gains.h[:, 8, kc:kc + 1], in1=rstd.h[:, 0:Tn], op0=ALU.mult, op1=ALU.mult),
                  reads=[x.d, gains.d, rstd.d], writes=[x.d])
        if samp:
            out_toks.append(fw.dma(SP, lambda e: e.dma_start(out=ysT.ap().rearrange("(kc p) t -> p kc t", p=128), in_=x.h[:, :, 0:64]), oslot("ys"), reads=[x.d]))
        else:
            out_toks.append(fw.dma(SP, lambda e, ti=ti: e.dma_start(out=ypT.ap()[:, ti * 512:(ti + 1) * 512].rearrange("(kc p) t -> p kc t", p=128), in_=x.h[:]), oslot("yp"), reads=[x.d]))
    for (kind_, ti_) in tiles:
        lctr[0] = 0
        try:
            run_tile(kind_, ti_)
        except StopIteration:
            break
        passno[0] += 1
    last = {}
    for t in out_toks:
        last[id(t[0])] = t
    fw.wait_tokens(SP, list(last.values()))
    fw.emit()
    return nc


_CACHE = {}


def make_in_maps(inp, ncores=8):
    W = prep_weights(inp)
    C = host_consts()
    maps = []
    f32 = np.float32
    for c in range(ncores):
        b = c // 2
        m = dict(W)
        for k, v in C.items():
            m["c_" + k] = v
        m["xpT"] = np.ascontiguousarray(np.asarray(inp["x_prompt"][b]).T)
        m["xsT"] = np.ascontiguousarray(np.asarray(inp["x_sample"][16 * c:16 * c + 16]).reshape(64, D).T)
        m["sret"] = np.ascontiguousarray(np.asarray(inp["state_ret"][:, 16 * c:16 * c + 16]))
        m["slru"] = np.ascontiguousarray(np.asarray(inp["state_lru"][:, 16 * c:16 * c + 16]).transpose(0, 2, 1))
        m["sconv"] = np.ascontiguousarray(np.asarray(inp["state_conv"][:, 16 * c:16 * c + 16]).transpose(0, 3, 1, 2))
        for nm, key in (("sssm_re", "state_ssm_re"), ("sssm_im", "state_ssm_im")):
            a = np.asarray(inp[key][:, 16 * c:16 * c + 16]).reshape(2, 16, 2, 64, 64).transpose(0, 2, 4, 3, 1).reshape(2, 128, 64, 16)
            m[nm] = np.ascontiguousarray(a)
        maps.append(m)
    return maps


def assemble(results):
    f32 = np.float32
    y_p = np.zeros((4, 2048, D), f32)
    y_s = np.zeros((128, 4, D), f32)
    ret_p = np.zeros((2, 4, 4, 256, 256), f32)
    ret_s = np.zeros((2, 128, 4, 256, 256), f32)
    lru_p = np.zeros((2, 4, 1024), f32)
    lru_s = np.zeros((2, 128, 1024), f32)
    conv_p = np.zeros((2, 4, 3, 1024), f32)
    conv_s = np.zeros((2, 128, 3, 1024), f32)
    ssm_p = [np.zeros((2, 4, 128, 64), f32), np.zeros((2, 4, 128, 64), f32)]
    ssm_s = [np.zeros((2, 128, 128, 64), f32), np.zeros((2, 128, 128, 64), f32)]
    for c, r in enumerate(results):
        sl = slice(16 * c, 16 * c + 16)
        y_s[sl] = r["ysT"].T.reshape(16, 4, D)
        ret_s[:, sl] = r["o_rets"]
        lru_s[:, sl] = r["o_lrus"].transpose(0, 2, 1)
        conv_s[:, sl] = r["o_convs"].transpose(0, 2, 3, 1)
        for ri, nm in enumerate(("o_ssms_re", "o_ssms_im")):
            a = r[nm].reshape(2, 2, 64, 64, 16).transpose(0, 4, 1, 3, 2).reshape(2, 16, 128, 64)
            ssm_s[ri][:, sl] = a
        if c % 2 == 0:
            b = c // 2
            y_p[b] = r["ypT"].T
            ret_p[:, b] = r["o_retp"]
            lru_p[:, b] = r["o_lrup"].transpose(0, 2, 1).reshape(2, 1024)
            conv_p[:, b] = r["o_convp"].transpose(0, 2, 1)
            for ri, nm in enumerate(("o_ssmp_re", "o_ssmp_im")):
                a = r[nm].reshape(2, 2, 64, 64).transpose(0, 1, 3, 2).reshape(2, 128, 64)
                ssm_p[ri][:, b] = a
    return (y_p, y_s, ret_p, ret_s, lru_p, lru_s, conv_p, conv_s, ssm_p[0], ssm_s[0], ssm_p[1], ssm_s[1])


def kernel(**inputs):
    nc = build({})
    maps = make_in_maps(inputs)
    res = run_bass_kernel_spmd(nc, maps, core_ids=list(range(8)))
    return assemble(res.results)
```

```python
import math
import numpy as np
import concourse.bass as bass
import concourse.mybir as mybir
from concourse.bass_utils import run_bass_kernel_spmd

F32 = mybir.dt.float32
BF16 = mybir.dt.bfloat16
AF = mybir.ActivationFunctionType
ALU = mybir.AluOpType
SEM_MAX = 12000

D = 2048
KC = 16
FH = 5632
EPS = 1e-6
GAM = [1.0 - 2.0 ** (-5 - h) for h in range(4)]
GELU_C = 2.0 * math.sqrt(2.0 / math.pi)


class Dep:
    __slots__ = ("w", "r")

    def __init__(self):
        self.w = None
        self.r = []


class Eng:
    def __init__(self, fw, name, is_pe=False):
        self.fw = fw
        self.name = name
        self.is_pe = is_pe
        self.ops = []
        self.count = 0
        self.sems = []
        self.seen = {}

    def token(self):
        i, v = divmod(self.count - 1, SEM_MAX)
        while len(self.sems) <= i:
            self.sems.append(self.fw.nc.alloc_semaphore(f"s_{self.name}_{len(self.sems)}"))
        return (self.sems[i], v + 1, self)


class Slot:
    def __init__(self, fw, name):
        self.sem = fw.nc.alloc_semaphore(name)
        self.val = 0


class FW:
    def __init__(self, nc):
        self.nc = nc
        self.pe = Eng(self, "pe", True)
        self.act = Eng(self, "act")
        self.dve = Eng(self, "dve")
        self.pool = Eng(self, "pool")
        self.sp = Eng(self, "sp")
        self.nslot = 0

    def slot(self):
        self.nslot += 1
        return Slot(self, f"dq{self.nslot}")

    @staticmethod
    def _flat(lst):
        out = []
        for d in lst:
            if isinstance(d, (list, tuple)):
                out.extend(FW._flat(d))
            else:
                out.append(d)
        return out

    def _waits(self, eng, reads, writes):
        deps = {}
        for d in reads:
            if d.w is not None:
                deps[id(d.w)] = d.w
        for d in writes:
            if d.w is not None:
                deps[id(d.w)] = d.w
            for t in d.r:
                deps[id(t)] = t
        waits = []
        for t in deps.values():
            sem, val, src = t
            if src is eng and eng.is_pe:
                continue
            k = id(sem)
            if eng.seen.get(k, 0) >= val:
                continue
            eng.seen[k] = val
            waits.append((sem, val))
        return waits

    def op(self, eng, fn, reads=(), writes=()):
        reads = self._flat(reads)
        writes = self._flat(writes)
        waits = self._waits(eng, reads, writes)
        eng.count += 1
        tok = eng.token()
        for d in writes:
            d.w = tok
            d.r = []
        for d in reads:
            if d.w is not tok:
                d.r.append(tok)
        eng.ops.append((waits, fn, (tok[0], 1)))
        return tok

    def dma(self, eng, fn, slot, reads=(), writes=(), n=1):
        reads = self._flat(reads)
        writes = self._flat(writes)
        waits = self._waits(eng, reads, writes)
        slot.val += 16 * n
        tok = (slot.sem, slot.val, slot)
        for d in writes:
            d.w = tok
            d.r = []
        for d in reads:
            d.r.append(tok)
        eng.ops.append((waits, fn, (slot.sem, 16)))
        return tok

    def wait_tokens(self, eng, toks):
        waits = []
        for (sem, val, src) in toks:
            if eng.seen.get(id(sem), 0) >= val:
                continue
            eng.seen[id(sem)] = val
            waits.append((sem, val))
        eng.ops.append((waits, None, None))

    def emit(self):
        nc = self.nc

        def run(eng, e):
            for waits, fn, inc in eng.ops:
                for sem, val in waits:
                    e.wait_ge(sem, val)
                if fn is None:
                    continue
                r = fn(e)
                if isinstance(r, (list, tuple)):
                    for x in r:
                        x.then_inc(inc[0], inc[1])
                else:
                    r.then_inc(inc[0], inc[1])

        with nc.Block() as block:
            @block.tensor
            def _(e):
                run(self.pe, e)

            @block.scalar
            def _(e):
                run(self.act, e)

            @block.vector
            def _(e):
                run(self.dve, e)

            @block.gpsimd
            def _(e):
                run(self.pool, e)

            @block.sync
            def _(e):
                run(self.sp, e)


class T:
    def __init__(self, h):
        self.h = h
        self.d = Dep()

    def __getitem__(self, k):
        return self.h[k]


def sap(t, off, dims, parts=128, p0=0):
    row = 1
    for s in t.shape[1:]:
        row *= s
    return bass.AP(t, p0 * row + off, [[row, parts]] + [list(d) for d in dims])


def host_consts():
    f32 = np.float32
    c = {}
    inv = (1.0 / np.power(f32(10000.0), np.linspace(0.0, 1.0, 128, dtype=f32))).astype(f32)
    posP = np.arange(2048, dtype=f32)
    angP = (posP[None, :] * inv[:, None]).astype(f32).astype(np.float64)
    posS = (16384 + (np.arange(64) % 4)).astype(f32)
    angS = (posS[None, :] * inv[:, None]).astype(f32).astype(np.float64)
    c["cosP"] = np.cos(angP).astype(f32)
    c["sinP"] = np.sin(angP).astype(f32)
    c["cosS"] = np.cos(angS).astype(f32)
    c["sinS"] = np.sin(angS).astype(f32)
    g = np.array(GAM, dtype=np.float64)
    nP = np.arange(512) % 128
    nS = np.arange(64) % 4
    qdP = np.power(g[:, None], nP[None, :] + 1.0)
    qdS = np.power(g[:, None], nS[None, :] + 1.0)
    c["qdecP"] = np.broadcast_to(qdP[None, :, 0:128], (128, 4, 128)).astype(f32).copy()
    c["qdecS"] = np.broadcast_to(qdS[None], (128, 4, 64)).astype(f32).copy()
    m = np.arange(128)
    kv = np.zeros((128, 16), f32)
    kv[:, 0:4] = (np.power(g[None, :], -(m[:, None] + 1.0)) / 16.0)
    kv[:, 4:8] = (np.power(g[None, :], 127.0 - m[:, None]) / 16.0)
    kv[:, 8:12] = (np.power(g[None, :], -((m[:, None] % 4) + 1.0)) / 16.0)
    kv[:, 12:16] = (np.power(g[None, :], 3.0 - (m[:, None] % 4)) / 16.0)
    c["kvec"] = kv
    mk = np.zeros((128, 192), f32)
    mk[:, 0:128] = (m[None, :] >= m[:, None]).astype(f32)
    ms = np.arange(64)
    mk[0:64, 128:192] = ((ms[None, :] >= ms[:, None]) & (ms[None, :] // 4 == ms[:, None] // 4)).astype(f32)
    c["masks"] = mk
    oh = np.zeros((128, 16), f32)
    oh[0:64] = (ms[:, None] // 4 == np.arange(16)[None, :]).astype(f32)
    c["onehot"] = oh
    io = np.zeros((128, 256), f32)
    io[:, 0:128] = np.eye(128, dtype=f32)
    io[:, 128:256] = 1.0
    c["identones"] = io
    st = np.zeros((128, 8, 240), f32)
    for a in range(8):
        for j in range(16):
            st[a * 16 + j, a, 112 + j] = 1.0
    c["strips"] = st
    bm = np.zeros((128, 128), f32)
    for s_ in range(8):
        for t_ in range(s_, 8):
            bm[s_ * 16:(s_ + 1) * 16, t_ * 16:(t_ + 1) * 16] = 1.0
    c["bmask"] = bm
    return c


def prep_weights(inp):
    f32 = np.float32
    w = {}
    gains = [inp["norm_mix_even"][0], inp["norm_mix_even"][1], inp["norm_mix_odd"][0], inp["norm_mix_odd"][1],
             inp["norm_ffn"][0], inp["norm_ffn"][1], inp["norm_ffn"][2], inp["norm_ffn"][3], inp["norm_final"]]
    w["gains"] = np.ascontiguousarray(np.stack([np.asarray(g).reshape(16, 128).T for g in gains], axis=1)).astype(f32)
    lv = np.zeros((128, 2, 8, 8), f32)
    for e in range(2):
        for i in range(4):
            lv[:, e, :, i] = np.asarray(inp["lru_conv_w"][e, i]).reshape(8, 128).T
        lv[:, e, :, 4] = np.asarray(inp["lru_conv_b"][e]).reshape(8, 128).T
        lv[:, e, :, 5] = np.asarray(inp["lru_ba"][e]).reshape(8, 128).T
        lv[:, e, :, 6] = np.asarray(inp["lru_bx"][e]).reshape(8, 128).T
        lv[:, e, :, 7] = np.asarray(inp["lru_lambda"][e]).reshape(8, 128).T
    w["lruvec"] = lv
    w["ssmd"] = np.ascontiguousarray(np.stack([np.asarray(inp["ssm_d"][o]).reshape(16, 128).T for o in range(2)], axis=1)).astype(f32)
    w["bglu"] = np.ascontiguousarray(np.stack([np.asarray(inp["b_glu"][o]).reshape(32, 128).T for o in range(2)], axis=1)).astype(f32)
    for nm in ("ssm_a_re", "ssm_a_im"):
        a = np.asarray(inp[nm]).reshape(2, 2, 64, 64).transpose(0, 1, 3, 2).reshape(2, 128, 64)
        w[nm] = np.ascontiguousarray(a)
    w["ssm_log_dt"] = np.ascontiguousarray(np.asarray(inp["ssm_log_dt"]).reshape(2, 128))
    for nm in ("ssm_b_re", "ssm_b_im"):
        a = np.asarray(inp[nm]).reshape(2, 2, 64, 64, 16).transpose(0, 1, 3, 2, 4).reshape(2, 128, 64, 16)
        w[nm] = np.ascontiguousarray(a)
    for nm in ("ssm_c_re", "ssm_c_im"):
        a = np.asarray(inp[nm]).reshape(2, 2, 64, 16, 64).transpose(0, 1, 4, 2, 3).reshape(2, 128, 64, 16)
        w[nm] = np.ascontiguousarray(a)
    for nm in ("w_in_even", "w_out_even", "w_glu", "w_ffn_gu", "w_ffn_down", "lru_wa", "lru_wx"):
        w[nm] = np.ascontiguousarray(np.asarray(inp[nm], dtype=f32))
    return w


def build(cfg):
    tiles = cfg.get("tiles", [("p", 0), ("p", 1), ("p", 2), ("p", 3), ("s", 0)])
    nlayers = cfg.get("nlayers", 4)
    dbg = cfg.get("dbg", False)
    do_odd = cfg.get("do_odd", True)

    nc = bass.Bass("TRN2", target_bir_lowering=False)
    fw = FW(nc)
    PE, ACT, DVE, POOL, SP = fw.pe, fw.act, fw.dve, fw.pool, fw.sp

    def din(name, shape):
        return nc.dram_tensor(name, list(shape), F32, kind="ExternalInput")

    def dout(name, shape):
        return nc.dram_tensor(name, list(shape), F32, kind="ExternalOutput")

    xpT = din("xpT", [D, 2048])
    xsT = din("xsT", [D, 64])
    sret = din("sret", [2, 16, 4, 256, 256])
    slru = din("slru", [2, 1024, 16])
    sconv = din("sconv", [2, 1024, 16, 3])
    sssm = [din("sssm_re", [2, 128, 64, 16]), din("sssm_im", [2, 128, 64, 16])]
    w_in = din("w_in_even", [2, D, 6144])
    w_out = din("w_out_even", [2, D, D])
    w_glu = din("w_glu", [2, D, 2 * D])
    w_gu = din("w_ffn_gu", [4, D, 2 * FH])
    w_dn = din("w_ffn_down", [4, FH, D])
    lru_wa = din("lru_wa", [2, 8, 128, 128])
    lru_wx = din("lru_wx", [2, 8, 128, 128])
    gains_d = din("gains", [128, 9, 16])
    lruvec_d = din("lruvec", [128, 2, 8, 8])
    ssmd_d = din("ssmd", [128, 2, 16])
    bglu_d = din("bglu", [128, 2, 32])
    a_d = [din("ssm_a_re", [2, 128, 64]), din("ssm_a_im", [2, 128, 64])]
    ldt_d = din("ssm_log_dt", [2, 128])
    b_d = [din("ssm_b_re", [2, 128, 64, 16]), din("ssm_b_im", [2, 128, 64, 16])]
    c_d = [din("ssm_c_re", [2, 128, 64, 16]), din("ssm_c_im", [2, 128, 64, 16])]
    cst = {k: din("c_" + k, v.shape) for k, v in host_consts().items()}

    ypT = dout("ypT", [D, 2048])
    ysT = dout("ysT", [D, 64])
    o_retp = dout("o_retp", [2, 4, 256, 256])
    o_rets = dout("o_rets", [2, 16, 4, 256, 256])
    o_lrup = dout("o_lrup", [2, 128, 8])
    o_lrus = dout("o_lrus", [2, 1024, 16])
    o_convp = dout("o_convp", [2, 1024, 3])
    o_convs = dout("o_convs", [2, 1024, 16, 3])
    o_ssmp = [dout("o_ssmp_re", [2, 128, 64]), dout("o_ssmp_im", [2, 128, 64])]
    o_ssms = [dout("o_ssms_re", [2, 128, 64, 16]), dout("o_ssms_im", [2, 128, 64, 16])]
    if dbg:
        o_dbg = dout("o_dbg", [8, 128, 16, 512])

    def sb(name, shape, dt=F32):
        return T(nc.alloc_sbuf_tensor("sb_" + name, list(shape), dt))

    x = sb("x", [128, KC, 512])
    xn = sb("xn", [128, KC, 512], BF16)
    cat = sb("cat", [128, KC, 512], BF16)
    NW = 2
    wb = [sb(f"wb{i}", [128, KC, 512], BF16) for i in range(NW)]
    wslot = [fw.slot() for _ in range(NW)]
    wctr = [0]
    gains = sb("gains", [128, 9, 16])
    lruvec = sb("lruvec", [128, 2, 8, 8])
    ssmd = sb("ssmd", [128, 2, 16])
    bglu = sb("bglu", [128, 2, 32])
    wa_sb = sb("wa_sb", [128, 2, 8, 128], BF16)
    wx_sb = sb("wx_sb", [128, 2, 8, 128], BF16)
    kvec = sb("kvec", [128, 16])
    masks = sb("masks", [128, 192])
    onehot = sb("onehot", [128, 16])
    identones = sb("identones", [128, 256], BF16)
    qdecP = sb("qdecP", [128, 4, 128])
    qdecS = sb("qdecS", [128, 4, 64])
    strips = sb("strips", [128, 8, 240], BF16)
    bmask = sb("bmask", [128, 128])
    Pst = sb("Pst", [128, 2, 2, 64])
    apar = sb("apar", [128, 2, 2, 64])
    dtb = sb("dtb", [128, 2, 64])
    lsp = sb("lsp", [128, 2, 8, 2])
    Sst = sb("Sst", [128, 2, 4, 2, 256])
    Sbf = sb("Sbf", [128, 4, 2, 256], BF16)
    hst = sb("hst", [128, 2, 8])
    convst = sb("convst", [128, 2, 8, 3])
    ident = identones.h[:, 0:128]
    ones = identones.h[:, 128:256]

    cslot = fw.slot()
    cslot2 = fw.slot()
    cdeps = []
    cdeps2 = []

    def cload(t, src, cast=False):
        if cast:
            fw.dma(POOL, lambda e: e.dma_start(out=t.h[:], in_=src), cslot2, writes=[t.d])
            cdeps2.append(t.d)
        else:
            fw.dma(SP, lambda e: e.dma_start(out=t.h[:], in_=src), cslot, writes=[t.d])
            cdeps.append(t.d)

    cload(gains, gains_d.ap())
    cload(lruvec, lruvec_d.ap())
    cload(ssmd, ssmd_d.ap())
    cload(bglu, bglu_d.ap())
    cload(wa_sb, lru_wa.ap().rearrange("e n k j -> k e n j"), cast=True)
    cload(wx_sb, lru_wx.ap().rearrange("e n k j -> k e n j"), cast=True)
    cload(kvec, cst["kvec"].ap())
    cload(masks, cst["masks"].ap())
    cload(onehot, cst["onehot"].ap())
    cload(identones, cst["identones"].ap(), cast=True)
    cload(qdecP, cst["qdecP"].ap())
    cload(qdecS, cst["qdecS"].ap())
    cload(strips, cst["strips"].ap(), cast=True)
    cload(bmask, cst["bmask"].ap())
    for ri in range(2):
        fw.dma(SP, lambda e, ri=ri: e.dma_start(out=apar.h[:, ri, :, :], in_=a_d[ri].ap().rearrange("o p g -> p o g")), cslot, writes=[apar.d])
    for gh in range(2):
        for o in range(2):
            fw.dma(SP, lambda e, gh=gh, o=o: e.dma_start(out=dtb.h[gh * 64:(gh + 1) * 64, o, :], in_=bass.AP(ldt_d, o * 128 + gh * 64, [[0, 64], [1, 64]])), cslot, writes=[dtb.d])
    cdeps.append(apar.d)
    cdeps.append(dtb.d)
    final_tok = (cslot.sem, cslot.val, cslot)
    for d_ in cdeps:
        d_.w = final_tok
    final_tok2 = (cslot2.sem, cslot2.val, cslot2)
    for d_ in cdeps2:
        d_.w = final_tok2

    fw.op(ACT, lambda e: e.activation(out=dtb.h[:], in_=dtb.h[:], func=AF.Exp), reads=[dtb.d], writes=[dtb.d])
    sp_t = sb("sp_t", [128, 2, 8])
    fw.op(ACT, lambda e: e.activation(out=sp_t.h[:], in_=lruvec.h[:, :, :, 7], func=AF.Exp, scale=-1.0), reads=[lruvec.d], writes=[sp_t.d])
    fw.op(ACT, lambda e: e.activation(out=sp_t.h[:], in_=sp_t.h[:], func=AF.Ln, bias=1.0), reads=[sp_t.d], writes=[sp_t.d])
    fw.op(DVE, lambda e: e.tensor_scalar(out=lsp.h[:, :, :, 0], in0=sp_t.h[:], scalar1=-8.0, scalar2=None, op0=ALU.mult), reads=[sp_t.d], writes=[lsp.d])
    fw.op(DVE, lambda e: e.tensor_scalar(out=lsp.h[:, :, :, 1], in0=sp_t.h[:], scalar1=-16.0, scalar2=None, op0=ALU.mult), reads=[sp_t.d], writes=[lsp.d])

    banks = [T(nc.alloc_psum_tensor(f"pb{i}", [128, 512], F32)) for i in range(8)]
    bctr = [0]

    def bank():
        b = banks[2 + bctr[0] % 6]
        bctr[0] += 1
        return b

    NPG = 124
    arena = nc.alloc_sbuf_tensor("arena", [128, NPG * 128], F32)
    pdeps = [Dep() for _ in range(NPG)]

    class Ph:
        def __init__(self, p0=0):
            self.p = p0

        def al(self, shape, dt=F32):
            nel = 1
            for s_ in shape[1:]:
                nel *= s_
            nb = nel * (4 if dt == F32 else 2)
            npg = (nb + 511) // 512
            assert self.p + npg <= NPG, (self.p, npg)
            v = arena[:, self.p * 128:(self.p + npg) * 128]
            if dt != F32:
                v = v.bitcast(dt)
            v = v[:, 0:nel]
            if len(shape) == 3:
                v = v.rearrange("p (a b) -> p a b", a=shape[1])
            elif len(shape) == 4:
                v = v.rearrange("p (a b c) -> p a b c", a=shape[1], b=shape[2])
            t = T(v)
            t.d = pdeps[self.p:self.p + npg]
            self.p += npg
            return t

    ph = Ph()
    qraw = ph.al([128, 2, 512])
    kraw = ph.al([128, 2, 512])
    qr = ph.al([128, 2, 512], BF16)
    kr = ph.al([128, 2, 512], BF16)
    qdd = ph.al([128, 2, 512], BF16)
    t1 = ph.al([128, 2, 512])
    t2 = ph.al([128, 2, 512])
    gate = ph.al([128, 2, 512], BF16)
    vtok = ph.al([128, 4, 1024], BF16)
    kdtok = ph.al([128, 4, 256], BF16)
    kdm = [ph.al([128, 256], BF16) for i in range(2)]
    PT = [ph.al([128, 128], BF16) for i in range(2)]
    oT = ph.al([128, 2, 512])
    sq = ph.al([128, 2, 512], BF16)
    S0f = [ph.al([128, 2, 256]) for i in range(3)]
    S0b = [ph.al([128, 2, 256], BF16) for i in range(3)]
    Sout = [ph.al([128, 2, 256]) for i in range(3)]
    p_ret_end = ph.p
    ph = Ph()
    xp_ = ph.al([128, 3 + 512 + 1])
    xps = ph.al([128, 16, 7])
    xc = ph.al([128, 512])
    xcb = ph.al([128, 512], BF16)
    rg = ph.al([128, 512])
    ig = ph.al([128, 512])
    av = ph.al([128, 512])
    mv = ph.al([128, 512])
    hT = ph.al([128, 512])
    h0s = ph.al([128, 8, 16])
    cv0s = ph.al([128, 8, 16, 3])
    lrus_o = ph.al([128, 8, 16])
    convs_o = ph.al([128, 8, 16, 3])
    y32 = ph.al([128, 512])
    g1 = ph.al([128, 512])
    g2 = ph.al([128, 512])
    ph = Ph()
    hact = [ph.al([128, 4, 512], BF16) for i in range(2)]
    sgt = [ph.al([128, 512]) for i in range(2)]
    ph = Ph()
    Wneg = ph.al([128, 8, 2, 64])
    Wpos = ph.al([128, 9, 2, 64])
    frfi = ph.al([128, 2, 64])
    wtmp = [ph.al([128, 64]) for i in range(8)]
    braw = [ph.al([128, 8, 16]) for i in range(2)]
    craw = [ph.al([128, 8, 16]) for i in range(2)]
    bbar = [ph.al([128, 8, 16]) for i in range(2)]
    tA = ph.al([128, 1024])
    tB = ph.al([128, 1024])
    Rp = [ph.al([128, 8, 8, 16], BF16) for i in range(2)]
    Op = [ph.al([128, 8, 8, 16], BF16) for i in range(2)]
    MTs = ph.al([128, 16, 128], BF16)
    QTs = [ph.al([128, 128], BF16) for i in range(2)]
    Ub2 = [ph.al([128, 16, 64], BF16) for i in range(2)]
    Yb = ph.al([128, 16, 64], BF16)
    Aar = ph.al([128, 2, 8, 65])
    Abf = ph.al([128, 2, 8, 64], BF16)
    h0s5 = ph.al([128, 2, 8, 16])
    Pp = ph.al([128, 2, 8, 16])
    Pf = ph.al([128, 2, 8, 16])
    y32b = ph.al([128, 512])
    g1b = ph.al([128, 512])
    g2b = ph.al([128, 512])
    Wd = ph.al([128, 7, 2, 64])
    rstd = sb("rstd", [128, 512])
    cosT = sb("cosT", [128, 512])
    sinT = sb("sinT", [128, 512])
    tslot = fw.slot()
    tslot2 = fw.slot()
    stslot2 = fw.slot()
    xslot = fw.slot()
    S0slot = [fw.slot() for _ in range(3)]
    S0bslot = [fw.slot() for _ in range(3)]
    Soslot = [fw.slot() for _ in range(3)]
    stslot = fw.slot()
    oslots = {}

    def oslot(k):
        if k not in oslots:
            oslots[k] = fw.slot()
        return oslots[k]

    out_toks = []

    NLOADS = 192
    wscr_l = [nc.dram_tensor(f"wscr{q}", [NLOADS // 2, 128, KC * 512], BF16, kind="Internal") for q in range(2)]
    wscr_ap = lambda idx: wscr_l[idx // (NLOADS // 2)].ap()[idx % (NLOADS // 2)]
    wsd = [Dep() for _ in range(NLOADS)]
    wslot_hw = [fw.slot() for _ in range(NW)]
    sslot = [fw.slot() for _ in range(NW)]
    lctr = [0]
    passno = [0]
    use_scr = cfg.get("use_scr", True)

    def load_w(dram_ap_list):
        i = wctr[0] % NW
        wctr[0] += 1
        t = wb[i]
        idx = lctr[0]
        lctr[0] += 1
        flat = t.h[:].rearrange("p a b -> p (a b)")
        if use_scr and passno[0] > 0:
            fw.dma(SP, lambda e: e.dma_start(out=flat, in_=wscr_ap(idx)), wslot_hw[i], reads=[wsd[idx]], writes=[t.d])
            return t

        def fn(e, t=t, lst=dram_ap_list):
            r = []
            for src, dst in lst:
                r.append(e.dma_start(out=dst(t.h), in_=src))
            return r
        fw.dma(POOL, fn, wslot[i], writes=[t.d], n=len(dram_ap_list))
        if use_scr and len(tiles) > 1:
            fw.dma(SP, lambda e: e.dma_start(out=wscr_ap(idx), in_=flat), sslot[i], reads=[t.d], writes=[wsd[idx]])
        return t

    def wsrc(wd, c0, ncol):
        return wd[:, c0:c0 + ncol].rearrange("(kc p) j -> p kc j", p=128)

    def rmsnorm(gi, Tn):
        fw.op(ACT, lambda e: e.activation(out=cat.h[:, :, 0:Tn], in_=x.h[:, :, 0:Tn], func=AF.Square), reads=[x.d], writes=[cat.d])
        b = bank()

        def mm(e):
            r = None
            for kc in range(KC):
                r = e.matmul(b.h[:, 0:Tn], lhsT=ones, rhs=cat.h[:, kc, 0:Tn], start=(kc == 0), stop=(kc == KC - 1))
            return r
        fw.op(PE, mm, reads=[cat.d, identones.d], writes=[b.d])
        fw.op(ACT, lambda e: e.activation(out=rstd.h[:, 0:Tn], in_=b.h[:, 0:Tn], func=AF.Sqrt, scale=1.0 / D, bias=EPS), reads=[b.d], writes=[rstd.d])
        fw.op(DVE, lambda e: e.reciprocal(out=rstd.h[:, 0:Tn], in_=rstd.h[:, 0:Tn]), reads=[rstd.d], writes=[rstd.d])
        for kc in range(KC):
            eng = DVE
            fw.op(eng, lambda e, kc=kc: e.scalar_tensor_tensor(out=xn.h[:, kc, 0:Tn], in0=x.h[:, kc, 0:Tn], scalar=gains.h[:, gi, kc:kc + 1],
                                                               in1=rstd.h[:, 0:Tn], op0=ALU.mult, op1=ALU.mult),
                  reads=[x.d, gains.d, rstd.d], writes=[xn.d])

    def proj_fm(wt, col0, src, Tn, nk=KC):
        b = bank()

        def mm(e):
            r = None
            for kc in range(nk):
                r = e.matmul(b.h[:, 0:Tn], lhsT=wt.h[:, kc, col0:col0 + 128], rhs=src.h[:, kc, 0:Tn], start=(kc == 0), stop=(kc == nk - 1))
            return r
        fw.op(PE, mm, reads=[wt.d, src.d], writes=[b.d])
        return b

    def add_to_x(b, oc, Tn):
        fw.op(DVE, lambda e: e.tensor_tensor(out=x.h[:, oc, 0:Tn], in0=b.h[:, 0:Tn], in1=x.h[:, oc, 0:Tn], op=ALU.add), reads=[b.d, x.d], writes=[x.d])

    def ffn(layer, Tn):
        rmsnorm(4 + layer, Tn)
        wg = w_gu.ap()[layer]
        wd = w_dn.ap()[layer]
        for hb in range(FH // 512):
            tg = load_w([(wsrc(wg, hb * 512, 512), lambda h: h[:])])
            tu = load_w([(wsrc(wg, FH + hb * 512, 512), lambda h: h[:])])
            ha = hact[hb % 2]
            for m in range(4):
                bg = proj_fm(tg, m * 128, xn, Tn)
                bu = proj_fm(tu, m * 128, xn, Tn)
                sg = sgt[m % 2]
                fw.op(ACT, lambda e, bg=bg, sg=sg: e.activation(out=sg.h[:, 0:Tn], in_=bg.h[:, 0:Tn], func=AF.Silu), reads=[bg.d], writes=[sg.d])
                fw.op(DVE, lambda e, bu=bu, sg=sg, ha=ha, m=m: e.tensor_tensor(out=ha.h[:, m, 0:Tn], in0=bu.h[:, 0:Tn], in1=sg.h[:, 0:Tn], op=ALU.mult),
                      reads=[bu.d, sg.d], writes=[ha.d])
            td = load_w([(wd[hb * 512:(hb + 1) * 512, :].rearrange("(kc p) j -> p kc j", p=128), lambda h: h[:].rearrange("p a b -> p (a b)").rearrange("p (kc j) -> p kc j", kc=4))])
            tdv = td.h[:].rearrange("p a b -> p (a b)").rearrange("p (kc j) -> p kc j", kc=4)
            for oc in range(KC):
                b = bank()

                def mm(e, b=b, oc=oc, ha=ha, tdv=tdv):
                    r = None
                    for kc in range(4):
                        r = e.matmul(b.h[:, 0:Tn], lhsT=tdv[:, kc, oc * 128:(oc + 1) * 128], rhs=ha.h[:, kc, 0:Tn], start=(kc == 0), stop=(kc == 3))
                    return r
                fw.op(PE, mm, reads=[td.d, ha.d], writes=[b.d])
                add_to_x(b, oc, Tn)

    def even_layer(e_, kind, ti, Tn):
        samp = kind == "s"
        NTC = 1 if samp else 4
        TR = 64 if samp else 128
        rmsnorm(e_, Tn)
        wi = w_in.ap()[e_]
        qdec = qdecS if samp else qdecP
        kv0 = 8 if samp else 0
        gd = [GAM[h] ** (4 if samp else 128) for h in range(4)]
        for half in range(2):
            tv = load_w([(wsrc(wi, 2048 + half * 512, 512), lambda h: h[:])])
            for tc in range(NTC):
                b = bank()

                def mm(e, b=b, tc=tc, tv=tv):
                    r = None
                    for kc in range(KC):
                        r = e.matmul(b.h[0:TR, :], lhsT=xn.h[:, kc, tc * 128:tc * 128 + TR], rhs=tv.h[:, kc, :], start=(kc == 0), stop=(kc == KC - 1))
                    return r
                fw.op(PE, mm, reads=[tv.d, xn.d], writes=[b.d])
                fw.op(ACT, lambda e, b=b, tc=tc, half=half: e.activation(out=vtok.h[0:TR, tc, half * 512:(half + 1) * 512], in_=b.h[0:TR, :], func=AF.Copy),
                      reads=[b.d], writes=[vtok.d])
        def _head(h):
            tqk = load_w([(wsrc(wi, h * 256, 256), lambda hh: hh[:, :, 0:256]), (wsrc(wi, 1024 + h * 256, 256), lambda hh: hh[:, :, 256:512])])
            for dc in range(2):
                bq = proj_fm(tqk, dc * 128, xn, Tn)
                fw.op(ACT, lambda e, bq=bq, dc=dc: e.activation(out=qraw.h[:, dc, 0:Tn], in_=bq.h[:, 0:Tn], func=AF.Copy), reads=[bq.d], writes=[qraw.d])
                bk = proj_fm(tqk, 256 + dc * 128, xn, Tn)
                fw.op(ACT, lambda e, bk=bk, dc=dc: e.activation(out=kraw.h[:, dc, 0:Tn], in_=bk.h[:, 0:Tn], func=AF.Copy), reads=[bk.d], writes=[kraw.d])
            tg = load_w([(wsrc(wi, 3072 + h * 256, 256), lambda hh: hh[:, :, 0:256])])
            for dc in range(2):
                bg = proj_fm(tg, dc * 128, xn, Tn)
                fw.op(ACT, lambda e, bg=bg, dc=dc: e.activation(out=gate.h[:, dc, 0:Tn], in_=bg.h[:, 0:Tn], func=AF.Silu), reads=[bg.d], writes=[gate.d])
            for raw, outt, eng in ((qraw, qr, DVE), (kraw, kr, POOL)):
                for dc in range(2):
                    fw.op(eng, lambda e, raw=raw, dc=dc: e.tensor_tensor(out=t1.h[:, dc, 0:Tn], in0=raw.h[:, dc, 0:Tn], in1=cosT.h[:, 0:Tn], op=ALU.mult), reads=[raw.d, cosT.d], writes=[t1.d])
                    fw.op(eng, lambda e, raw=raw, dc=dc: e.tensor_tensor(out=t2.h[:, dc, 0:Tn], in0=raw.h[:, dc, 0:Tn], in1=sinT.h[:, 0:Tn], op=ALU.mult), reads=[raw.d, sinT.d], writes=[t2.d])
                fw.op(eng, lambda e, outt=outt: e.tensor_tensor(out=outt.h[:, 0, 0:Tn], in0=t1.h[:, 0, 0:Tn], in1=t2.h[:, 1, 0:Tn], op=ALU.subtract), reads=[t1.d, t2.d], writes=[outt.d])
                fw.op(eng, lambda e, outt=outt: e.tensor_tensor(out=outt.h[:, 1, 0:Tn], in0=t1.h[:, 1, 0:Tn], in1=t2.h[:, 0, 0:Tn], op=ALU.add), reads=[t1.d, t2.d], writes=[outt.d])
            if samp:
                qdb = sap(qdecS.h, h * 64, [[0, 2], [1, 64]])
                fw.op(DVE, lambda e, qdb=qdb: e.tensor_tensor(out=qdd.h[:, :, 0:64], in0=qr.h[:, :, 0:64], in1=qdb, op=ALU.mult), reads=[qr.d, qdecS.d], writes=[qdd.d])
            else:
                qdb = sap(qdecP.h, h * 128, [[0, 2], [0, 4], [1, 128]])
                fw.op(DVE, lambda e, qdb=qdb: e.tensor_tensor(out=qdd.h[:, :, :].rearrange("p a (c n) -> p a c n", c=4), in0=qr.h[:, :, :].rearrange("p a (c n) -> p a c n", c=4), in1=qdb, op=ALU.mult), reads=[qr.d, qdecP.d], writes=[qdd.d])
            for tc in range(NTC):
                b = bank()
                bb = b.h[:, 0:128].bitcast(BF16)

                def tr(e, tc=tc, bb=bb):
                    r = None
                    for dc in range(2):
                        r = e.transpose(out=bb[0:TR, dc * 128:(dc + 1) * 128], in_=kr.h[:, dc, tc * 128:tc * 128 + TR], identity=ident)
                    return r
                fw.op(PE, tr, reads=[kr.d, identones.d], writes=[b.d])
                fw.op(DVE, lambda e, tc=tc, bb=bb: e.tensor_scalar(out=kdtok.h[0:TR, tc, :], in0=bb[0:TR, :], scalar1=kvec.h[0:TR, kv0 + 4 + h:kv0 + 5 + h], scalar2=None, op0=ALU.mult),
                      reads=[b.d, kvec.d], writes=[kdtok.d])
            po = [banks[0], banks[1]]
            for c in range(NTC):
                bs = bank()

                def sc(e, bs=bs, c=c):
                    r = None
                    for dc in range(2):
                        r = e.matmul(bs.h[0:TR, 0:TR], lhsT=kr.h[:, dc, c * 128:c * 128 + TR], rhs=qdd.h[:, dc, c * 128:c * 128 + TR], start=(dc == 0), stop=(dc == 1))
                    return r
                fw.op(PE, sc, reads=[kr.d, qdd.d], writes=[bs.d])
                pt = PT[c % 2]
                moff = 128 if samp else 0
                fw.op(DVE, lambda e, bs=bs, pt=pt: e.scalar_tensor_tensor(out=pt.h[0:TR, 0:TR], in0=bs.h[0:TR, 0:TR], scalar=kvec.h[0:TR, kv0 + h:kv0 + h + 1],
                                                                             in1=masks.h[0:TR, moff:moff + TR], op0=ALU.mult, op1=ALU.mult),
                      reads=[bs.d, kvec.d, masks.d], writes=[pt.d])
                if not samp:
                    for ec in range(2):
                        def om(e, ec=ec, c=c, pt=pt):
                            e.matmul(po[ec].h[:, c * 128:(c + 1) * 128], lhsT=vtok.h[:, c, h * 256 + ec * 128:h * 256 + (ec + 1) * 128], rhs=pt.h[:, :], start=True, stop=False)
                            r = None
                            for dc in range(2):
                                r = e.matmul(po[ec].h[:, c * 128:(c + 1) * 128], lhsT=Sbf.h[:, h, dc, ec * 128:(ec + 1) * 128], rhs=qdd.h[:, dc, c * 128:(c + 1) * 128], start=False, stop=(dc == 1))
                            return r
                        fw.op(PE, om, reads=[vtok.d, pt.d, Sbf.d, qdd.d], writes=[po[ec].d])
                    bS = bank()

                    def su(e, bS=bS, c=c):
                        r = None
                        for dc in range(2):
                            r = e.matmul(bS.h[:, dc * 256:(dc + 1) * 256], lhsT=kdtok.h[:, c, dc * 128:(dc + 1) * 128], rhs=vtok.h[:, c, h * 256:(h + 1) * 256], start=True, stop=True)
                        return r
                    fw.op(PE, su, reads=[kdtok.d, vtok.d], writes=[bS.d])
                    fw.op(DVE, lambda e, bS=bS: e.scalar_tensor_tensor(out=Sst.h[:, e_, h, :, :], in0=Sst.h[:, e_, h, :, :], scalar=gd[h], in1=bS.h[:, :].rearrange("p (a b) -> p a b", a=2), op0=ALU.mult, op1=ALU.add),
                          reads=[bS.d, Sst.d], writes=[Sst.d])
                    fw.op(ACT, lambda e: e.activation(out=Sbf.h[:, h, :, :], in_=Sst.h[:, e_, h, :, :], func=AF.Copy), reads=[Sst.d], writes=[Sbf.d])
                else:
                    for ec in range(2):
                        fw.op(PE, lambda e, ec=ec, pt=pt: e.matmul(po[ec].h[:, 0:64], lhsT=vtok.h[0:64, 0, h * 256 + ec * 128:h * 256 + (ec + 1) * 128], rhs=pt.h[0:64, 0:64], start=True, stop=True),
                              reads=[vtok.d, pt.d], writes=[po[ec].d])
                    for j in range(16):
                        si = (h * 16 + j) % 3
                        s0f, s0b, so_ = S0f[si], S0b[si], Sout[si]
                        src = sret.ap()[e_, j, h].rearrange("(dc p) e -> p dc e", p=128)
                        fw.dma(SP, lambda e, s0f=s0f, src=src: e.dma_start(out=s0f.h[:], in_=src), S0slot[si], writes=[s0f.d])
                        fw.dma(POOL, lambda e, s0b=s0b, src=src: e.dma_start(out=s0b.h[:], in_=src), S0bslot[si], writes=[s0b.d])
                        for ec in range(2):
                            def im(e, ec=ec, j=j, s0b=s0b):
                                r = None
                                for dc in range(2):
                                    r = e.matmul(po[ec].h[:, 4 * j:4 * j + 4], lhsT=s0b.h[:, dc, ec * 128:(ec + 1) * 128], rhs=qdd.h[:, dc, 4 * j:4 * j + 4], start=False, stop=(dc == 1), skip_group_check=True)
                                return r
                            fw.op(PE, im, reads=[s0b.d, qdd.d], writes=[po[ec].d])
                        km = kdm[j % 2]
                        fw.op(DVE, lambda e, km=km, j=j: e.tensor_scalar(out=km.h[0:64, :], in0=kdtok.h[0:64, 0, :], scalar1=onehot.h[0:64, j:j + 1], scalar2=None, op0=ALU.mult),
                              reads=[kdtok.d, onehot.d], writes=[km.d])
                        bS = bank()

                        def su(e, bS=bS, km=km):
                            r = None
                            for dc in range(2):
                                r = e.matmul(bS.h[:, dc * 256:(dc + 1) * 256], lhsT=km.h[0:64, dc * 128:(dc + 1) * 128], rhs=vtok.h[0:64, 0, h * 256:(h + 1) * 256], start=True, stop=True)
                            return r
                        fw.op(PE, su, reads=[km.d, vtok.d], writes=[bS.d])
                        fw.op(DVE, lambda e, bS=bS, s0f=s0f, so_=so_: e.scalar_tensor_tensor(out=so_.h[:], in0=s0f.h[:], scalar=gd[h], in1=bS.h[:, :].rearrange("p (a b) -> p a b", a=2), op0=ALU.mult, op1=ALU.add),
                              reads=[bS.d, s0f.d], writes=[so_.d])
                        dst = o_rets.ap()[e_, j, h].rearrange("(dc p) e -> p dc e", p=128)
                        out_toks.append(fw.dma(SP, lambda e, so_=so_, dst=dst: e.dma_start(out=dst, in_=so_.h[:]), Soslot[si], reads=[so_.d]))
            for ec in range(2):
                fw.op(ACT, lambda e, ec=ec: e.activation(out=oT.h[:, ec, 0:Tn], in_=po[ec].h[:, 0:Tn], func=AF.Copy), reads=[po[ec].d], writes=[oT.d])
            fw.op(ACT, lambda e: e.activation(out=sq.h[:, :, 0:Tn], in_=oT.h[:, :, 0:Tn], func=AF.Square), reads=[oT.d], writes=[sq.d])
            bn = bank()

            def nm(e, bn=bn):
                r = None
                for ec in range(2):
                    r = e.matmul(bn.h[:, 0:Tn], lhsT=ones, rhs=sq.h[:, ec, 0:Tn], start=(ec == 0), stop=(ec == 1))
                return r
            fw.op(PE, nm, reads=[sq.d, identones.d], writes=[bn.d])
            fw.op(ACT, lambda e, bn=bn: e.activation(out=rstd.h[:, 0:Tn], in_=bn.h[:, 0:Tn], func=AF.Sqrt, scale=1.0 / 256, bias=EPS), reads=[bn.d], writes=[rstd.d])
            fw.op(DVE, lambda e: e.reciprocal(out=rstd.h[:, 0:Tn], in_=rstd.h[:, 0:Tn]), reads=[rstd.d], writes=[rstd.d])
            for ec in range(2):
                fw.op(DVE, lambda e, ec=ec: e.tensor_tensor(out=t1.h[:, ec, 0:Tn], in0=oT.h[:, ec, 0:Tn], in1=rstd.h[:, 0:Tn], op=ALU.mult), reads=[oT.d, rstd.d], writes=[t1.d])
                fw.op(DVE, lambda e, ec=ec: e.tensor_tensor(out=cat.h[:, 2 * h + ec, 0:Tn], in0=t1.h[:, ec, 0:Tn], in1=gate.h[:, ec, 0:Tn], op=ALU.mult), reads=[t1.d, gate.d], writes=[cat.d])
            if (not samp) and ti == 3:
                dst = o_retp.ap()[e_, h].rearrange("(dc p) e -> p dc e", p=128)
                out_toks.append(fw.dma(SP, lambda e, dst=dst: e.dma_start(out=dst, in_=Sst.h[:, e_, h, :, :]), oslot(("retp", e_, h)), reads=[Sst.d]))
        for h_ in range(4):
            _head(h_)
        if samp:
            fw.dma(SP, lambda e: e.dma_start(out=h0s.h[:], in_=slru.ap()[e_].rearrange("(n p) j -> p n j", p=128)), stslot, writes=[h0s.d])
            fw.dma(SP, lambda e: e.dma_start(out=cv0s.h[:], in_=sconv.ap()[e_].rearrange("(n p) j i -> p n j i", p=128)), stslot2, writes=[cv0s.d])
        def _blk(nb):
            txy = load_w([(wsrc(wi, 4096 + nb * 128, 128), lambda hh: hh[:, :, 0:128]), (wsrc(wi, 5120 + nb * 128, 128), lambda hh: hh[:, :, 128:256])])
            bx_ = proj_fm(txy, 0, xn, Tn)
            by_ = proj_fm(txy, 128, xn, Tn)
            lv = lambda k: lruvec.h[:, e_, nb, k:k + 1]
            if not samp:
                fw.op(ACT, lambda e, bx_=bx_: e.activation(out=xp_.h[:, 3:3 + Tn], in_=bx_.h[:, 0:Tn], func=AF.Copy), reads=[bx_.d], writes=[xp_.d])
                if ti == 0:
                    fw.op(DVE, lambda e: e.memset(xp_.h[:, 0:3], 0.0), writes=[xp_.d])
                else:
                    fw.op(DVE, lambda e, nb=nb: e.tensor_copy(out=xp_.h[:, 0:3], in_=convst.h[:, e_, nb, :]), reads=[convst.d], writes=[xp_.d])
                xin = lambda i: xp_.h[:, i:i + Tn]
                xco = xc.h[:, 0:Tn]
            else:
                fw.op(ACT, lambda e, bx_=bx_: e.activation(out=xps.h[:, :, 3:7], in_=bx_.h[:, 0:64].rearrange("p (j t) -> p j t", t=4), func=AF.Copy), reads=[bx_.d], writes=[xps.d])
                fw.op(DVE, lambda e, nb=nb: e.tensor_copy(out=xps.h[:, :, 0:3], in_=cv0s.h[:, nb, :, :]), reads=[cv0s.d], writes=[xps.d])
                xin = lambda i: xps.h[:, :, i:i + 4]
                xco = xc.h[:, 0:64].rearrange("p (j t) -> p j t", t=4)
            fw.op(DVE, lambda e, xin=xin, xco=xco, nb=nb: e.tensor_scalar(out=xco, in0=xin(0), scalar1=lruvec.h[:, e_, nb, 0:1], scalar2=lruvec.h[:, e_, nb, 4:5], op0=ALU.mult, op1=ALU.add),
                  reads=[xp_.d, xps.d, lruvec.d], writes=[xc.d])
            for i in range(1, 4):
                fw.op(DVE, lambda e, xin=xin, xco=xco, nb=nb, i=i: e.scalar_tensor_tensor(out=xco, in0=xin(i), scalar=lruvec.h[:, e_, nb, i:i + 1], in1=xco, op0=ALU.mult, op1=ALU.add),
                      reads=[xp_.d, xps.d, lruvec.d, xc.d], writes=[xc.d])
            if not samp:
                fw.op(POOL, lambda e, nb=nb: e.tensor_copy(out=convst.h[:, e_, nb, :], in_=xp_.h[:, Tn:Tn + 3]), reads=[xp_.d], writes=[convst.d])
            else:
                fw.op(POOL, lambda e, nb=nb: e.tensor_copy(out=convs_o.h[:, nb, :, :], in_=xps.h[:, :, 4:7]), reads=[xps.d], writes=[convs_o.d])
            fw.op(ACT, lambda e: e.activation(out=xcb.h[:, 0:Tn], in_=xc.h[:, 0:Tn], func=AF.Copy), reads=[xc.d], writes=[xcb.d])
            br = bank()
            fw.op(PE, lambda e, br=br, nb=nb: e.matmul(br.h[:, 0:Tn], lhsT=wa_sb.h[:, e_, nb, :], rhs=xcb.h[:, 0:Tn], start=True, stop=True), reads=[wa_sb.d, xcb.d], writes=[br.d])
            bi = bank()
            fw.op(PE, lambda e, bi=bi, nb=nb: e.matmul(bi.h[:, 0:Tn], lhsT=wx_sb.h[:, e_, nb, :], rhs=xcb.h[:, 0:Tn], start=True, stop=True), reads=[wx_sb.d, xcb.d], writes=[bi.d])
            fw.op(ACT, lambda e, br=br, nb=nb: e.activation(out=rg.h[:, 0:Tn], in_=br.h[:, 0:Tn], func=AF.Sigmoid, bias=lruvec.h[:, e_, nb, 5:6]), reads=[br.d, lruvec.d], writes=[rg.d])
            fw.op(ACT, lambda e, bi=bi, nb=nb: e.activation(out=ig.h[:, 0:Tn], in_=bi.h[:, 0:Tn], func=AF.Sigmoid, bias=lruvec.h[:, e_, nb, 6:7]), reads=[bi.d, lruvec.d], writes=[ig.d])
            fw.op(ACT, lambda e, nb=nb: e.activation(out=av.h[:, 0:Tn], in_=rg.h[:, 0:Tn], func=AF.Exp, scale=lsp.h[:, e_, nb, 0:1]), reads=[rg.d, lsp.d], writes=[av.d])
            fw.op(ACT, lambda e, nb=nb: e.activation(out=mv.h[:, 0:Tn], in_=rg.h[:, 0:Tn], func=AF.Exp, scale=lsp.h[:, e_, nb, 1:2]), reads=[rg.d, lsp.d], writes=[mv.d])
            fw.op(ACT, lambda e: e.activation(out=mv.h[:, 0:Tn], in_=mv.h[:, 0:Tn], func=AF.Sqrt, scale=-1.0, bias=1.0), reads=[mv.d], writes=[mv.d])
            fw.op(DVE, lambda e: e.tensor_tensor(out=mv.h[:, 0:Tn], in0=mv.h[:, 0:Tn], in1=ig.h[:, 0:Tn], op=ALU.mult), reads=[mv.d, ig.d], writes=[mv.d])
            fw.op(DVE, lambda e: e.tensor_tensor(out=mv.h[:, 0:Tn], in0=mv.h[:, 0:Tn], in1=xc.h[:, 0:Tn], op=ALU.mult), reads=[mv.d, xc.d], writes=[mv.d])
            if not samp:
                init = 0.0 if ti == 0 else hst.h[:, e_, nb:nb + 1]
                fw.op(DVE, lambda e, init=init: e.tensor_tensor_scan(out=hT.h[:, 0:Tn], data0=av.h[:, 0:Tn], data1=mv.h[:, 0:Tn], initial=init, op0=ALU.mult, op1=ALU.add),
                      reads=[av.d, mv.d, hst.d], writes=[hT.d])
                fw.op(POOL, lambda e, nb=nb: e.tensor_copy(out=hst.h[:, e_, nb:nb + 1], in_=hT.h[:, Tn - 1:Tn]), reads=[hT.d], writes=[hst.d])
            else:
                av3 = av.h[:, 0:64].rearrange("p (j t) -> p j t", t=4)
                mv3 = mv.h[:, 0:64].rearrange("p (j t) -> p j t", t=4)
                fw.op(DVE, lambda e, nb=nb: e.tensor_tensor(out=g1.h[:, 0:16], in0=av3[:, :, 0], in1=h0s.h[:, nb, :], op=ALU.mult), reads=[av.d, h0s.d], writes=[g1.d])
                fw.op(DVE, lambda e: e.tensor_tensor(out=mv3[:, :, 0], in0=mv3[:, :, 0], in1=g1.h[:, 0:16], op=ALU.add), reads=[mv.d, g1.d], writes=[mv.d])
                fw.op(DVE, lambda e: e.memset(av3[:, :, 0], 0.0), reads=[g1.d], writes=[av.d])
                fw.op(DVE, lambda e: e.tensor_tensor_scan(out=hT.h[:, 0:64], data0=av.h[:, 0:64], data1=mv.h[:, 0:64], initial=0.0, op0=ALU.mult, op1=ALU.add),
                      reads=[av.d, mv.d], writes=[hT.d])
                fw.op(POOL, lambda e, nb=nb: e.tensor_copy(out=lrus_o.h[:, nb, :], in_=hT.h[:, 0:64].rearrange("p (j t) -> p j t", t=4)[:, :, 3]), reads=[hT.d], writes=[lrus_o.d])
            fw.op(ACT, lambda e, by_=by_: e.activation(out=y32.h[:, 0:Tn], in_=by_.h[:, 0:Tn], func=AF.Copy), reads=[by_.d], writes=[y32.d])
            gelu_mul(y32, hT, cat.h[:, 8 + nb, 0:Tn], cat, Tn, g1, g2)
        for nb_ in range(8):
            _blk(nb_)
        if (not samp) and ti == 3:
            out_toks.append(fw.dma(SP, lambda e: e.dma_start(out=o_lrup.ap()[e_], in_=hst.h[:, e_, :]), oslot(("lrup", e_)), reads=[hst.d]))
            out_toks.append(fw.dma(SP, lambda e: e.dma_start(out=o_convp.ap()[e_].rearrange("(n p) i -> p n i", p=128), in_=convst.h[:, e_, :, :]), oslot(("convp", e_)), reads=[convst.d]))
        if samp:
            out_toks.append(fw.dma(SP, lambda e: e.dma_start(out=o_lrus.ap()[e_].rearrange("(n p) j -> p n j", p=128), in_=lrus_o.h[:]), oslot(("lrus", e_)), reads=[lrus_o.d]))
            out_toks.append(fw.dma(SP, lambda e: e.dma_start(out=o_convs.ap()[e_].rearrange("(n p) j i -> p n j i", p=128), in_=convs_o.h[:]), oslot(("convs", e_)), reads=[convs_o.d]))
        wo = w_out.ap()[e_]
        for og in range(4):
            two = load_w([(wsrc(wo, og * 512, 512), lambda hh: hh[:])])
            for m in range(4):
                b = proj_fm(two, m * 128, cat, Tn)
                add_to_x(b, og * 4 + m, Tn)

    def gelu_mul(src, mul, out_ap, out_t, Tn, g1, g2):
        s = src.h[:, 0:Tn]
        fw.op(DVE, lambda e: e.tensor_tensor(out=g1.h[:, 0:Tn], in0=s, in1=s, op=ALU.mult), reads=[src.d], writes=[g1.d])
        fw.op(DVE, lambda e: e.tensor_scalar(out=g1.h[:, 0:Tn], in0=g1.h[:, 0:Tn], scalar1=0.044715, scalar2=1.0, op0=ALU.mult, op1=ALU.add), reads=[g1.d], writes=[g1.d])
        fw.op(DVE, lambda e: e.tensor_tensor(out=g1.h[:, 0:Tn], in0=g1.h[:, 0:Tn], in1=s, op=ALU.mult), reads=[g1.d, src.d], writes=[g1.d])
        fw.op(ACT, lambda e: e.activation(out=g2.h[:, 0:Tn], in_=g1.h[:, 0:Tn], func=AF.Sigmoid, scale=GELU_C), reads=[g1.d], writes=[g2.d])
        if mul is not None:
            fw.op(DVE, lambda e: e.tensor_tensor(out=g2.h[:, 0:Tn], in0=g2.h[:, 0:Tn], in1=s, op=ALU.mult), reads=[g2.d, src.d], writes=[g2.d])
            fw.op(DVE, lambda e: e.tensor_tensor(out=out_ap, in0=g2.h[:, 0:Tn], in1=mul.h[:, 0:Tn], op=ALU.mult), reads=[g2.d, mul.d], writes=[out_t.d])
        else:
            fw.op(DVE, lambda e: e.tensor_tensor(out=out_ap, in0=g2.h[:, 0:Tn], in1=s, op=ALU.mult), reads=[g2.d, src.d], writes=[out_t.d])

    def odd_layer(o_, kind, ti, Tn):
        rmsnorm(2 + o_, Tn)
        if not do_odd:
            return
        s5_layer(o_, kind, ti, Tn)
        if cfg.get("stop_s5"):
            raise StopIteration
        wg = w_glu.ap()[o_]
        for og in range(4):
            t1w = load_w([(wsrc(wg, og * 512, 512), lambda hh: hh[:])])
            t2w = load_w([(wsrc(wg, D + og * 512, 512), lambda hh: hh[:])])
            for m in range(4):
                oc = og * 4 + m
                b1 = proj_fm(t1w, m * 128, cat, Tn)
                b2 = proj_fm(t2w, m * 128, cat, Tn)
                sg = sgt[m % 2]
                fw.op(ACT, lambda e, b2=b2, sg=sg, oc=oc: e.activation(out=sg.h[:, 0:Tn], in_=b2.h[:, 0:Tn], func=AF.Sigmoid, bias=bglu.h[:, o_, 16 + oc:17 + oc]), reads=[b2.d, bglu.d], writes=[sg.d])
                fw.op(DVE, lambda e, b1=b1, sg=sg, oc=oc: e.scalar_tensor_tensor(out=sg.h[:, 0:Tn], in0=b1.h[:, 0:Tn], scalar=bglu.h[:, o_, oc:oc + 1], in1=sg.h[:, 0:Tn], op0=ALU.add, op1=ALU.mult),
                      reads=[b1.d, sg.d, bglu.d], writes=[sg.d])
                fw.op(DVE, lambda e, sg=sg, oc=oc: e.tensor_tensor(out=x.h[:, oc, 0:Tn], in0=sg.h[:, 0:Tn], in1=x.h[:, oc, 0:Tn], op=ALU.add), reads=[sg.d, x.d], writes=[x.d])

    def vap(t, off, dims, parts=128, p0=0):
        v = t.h
        ps = v.ap[0][0]
        return bass.AP(arena, v.offset + p0 * ps + off, [[ps, parts]] + [list(d_) for d_ in dims])

    bslots = [fw.slot() for _ in range(4)]
    h0slots = [fw.slot() for _ in range(2)]
    TWO_PI = 2.0 * math.pi

    def s5_layer(o_, kind, ti, Tn):
        samp = kind == "s"
        n = 16 if samp else 64
        SL = 4 if samp else 8
        s_list = list(range(4, 8)) if samp else list(range(8))
        W = wtmp

        def tt(eng, out, i0, i1, op, rd, wr):
            fw.op(eng, lambda e: e.tensor_tensor(out=out, in0=i0, in1=i1, op=op), reads=rd, writes=wr)

        def ts(eng, out, i0, s1, s2, op0, op1, rd, wr):
            if op1 is None:
                fw.op(eng, lambda e: e.tensor_scalar(out=out, in0=i0, scalar1=s1, scalar2=None, op0=op0), reads=rd, writes=wr)
            else:
                fw.op(eng, lambda e: e.tensor_scalar(out=out, in0=i0, scalar1=s1, scalar2=s2, op0=op0, op1=op1), reads=rd, writes=wr)

        are = apar.h[:, 0, o_, :]
        aim = apar.h[:, 1, o_, :]
        dt_ = dtb.h[:, o_, :]
        wd = [w_.d for w_ in W]
        tt(DVE, W[0].h[:], are, dt_, ALU.mult, [apar.d, dtb.d], [W[0].d])
        tt(DVE, W[1].h[:], aim, dt_, ALU.mult, [apar.d, dtb.d], [W[1].d])
        fw.op(ACT, lambda e: e.activation(out=W[2].h[:], in_=W[0].h[:], func=AF.Exp), reads=[W[0].d], writes=[W[2].d])
        fw.op(ACT, lambda e: e.activation(out=W[3].h[:], in_=W[0].h[:], func=AF.Exp, scale=-1.0), reads=[W[0].d], writes=[W[3].d])
        for dst, shift in ((W[5], 0.0), (W[6], math.pi / 2)):
            ts(DVE, dst.h[:], W[1].h[:], shift, None, ALU.add, None, [W[1].d], [dst.d])
            ts(DVE, W[7].h[:], W[1].h[:], shift, None, ALU.add, None, [W[1].d], [W[7].d])
            for kthr in range(5):
                thr = (2 * kthr + 1) * math.pi
                ts(DVE, W[4].h[:], W[7].h[:], thr, -TWO_PI, ALU.is_gt, ALU.mult, [W[7].d], [W[4].d])
                tt(DVE, dst.h[:], dst.h[:], W[4].h[:], ALU.add, [dst.d, W[4].d], [dst.d])
        fw.op(ACT, lambda e: e.activation(out=W[5].h[:], in_=W[5].h[:], func=AF.Sin), reads=[W[5].d], writes=[W[5].d])
        fw.op(ACT, lambda e: e.activation(out=W[6].h[:], in_=W[6].h[:], func=AF.Sin), reads=[W[6].d], writes=[W[6].d])
        wp = lambda t_, ri: Wpos.h[:, t_, ri, :]
        wn = lambda s_, ri: Wneg.h[:, s_, ri, :]
        tt(DVE, wp(1, 0), W[2].h[:], W[6].h[:], ALU.mult, [W[2].d, W[6].d], [Wpos.d])
        tt(DVE, wp(1, 1), W[2].h[:], W[5].h[:], ALU.mult, [W[2].d, W[5].d], [Wpos.d])
        tt(DVE, wn(1, 0), W[3].h[:], W[6].h[:], ALU.mult, [W[3].d, W[6].d], [Wneg.d])
        fw.op(DVE, lambda e: e.scalar_tensor_tensor(out=wn(1, 1), in0=W[3].h[:], scalar=-1.0, in1=W[5].h[:], op0=ALU.mult, op1=ALU.mult), reads=[W[3].d, W[5].d], writes=[Wneg.d])
        for arr in (Wpos, Wneg):
            fw.op(DVE, lambda e, arr=arr: e.memset(arr.h[:, 0, 0, :], 1.0), writes=[arr.d])
            fw.op(DVE, lambda e, arr=arr: e.memset(arr.h[:, 0, 1, :], 0.0), writes=[arr.d])

        def cmul(outr, outi, ar, ai, br, bi, rd, wr):
            tt(DVE, W[0].h[:], ar, br, ALU.mult, rd, [W[0].d])
            tt(DVE, W[4].h[:], ai, bi, ALU.mult, rd, [W[4].d])
            tt(DVE, outr, W[0].h[:], W[4].h[:], ALU.subtract, [W[0].d, W[4].d], wr)
            tt(DVE, W[0].h[:], ar, bi, ALU.mult, rd + wr, [W[0].d])
            tt(DVE, W[4].h[:], ai, br, ALU.mult, rd + wr, [W[4].d])
            tt(DVE, outi, W[0].h[:], W[4].h[:], ALU.add, [W[0].d, W[4].d], wr)
        for t_ in range(2, 9):
            cmul(wp(t_, 0), wp(t_, 1), wp(t_ - 1, 0), wp(t_ - 1, 1), wp(1, 0), wp(1, 1), [Wpos.d], [Wpos.d])
        for s_ in range(2, 8):
            cmul(wn(s_, 0), wn(s_, 1), wn(s_ - 1, 0), wn(s_ - 1, 1), wn(1, 0), wn(1, 1), [Wneg.d], [Wneg.d])
        if not samp:
            wdv = lambda k_, ri: Wd.h[:, k_, ri, :]
            fw.op(DVE, lambda e: e.tensor_copy(out=Wd.h[:, 0, :, :], in_=Wpos.h[:, 8, :, :]), reads=[Wpos.d], writes=[Wd.d])
            for k_ in range(1, 7):
                tt(DVE, W[0].h[:], wdv(k_ - 1, 0), wdv(k_ - 1, 0), ALU.mult, [Wd.d], [W[0].d])
                tt(DVE, W[4].h[:], wdv(k_ - 1, 1), wdv(k_ - 1, 1), ALU.mult, [Wd.d], [W[4].d])
                tt(DVE, wdv(k_, 0), W[0].h[:], W[4].h[:], ALU.subtract, [W[0].d, W[4].d], [Wd.d])
                fw.op(DVE, lambda e, k_=k_: e.scalar_tensor_tensor(out=wdv(k_, 1), in0=wdv(k_ - 1, 0), scalar=2.0, in1=wdv(k_ - 1, 1), op0=ALU.mult, op1=ALU.mult), reads=[Wd.d], writes=[Wd.d])
        fr_ = frfi.h[:, 0, :]
        fi_ = frfi.h[:, 1, :]
        tt(DVE, W[2].h[:], are, are, ALU.mult, [apar.d], [W[2].d])
        tt(DVE, W[3].h[:], aim, aim, ALU.mult, [apar.d], [W[3].d])
        tt(DVE, W[2].h[:], W[2].h[:], W[3].h[:], ALU.add, [W[2].d, W[3].d], [W[2].d])
        fw.op(DVE, lambda e: e.reciprocal(out=W[2].h[:], in_=W[2].h[:]), reads=[W[2].d], writes=[W[2].d])
        ts(DVE, W[3].h[:], wp(1, 0), -1.0, None, ALU.add, None, [Wpos.d], [W[3].d])
        tt(DVE, W[5].h[:], W[3].h[:], are, ALU.mult, [W[3].d, apar.d], [W[5].d])
        tt(DVE, W[6].h[:], wp(1, 1), aim, ALU.mult, [Wpos.d, apar.d], [W[6].d])
        tt(DVE, W[5].h[:], W[5].h[:], W[6].h[:], ALU.add, [W[5].d, W[6].d], [W[5].d])
        tt(DVE, fr_, W[5].h[:], W[2].h[:], ALU.mult, [W[5].d, W[2].d], [frfi.d])
        tt(DVE, W[5].h[:], wp(1, 1), are, ALU.mult, [Wpos.d, apar.d], [W[5].d])
        tt(DVE, W[6].h[:], W[3].h[:], aim, ALU.mult, [W[3].d, apar.d], [W[6].d])
        tt(DVE, W[5].h[:], W[5].h[:], W[6].h[:], ALU.subtract, [W[5].d, W[6].d], [W[5].d])
        tt(DVE, fi_, W[5].h[:], W[2].h[:], ALU.mult, [W[5].d, W[2].d], [frfi.d])

        bHr, bHi = banks[0], banks[1]
        def grp_views(gh, g8):
            P0 = 64 * gh
            r0 = Rp[0].h[P0:P0 + 64, g8, :, :].rearrange("p s j -> p (s j)")
            r1 = Rp[1].h[P0:P0 + 64, g8, :, :].rearrange("p s j -> p (s j)")
            o0 = Op[0].h[P0:P0 + 64, g8, :, :].rearrange("p s j -> p (s j)")
            o1 = Op[1].h[P0:P0 + 64, g8, :, :].rearrange("p s j -> p (s j)")
            return P0, r0, r1, o0, o1

        def st_P1(bt):
            g0 = bt * 8
            for ri in range(2):
                fw.dma(SP, lambda e, ri=ri: e.dma_start(out=braw[ri].h[:], in_=b_d[ri].ap()[o_, :, g0:g0 + 8, :]), bslots[ri], writes=[braw[ri].d])
                fw.dma(SP, lambda e, ri=ri: e.dma_start(out=craw[ri].h[:], in_=c_d[ri].ap()[o_, :, g0:g0 + 8, :]), bslots[2 + ri], writes=[craw[ri].d])
            frb = vap(frfi, g0, [[1, 8], [0, 16]])
            fib = vap(frfi, 64 + g0, [[1, 8], [0, 16]])
            v8 = lambda t_: t_.h[:, :, :]
            tA3 = tA.h[:, 0:128].rearrange("p (a b) -> p a b", a=8)
            tB3 = tB.h[:, 0:128].rearrange("p (a b) -> p a b", a=8)
            tt(DVE, tA3, v8(braw[0]), frb, ALU.mult, [braw[0].d, frfi.d], [tA.d])
            tt(DVE, tB3, v8(braw[1]), fib, ALU.mult, [braw[1].d, frfi.d], [tB.d])
            tt(DVE, v8(bbar[0]), tA3, tB3, ALU.subtract, [tA.d, tB.d], [bbar[0].d])
            tt(DVE, tA3, v8(braw[1]), frb, ALU.mult, [braw[1].d, frfi.d], [tA.d])
            tt(DVE, tB3, v8(braw[0]), fib, ALU.mult, [braw[0].d, frfi.d], [tB.d])
            tt(DVE, v8(bbar[1]), tA3, tB3, ALU.add, [tA.d, tB.d], [bbar[1].d])
            tA4 = tA.h[:, :].rearrange("p (a b c) -> p a b c", a=8, b=8)
            tB4 = tB.h[:, :].rearrange("p (a b c) -> p a b c", a=8, b=8)
            wnr = vap(Wneg, g0, [[1, 8], [128, 8], [0, 16]])
            wni = vap(Wneg, 64 + g0, [[1, 8], [128, 8], [0, 16]])
            wpr = vap(Wpos, g0, [[1, 8], [128, 8], [0, 16]])
            wpi = vap(Wpos, 64 + g0, [[1, 8], [128, 8], [0, 16]])
            bb4 = [vap(bbar[ri], 0, [[16, 8], [0, 8], [1, 16]]) for ri in range(2)]
            cc4 = [vap(craw[ri], 0, [[16, 8], [0, 8], [1, 16]]) for ri in range(2)]
            e1, e2 = DVE, POOL
            tt(e1, tA4, bb4[0], wnr, ALU.mult, [bbar[0].d, Wneg.d], [tA.d])
            tt(e1, tB4, bb4[1], wni, ALU.mult, [bbar[1].d, Wneg.d], [tB.d])
            tt(e1, Rp[0].h[:], tA4, tB4, ALU.subtract, [tA.d, tB.d], [Rp[0].d])
            tt(e1, tA4, bb4[1], wnr, ALU.mult, [bbar[1].d, Wneg.d], [tA.d])
            tt(e1, tB4, bb4[0], wni, ALU.mult, [bbar[0].d, Wneg.d], [tB.d])
            tt(e1, Rp[1].h[:], tA4, tB4, ALU.add, [tA.d, tB.d], [Rp[1].d])

        def st_P2(bt):
            g0 = bt * 8
            tA4 = tA.h[:, :].rearrange("p (a b c) -> p a b c", a=8, b=8)
            tB4 = tB.h[:, :].rearrange("p (a b c) -> p a b c", a=8, b=8)
            wpr = vap(Wpos, g0, [[1, 8], [128, 8], [0, 16]])
            wpi = vap(Wpos, 64 + g0, [[1, 8], [128, 8], [0, 16]])
            cc4 = [vap(craw[ri], 0, [[16, 8], [0, 8], [1, 16]]) for ri in range(2)]
            e1 = DVE
            tt(e1, tA4, cc4[0], wpr, ALU.mult, [craw[0].d, Wpos.d], [tA.d])
            tt(e1, tB4, cc4[1], wpi, ALU.mult, [craw[1].d, Wpos.d], [tB.d])
            tt(e1, Op[0].h[:], tA4, tB4, ALU.subtract, [tA.d, tB.d], [Op[0].d])
            tt(e1, tA4, cc4[0], wpi, ALU.mult, [craw[0].d, Wpos.d], [tA.d])
            tt(e1, tB4, cc4[1], wpr, ALU.mult, [craw[1].d, Wpos.d], [tB.d])
            fw.op(e1, lambda e: e.scalar_tensor_tensor(out=Op[1].h[:].rearrange("p a b c -> p (a b c)"), in0=tA.h[:, :], scalar=-1.0, in1=tB.h[:, :], op0=ALU.mult, op1=ALU.subtract),
                  reads=[tA.d, tB.d], writes=[Op[1].d])

        def st_U(bt):
            Ubc = Ub2[bt % 2]
            for gh in range(2):
                ft = bt + 8 * gh
                for g8 in range(8):
                    gi = gh * 8 + g8
                    b3 = bank()

                    def um(e, b3=b3, g8=g8, ft=ft):
                        r = None
                        for k_, s_ in enumerate(s_list):
                            rhs = xn.h[:, ft, 0:Tn].rearrange("p (c s) -> p s c", s=SL)[:, s_ - (8 - SL), :]
                            r = e.matmul(b3.h[:, 0:n], lhsT=strips.h[:, g8, 112 - 16 * s_:240 - 16 * s_], rhs=rhs, start=(k_ == 0), stop=(k_ == len(s_list) - 1))
                        return r
                    fw.op(PE, um, reads=[strips.d, xn.d], writes=[b3.d])
                    fw.op(ACT, lambda e, b3=b3, gi=gi: e.activation(out=Ubc.h[:, gi, 0:n], in_=b3.h[:, 0:n], func=AF.Copy), reads=[b3.d], writes=[Ubc.d])

        def st_A(bt):
            Ubc = Ub2[bt % 2]
            for gh in range(2):
                for g8 in range(8):
                    gi = gh * 8 + g8
                    P0, r0, r1, o0, o1 = grp_views(gh, g8)
                    b = bank()

                    def mt(e, b=b, r0=r0, r1=r1, o0=o0, o1=o1):
                        e.matmul(b.h[:, 0:128], lhsT=r0, rhs=o0, start=True, stop=False)
                        return e.matmul(b.h[:, 0:128], lhsT=r1, rhs=o1, start=False, stop=True)
                    fw.op(PE, mt, reads=[Rp[0].d, Rp[1].d, Op[0].d, Op[1].d], writes=[b.d])
                    fw.op(DVE, lambda e, b=b, gi=gi: e.tensor_tensor(out=MTs.h[:, gi, :], in0=b.h[:, 0:128], in1=bmask.h[:], op=ALU.mult), reads=[b.d, bmask.d], writes=[MTs.d])
                    b2 = bank()
                    bb2 = b2.h[:, 0:64].bitcast(BF16)

                    def qt(e, bb2=bb2, r0=r0, r1=r1, P0=P0):
                        idn = ident[P0:P0 + 64, P0:P0 + 64]
                        e.transpose(out=bb2[:, 0:64], in_=r0, identity=idn)
                        return e.transpose(out=bb2[:, 64:128], in_=r1, identity=idn)
                    fw.op(PE, qt, reads=[Rp[0].d, Rp[1].d, identones.d], writes=[b2.d])
                    q_ = QTs[gi % 2]
                    fw.op(ACT, lambda e, bb2=bb2, q_=q_: e.activation(out=q_.h[:], in_=bb2, func=AF.Copy), reads=[b2.d], writes=[q_.d])
                    fw.op(PE, lambda e, Ubc=Ubc, q_=q_, gi=gi, g8=g8, P0=P0: e.matmul(bHr.h[P0:P0 + 64, g8 * 64:g8 * 64 + n], lhsT=q_.h[:, 0:64], rhs=Ubc.h[:, gi, 0:n], start=True, stop=True),
                          reads=[q_.d, Ubc.d], writes=[bHr.d])
                    fw.op(PE, lambda e, Ubc=Ubc, q_=q_, gi=gi, g8=g8, P0=P0: e.matmul(bHi.h[P0:P0 + 64, g8 * 64:g8 * 64 + n], lhsT=q_.h[:, 64:128], rhs=Ubc.h[:, gi, 0:n], start=True, stop=True),
                          reads=[q_.d, Ubc.d], writes=[bHi.d])

        def st_S(bt):
            g0 = bt * 8
            Hv = [bHr.h[:, :].rearrange("p (g c) -> p g c", g=8)[:, :, 0:n], bHi.h[:, :].rearrange("p (g c) -> p g c", g=8)[:, :, 0:n]]
            if not samp:
                for ri in range(2):
                    fw.op(ACT, lambda e, ri=ri: e.activation(out=Aar.h[:, ri, :, 1:n + 1], in_=Hv[ri], func=AF.Copy), reads=[(bHr, bHi)[ri].d], writes=[Aar.d])
                if ti == 0:
                    fw.op(DVE, lambda e: e.memset(Aar.h[:, :, :, 0], 0.0), writes=[Aar.d])
                else:
                    fw.op(DVE, lambda e: e.tensor_copy(out=Aar.h[:, :, :, 0], in_=Pst.h[:, o_, :, g0:g0 + 8]), reads=[Pst.d], writes=[Aar.d])
                T1f = tA.h[:, :].rearrange("p (r g c) -> p r g c", r=2, g=8)
                T2f = tB.h[:, :].rearrange("p (r g c) -> p r g c", r=2, g=8)
                L = n + 1
                wr0 = vap(Wd, g0, [[0, 2], [1, 8], [0, n]])
                wi0 = vap(Wd, 64 + g0, [[0, 2], [1, 8], [0, n]])
                tt(DVE, T1f[:, :, :, 0:n], Aar.h[:, :, :, 1:L], wr0, ALU.mult, [Aar.d, Wd.d], [tA.d])
                tt(DVE, T2f[:, :, :, 0:n], Aar.h[:, :, :, 1:L], wi0, ALU.mult, [Aar.d, Wd.d], [tB.d])
                tt(DVE, Aar.h[:, 0, :, 1:L], T1f[:, 0, :, 0:n], T2f[:, 1, :, 0:n], ALU.subtract, [tA.d, tB.d], [Aar.d])
                tt(DVE, Aar.h[:, 1, :, 1:L], T1f[:, 1, :, 0:n], T2f[:, 0, :, 0:n], ALU.add, [tA.d, tB.d], [Aar.d])
                for k_ in range(7):
                    d_ = 1 << k_
                    Lc = L - d_
                    wr_ = vap(Wd, k_ * 128 + g0, [[0, 2], [1, 8], [0, Lc]])
                    wi_ = vap(Wd, k_ * 128 + 64 + g0, [[0, 2], [1, 8], [0, Lc]])
                    tt(DVE, T1f[:, :, :, 0:Lc], Aar.h[:, :, :, 0:Lc], wr_, ALU.mult, [Aar.d, Wd.d], [tA.d])
                    tt(DVE, T2f[:, :, :, 0:Lc], Aar.h[:, :, :, 0:Lc], wi_, ALU.mult, [Aar.d, Wd.d], [tB.d])
                    tt(DVE, Aar.h[:, 0, :, d_:L], Aar.h[:, 0, :, d_:L], T1f[:, 0, :, 0:Lc], ALU.add, [Aar.d, tA.d], [Aar.d])
                    tt(DVE, Aar.h[:, 0, :, d_:L], Aar.h[:, 0, :, d_:L], T2f[:, 1, :, 0:Lc], ALU.subtract, [Aar.d, tB.d], [Aar.d])
                    tt(DVE, Aar.h[:, 1, :, d_:L], Aar.h[:, 1, :, d_:L], T1f[:, 1, :, 0:Lc], ALU.add, [Aar.d, tA.d], [Aar.d])
                    tt(DVE, Aar.h[:, 1, :, d_:L], Aar.h[:, 1, :, d_:L], T2f[:, 0, :, 0:Lc], ALU.add, [Aar.d, tB.d], [Aar.d])
                fw.op(DVE, lambda e: e.tensor_copy(out=Pst.h[:, o_, :, g0:g0 + 8], in_=Aar.h[:, :, :, n]), reads=[Aar.d], writes=[Pst.d])
                fw.op(ACT, lambda e: e.activation(out=Abf.h[:, :, :, 0:n], in_=Aar.h[:, :, :, 0:n], func=AF.Copy), reads=[Aar.d], writes=[Abf.d])
            else:
                for ri in range(2):
                    fw.dma(SP, lambda e, ri=ri: e.dma_start(out=h0s5.h[:, ri, :, :], in_=sssm[ri].ap()[o_, :, g0:g0 + 8, :]), h0slots[ri], writes=[h0s5.d])
                wm3r = vap(Wneg, 3 * 128 + g0, [[1, 8], [0, 16]])
                wm3i = vap(Wneg, 3 * 128 + 64 + g0, [[1, 8], [0, 16]])
                w7r = vap(Wpos, 7 * 128 + g0, [[1, 8], [0, 16]])
                w7i = vap(Wpos, 7 * 128 + 64 + g0, [[1, 8], [0, 16]])
                tA3 = tA.h[:, 0:128].rearrange("p (a b) -> p a b", a=8)
                tB3 = tB.h[:, 0:128].rearrange("p (a b) -> p a b", a=8)

                def cm3(outr, outi, xr, xi, wr_, wi_, rd, wrd):
                    tt(DVE, tA3, xr, wr_, ALU.mult, rd, [tA.d])
                    tt(DVE, tB3, xi, wi_, ALU.mult, rd, [tB.d])
                    tt(DVE, outr, tA3, tB3, ALU.subtract, [tA.d, tB.d], wrd)
                    tt(DVE, tA3, xi, wr_, ALU.mult, rd, [tA.d])
                    tt(DVE, tB3, xr, wi_, ALU.mult, rd, [tB.d])
                    tt(DVE, outi, tA3, tB3, ALU.add, [tA.d, tB.d], wrd)
                cm3(Pp.h[:, 0, :, :], Pp.h[:, 1, :, :], h0s5.h[:, 0, :, :], h0s5.h[:, 1, :, :], wm3r, wm3i, [h0s5.d, Wneg.d], [Pp.d])
                fw.op(ACT, lambda e: e.activation(out=Abf.h[:, :, :, 0:16], in_=Pp.h[:, :, :, :], func=AF.Copy), reads=[Pp.d], writes=[Abf.d])
                for ri in range(2):
                    tt(DVE, Aar.h[:, ri, :, 0:16], Hv[ri], Pp.h[:, ri, :, :], ALU.add, [(bHr, bHi)[ri].d, Pp.d], [Aar.d])
                cm3(Pf.h[:, 0, :, :], Pf.h[:, 1, :, :], Aar.h[:, 0, :, 0:16], Aar.h[:, 1, :, 0:16], w7r, w7i, [Aar.d, Wpos.d], [Pf.d])
                for ri in range(2):
                    out_toks.append(fw.dma(SP, lambda e, ri=ri: e.dma_start(out=o_ssms[ri].ap()[o_, :, g0:g0 + 8, :], in_=Pf.h[:, ri, :, :]), oslot(("ssms", ri)), reads=[Pf.d]))

        def st_Y(bt):
            Ubc = Ub2[bt % 2]
            for gh in range(2):
                for g8 in range(8):
                    gi = gh * 8 + g8
                    P0, r0, r1, o0, o1 = grp_views(gh, g8)
                    b = bank()

                    def ym(e, Ubc=Ubc, b=b, gi=gi, g8=g8, P0=P0, o0=o0, o1=o1):
                        e.matmul(b.h[:, 0:n], lhsT=MTs.h[:, gi, :], rhs=Ubc.h[:, gi, 0:n], start=True, stop=False)
                        e.matmul(b.h[:, 0:n], lhsT=o0, rhs=Abf.h[P0:P0 + 64, 0, g8, 0:n], start=False, stop=False)
                        return e.matmul(b.h[:, 0:n], lhsT=o1, rhs=Abf.h[P0:P0 + 64, 1, g8, 0:n], start=False, stop=True)
                    fw.op(PE, ym, reads=[MTs.d, Ubc.d, Op[0].d, Op[1].d, Abf.d], writes=[b.d])
                    fw.op(ACT, lambda e, b=b, gi=gi: e.activation(out=Yb.h[:, gi, 0:n], in_=b.h[:, 0:n], func=AF.Copy), reads=[b.d], writes=[Yb.d])

        def st_B(bt):
            for gh in range(2):
                ft = bt + 8 * gh
                bY = bank()

                def bc(e, bY=bY, gh=gh):
                    r = None
                    for t_ in s_list:
                        for g8 in range(8):
                            r = e.matmul(bY.h[:, t_ * 64:t_ * 64 + n], lhsT=strips.h[:, t_, 112 - 16 * g8:240 - 16 * g8], rhs=Yb.h[:, gh * 8 + g8, 0:n], start=(g8 == 0), stop=(g8 == 7))
                    return r
                fw.op(PE, bc, reads=[strips.d, Yb.d], writes=[bY.d])
                src = bY.h[:, :].rearrange("p (t c) -> p t c", t=8)[:, 8 - SL:8, 0:n]
                dst = y32b.h[:, 0:Tn].rearrange("p (c t) -> p t c", t=SL)
                fw.op(ACT, lambda e, src=src, dst=dst: e.activation(out=dst, in_=src, func=AF.Copy), reads=[bY.d], writes=[y32b.d])
                fw.op(DVE, lambda e, ft=ft: e.scalar_tensor_tensor(out=y32b.h[:, 0:Tn], in0=xn.h[:, ft, 0:Tn], scalar=ssmd.h[:, o_, ft:ft + 1], in1=y32b.h[:, 0:Tn], op0=ALU.mult, op1=ALU.add),
                      reads=[xn.d, ssmd.d, y32b.d], writes=[y32b.d])
                gelu_mul(y32b, None, cat.h[:, ft, 0:Tn], cat, Tn, g1b, g2b)

        st_P1(0)
        st_P2(0)
        st_U(0)
        for b_ in range(8):
            st_A(b_)
            if b_ < 7:
                st_U(b_ + 1)
            st_S(b_)
            if b_ < 7:
                st_P1(b_ + 1)
            st_Y(b_)
            if b_ < 7:
                st_P2(b_ + 1)
            st_B(b_)
        if (not samp) and ti == 3:
            w1r = Wneg.h[:, 1, 0, :]
            w1i = Wneg.h[:, 1, 1, :]
            pr = Pst.h[:, o_, 0, :]
            pi_ = Pst.h[:, o_, 1, :]
            tt(DVE, W[0].h[:], pr, w1r, ALU.mult, [Pst.d, Wneg.d], [W[0].d])
            tt(DVE, W[4].h[:], pi_, w1i, ALU.mult, [Pst.d, Wneg.d], [W[4].d])
            tt(DVE, W[2].h[:], W[0].h[:], W[4].h[:], ALU.subtract, [W[0].d, W[4].d], [W[2].d])
            tt(DVE, W[0].h[:], pi_, w1r, ALU.mult, [Pst.d, Wneg.d], [W[0].d])
            tt(DVE, W[4].h[:], pr, w1i, ALU.mult, [Pst.d, Wneg.d], [W[4].d])
            tt(DVE, W[3].h[:], W[0].h[:], W[4].h[:], ALU.add, [W[0].d, W[4].d], [W[3].d])
            out_toks.append(fw.dma(SP, lambda e: e.dma_start(out=o_ssmp[0].ap()[o_], in_=W[2].h[:]), oslot(("ssmp", 0, o_)), reads=[W[2].d]))
            out_toks.append(fw.dma(SP, lambda e: e.dma_start(out=o_ssmp[1].ap()[o_], in_=W[3].h[:]), oslot(("ssmp", 1, o_)), reads=[W[3].d]))

    dctr = [0]

    def run_tile(kind, ti):
        samp = kind == "s"
        Tn = 64 if samp else 512
        if samp:
            fw.dma(SP, lambda e: e.dma_start(out=x.h[:, :, 0:64], in_=xsT.ap().rearrange("(kc p) t -> p kc t", p=128)), xslot, writes=[x.d])
            fw.dma(SP, lambda e: e.dma_start(out=cosT.h[:, 0:64], in_=cst["cosS"].ap()), tslot, writes=[cosT.d])
            fw.dma(SP, lambda e: e.dma_start(out=sinT.h[:, 0:64], in_=cst["sinS"].ap()), tslot2, writes=[sinT.d])
        else:
            fw.dma(SP, lambda e, ti=ti: e.dma_start(out=x.h[:], in_=xpT.ap()[:, ti * 512:(ti + 1) * 512].rearrange("(kc p) t -> p kc t", p=128)), xslot, writes=[x.d])
            fw.dma(SP, lambda e, ti=ti: e.dma_start(out=cosT.h[:], in_=cst["cosP"].ap()[:, ti * 512:(ti + 1) * 512]), tslot, writes=[cosT.d])
            fw.dma(SP, lambda e, ti=ti: e.dma_start(out=sinT.h[:], in_=cst["sinP"].ap()[:, ti * 512:(ti + 1) * 512]), tslot2, writes=[sinT.d])
            if ti == 0:
                fw.op(DVE, lambda e: e.memset(Sst.h[:], 0.0), writes=[Sst.d])
                fw.op(POOL, lambda e: e.memset(Sbf.h[:], 0.0), writes=[Sbf.d])
        for layer in range(nlayers):
            if layer % 2 == 0:
                if not samp:
                    fw.op(ACT, lambda e, layer=layer: e.activation(out=Sbf.h[:], in_=Sst.h[:, layer // 2, :, :, :], func=AF.Copy), reads=[Sst.d], writes=[Sbf.d])
                even_layer(layer // 2, kind, ti, Tn)
            else:
                odd_layer(layer // 2, kind, ti, Tn)
            ffn(layer, Tn)
            if dbg and dctr[0] < 8:
                di = dctr[0]
                out_toks.append(fw.dma(SP, lambda e, di=di: e.dma_start(out=o_dbg.ap()[di], in_=x.h[:]), oslot(("dbg", di)), reads=[x.d]))
                dctr[0] += 1
        if (not samp) and nlayers > 0:
            pass
        fw.op(ACT, lambda e: e.activation(out=cat.h[:, :, 0:Tn], in_=x.h[:, :, 0:Tn], func=AF.Square), reads=[x.d], writes=[cat.d])
        b = bank()

        def mmf(e, b=b, Tn=Tn):
            r = None
            for kc in range(KC):
                r = e.matmul(b.h[:, 0:Tn], lhsT=ones, rhs=cat.h[:, kc, 0:Tn], start=(kc == 0), stop=(kc == KC - 1))
            return r
        fw.op(PE, mmf, reads=[cat.d, identones.d], writes=[b.d])
        fw.op(ACT, lambda e, b=b, Tn=Tn: e.activation(out=rstd.h[:, 0:Tn], in_=b.h[:, 0:Tn], func=AF.Sqrt, scale=1.0 / D, bias=EPS), reads=[b.d], writes=[rstd.d])
        fw.op(DVE, lambda e, Tn=Tn: e.reciprocal(out=rstd.h[:, 0:Tn], in_=rstd.h[:, 0:Tn]), reads=[rstd.d], writes=[rstd.d])
        for kc in range(KC):
            fw.op(DVE, lambda e, kc=kc, Tn=Tn: e.scalar_tensor_tensor(out=x.h[:, kc, 0:Tn], in0=x.h[:, kc, 0:Tn], scalar=gains.h[:, 8, kc:kc + 1], in1=rstd.h[:, 0:Tn], op0=ALU.mult, op1=ALU.mult),
                  reads=[x.d, gains.d, rstd.d], writes=[x.d])
        if samp:
            out_toks.append(fw.dma(SP, lambda e: e.dma_start(out=ysT.ap().rearrange("(kc p) t -> p kc t", p=128), in_=x.h[:, :, 0:64]), oslot("ys"), reads=[x.d]))
        else:
            out_toks.append(fw.dma(SP, lambda e, ti=ti: e.dma_start(out=ypT.ap()[:, ti * 512:(ti + 1) * 512].rearrange("(kc p) t -> p kc t", p=128), in_=x.h[:]), oslot("yp"), reads=[x.d]))
    for (kind_, ti_) in tiles:
        lctr[0] = 0
        try:
            run_tile(kind_, ti_)
        except StopIteration:
            break
        passno[0] += 1
    last = {}
    for t in out_toks:
        last[id(t[0])] = t
    fw.wait_tokens(SP, list(last.values()))
    fw.emit()
    return nc


_CACHE = {}


def make_in_maps(inp, ncores=8):
    W = prep_weights(inp)
    C = host_consts()
    maps = []
    f32 = np.float32
    for c in range(ncores):
        b = c // 2
        m = dict(W)
        for k, v in C.items():
            m["c_" + k] = v
        m["xpT"] = np.ascontiguousarray(np.asarray(inp["x_prompt"][b]).T)
        m["xsT"] = np.ascontiguousarray(np.asarray(inp["x_sample"][16 * c:16 * c + 16]).reshape(64, D).T)
        m["sret"] = np.ascontiguousarray(np.asarray(inp["state_ret"][:, 16 * c:16 * c + 16]))
        m["slru"] = np.ascontiguousarray(np.asarray(inp["state_lru"][:, 16 * c:16 * c + 16]).transpose(0, 2, 1))
        m["sconv"] = np.ascontiguousarray(np.asarray(inp["state_conv"][:, 16 * c:16 * c + 16]).transpose(0, 3, 1, 2))
        for nm, key in (("sssm_re", "state_ssm_re"), ("sssm_im", "state_ssm_im")):
            a = np.asarray(inp[key][:, 16 * c:16 * c + 16]).reshape(2, 16, 2, 64, 64).transpose(0, 2, 4, 3, 1).reshape(2, 128, 64, 16)
            m[nm] = np.ascontiguousarray(a)
        maps.append(m)
    return maps


def assemble(results):
    f32 = np.float32
    y_p = np.zeros((4, 2048, D), f32)
    y_s = np.zeros((128, 4, D), f32)
    ret_p = np.zeros((2, 4, 4, 256, 256), f32)
    ret_s = np.zeros((2, 128, 4, 256, 256), f32)
    lru_p = np.zeros((2, 4, 1024), f32)
    lru_s = np.zeros((2, 128, 1024), f32)
    conv_p = np.zeros((2, 4, 3, 1024), f32)
    conv_s = np.zeros((2, 128, 3, 1024), f32)
    ssm_p = [np.zeros((2, 4, 128, 64), f32), np.zeros((2, 4, 128, 64), f32)]
    ssm_s = [np.zeros((2, 128, 128, 64), f32), np.zeros((2, 128, 128, 64), f32)]
    for c, r in enumerate(results):
        sl = slice(16 * c, 16 * c + 16)
        y_s[sl] = r["ysT"].T.reshape(16, 4, D)
        ret_s[:, sl] = r["o_rets"]
        lru_s[:, sl] = r["o_lrus"].transpose(0, 2, 1)
        conv_s[:, sl] = r["o_convs"].transpose(0, 2, 3, 1)
        for ri, nm in enumerate(("o_ssms_re", "o_ssms_im")):
            a = r[nm].reshape(2, 2, 64, 64, 16).transpose(0, 4, 1, 3, 2).reshape(2, 16, 128, 64)
            ssm_s[ri][:, sl] = a
        if c % 2 == 0:
            b = c // 2
            y_p[b] = r["ypT"].T
            ret_p[:, b] = r["o_retp"]
            lru_p[:, b] = r["o_lrup"].transpose(0, 2, 1).reshape(2, 1024)
            conv_p[:, b] = r["o_convp"].transpose(0, 2, 1)
            for ri, nm in enumerate(("o_ssmp_re", "o_ssmp_im")):
                a = r[nm].reshape(2, 2, 64, 64).transpose(0, 1, 3, 2).reshape(2, 128, 64)
                ssm_p[ri][:, b] = a
    return (y_p, y_s, ret_p, ret_s, lru_p, lru_s, conv_p, conv_s, ssm_p[0], ssm_s[0], ssm_p[1], ssm_s[1])


def kernel(**inputs):
    nc = build({})
    maps = make_in_maps(inputs)
    res = run_bass_kernel_spmd(nc, maps, core_ids=list(range(8)))
    return assemble(res.results)
```

```python
import math
import numpy as np
import concourse.bass as bass
import concourse.mybir as mybir
from concourse.bass_utils import run_bass_kernel_spmd

F32 = mybir.dt.float32
BF16 = mybir.dt.bfloat16
AF = mybir.ActivationFunctionType
ALU = mybir.AluOpType
SEM_MAX = 12000

D = 2048
KC = 16
FH = 5632
EPS = 1e-6
GAM = [1.0 - 2.0 ** (-5 - h) for h in range(4)]
GELU_C = 2.0 * math.sqrt(2.0 / math.pi)


class Dep:
    __slots__ = ("w", "r")

    def __init__(self):
        self.w = None
        self.r = []


class Eng:
    def __init__(self, fw, name, is_pe=False):
        self.fw = fw
        self.name = name
        self.is_pe = is_pe
        self.ops = []
        self.count = 0
        self.sems = []
        self.seen = {}

    def token(self):
        i, v = divmod(self.count - 1, SEM_MAX)
        while len(self.sems) <= i:
            self.sems.append(self.fw.nc.alloc_semaphore(f"s_{self.name}_{len(self.sems)}"))
        return (self.sems[i], v + 1, self)


class Slot:
    def __init__(self, fw, name):
        self.sem = fw.nc.alloc_semaphore(name)
        self.val = 0


class FW:
    def __init__(self, nc):
        self.nc = nc
        self.pe = Eng(self, "pe", True)
        self.act = Eng(self, "act")
        self.dve = Eng(self, "dve")
        self.pool = Eng(self, "pool")
        self.sp = Eng(self, "sp")
        self.nslot = 0

    def slot(self):
        self.nslot += 1
        return Slot(self, f"dq{self.nslot}")

    @staticmethod
    def _flat(lst):
        out = []
        for d in lst:
            if isinstance(d, (list, tuple)):
                out.extend(FW._flat(d))
            else:
                out.append(d)
        return out

    def _waits(self, eng, reads, writes):
        deps = {}
        for d in reads:
            if d.w is not None:
                deps[id(d.w)] = d.w
        for d in writes:
            if d.w is not None:
                deps[id(d.w)] = d.w
            for t in d.r:
                deps[id(t)] = t
        waits = []
        for t in deps.values():
            sem, val, src = t
            if src is eng and eng.is_pe:
                continue
            k = id(sem)
            if eng.seen.get(k, 0) >= val:
                continue
            eng.seen[k] = val
            waits.append((sem, val))
        return waits

    def op(self, eng, fn, reads=(), writes=()):
        reads = self._flat(reads)
        writes = self._flat(writes)
        waits = self._waits(eng, reads, writes)
        eng.count += 1
        tok = eng.token()
        for d in writes:
            d.w = tok
            d.r = []
        for d in reads:
            if d.w is not tok:
                d.r.append(tok)
        eng.ops.append((waits, fn, (tok[0], 1)))
        return tok

    def dma(self, eng, fn, slot, reads=(), writes=(), n=1):
        reads = self._flat(reads)
        writes = self._flat(writes)
        waits = self._waits(eng, reads, writes)
        slot.val += 16 * n
        tok = (slot.sem, slot.val, slot)
        for d in writes:
            d.w = tok
            d.r = []
        for d in reads:
            d.r.append(tok)
        eng.ops.append((waits, fn, (slot.sem, 16)))
        return tok

    def wait_tokens(self, eng, toks):
        waits = []
        for (sem, val, src) in toks:
            if eng.seen.get(id(sem), 0) >= val:
                continue
            eng.seen[id(sem)] = val
            waits.append((sem, val))
        eng.ops.append((waits, None, None))

    def emit(self):
        nc = self.nc

        def run(eng, e):
            for waits, fn, inc in eng.ops:
                for sem, val in waits:
                    e.wait_ge(sem, val)
                if fn is None:
                    continue
                r = fn(e)
                if isinstance(r, (list, tuple)):
                    for x in r:
                        x.then_inc(inc[0], inc[1])
                else:
                    r.then_inc(inc[0], inc[1])

        with nc.Block() as block:
            @block.tensor
            def _(e):
                run(self.pe, e)

            @block.scalar
            def _(e):
                run(self.act, e)

            @block.vector
            def _(e):
                run(self.dve, e)

            @block.gpsimd
            def _(e):
                run(self.pool, e)

            @block.sync
            def _(e):
                run(self.sp, e)


class T:
    def __init__(self, h):
        self.h = h
        self.d = Dep()

    def __getitem__(self, k):
        return self.h[k]


def sap(t, off, dims, parts=128, p0=0):
    row = 1
    for s in t.shape[1:]:
        row *= s
    return bass.AP(t, p0 * row + off, [[row, parts]] + [list(d) for d in dims])


def host_consts():
    f32 = np.float32
    c = {}
    inv = (1.0 / np.power(f32(10000.0), np.linspace(0.0, 1.0, 128, dtype=f32))).astype(f32)
    posP = np.arange(2048, dtype=f32)
    angP = (posP[None, :] * inv[:, None]).astype(f32).astype(np.float64)
    posS = (16384 + (np.arange(64) % 4)).astype(f32)
    angS = (posS[None, :] * inv[:, None]).astype(f32).astype(np.float64)
    c["cosP"] = np.cos(angP).astype(f32)
    c["sinP"] = np.sin(angP).astype(f32)
    c["cosS"] = np.cos(angS).astype(f32)
    c["sinS"] = np.sin(angS).astype(f32)
    g = np.array(GAM, dtype=np.float64)
    nP = np.arange(512) % 128
    nS = np.arange(64) % 4
    qdP = np.power(g[:, None], nP[None, :] + 1.0)
    qdS = np.power(g[:, None], nS[None, :] + 1.0)
    c["qdecP"] = np.broadcast_to(qdP[None, :, 0:128], (128, 4, 128)).astype(f32).copy()
    c["qdecS"] = np.broadcast_to(qdS[None], (128, 4, 64)).astype(f32).copy()
    m = np.arange(128)
    kv = np.zeros((128, 16), f32)
    kv[:, 0:4] = (np.power(g[None, :], -(m[:, None] + 1.0)) / 16.0)
    kv[:, 4:8] = (np.power(g[None, :], 127.0 - m[:, None]) / 16.0)
    kv[:, 8:12] = (np.power(g[None, :], -((m[:, None] % 4) + 1.0)) / 16.0)
    kv[:, 12:16] = (np.power(g[None, :], 3.0 - (m[:, None] % 4)) / 16.0)
    c["kvec"] = kv
    mk = np.zeros((128, 192), f32)
    mk[:, 0:128] = (m[None, :] >= m[:, None]).astype(f32)
    ms = np.arange(64)
    mk[0:64, 128:192] = ((ms[None, :] >= ms[:, None]) & (ms[None, :] // 4 == ms[:, None] // 4)).astype(f32)
    c["masks"] = mk
    oh = np.zeros((128, 16), f32)
    oh[0:64] = (ms[:, None] // 4 == np.arange(16)[None, :]).astype(f32)
    c["onehot"] = oh
    io = np.zeros((128, 256), f32)
    io[:, 0:128] = np.eye(128, dtype=f32)
    io[:, 128:256] = 1.0
    c["identones"] = io
    st = np.zeros((128, 8, 240), f32)
    for a in range(8):
        for j in range(16):
            st[a * 16 + j, a, 112 + j] = 1.0
    c["strips"] = st
    bm = np.zeros((128, 128), f32)
    for s_ in range(8):
        for t_ in range(s_, 8):
            bm[s_ * 16:(s_ + 1) * 16, t_ * 16:(t_ + 1) * 16] = 1.0
    c["bmask"] = bm
    return c


def prep_weights(inp):
    f32 = np.float32
    w = {}
    gains = [inp["norm_mix_even"][0], inp["norm_mix_even"][1], inp["norm_mix_odd"][0], inp["norm_mix_odd"][1],
             inp["norm_ffn"][0], inp["norm_ffn"][1], inp["norm_ffn"][2], inp["norm_ffn"][3], inp["norm_final"]]
    w["gains"] = np.ascontiguousarray(np.stack([np.asarray(g).reshape(16, 128).T for g in gains], axis=1)).astype(f32)
    lv = np.zeros((128, 2, 8, 8), f32)
    for e in range(2):
        for i in range(4):
            lv[:, e, :, i] = np.asarray(inp["lru_conv_w"][e, i]).reshape(8, 128).T
        lv[:, e, :, 4] = np.asarray(inp["lru_conv_b"][e]).reshape(8, 128).T
        lv[:, e, :, 5] = np.asarray(inp["lru_ba"][e]).reshape(8, 128).T
        lv[:, e, :, 6] = np.asarray(inp["lru_bx"][e]).reshape(8, 128).T
        lv[:, e, :, 7] = np.asarray(inp["lru_lambda"][e]).reshape(8, 128).T
    w["lruvec"] = lv
    w["ssmd"] = np.ascontiguousarray(np.stack([np.asarray(inp["ssm_d"][o]).reshape(16, 128).T for o in range(2)], axis=1)).astype(f32)
    w["bglu"] = np.ascontiguousarray(np.stack([np.asarray(inp["b_glu"][o]).reshape(32, 128).T for o in range(2)], axis=1)).astype(f32)
    for nm in ("ssm_a_re", "ssm_a_im"):
        a = np.asarray(inp[nm]).reshape(2, 2, 64, 64).transpose(0, 1, 3, 2).reshape(2, 128, 64)
        w[nm] = np.ascontiguousarray(a)
    w["ssm_log_dt"] = np.ascontiguousarray(np.asarray(inp["ssm_log_dt"]).reshape(2, 128))
    for nm in ("ssm_b_re", "ssm_b_im"):
        a = np.asarray(inp[nm]).reshape(2, 2, 64, 64, 16).transpose(0, 1, 3, 2, 4).reshape(2, 128, 64, 16)
        w[nm] = np.ascontiguousarray(a)
    for nm in ("ssm_c_re", "ssm_c_im"):
        a = np.asarray(inp[nm]).reshape(2, 2, 64, 16, 64).transpose(0, 1, 4, 2, 3).reshape(2, 128, 64, 16)
        w[nm] = np.ascontiguousarray(a)
    for nm in ("w_in_even", "w_out_even", "w_glu", "w_ffn_gu", "w_ffn_down", "lru_wa", "lru_wx"):
        w[nm] = np.ascontiguousarray(np.asarray(inp[nm], dtype=f32))
    return w


def build(cfg):
    tiles = cfg.get("tiles", [("p", 0), ("p", 1), ("p", 2), ("p", 3), ("s", 0)])
    nlayers = cfg.get("nlayers", 4)
    dbg = cfg.get("dbg", False)
    do_odd = cfg.get("do_odd", True)

    nc = bass.Bass("TRN2", target_bir_lowering=False)
    fw = FW(nc)
    PE, ACT, DVE, POOL, SP = fw.pe, fw.act, fw.dve, fw.pool, fw.sp

    def din(name, shape):
        return nc.dram_tensor(name, list(shape), F32, kind="ExternalInput")

    def dout(name, shape):
        return nc.dram_tensor(name, list(shape), F32, kind="ExternalOutput")

    xpT = din("xpT", [D, 2048])
    xsT = din("xsT", [D, 64])
    sret = din("sret", [2, 16, 4, 256, 256])
    slru = din("slru", [2, 1024, 16])
    sconv = din("sconv", [2, 1024, 16, 3])
    sssm = [din("sssm_re", [2, 128, 64, 16]), din("sssm_im", [2, 128, 64, 16])]
    w_in = din("w_in_even", [2, D, 6144])
    w_out = din("w_out_even", [2, D, D])
    w_glu = din("w_glu", [2, D, 2 * D])
    w_gu = din("w_ffn_gu", [4, D, 2 * FH])
    w_dn = din("w_ffn_down", [4, FH, D])
    lru_wa = din("lru_wa", [2, 8, 128, 128])
    lru_wx = din("lru_wx", [2, 8, 128, 128])
    gains_d = din("gains", [128, 9, 16])
    lruvec_d = din("lruvec", [128, 2, 8, 8])
    ssmd_d = din("ssmd", [128, 2, 16])
    bglu_d = din("bglu", [128, 2, 32])
    a_d = [din("ssm_a_re", [2, 128, 64]), din("ssm_a_im", [2, 128, 64])]
    ldt_d = din("ssm_log_dt", [2, 128])
    b_d = [din("ssm_b_re", [2, 128, 64, 16]), din("ssm_b_im", [2, 128, 64, 16])]
    c_d = [din("ssm_c_re", [2, 128, 64, 16]), din("ssm_c_im", [2, 128, 64, 16])]
    cst = {k: din("c_" + k, v.shape) for k, v in host_consts().items()}

    ypT = dout("ypT", [D, 2048])
    ysT = dout("ysT", [D, 64])
    o_retp = dout("o_retp", [2, 4, 256, 256])
    o_rets = dout("o_rets", [2, 16, 4, 256, 256])
    o_lrup = dout("o_lrup", [2, 128, 8])
    o_lrus = dout("o_lrus", [2, 1024, 16])
    o_convp = dout("o_convp", [2, 1024, 3])
    o_convs = dout("o_convs", [2, 1024, 16, 3])
    o_ssmp = [dout("o_ssmp_re", [2, 128, 64]), dout("o_ssmp_im", [2, 128, 64])]
    o_ssms = [dout("o_ssms_re", [2, 128, 64, 16]), dout("o_ssms_im", [2, 128, 64, 16])]
    if dbg:
        o_dbg = dout("o_dbg", [8, 128, 16, 512])

    def sb(name, shape, dt=F32):
        return T(nc.alloc_sbuf_tensor("sb_" + name, list(shape), dt))

    x = sb("x", [128, KC, 512])
    xn = sb("xn", [128, KC, 512], BF16)
    cat = sb("cat", [128, KC, 512], BF16)
    NW = 2
    wb = [sb(f"wb{i}", [128, KC, 512], BF16) for i in range(NW)]
    wslot = [fw.slot() for _ in range(NW)]
    wctr = [0]
    gains = sb("gains", [128, 9, 16])
    lruvec = sb("lruvec", [128, 2, 8, 8])
    ssmd = sb("ssmd", [128, 2, 16])
    bglu = sb("bglu", [128, 2, 32])
    wa_sb = sb("wa_sb", [128, 2, 8, 128], BF16)
    wx_sb = sb("wx_sb", [128, 2, 8, 128], BF16)
    kvec = sb("kvec", [128, 16])
    masks = sb("masks", [128, 192])
    onehot = sb("onehot", [128, 16])
    identones = sb("identones", [128, 256], BF16)
    qdecP = sb("qdecP", [128, 4, 128])
    qdecS = sb("qdecS", [128, 4, 64])
    strips = sb("strips", [128, 8, 240], BF16)
    bmask = sb("bmask", [128, 128])
    Pst = sb("Pst", [128, 2, 2, 64])
    apar = sb("apar", [128, 2, 2, 64])
    dtb = sb("dtb", [128, 2, 64])
    lsp = sb("lsp", [128, 2, 8, 2])
    Sst = sb("Sst", [128, 2, 4, 2, 256])
    Sbf = sb("Sbf", [128, 4, 2, 256], BF16)
    hst = sb("hst", [128, 2, 8])
    convst = sb("convst", [128, 2, 8, 3])
    ident = identones.h[:, 0:128]
    ones = identones.h[:, 128:256]

    cslot = fw.slot()
    cslot2 = fw.slot()
    cdeps = []
    cdeps2 = []

    def cload(t, src, cast=False):
        if cast:
            fw.dma(POOL, lambda e: e.dma_start(out=t.h[:], in_=src), cslot2, writes=[t.d])
            cdeps2.append(t.d)
        else:
            fw.dma(SP, lambda e: e.dma_start(out=t.h[:], in_=src), cslot, writes=[t.d])
            cdeps.append(t.d)

    cload(gains, gains_d.ap())
    cload(lruvec, lruvec_d.ap())
    cload(ssmd, ssmd_d.ap())
    cload(bglu, bglu_d.ap())
    cload(wa_sb, lru_wa.ap().rearrange("e n k j -> k e n j"), cast=True)
    cload(wx_sb, lru_wx.ap().rearrange("e n k j -> k e n j"), cast=True)
    cload(kvec, cst["kvec"].ap())
    cload(masks, cst["masks"].ap())
    cload(onehot, cst["onehot"].ap())
    cload(identones, cst["identones"].ap(), cast=True)
    cload(qdecP, cst["qdecP"].ap())
    cload(qdecS, cst["qdecS"].ap())
    cload(strips, cst["strips"].ap(), cast=True)
    cload(bmask, cst["bmask"].ap())
    for ri in range(2):
        fw.dma(SP, lambda e, ri=ri: e.dma_start(out=apar.h[:, ri, :, :], in_=a_d[ri].ap().rearrange("o p g -> p o g")), cslot, writes=[apar.d])
    for gh in range(2):
        for o in range(2):
            fw.dma(SP, lambda e, gh=gh, o=o: e.dma_start(out=dtb.h[gh * 64:(gh + 1) * 64, o, :], in_=bass.AP(ldt_d, o * 128 + gh * 64, [[0, 64], [1, 64]])), cslot, writes=[dtb.d])
    cdeps.append(apar.d)
    cdeps.append(dtb.d)
    final_tok = (cslot.sem, cslot.val, cslot)
    for d_ in cdeps:
        d_.w = final_tok
    final_tok2 = (cslot2.sem, cslot2.val, cslot2)
    for d_ in cdeps2:
        d_.w = final_tok2

    fw.op(ACT, lambda e: e.activation(out=dtb.h[:], in_=dtb.h[:], func=AF.Exp), reads=[dtb.d], writes=[dtb.d])
    sp_t = sb("sp_t", [128, 2, 8])
    fw.op(ACT, lambda e: e.activation(out=sp_t.h[:], in_=lruvec.h[:, :, :, 7], func=AF.Exp, scale=-1.0), reads=[lruvec.d], writes=[sp_t.d])
    fw.op(ACT, lambda e: e.activation(out=sp_t.h[:], in_=sp_t.h[:], func=AF.Ln, bias=1.0), reads=[sp_t.d], writes=[sp_t.d])
    fw.op(DVE, lambda e: e.tensor_scalar(out=lsp.h[:, :, :, 0], in0=sp_t.h[:], scalar1=-8.0, scalar2=None, op0=ALU.mult), reads=[sp_t.d], writes=[lsp.d])
    fw.op(DVE, lambda e: e.tensor_scalar(out=lsp.h[:, :, :, 1], in0=sp_t.h[:], scalar1=-16.0, scalar2=None, op0=ALU.mult), reads=[sp_t.d], writes=[lsp.d])

    banks = [T(nc.alloc_psum_tensor(f"pb{i}", [128, 512], F32)) for i in range(8)]
    bctr = [0]

    def bank():
        b = banks[2 + bctr[0] % 6]
        bctr[0] += 1
        return b

    NPG = 124
    arena = nc.alloc_sbuf_tensor("arena", [128, NPG * 128], F32)
    pdeps = [Dep() for _ in range(NPG)]

    class Ph:
        def __init__(self, p0=0):
            self.p = p0

        def al(self, shape, dt=F32):
            nel = 1
            for s_ in shape[1:]:
                nel *= s_
            nb = nel * (4 if dt == F32 else 2)
            npg = (nb + 511) // 512
            assert self.p + npg <= NPG, (self.p, npg)
            v = arena[:, self.p * 128:(self.p + npg) * 128]
            if dt != F32:
                v = v.bitcast(dt)
            v = v[:, 0:nel]
            if len(shape) == 3:
                v = v.rearrange("p (a b) -> p a b", a=shape[1])
            elif len(shape) == 4:
                v = v.rearrange("p (a b c) -> p a b c", a=shape[1], b=shape[2])
            t = T(v)
            t.d = pdeps[self.p:self.p + npg]
            self.p += npg
            return t

    ph = Ph()
    qraw = ph.al([128, 2, 512])
    kraw = ph.al([128, 2, 512])
    qr = ph.al([128, 2, 512], BF16)
    kr = ph.al([128, 2, 512], BF16)
    qdd = ph.al([128, 2, 512], BF16)
    t1 = ph.al([128, 2, 512])
    t2 = ph.al([128, 2, 512])
    gate = ph.al([128, 2, 512], BF16)
    vtok = ph.al([128, 4, 1024], BF16)
    kdtok = ph.al([128, 4, 256], BF16)
    kdm = [ph.al([128, 256], BF16) for i in range(2)]
    PT = [ph.al([128, 128], BF16) for i in range(2)]
    oT = ph.al([128, 2, 512])
    sq = ph.al([128, 2, 512], BF16)
    S0f = [ph.al([128, 2, 256]) for i in range(3)]
    S0b = [ph.al([128, 2, 256], BF16) for i in range(3)]
    Sout = [ph.al([128, 2, 256]) for i in range(3)]
    p_ret_end = ph.p
    ph = Ph()
    xp_ = ph.al([128, 3 + 512 + 1])
    xps = ph.al([128, 16, 7])
    xc = ph.al([128, 512])
    xcb = ph.al([128, 512], BF16)
    rg = ph.al([128, 512])
    ig = ph.al([128, 512])
    av = ph.al([128, 512])
    mv = ph.al([128, 512])
    hT = ph.al([128, 512])
    h0s = ph.al([128, 8, 16])
    cv0s = ph.al([128, 8, 16, 3])
    lrus_o = ph.al([128, 8, 16])
    convs_o = ph.al([128, 8, 16, 3])
    y32 = ph.al([128, 512])
    g1 = ph.al([128, 512])
    g2 = ph.al([128, 512])
    ph = Ph()
    hact = [ph.al([128, 4, 512], BF16) for i in range(2)]
    sgt = [ph.al([128, 512]) for i in range(2)]
    ph = Ph()
    Wneg = ph.al([128, 8, 2, 64])
    Wpos = ph.al([128, 9, 2, 64])
    frfi = ph.al([128, 2, 64])
    wtmp = [ph.al([128, 64]) for i in range(8)]
    braw = [ph.al([128, 8, 16]) for i in range(2)]
    craw = [ph.al([128, 8, 16]) for i in range(2)]
    bbar = [ph.al([128, 8, 16]) for i in range(2)]
    tA = ph.al([128, 1024])
    tB = ph.al([128, 1024])
    Rp = [ph.al([128, 8, 8, 16], BF16) for i in range(2)]
    Op = [ph.al([128, 8, 8, 16], BF16) for i in range(2)]
    MTs = ph.al([128, 16, 128], BF16)
    QTs = [ph.al([128, 128], BF16) for i in range(2)]
    Ub2 = [ph.al([128, 16, 64], BF16) for i in range(2)]
    Yb = ph.al([128, 16, 64], BF16)
    Aar = ph.al([128, 2, 8, 65])
    Abf = ph.al([128, 2, 8, 64], BF16)
    h0s5 = ph.al([128, 2, 8, 16])
    Pp = ph.al([128, 2, 8, 16])
    Pf = ph.al([128, 2, 8, 16])
    y32b = ph.al([128, 512])
    g1b = ph.al([128, 512])
    g2b = ph.al([128, 512])
    Wd = ph.al([128, 7, 2, 64])
    rstd = sb("rstd", [128, 512])
    cosT = sb("cosT", [128, 512])
    sinT = sb("sinT", [128, 512])
    tslot = fw.slot()
    tslot2 = fw.slot()
    stslot2 = fw.slot()
    xslot = fw.slot()
    S0slot = [fw.slot() for _ in range(3)]
    S0bslot = [fw.slot() for _ in range(3)]
    Soslot = [fw.slot() for _ in range(3)]
    stslot = fw.slot()
    oslots = {}

    def oslot(k):
        if k not in oslots:
            oslots[k] = fw.slot()
        return oslots[k]

    out_toks = []

    NLOADS = 192
    wscr_l = [nc.dram_tensor(f"wscr{q}", [NLOADS // 2, 128, KC * 512], BF16, kind="Internal") for q in range(2)]
    wscr_ap = lambda idx: wscr_l[idx // (NLOADS // 2)].ap()[idx % (NLOADS // 2)]
    wsd = [Dep() for _ in range(NLOADS)]
    wslot_hw = [fw.slot() for _ in range(NW)]
    sslot = [fw.slot() for _ in range(NW)]
    lctr = [0]
    passno = [0]
    use_scr = cfg.get("use_scr", True)

    def load_w(dram_ap_list):
        i = wctr[0] % NW
        wctr[0] += 1
        t = wb[i]
        idx = lctr[0]
        lctr[0] += 1
        flat = t.h[:].rearrange("p a b -> p (a b)")
        if use_scr and passno[0] > 0:
            fw.dma(SP, lambda e: e.dma_start(out=flat, in_=wscr_ap(idx)), wslot_hw[i], reads=[wsd[idx]], writes=[t.d])
            return t

        def fn(e, t=t, lst=dram_ap_list):
            r = []
            for src, dst in lst:
                r.append(e.dma_start(out=dst(t.h), in_=src))
            return r
        fw.dma(POOL, fn, wslot[i], writes=[t.d], n=len(dram_ap_list))
        if use_scr and len(tiles) > 1:
            fw.dma(SP, lambda e: e.dma_start(out=wscr_ap(idx), in_=flat), sslot[i], reads=[t.d], writes=[wsd[idx]])
        return t

    def wsrc(wd, c0, ncol):
        return wd[:, c0:c0 + ncol].rearrange("(kc p) j -> p kc j", p=128)

    def rmsnorm(gi, Tn):
        fw.op(ACT, lambda e: e.activation(out=cat.h[:, :, 0:Tn], in_=x.h[:, :, 0:Tn], func=AF.Square), reads=[x.d], writes=[cat.d])
        b = bank()

        def mm(e):
            r = None
            for kc in range(KC):
                r = e.matmul(b.h[:, 0:Tn], lhsT=ones, rhs=cat.h[:, kc, 0:Tn], start=(kc == 0), stop=(kc == KC - 1))
            return r
        fw.op(PE, mm, reads=[cat.d, identones.d], writes=[b.d])
        fw.op(ACT, lambda e: e.activation(out=rstd.h[:, 0:Tn], in_=b.h[:, 0:Tn], func=AF.Sqrt, scale=1.0 / D, bias=EPS), reads=[b.d], writes=[rstd.d])
        fw.op(DVE, lambda e: e.reciprocal(out=rstd.h[:, 0:Tn], in_=rstd.h[:, 0:Tn]), reads=[rstd.d], writes=[rstd.d])
        for kc in range(KC):
            eng = DVE
            fw.op(eng, lambda e, kc=kc: e.scalar_tensor_tensor(out=xn.h[:, kc, 0:Tn], in0=x.h[:, kc, 0:Tn], scalar=gains.h[:, gi, kc:kc + 1],
                                                               in1=rstd.h[:, 0:Tn], op0=ALU.mult, op1=ALU.mult),
                  reads=[x.d, gains.d, rstd.d], writes=[xn.d])

    def proj_fm(wt, col0, src, Tn, nk=KC):
        b = bank()

        def mm(e):
            r = None
            for kc in range(nk):
                r = e.matmul(b.h[:, 0:Tn], lhsT=wt.h[:, kc, col0:col0 + 128], rhs=src.h[:, kc, 0:Tn], start=(kc == 0), stop=(kc == nk - 1))
            return r
        fw.op(PE, mm, reads=[wt.d, src.d], writes=[b.d])
        return b

    def add_to_x(b, oc, Tn):
        fw.op(DVE, lambda e: e.tensor_tensor(out=x.h[:, oc, 0:Tn], in0=b.h[:, 0:Tn], in1=x.h[:, oc, 0:Tn], op=ALU.add), reads=[b.d, x.d], writes=[x.d])

    def ffn(layer, Tn):
        rmsnorm(4 + layer, Tn)
        wg = w_gu.ap()[layer]
        wd = w_dn.ap()[layer]
        for hb in range(FH // 512):
            tg = load_w([(wsrc(wg, hb * 512, 512), lambda h: h[:])])
            tu = load_w([(wsrc(wg, FH + hb * 512, 512), lambda h: h[:])])
            ha = hact[hb % 2]
            for m in range(4):
                bg = proj_fm(tg, m * 128, xn, Tn)
                bu = proj_fm(tu, m * 128, xn, Tn)
                sg = sgt[m % 2]
                fw.op(ACT, lambda e, bg=bg, sg=sg: e.activation(out=sg.h[:, 0:Tn], in_=bg.h[:, 0:Tn], func=AF.Silu), reads=[bg.d], writes=[sg.d])
                fw.op(DVE, lambda e, bu=bu, sg=sg, ha=ha, m=m: e.tensor_tensor(out=ha.h[:, m, 0:Tn], in0=bu.h[:, 0:Tn], in1=sg.h[:, 0:Tn], op=ALU.mult),
                      reads=[bu.d, sg.d], writes=[ha.d])
            td = load_w([(wd[hb * 512:(hb + 1) * 512, :].rearrange("(kc p) j -> p kc j", p=128), lambda h: h[:].rearrange("p a b -> p (a b)").rearrange("p (kc j) -> p kc j", kc=4))])
            tdv = td.h[:].rearrange("p a b -> p (a b)").rearrange("p (kc j) -> p kc j", kc=4)
            for oc in range(KC):
                b = bank()

                def mm(e, b=b, oc=oc, ha=ha, tdv=tdv):
                    r = None
                    for kc in range(4):
                        r = e.matmul(b.h[:, 0:Tn], lhsT=tdv[:, kc, oc * 128:(oc + 1) * 128], rhs=ha.h[:, kc, 0:Tn], start=(kc == 0), stop=(kc == 3))
                    return r
                fw.op(PE, mm, reads=[td.d, ha.d], writes=[b.d])
                add_to_x(b, oc, Tn)

    def even_layer(e_, kind, ti, Tn):
        samp = kind == "s"
        NTC = 1 if samp else 4
        TR = 64 if samp else 128
        rmsnorm(e_, Tn)
        wi = w_in.ap()[e_]
        qdec = qdecS if samp else qdecP
        kv0 = 8 if samp else 0
        gd = [GAM[h] ** (4 if samp else 128) for h in range(4)]
        for half in range(2):
            tv = load_w([(wsrc(wi, 2048 + half * 512, 512), lambda h: h[:])])
            for tc in range(NTC):
                b = bank()

                def mm(e, b=b, tc=tc, tv=tv):
                    r = None
                    for kc in range(KC):
                        r = e.matmul(b.h[0:TR, :], lhsT=xn.h[:, kc, tc * 128:tc * 128 + TR], rhs=tv.h[:, kc, :], start=(kc == 0), stop=(kc == KC - 1))
                    return r
                fw.op(PE, mm, reads=[tv.d, xn.d], writes=[b.d])
                fw.op(ACT, lambda e, b=b, tc=tc, half=half: e.activation(out=vtok.h[0:TR, tc, half * 512:(half + 1) * 512], in_=b.h[0:TR, :], func=AF.Copy),
                      reads=[b.d], writes=[vtok.d])
        def _head(h):
            tqk = load_w([(wsrc(wi, h * 256, 256), lambda hh: hh[:, :, 0:256]), (wsrc(wi, 1024 + h * 256, 256), lambda hh: hh[:, :, 256:512])])
            for dc in range(2):
                bq = proj_fm(tqk, dc * 128, xn, Tn)
                fw.op(ACT, lambda e, bq=bq, dc=dc: e.activation(out=qraw.h[:, dc, 0:Tn], in_=bq.h[:, 0:Tn], func=AF.Copy), reads=[bq.d], writes=[qraw.d])
                bk = proj_fm(tqk, 256 + dc * 128, xn, Tn)
                fw.op(ACT, lambda e, bk=bk, dc=dc: e.activation(out=kraw.h[:, dc, 0:Tn], in_=bk.h[:, 0:Tn], func=AF.Copy), reads=[bk.d], writes=[kraw.d])
            tg = load_w([(wsrc(wi, 3072 + h * 256, 256), lambda hh: hh[:, :, 0:256])])
            for dc in range(2):
                bg = proj_fm(tg, dc * 128, xn, Tn)
                fw.op(ACT, lambda e, bg=bg, dc=dc: e.activation(out=gate.h[:, dc, 0:Tn], in_=bg.h[:, 0:Tn], func=AF.Silu), reads=[bg.d], writes=[gate.d])
            for raw, outt, eng in ((qraw, qr, DVE), (kraw, kr, POOL)):
                for dc in range(2):
                    fw.op(eng, lambda e, raw=raw, dc=dc: e.tensor_tensor(out=t1.h[:, dc, 0:Tn], in0=raw.h[:, dc, 0:Tn], in1=cosT.h[:, 0:Tn], op=ALU.mult), reads=[raw.d, cosT.d], writes=[t1.d])
                    fw.op(eng, lambda e, raw=raw, dc=dc: e.tensor_tensor(out=t2.h[:, dc, 0:Tn], in0=raw.h[:, dc, 0:Tn], in1=sinT.h[:, 0:Tn], op=ALU.mult), reads=[raw.d, sinT.d], writes=[t2.d])
                fw.op(eng, lambda e, outt=outt: e.tensor_tensor(out=outt.h[:, 0, 0:Tn], in0=t1.h[:, 0, 0:Tn], in1=t2.h[:, 1, 0:Tn], op=ALU.subtract), reads=[t1.d, t2.d], writes=[outt.d])
                fw.op(eng, lambda e, outt=outt: e.tensor_tensor(out=outt.h[:, 1, 0:Tn], in0=t1.h[:, 1, 0:Tn], in1=t2.h[:, 0, 0:Tn], op=ALU.add), reads=[t1.d, t2.d], writes=[outt.d])
            if samp:
                qdb = sap(qdecS.h, h * 64, [[0, 2], [1, 64]])
                fw.op(DVE, lambda e, qdb=qdb: e.tensor_tensor(out=qdd.h[:, :, 0:64], in0=qr.h[:, :, 0:64], in1=qdb, op=ALU.mult), reads=[qr.d, qdecS.d], writes=[qdd.d])
            else:
                qdb = sap(qdecP.h, h * 128, [[0, 2], [0, 4], [1, 128]])
                fw.op(DVE, lambda e, qdb=qdb: e.tensor_tensor(out=qdd.h[:, :, :].rearrange("p a (c n) -> p a c n", c=4), in0=qr.h[:, :, :].rearrange("p a (c n) -> p a c n", c=4), in1=qdb, op=ALU.mult), reads=[qr.d, qdecP.d], writes=[qdd.d])
            for tc in range(NTC):
                b = bank()
                bb = b.h[:, 0:128].bitcast(BF16)

                def tr(e, tc=tc, bb=bb):
                    r = None
                    for dc in range(2):
                        r = e.transpose(out=bb[0:TR, dc * 128:(dc + 1) * 128], in_=kr.h[:, dc, tc * 128:tc * 128 + TR], identity=ident)
                    return r
                fw.op(PE, tr, reads=[kr.d, identones.d], writes=[b.d])
                fw.op(DVE, lambda e, tc=tc, bb=bb: e.tensor_scalar(out=kdtok.h[0:TR, tc, :], in0=bb[0:TR, :], scalar1=kvec.h[0:TR, kv0 + 4 + h:kv0 + 5 + h], scalar2=None, op0=ALU.mult),
                      reads=[b.d, kvec.d], writes=[kdtok.d])
            po = [banks[0], banks[1]]
            for c in range(NTC):
                bs = bank()

                def sc(e, bs=bs, c=c):
                    r = None
                    for dc in range(2):
                        r = e.matmul(bs.h[0:TR, 0:TR], lhsT=kr.h[:, dc, c * 128:c * 128 + TR], rhs=qdd.h[:, dc, c * 128:c * 128 + TR], start=(dc == 0), stop=(dc == 1))
                    return r
                fw.op(PE, sc, reads=[kr.d, qdd.d], writes=[bs.d])
                pt = PT[c % 2]
                moff = 128 if samp else 0
                fw.op(DVE, lambda e, bs=bs, pt=pt: e.scalar_tensor_tensor(out=pt.h[0:TR, 0:TR], in0=bs.h[0:TR, 0:TR], scalar=kvec.h[0:TR, kv0 + h:kv0 + h + 1],
                                                                             in1=masks.h[0:TR, moff:moff + TR], op0=ALU.mult, op1=ALU.mult),
                      reads=[bs.d, kvec.d, masks.d], writes=[pt.d])
                if not samp:
                    for ec in range(2):
                        def om(e, ec=ec, c=c, pt=pt):
                            e.matmul(po[ec].h[:, c * 128:(c + 1) * 128], lhsT=vtok.h[:, c, h * 256 + ec * 128:h * 256 + (ec + 1) * 128], rhs=pt.h[:, :], start=True, stop=False)
                            r = None
                            for dc in range(2):
                                r = e.matmul(po[ec].h[:, c * 128:(c + 1) * 128], lhsT=Sbf.h[:, h, dc, ec * 128:(ec + 1) * 128], rhs=qdd.h[:, dc, c * 128:(c + 1) * 128], start=False, stop=(dc == 1))
                            return r
                        fw.op(PE, om, reads=[vtok.d, pt.d, Sbf.d, qdd.d], writes=[po[ec].d])
                    bS = bank()

                    def su(e, bS=bS, c=c):
                        r = None
                        for dc in range(2):
                            r = e.matmul(bS.h[:, dc * 256:(dc + 1) * 256], lhsT=kdtok.h[:, c, dc * 128:(dc + 1) * 128], rhs=vtok.h[:, c, h * 256:(h + 1) * 256], start=True, stop=True)
                        return r
                    fw.op(PE, su, reads=[kdtok.d, vtok.d], writes=[bS.d])
                    fw.op(DVE, lambda e, bS=bS: e.scalar_tensor_tensor(out=Sst.h[:, e_, h, :, :], in0=Sst.h[:, e_, h, :, :], scalar=gd[h], in1=bS.h[:, :].rearrange("p (a b) -> p a b", a=2), op0=ALU.mult, op1=ALU.add),
                          reads=[bS.d, Sst.d], writes=[Sst.d])
                    fw.op(ACT, lambda e: e.activation(out=Sbf.h[:, h, :, :], in_=Sst.h[:, e_, h, :, :], func=AF.Copy), reads=[Sst.d], writes=[Sbf.d])
                else:
                    for ec in range(2):
                        fw.op(PE, lambda e, ec=ec, pt=pt: e.matmul(po[ec].h[:, 0:64], lhsT=vtok.h[0:64, 0, h * 256 + ec * 128:h * 256 + (ec + 1) * 128], rhs=pt.h[0:64, 0:64], start=True, stop=True),
                              reads=[vtok.d, pt.d], writes=[po[ec].d])
                    for j in range(16):
                        si = (h * 16 + j) % 3
                        s0f, s0b, so_ = S0f[si], S0b[si], Sout[si]
                        src = sret.ap()[e_, j, h].rearrange("(dc p) e -> p dc e", p=128)
                        fw.dma(SP, lambda e, s0f=s0f, src=src: e.dma_start(out=s0f.h[:], in_=src), S0slot[si], writes=[s0f.d])
                        fw.dma(POOL, lambda e, s0b=s0b, src=src: e.dma_start(out=s0b.h[:], in_=src), S0bslot[si], writes=[s0b.d])
                        for ec in range(2):
                            def im(e, ec=ec, j=j, s0b=s0b):
                                r = None
                                for dc in range(2):
                                    r = e.matmul(po[ec].h[:, 4 * j:4 * j + 4], lhsT=s0b.h[:, dc, ec * 128:(ec + 1) * 128], rhs=qdd.h[:, dc, 4 * j:4 * j + 4], start=False, stop=(dc == 1), skip_group_check=True)
                                return r
                            fw.op(PE, im, reads=[s0b.d, qdd.d], writes=[po[ec].d])
                        km = kdm[j % 2]
                        fw.op(DVE, lambda e, km=km, j=j: e.tensor_scalar(out=km.h[0:64, :], in0=kdtok.h[0:64, 0, :], scalar1=onehot.h[0:64, j:j + 1], scalar2=None, op0=ALU.mult),
                              reads=[kdtok.d, onehot.d], writes=[km.d])
                        bS = bank()

                        def su(e, bS=bS, km=km):
                            r = None
                            for dc in range(2):
                                r = e.matmul(bS.h[:, dc * 256:(dc + 1) * 256], lhsT=km.h[0:64, dc * 128:(dc + 1) * 128], rhs=vtok.h[0:64, 0, h * 256:(h + 1) * 256], start=True, stop=True)
                            return r
                        fw.op(PE, su, reads=[km.d, vtok.d], writes=[bS.d])
                        fw.op(DVE, lambda e, bS=bS, s0f=s0f, so_=so_: e.scalar_tensor_tensor(out=so_.h[:], in0=s0f.h[:], scalar=gd[h], in1=bS.h[:, :].rearrange("p (a b) -> p a b", a=2), op0=ALU.mult, op1=ALU.add),
                              reads=[bS.d, s0f.d], writes=[so_.d])
                        dst = o_rets.ap()[e_, j, h].rearrange("(dc p) e -> p dc e", p=128)
                        out_toks.append(fw.dma(SP, lambda e, so_=so_, dst=dst: e.dma_start(out=dst, in_=so_.h[:]), Soslot[si], reads=[so_.d]))
            for ec in range(2):
                fw.op(ACT, lambda e, ec=ec: e.activation(out=oT.h[:, ec, 0:Tn], in_=po[ec].h[:, 0:Tn], func=AF.Copy), reads=[po[ec].d], writes=[oT.d])
            fw.op(ACT, lambda e: e.activation(out=sq.h[:, :, 0:Tn], in_=oT.h[:, :, 0:Tn], func=AF.Square), reads=[oT.d], writes=[sq.d])
            bn = bank()

            def nm(e, bn=bn):
                r = None
                for ec in range(2):
                    r = e.matmul(bn.h[:, 0:Tn], lhsT=ones, rhs=sq.h[:, ec, 0:Tn], start=(ec == 0), stop=(ec == 1))
                return r
            fw.op(PE, nm, reads=[sq.d, identones.d], writes=[bn.d])
            fw.op(ACT, lambda e, bn=bn: e.activation(out=rstd.h[:, 0:Tn], in_=bn.h[:, 0:Tn], func=AF.Sqrt, scale=1.0 / 256, bias=EPS), reads=[bn.d], writes=[rstd.d])
            fw.op(DVE, lambda e: e.reciprocal(out=rstd.h[:, 0:Tn], in_=rstd.h[:, 0:Tn]), reads=[rstd.d], writes=[rstd.d])
            for ec in range(2):
                fw.op(DVE, lambda e, ec=ec: e.tensor_tensor(out=t1.h[:, ec, 0:Tn], in0=oT.h[:, ec, 0:Tn], in1=rstd.h[:, 0:Tn], op=ALU.mult), reads=[oT.d, rstd.d], writes=[t1.d])
                fw.op(DVE, lambda e, ec=ec: e.tensor_tensor(out=cat.h[:, 2 * h + ec, 0:Tn], in0=t1.h[:, ec, 0:Tn], in1=gate.h[:, ec, 0:Tn], op=ALU.mult), reads=[t1.d, gate.d], writes=[cat.d])
            if (not samp) and ti == 3:
                dst = o_retp.ap()[e_, h].rearrange("(dc p) e -> p dc e", p=128)
                out_toks.append(fw.dma(SP, lambda e, dst=dst: e.dma_start(out=dst, in_=Sst.h[:, e_, h, :, :]), oslot(("retp", e_, h)), reads=[Sst.d]))
        for h_ in range(4):
            _head(h_)
        if samp:
            fw.dma(SP, lambda e: e.dma_start(out=h0s.h[:], in_=slru.ap()[e_].rearrange("(n p) j -> p n j", p=128)), stslot, writes=[h0s.d])
            fw.dma(SP, lambda e: e.dma_start(out=cv0s.h[:], in_=sconv.ap()[e_].rearrange("(n p) j i -> p n j i", p=128)), stslot2, writes=[cv0s.d])
        def _blk(nb):
            txy = load_w([(wsrc(wi, 4096 + nb * 128, 128), lambda hh: hh[:, :, 0:128]), (wsrc(wi, 5120 + nb * 128, 128), lambda hh: hh[:, :, 128:256])])
            bx_ = proj_fm(txy, 0, xn, Tn)
            by_ = proj_fm(txy, 128, xn, Tn)
            lv = lambda k: lruvec.h[:, e_, nb, k:k + 1]
            if not samp:
                fw.op(ACT, lambda e, bx_=bx_: e.activation(out=xp_.h[:, 3:3 + Tn], in_=bx_.h[:, 0:Tn], func=AF.Copy), reads=[bx_.d], writes=[xp_.d])
                if ti == 0:
                    fw.op(DVE, lambda e: e.memset(xp_.h[:, 0:3], 0.0), writes=[xp_.d])
                else:
                    fw.op(DVE, lambda e, nb=nb: e.tensor_copy(out=xp_.h[:, 0:3], in_=convst.h[:, e_, nb, :]), reads=[convst.d], writes=[xp_.d])
                xin = lambda i: xp_.h[:, i:i + Tn]
                xco = xc.h[:, 0:Tn]
            else:
                fw.op(ACT, lambda e, bx_=bx_: e.activation(out=xps.h[:, :, 3:7], in_=bx_.h[:, 0:64].rearrange("p (j t) -> p j t", t=4), func=AF.Copy), reads=[bx_.d], writes=[xps.d])
                fw.op(DVE, lambda e, nb=nb: e.tensor_copy(out=xps.h[:, :, 0:3], in_=cv0s.h[:, nb, :, :]), reads=[cv0s.d], writes=[xps.d])
                xin = lambda i: xps.h[:, :, i:i + 4]
                xco = xc.h[:, 0:64].rearrange("p (j t) -> p j t", t=4)
            fw.op(DVE, lambda e, xin=xin, xco=xco, nb=nb: e.tensor_scalar(out=xco, in0=xin(0), scalar1=lruvec.h[:, e_, nb, 0:1], scalar2=lruvec.h[:, e_, nb, 4:5], op0=ALU.mult, op1=ALU.add),
                  reads=[xp_.d, xps.d, lruvec.d], writes=[xc.d])
            for i in range(1, 4):
                fw.op(DVE, lambda e, xin=xin, xco=xco, nb=nb, i=i: e.scalar_tensor_tensor(out=xco, in0=xin(i), scalar=lruvec.h[:, e_, nb, i:i + 1], in1=xco, op0=ALU.mult, op1=ALU.add),
                      reads=[xp_.d, xps.d, lruvec.d, xc.d], writes=[xc.d])
            if not samp:
                fw.op(POOL, lambda e, nb=nb: e.tensor_copy(out=convst.h[:, e_, nb, :], in_=xp_.h[:, Tn:Tn + 3]), reads=[xp_.d], writes=[convst.d])
            else:
                fw.op(POOL, lambda e, nb=nb: e.tensor_copy(out=convs_o.h[:, nb, :, :], in_=xps.h[:, :, 4:7]), reads=[xps.d], writes=[convs_o.d])
            fw.op(ACT, lambda e: e.activation(out=xcb.h[:, 0:Tn], in_=xc.h[:, 0:Tn], func=AF.Copy), reads=[xc.d], writes=[xcb.d])
            br = bank()
            fw.op(PE, lambda e, br=br, nb=nb: e.matmul(br.h[:, 0:Tn], lhsT=wa_sb.h[:, e_, nb, :], rhs=xcb.h[:, 0:Tn], start=True, stop=True), reads=[wa_sb.d, xcb.d], writes=[br.d])
            bi = bank()
            fw.op(PE, lambda e, bi=bi, nb=nb: e.matmul(bi.h[:, 0:Tn], lhsT=wx_sb.h[:, e_, nb, :], rhs=xcb.h[:, 0:Tn], start=True, stop=True), reads=[wx_sb.d, xcb.d], writes=[bi.d])
            fw.op(ACT, lambda e, br=br, nb=nb: e.activation(out=rg.h[:, 0:Tn], in_=br.h[:, 0:Tn], func=AF.Sigmoid, bias=lruvec.h[:, e_, nb, 5:6]), reads=[br.d, lruvec.d], writes=[rg.d])
            fw.op(ACT, lambda e, bi=bi, nb=nb: e.activation(out=ig.h[:, 0:Tn], in_=bi.h[:, 0:Tn], func=AF.Sigmoid, bias=lruvec.h[:, e_, nb, 6:7]), reads=[bi.d, lruvec.d], writes=[ig.d])
            fw.op(ACT, lambda e, nb=nb: e.activation(out=av.h[:, 0:Tn], in_=rg.h[:, 0:Tn], func=AF.Exp, scale=lsp.h[:, e_, nb, 0:1]), reads=[rg.d, lsp.d], writes=[av.d])
            fw.op(ACT, lambda e, nb=nb: e.activation(out=mv.h[:, 0:Tn], in_=rg.h[:, 0:Tn], func=AF.Exp, scale=lsp.h[:, e_, nb, 1:2]), reads=[rg.d, lsp.d], writes=[mv.d])
            fw.op(ACT, lambda e: e.activation(out=mv.h[:, 0:Tn], in_=mv.h[:, 0:Tn], func=AF.Sqrt, scale=-1.0, bias=1.0), reads=[mv.d], writes=[mv.d])
            fw.op(DVE, lambda e: e.tensor_tensor(out=mv.h[:, 0:Tn], in0=mv.h[:, 0:Tn], in1=ig.h[:, 0:Tn], op=ALU.mult), reads=[mv.d, ig.d], writes=[mv.d])
            fw.op(DVE, lambda e: e.tensor_tensor(out=mv.h[:, 0:Tn], in0=mv.h[:, 0:Tn], in1=xc.h[:, 0:Tn], op=ALU.mult), reads=[mv.d, xc.d], writes=[mv.d])
            if not samp:
                init = 0.0 if ti == 0 else hst.h[:, e_, nb:nb + 1]
                fw.op(DVE, lambda e, init=init: e.tensor_tensor_scan(out=hT.h[:, 0:Tn], data0=av.h[:, 0:Tn], data1=mv.h[:, 0:Tn], initial=init, op0=ALU.mult, op1=ALU.add),
                      reads=[av.d, mv.d, hst.d], writes=[hT.d])
                fw.op(POOL, lambda e, nb=nb: e.tensor_copy(out=hst.h[:, e_, nb:nb + 1], in_=hT.h[:, Tn - 1:Tn]), reads=[hT.d], writes=[hst.d])
            else:
                av3 = av.h[:, 0:64].rearrange("p (j t) -> p j t", t=4)
                mv3 = mv.h[:, 0:64].rearrange("p (j t) -> p j t", t=4)
                fw.op(DVE, lambda e, nb=nb: e.tensor_tensor(out=g1.h[:, 0:16], in0=av3[:, :, 0], in1=h0s.h[:, nb, :], op=ALU.mult), reads=[av.d, h0s.d], writes=[g1.d])
                fw.op(DVE, lambda e: e.tensor_tensor(out=mv3[:, :, 0], in0=mv3[:, :, 0], in1=g1.h[:, 0:16], op=ALU.add), reads=[mv.d, g1.d], writes=[mv.d])
                fw.op(DVE, lambda e: e.memset(av3[:, :, 0], 0.0), reads=[g1.d], writes=[av.d])
                fw.op(DVE, lambda e: e.tensor_tensor_scan(out=hT.h[:, 0:64], data0=av.h[:, 0:64], data1=mv.h[:, 0:64], initial=0.0, op0=ALU.mult, op1=ALU.add),
                      reads=[av.d, mv.d], writes=[hT.d])
                fw.op(POOL, lambda e, nb=nb: e.tensor_copy(out=lrus_o.h[:, nb, :], in_=hT.h[:, 0:64].rearrange("p (j t) -> p j t", t=4)[:, :, 3]), reads=[hT.d], writes=[lrus_o.d])
            fw.op(ACT, lambda e, by_=by_: e.activation(out=y32.h[:, 0:Tn], in_=by_.h[:, 0:Tn], func=AF.Copy), reads=[by_.d], writes=[y32.d])
            gelu_mul(y32, hT, cat.h[:, 8 + nb, 0:Tn], cat, Tn, g1, g2)
        for nb_ in range(8):
            _blk(nb_)
        if (not samp) and ti == 3:
            out_toks.append(fw.dma(SP, lambda e: e.dma_start(out=o_lrup.ap()[e_], in_=hst.h[:, e_, :]), oslot(("lrup", e_)), reads=[hst.d]))
            out_toks.append(fw.dma(SP, lambda e: e.dma_start(out=o_convp.ap()[e_].rearrange("(n p) i -> p n i", p=128), in_=convst.h[:, e_, :, :]), oslot(("convp", e_)), reads=[convst.d]))
        if samp:
            out_toks.append(fw.dma(SP, lambda e: e.dma_start(out=o_lrus.ap()[e_].rearrange("(n p) j -> p n j", p=128), in_=lrus_o.h[:]), oslot(("lrus", e_)), reads=[lrus_o.d]))
            out_toks.append(fw.dma(SP, lambda e: e.dma_start(out=o_convs.ap()[e_].rearrange("(n p) j i -> p n j i", p=128), in_=convs_o.h[:]), oslot(("convs", e_)), reads=[convs_o.d]))
        wo = w_out.ap()[e_]
        for og in range(4):
            two = load_w([(wsrc(wo, og * 512, 512), lambda hh: hh[:])])
            for m in range(4):
                b = proj_fm(two, m * 128, cat, Tn)
                add_to_x(b, og * 4 + m, Tn)

    def gelu_mul(src, mul, out_ap, out_t, Tn, g1, g2):
        s = src.h[:, 0:Tn]
        fw.op(DVE, lambda e: e.tensor_tensor(out=g1.h[:, 0:Tn], in0=s, in1=s, op=ALU.mult), reads=[src.d], writes=[g1.d])
        fw.op(DVE, lambda e: e.tensor_scalar(out=g1.h[:, 0:Tn], in0=g1.h[:, 0:Tn], scalar1=0.044715, scalar2=1.0, op0=ALU.mult, op1=ALU.add), reads=[g1.d], writes=[g1.d])
        fw.op(DVE, lambda e: e.tensor_tensor(out=g1.h[:, 0:Tn], in0=g1.h[:, 0:Tn], in1=s, op=ALU.mult), reads=[g1.d, src.d], writes=[g1.d])
        fw.op(ACT, lambda e: e.activation(out=g2.h[:, 0:Tn], in_=g1.h[:, 0:Tn], func=AF.Sigmoid, scale=GELU_C), reads=[g1.d], writes=[g2.d])
        if mul is not None:
            fw.op(DVE, lambda e: e.tensor_tensor(out=g2.h[:, 0:Tn], in0=g2.h[:, 0:Tn], in1=s, op=ALU.mult), reads=[g2.d, src.d], writes=[g2.d])
            fw.op(DVE, lambda e: e.tensor_tensor(out=out_ap, in0=g2.h[:, 0:Tn], in1=mul.h[:, 0:Tn], op=ALU.mult), reads=[g2.d, mul.d], writes=[out_t.d])
        else:
            fw.op(DVE, lambda e: e.tensor_tensor(out=out_ap, in0=g2.h[:, 0:Tn], in1=s, op=ALU.mult), reads=[g2.d, src.d], writes=[out_t.d])

    def odd_layer(o_, kind, ti, Tn):
        rmsnorm(2 + o_, Tn)
        if not do_odd:
            return
        s5_layer(o_, kind, ti, Tn)
        if cfg.get("stop_s5"):
            raise StopIteration
        wg = w_glu.ap()[o_]
        for og in range(4):
            t1w = load_w([(wsrc(wg, og * 512, 512), lambda hh: hh[:])])
            t2w = load_w([(wsrc(wg, D + og * 512, 512), lambda hh: hh[:])])
            for m in range(4):
                oc = og * 4 + m
                b1 = proj_fm(t1w, m * 128, cat, Tn)
                b2 = proj_fm(t2w, m * 128, cat, Tn)
                sg = sgt[m % 2]
                fw.op(ACT, lambda e, b2=b2, sg=sg, oc=oc: e.activation(out=sg.h[:, 0:Tn], in_=b2.h[:, 0:Tn], func=AF.Sigmoid, bias=bglu.h[:, o_, 16 + oc:17 + oc]), reads=[b2.d, bglu.d], writes=[sg.d])
                fw.op(DVE, lambda e, b1=b1, sg=sg, oc=oc: e.scalar_tensor_tensor(out=sg.h[:, 0:Tn], in0=b1.h[:, 0:Tn], scalar=bglu.h[:, o_, oc:oc + 1], in1=sg.h[:, 0:Tn], op0=ALU.add, op1=ALU.mult),
                      reads=[b1.d, sg.d, bglu.d], writes=[sg.d])
                fw.op(DVE, lambda e, sg=sg, oc=oc: e.tensor_tensor(out=x.h[:, oc, 0:Tn], in0=sg.h[:, 0:Tn], in1=x.h[:, oc, 0:Tn], op=ALU.add), reads=[sg.d, x.d], writes=[x.d])

    def vap(t, off, dims, parts=128, p0=0):
        v = t.h
        ps = v.ap[0][0]
        return bass.AP(arena, v.offset + p0 * ps + off, [[ps, parts]] + [list(d_) for d_ in dims])

    bslots = [fw.slot() for _ in range(4)]
    s5scr = nc.dram_tensor("s5scr", [2, 8, 4, 128, 1024], BF16, kind="Internal")
    s5d = [[[Dep() for _ in range(4)] for _ in range(8)] for _ in range(2)]
    s5st = [fw.slot() for _ in range(4)]
    s5ld = [fw.slot() for _ in range(4)]
    h0slots = [fw.slot() for _ in range(2)]
    TWO_PI = 2.0 * math.pi

    def s5_layer(o_, kind, ti, Tn):
        samp = kind == "s"
        n = 16 if samp else 64
        SL = 4 if samp else 8
        s_list = list(range(4, 8)) if samp else list(range(8))
        W = wtmp

        def tt(eng, out, i0, i1, op, rd, wr):
            fw.op(eng, lambda e: e.tensor_tensor(out=out, in0=i0, in1=i1, op=op), reads=rd, writes=wr)

        def ts(eng, out, i0, s1, s2, op0, op1, rd, wr):
            if op1 is None:
                fw.op(eng, lambda e: e.tensor_scalar(out=out, in0=i0, scalar1=s1, scalar2=None, op0=op0), reads=rd, writes=wr)
            else:
                fw.op(eng, lambda e: e.tensor_scalar(out=out, in0=i0, scalar1=s1, scalar2=s2, op0=op0, op1=op1), reads=rd, writes=wr)

        are = apar.h[:, 0, o_, :]
        aim = apar.h[:, 1, o_, :]
        dt_ = dtb.h[:, o_, :]
        wd = [w_.d for w_ in W]
        tt(DVE, W[0].h[:], are, dt_, ALU.mult, [apar.d, dtb.d], [W[0].d])
        tt(DVE, W[1].h[:], aim, dt_, ALU.mult, [apar.d, dtb.d], [W[1].d])
        fw.op(ACT, lambda e: e.activation(out=W[2].h[:], in_=W[0].h[:], func=AF.Exp), reads=[W[0].d], writes=[W[2].d])
        fw.op(ACT, lambda e: e.activation(out=W[3].h[:], in_=W[0].h[:], func=AF.Exp, scale=-1.0), reads=[W[0].d], writes=[W[3].d])
        for dst, shift in ((W[5], 0.0), (W[6], math.pi / 2)):
            ts(DVE, dst.h[:], W[1].h[:], shift, None, ALU.add, None, [W[1].d], [dst.d])
            ts(DVE, W[7].h[:], W[1].h[:], shift, None, ALU.add, None, [W[1].d], [W[7].d])
            for kthr in range(5):
                thr = (2 * kthr + 1) * math.pi
                ts(DVE, W[4].h[:], W[7].h[:], thr, -TWO_PI, ALU.is_gt, ALU.mult, [W[7].d], [W[4].d])
                tt(DVE, dst.h[:], dst.h[:], W[4].h[:], ALU.add, [dst.d, W[4].d], [dst.d])
        fw.op(ACT, lambda e: e.activation(out=W[5].h[:], in_=W[5].h[:], func=AF.Sin), reads=[W[5].d], writes=[W[5].d])
        fw.op(ACT, lambda e: e.activation(out=W[6].h[:], in_=W[6].h[:], func=AF.Sin), reads=[W[6].d], writes=[W[6].d])
        wp = lambda t_, ri: Wpos.h[:, t_, ri, :]
        wn = lambda s_, ri: Wneg.h[:, s_, ri, :]
        tt(DVE, wp(1, 0), W[2].h[:], W[6].h[:], ALU.mult, [W[2].d, W[6].d], [Wpos.d])
        tt(DVE, wp(1, 1), W[2].h[:], W[5].h[:], ALU.mult, [W[2].d, W[5].d], [Wpos.d])
        tt(DVE, wn(1, 0), W[3].h[:], W[6].h[:], ALU.mult, [W[3].d, W[6].d], [Wneg.d])
        fw.op(DVE, lambda e: e.scalar_tensor_tensor(out=wn(1, 1), in0=W[3].h[:], scalar=-1.0, in1=W[5].h[:], op0=ALU.mult, op1=ALU.mult), reads=[W[3].d, W[5].d], writes=[Wneg.d])
        for arr in (Wpos, Wneg):
            fw.op(DVE, lambda e, arr=arr: e.memset(arr.h[:, 0, 0, :], 1.0), writes=[arr.d])
            fw.op(DVE, lambda e, arr=arr: e.memset(arr.h[:, 0, 1, :], 0.0), writes=[arr.d])

        def cmul(outr, outi, ar, ai, br, bi, rd, wr):
            tt(DVE, W[0].h[:], ar, br, ALU.mult, rd, [W[0].d])
            tt(DVE, W[4].h[:], ai, bi, ALU.mult, rd, [W[4].d])
            tt(DVE, outr, W[0].h[:], W[4].h[:], ALU.subtract, [W[0].d, W[4].d], wr)
            tt(DVE, W[0].h[:], ar, bi, ALU.mult, rd + wr, [W[0].d])
            tt(DVE, W[4].h[:], ai, br, ALU.mult, rd + wr, [W[4].d])
            tt(DVE, outi, W[0].h[:], W[4].h[:], ALU.add, [W[0].d, W[4].d], wr)
        for t_ in range(2, 9):
            cmul(wp(t_, 0), wp(t_, 1), wp(t_ - 1, 0), wp(t_ - 1, 1), wp(1, 0), wp(1, 1), [Wpos.d], [Wpos.d])
        for s_ in range(2, 8):
            cmul(wn(s_, 0), wn(s_, 1), wn(s_ - 1, 0), wn(s_ - 1, 1), wn(1, 0), wn(1, 1), [Wneg.d], [Wneg.d])
        if not samp:
            wdv = lambda k_, ri: Wd.h[:, k_, ri, :]
            fw.op(DVE, lambda e: e.tensor_copy(out=Wd.h[:, 0, :, :], in_=Wpos.h[:, 8, :, :]), reads=[Wpos.d], writes=[Wd.d])
            for k_ in range(1, 7):
                tt(DVE, W[0].h[:], wdv(k_ - 1, 0), wdv(k_ - 1, 0), ALU.mult, [Wd.d], [W[0].d])
                tt(DVE, W[4].h[:], wdv(k_ - 1, 1), wdv(k_ - 1, 1), ALU.mult, [Wd.d], [W[4].d])
                tt(DVE, wdv(k_, 0), W[0].h[:], W[4].h[:], ALU.subtract, [W[0].d, W[4].d], [Wd.d])
                fw.op(DVE, lambda e, k_=k_: e.scalar_tensor_tensor(out=wdv(k_, 1), in0=wdv(k_ - 1, 0), scalar=2.0, in1=wdv(k_ - 1, 1), op0=ALU.mult, op1=ALU.mult), reads=[Wd.d], writes=[Wd.d])
        fr_ = frfi.h[:, 0, :]
        fi_ = frfi.h[:, 1, :]
        tt(DVE, W[2].h[:], are, are, ALU.mult, [apar.d], [W[2].d])
        tt(DVE, W[3].h[:], aim, aim, ALU.mult, [apar.d], [W[3].d])
        tt(DVE, W[2].h[:], W[2].h[:], W[3].h[:], ALU.add, [W[2].d, W[3].d], [W[2].d])
        fw.op(DVE, lambda e: e.reciprocal(out=W[2].h[:], in_=W[2].h[:]), reads=[W[2].d], writes=[W[2].d])
        ts(DVE, W[3].h[:], wp(1, 0), -1.0, None, ALU.add, None, [Wpos.d], [W[3].d])
        tt(DVE, W[5].h[:], W[3].h[:], are, ALU.mult, [W[3].d, apar.d], [W[5].d])
        tt(DVE, W[6].h[:], wp(1, 1), aim, ALU.mult, [Wpos.d, apar.d], [W[6].d])
        tt(DVE, W[5].h[:], W[5].h[:], W[6].h[:], ALU.add, [W[5].d, W[6].d], [W[5].d])
        tt(DVE, fr_, W[5].h[:], W[2].h[:], ALU.mult, [W[5].d, W[2].d], [frfi.d])
        tt(DVE, W[5].h[:], wp(1, 1), are, ALU.mult, [Wpos.d, apar.d], [W[5].d])
        tt(DVE, W[6].h[:], W[3].h[:], aim, ALU.mult, [W[3].d, apar.d], [W[6].d])
        tt(DVE, W[5].h[:], W[5].h[:], W[6].h[:], ALU.subtract, [W[5].d, W[6].d], [W[5].d])
        tt(DVE, fi_, W[5].h[:], W[2].h[:], ALU.mult, [W[5].d, W[2].d], [frfi.d])

        bHr, bHi = banks[0], banks[1]
        def grp_views(gh, g8):
            P0 = 64 * gh
            r0 = Rp[0].h[P0:P0 + 64, g8, :, :].rearrange("p s j -> p (s j)")
            r1 = Rp[1].h[P0:P0 + 64, g8, :, :].rearrange("p s j -> p (s j)")
            o0 = Op[0].h[P0:P0 + 64, g8, :, :].rearrange("p s j -> p (s j)")
            o1 = Op[1].h[P0:P0 + 64, g8, :, :].rearrange("p s j -> p (s j)")
            return P0, r0, r1, o0, o1

        def st_P1(bt):
            cached = use_scr and passno[0] > 0
            if cached:
                for k_ in range(2):
                    fw.dma(SP, lambda e, k_=k_: e.dma_start(out=Rp[k_].h[:].rearrange("p a b c -> p (a b c)"), in_=s5scr.ap()[o_, bt, k_]), s5ld[k_], reads=[s5d[o_][bt][k_]], writes=[Rp[k_].d])
                return
            if True:
                g0 = bt * 8
                for ri in range(2):
                    fw.dma(SP, lambda e, ri=ri: e.dma_start(out=braw[ri].h[:], in_=b_d[ri].ap()[o_, :, g0:g0 + 8, :]), bslots[ri], writes=[braw[ri].d])
                    fw.dma(SP, lambda e, ri=ri: e.dma_start(out=craw[ri].h[:], in_=c_d[ri].ap()[o_, :, g0:g0 + 8, :]), bslots[2 + ri], writes=[craw[ri].d])
                frb = vap(frfi, g0, [[1, 8], [0, 16]])
                fib = vap(frfi, 64 + g0, [[1, 8], [0, 16]])
                v8 = lambda t_: t_.h[:, :, :]
                tA3 = tA.h[:, 0:128].rearrange("p (a b) -> p a b", a=8)
                tB3 = tB.h[:, 0:128].rearrange("p (a b) -> p a b", a=8)
                tt(DVE, tA3, v8(braw[0]), frb, ALU.mult, [braw[0].d, frfi.d], [tA.d])
                tt(DVE, tB3, v8(braw[1]), fib, ALU.mult, [braw[1].d, frfi.d], [tB.d])
                tt(DVE, v8(bbar[0]), tA3, tB3, ALU.subtract, [tA.d, tB.d], [bbar[0].d])
                tt(DVE, tA3, v8(braw[1]), frb, ALU.mult, [braw[1].d, frfi.d], [tA.d])
                tt(DVE, tB3, v8(braw[0]), fib, ALU.mult, [braw[0].d, frfi.d], [tB.d])
                tt(DVE, v8(bbar[1]), tA3, tB3, ALU.add, [tA.d, tB.d], [bbar[1].d])
                tA4 = tA.h[:, :].rearrange("p (a b c) -> p a b c", a=8, b=8)
                tB4 = tB.h[:, :].rearrange("p (a b c) -> p a b c", a=8, b=8)
                wnr = vap(Wneg, g0, [[1, 8], [128, 8], [0, 16]])
                wni = vap(Wneg, 64 + g0, [[1, 8], [128, 8], [0, 16]])
                wpr = vap(Wpos, g0, [[1, 8], [128, 8], [0, 16]])
                wpi = vap(Wpos, 64 + g0, [[1, 8], [128, 8], [0, 16]])
                bb4 = [vap(bbar[ri], 0, [[16, 8], [0, 8], [1, 16]]) for ri in range(2)]
                cc4 = [vap(craw[ri], 0, [[16, 8], [0, 8], [1, 16]]) for ri in range(2)]
                e1, e2 = DVE, POOL
                tt(e1, tA4, bb4[0], wnr, ALU.mult, [bbar[0].d, Wneg.d], [tA.d])
                tt(e1, tB4, bb4[1], wni, ALU.mult, [bbar[1].d, Wneg.d], [tB.d])
                tt(e1, Rp[0].h[:], tA4, tB4, ALU.subtract, [tA.d, tB.d], [Rp[0].d])
                tt(e1, tA4, bb4[1], wnr, ALU.mult, [bbar[1].d, Wneg.d], [tA.d])
                tt(e1, tB4, bb4[0], wni, ALU.mult, [bbar[0].d, Wneg.d], [tB.d])
                tt(e1, Rp[1].h[:], tA4, tB4, ALU.add, [tA.d, tB.d], [Rp[1].d])


            if use_scr and len(tiles) > 1:
                for k_ in range(2):
                    fw.dma(SP, lambda e, k_=k_: e.dma_start(out=s5scr.ap()[o_, bt, k_], in_=Rp[k_].h[:].rearrange("p a b c -> p (a b c)")), s5st[k_], reads=[Rp[k_].d], writes=[s5d[o_][bt][k_]])

        def st_P2(bt):
            cached = use_scr and passno[0] > 0
            if cached:
                for k_ in range(2):
                    fw.dma(SP, lambda e, k_=k_: e.dma_start(out=Op[k_].h[:].rearrange("p a b c -> p (a b c)"), in_=s5scr.ap()[o_, bt, 2 + k_]), s5ld[2 + k_], reads=[s5d[o_][bt][2 + k_]], writes=[Op[k_].d])
                return
            if True:
                g0 = bt * 8
                tA4 = tA.h[:, :].rearrange("p (a b c) -> p a b c", a=8, b=8)
                tB4 = tB.h[:, :].rearrange("p (a b c) -> p a b c", a=8, b=8)
                wpr = vap(Wpos, g0, [[1, 8], [128, 8], [0, 16]])
                wpi = vap(Wpos, 64 + g0, [[1, 8], [128, 8], [0, 16]])
                cc4 = [vap(craw[ri], 0, [[16, 8], [0, 8], [1, 16]]) for ri in range(2)]
                e1 = DVE
                tt(e1, tA4, cc4[0], wpr, ALU.mult, [craw[0].d, Wpos.d], [tA.d])
                tt(e1, tB4, cc4[1], wpi, ALU.mult, [craw[1].d, Wpos.d], [tB.d])
                tt(e1, Op[0].h[:], tA4, tB4, ALU.subtract, [tA.d, tB.d], [Op[0].d])
                tt(e1, tA4, cc4[0], wpi, ALU.mult, [craw[0].d, Wpos.d], [tA.d])
                tt(e1, tB4, cc4[1], wpr, ALU.mult, [craw[1].d, Wpos.d], [tB.d])
                fw.op(e1, lambda e: e.scalar_tensor_tensor(out=Op[1].h[:].rearrange("p a b c -> p (a b c)"), in0=tA.h[:, :], scalar=-1.0, in1=tB.h[:, :], op0=ALU.mult, op1=ALU.subtract),
                      reads=[tA.d, tB.d], writes=[Op[1].d])


            if use_scr and len(tiles) > 1:
                for k_ in range(2):
                    fw.dma(SP, lambda e, k_=k_: e.dma_start(out=s5scr.ap()[o_, bt, 2 + k_], in_=Op[k_].h[:].rearrange("p a b c -> p (a b c)")), s5st[2 + k_], reads=[Op[k_].d], writes=[s5d[o_][bt][2 + k_]])

        def st_U(bt):
            Ubc = Ub2[bt % 2]
            for gh in range(2):
                ft = bt + 8 * gh
                for g8 in range(8):
                    gi = gh * 8 + g8
                    b3 = bank()

                    def um(e, b3=b3, g8=g8, ft=ft):
                        r = None
                        for k_, s_ in enumerate(s_list):
                            rhs = xn.h[:, ft, 0:Tn].rearrange("p (c s) -> p s c", s=SL)[:, s_ - (8 - SL), :]
                            r = e.matmul(b3.h[:, 0:n], lhsT=strips.h[:, g8, 112 - 16 * s_:240 - 16 * s_], rhs=rhs, start=(k_ == 0), stop=(k_ == len(s_list) - 1))
                        return r
                    fw.op(PE, um, reads=[strips.d, xn.d], writes=[b3.d])
                    fw.op(ACT, lambda e, b3=b3, gi=gi: e.activation(out=Ubc.h[:, gi, 0:n], in_=b3.h[:, 0:n], func=AF.Copy), reads=[b3.d], writes=[Ubc.d])

        def st_A(bt):
            Ubc = Ub2[bt % 2]
            for gh in range(2):
                for g8 in range(8):
                    gi = gh * 8 + g8
                    P0, r0, r1, o0, o1 = grp_views(gh, g8)
                    b = bank()

                    def mt(e, b=b, r0=r0, r1=r1, o0=o0, o1=o1):
                        e.matmul(b.h[:, 0:128], lhsT=r0, rhs=o0, start=True, stop=False)
                        return e.matmul(b.h[:, 0:128], lhsT=r1, rhs=o1, start=False, stop=True)
                    fw.op(PE, mt, reads=[Rp[0].d, Rp[1].d, Op[0].d, Op[1].d], writes=[b.d])
                    fw.op(DVE, lambda e, b=b, gi=gi: e.tensor_tensor(out=MTs.h[:, gi, :], in0=b.h[:, 0:128], in1=bmask.h[:], op=ALU.mult), reads=[b.d, bmask.d], writes=[MTs.d])
                    b2 = bank()
                    bb2 = b2.h[:, 0:64].bitcast(BF16)

                    def qt(e, bb2=bb2, r0=r0, r1=r1, P0=P0):
                        idn = ident[P0:P0 + 64, P0:P0 + 64]
                        e.transpose(out=bb2[:, 0:64], in_=r0, identity=idn)
                        return e.transpose(out=bb2[:, 64:128], in_=r1, identity=idn)
                    fw.op(PE, qt, reads=[Rp[0].d, Rp[1].d, identones.d], writes=[b2.d])
                    q_ = QTs[gi % 2]
                    fw.op(ACT, lambda e, bb2=bb2, q_=q_: e.activation(out=q_.h[:], in_=bb2, func=AF.Copy), reads=[b2.d], writes=[q_.d])
                    fw.op(PE, lambda e, Ubc=Ubc, q_=q_, gi=gi, g8=g8, P0=P0: e.matmul(bHr.h[P0:P0 + 64, g8 * 64:g8 * 64 + n], lhsT=q_.h[:, 0:64], rhs=Ubc.h[:, gi, 0:n], start=True, stop=True),
                          reads=[q_.d, Ubc.d], writes=[bHr.d])
                    fw.op(PE, lambda e, Ubc=Ubc, q_=q_, gi=gi, g8=g8, P0=P0: e.matmul(bHi.h[P0:P0 + 64, g8 * 64:g8 * 64 + n], lhsT=q_.h[:, 64:128], rhs=Ubc.h[:, gi, 0:n], start=True, stop=True),
                          reads=[q_.d, Ubc.d], writes=[bHi.d])

        def st_S(bt):
            g0 = bt * 8
            Hv = [bHr.h[:, :].rearrange("p (g c) -> p g c", g=8)[:, :, 0:n], bHi.h[:, :].rearrange("p (g c) -> p g c", g=8)[:, :, 0:n]]
            if not samp:
                for ri in range(2):
                    fw.op(ACT, lambda e, ri=ri: e.activation(out=Aar.h[:, ri, :, 1:n + 1], in_=Hv[ri], func=AF.Copy), reads=[(bHr, bHi)[ri].d], writes=[Aar.d])
                if ti == 0:
                    fw.op(DVE, lambda e: e.memset(Aar.h[:, :, :, 0], 0.0), writes=[Aar.d])
                else:
                    fw.op(DVE, lambda e: e.tensor_copy(out=Aar.h[:, :, :, 0], in_=Pst.h[:, o_, :, g0:g0 + 8]), reads=[Pst.d], writes=[Aar.d])
                T1f = tA.h[:, :].rearrange("p (r g c) -> p r g c", r=2, g=8)
                T2f = tB.h[:, :].rearrange("p (r g c) -> p r g c", r=2, g=8)
                L = n + 1
                wr0 = vap(Wd, g0, [[0, 2], [1, 8], [0, n]])
                wi0 = vap(Wd, 64 + g0, [[0, 2], [1, 8], [0, n]])
                tt(DVE, T1f[:, :, :, 0:n], Aar.h[:, :, :, 1:L], wr0, ALU.mult, [Aar.d, Wd.d], [tA.d])
                tt(DVE, T2f[:, :, :, 0:n], Aar.h[:, :, :, 1:L], wi0, ALU.mult, [Aar.d, Wd.d], [tB.d])
                tt(DVE, Aar.h[:, 0, :, 1:L], T1f[:, 0, :, 0:n], T2f[:, 1, :, 0:n], ALU.subtract, [tA.d, tB.d], [Aar.d])
                tt(DVE, Aar.h[:, 1, :, 1:L], T1f[:, 1, :, 0:n], T2f[:, 0, :, 0:n], ALU.add, [tA.d, tB.d], [Aar.d])
                for k_ in range(7):
                    d_ = 1 << k_
                    Lc = L - d_
                    wr_ = vap(Wd, k_ * 128 + g0, [[0, 2], [1, 8], [0, Lc]])
                    wi_ = vap(Wd, k_ * 128 + 64 + g0, [[0, 2], [1, 8], [0, Lc]])
                    tt(DVE, T1f[:, :, :, 0:Lc], Aar.h[:, :, :, 0:Lc], wr_, ALU.mult, [Aar.d, Wd.d], [tA.d])
                    tt(DVE, T2f[:, :, :, 0:Lc], Aar.h[:, :, :, 0:Lc], wi_, ALU.mult, [Aar.d, Wd.d], [tB.d])
                    tt(DVE, Aar.h[:, 0, :, d_:L], Aar.h[:, 0, :, d_:L], T1f[:, 0, :, 0:Lc], ALU.add, [Aar.d, tA.d], [Aar.d])
                    tt(DVE, Aar.h[:, 0, :, d_:L], Aar.h[:, 0, :, d_:L], T2f[:, 1, :, 0:Lc], ALU.subtract, [Aar.d, tB.d], [Aar.d])
                    tt(DVE, Aar.h[:, 1, :, d_:L], Aar.h[:, 1, :, d_:L], T1f[:, 1, :, 0:Lc], ALU.add, [Aar.d, tA.d], [Aar.d])
                    tt(DVE, Aar.h[:, 1, :, d_:L], Aar.h[:, 1, :, d_:L], T2f[:, 0, :, 0:Lc], ALU.add, [Aar.d, tB.d], [Aar.d])
                fw.op(DVE, lambda e: e.tensor_copy(out=Pst.h[:, o_, :, g0:g0 + 8], in_=Aar.h[:, :, :, n]), reads=[Aar.d], writes=[Pst.d])
                fw.op(ACT, lambda e: e.activation(out=Abf.h[:, :, :, 0:n], in_=Aar.h[:, :, :, 0:n], func=AF.Copy), reads=[Aar.d], writes=[Abf.d])
            else:
                for ri in range(2):
                    fw.dma(SP, lambda e, ri=ri: e.dma_start(out=h0s5.h[:, ri, :, :], in_=sssm[ri].ap()[o_, :, g0:g0 + 8, :]), h0slots[ri], writes=[h0s5.d])
                wm3r = vap(Wneg, 3 * 128 + g0, [[1, 8], [0, 16]])
                wm3i = vap(Wneg, 3 * 128 + 64 + g0, [[1, 8], [0, 16]])
                w7r = vap(Wpos, 7 * 128 + g0, [[1, 8], [0, 16]])
                w7i = vap(Wpos, 7 * 128 + 64 + g0, [[1, 8], [0, 16]])
                tA3 = tA.h[:, 0:128].rearrange("p (a b) -> p a b", a=8)
                tB3 = tB.h[:, 0:128].rearrange("p (a b) -> p a b", a=8)

                def cm3(outr, outi, xr, xi, wr_, wi_, rd, wrd):
                    tt(DVE, tA3, xr, wr_, ALU.mult, rd, [tA.d])
                    tt(DVE, tB3, xi, wi_, ALU.mult, rd, [tB.d])
                    tt(DVE, outr, tA3, tB3, ALU.subtract, [tA.d, tB.d], wrd)
                    tt(DVE, tA3, xi, wr_, ALU.mult, rd, [tA.d])
                    tt(DVE, tB3, xr, wi_, ALU.mult, rd, [tB.d])
                    tt(DVE, outi, tA3, tB3, ALU.add, [tA.d, tB.d], wrd)
                cm3(Pp.h[:, 0, :, :], Pp.h[:, 1, :, :], h0s5.h[:, 0, :, :], h0s5.h[:, 1, :, :], wm3r, wm3i, [h0s5.d, Wneg.d], [Pp.d])
                fw.op(ACT, lambda e: e.activation(out=Abf.h[:, :, :, 0:16], in_=Pp.h[:, :, :, :], func=AF.Copy), reads=[Pp.d], writes=[Abf.d])
                for ri in range(2):
                    tt(DVE, Aar.h[:, ri, :, 0:16], Hv[ri], Pp.h[:, ri, :, :], ALU.add, [(bHr, bHi)[ri].d, Pp.d], [Aar.d])
                cm3(Pf.h[:, 0, :, :], Pf.h[:, 1, :, :], Aar.h[:, 0, :, 0:16], Aar.h[:, 1, :, 0:16], w7r, w7i, [Aar.d, Wpos.d], [Pf.d])
                for ri in range(2):
                    out_toks.append(fw.dma(SP, lambda e, ri=ri: e.dma_start(out=o_ssms[ri].ap()[o_, :, g0:g0 + 8, :], in_=Pf.h[:, ri, :, :]), oslot(("ssms", ri)), reads=[Pf.d]))

        def st_Y(bt):
            Ubc = Ub2[bt % 2]
            for gh in range(2):
                for g8 in range(8):
                    gi = gh * 8 + g8
                    P0, r0, r1, o0, o1 = grp_views(gh, g8)
                    b = bank()

                    def ym(e, Ubc=Ubc, b=b, gi=gi, g8=g8, P0=P0, o0=o0, o1=o1):
                        e.matmul(b.h[:, 0:n], lhsT=MTs.h[:, gi, :], rhs=Ubc.h[:, gi, 0:n], start=True, stop=False)
                        e.matmul(b.h[:, 0:n], lhsT=o0, rhs=Abf.h[P0:P0 + 64, 0, g8, 0:n], start=False, stop=False)
                        return e.matmul(b.h[:, 0:n], lhsT=o1, rhs=Abf.h[P0:P0 + 64, 1, g8, 0:n], start=False, stop=True)
                    fw.op(PE, ym, reads=[MTs.d, Ubc.d, Op[0].d, Op[1].d, Abf.d], writes=[b.d])
                    fw.op(ACT, lambda e, b=b, gi=gi: e.activation(out=Yb.h[:, gi, 0:n], in_=b.h[:, 0:n], func=AF.Copy), reads=[b.d], writes=[Yb.d])

        def st_B(bt):
            for gh in range(2):
                ft = bt + 8 * gh
                bY = bank()

                def bc(e, bY=bY, gh=gh):
                    r = None
                    for t_ in s_list:
                        for g8 in range(8):
                            r = e.matmul(bY.h[:, t_ * 64:t_ * 64 + n], lhsT=strips.h[:, t_, 112 - 16 * g8:240 - 16 * g8], rhs=Yb.h[:, gh * 8 + g8, 0:n], start=(g8 == 0), stop=(g8 == 7))
                    return r
                fw.op(PE, bc, reads=[strips.d, Yb.d], writes=[bY.d])
                src = bY.h[:, :].rearrange("p (t c) -> p t c", t=8)[:, 8 - SL:8, 0:n]
                dst = y32b.h[:, 0:Tn].rearrange("p (c t) -> p t c", t=SL)
                fw.op(ACT, lambda e, src=src, dst=dst: e.activation(out=dst, in_=src, func=AF.Copy), reads=[bY.d], writes=[y32b.d])
                fw.op(DVE, lambda e, ft=ft: e.scalar_tensor_tensor(out=y32b.h[:, 0:Tn], in0=xn.h[:, ft, 0:Tn], scalar=ssmd.h[:, o_, ft:ft + 1], in1=y32b.h[:, 0:Tn], op0=ALU.mult, op1=ALU.add),
                      reads=[xn.d, ssmd.d, y32b.d], writes=[y32b.d])
                gelu_mul(y32b, None, cat.h[:, ft, 0:Tn], cat, Tn, g1b, g2b)

        st_P1(0)
        st_P2(0)
        st_U(0)
        for b_ in range(8):
            st_A(b_)
            if b_ < 7:
                st_U(b_ + 1)
            st_S(b_)
            if b_ < 7:
                st_P1(b_ + 1)
            st_Y(b_)
            if b_ < 7:
                st_P2(b_ + 1)
            st_B(b_)
        if (not samp) and ti == 3:
            w1r = Wneg.h[:, 1, 0, :]
            w1i = Wneg.h[:, 1, 1, :]
            pr = Pst.h[:, o_, 0, :]
            pi_ = Pst.h[:, o_, 1, :]
            tt(DVE, W[0].h[:], pr, w1r, ALU.mult, [Pst.d, Wneg.d], [W[0].d])
            tt(DVE, W[4].h[:], pi_, w1i, ALU.mult, [Pst.d, Wneg.d], [W[4].d])
            tt(DVE, W[2].h[:], W[0].h[:], W[4].h[:], ALU.subtract, [W[0].d, W[4].d], [W[2].d])
            tt(DVE, W[0].h[:], pi_, w1r, ALU.mult, [Pst.d, Wneg.d], [W[0].d])
            tt(DVE, W[4].h[:], pr, w1i, ALU.mult, [Pst.d, Wneg.d], [W[4].d])
            tt(DVE, W[3].h[:], W[0].h[:], W[4].h[:], ALU.add, [W[0].d, W[4].d], [W[3].d])
            out_toks.append(fw.dma(SP, lambda e: e.dma_start(out=o_ssmp[0].ap()[o_], in_=W[2].h[:]), oslot(("ssmp", 0, o_)), reads=[W[2].d]))
            out_toks.append(fw.dma(SP, lambda e: e.dma_start(out=o_ssmp[1].ap()[o_], in_=W[3].h[:]), oslot(("ssmp", 1, o_)), reads=[W[3].d]))

    dctr = [0]

    def run_tile(kind, ti):
        samp = kind == "s"
        Tn = 64 if samp else 512
        if samp:
            fw.dma(SP, lambda e: e.dma_start(out=x.h[:, :, 0:64], in_=xsT.ap().rearrange("(kc p) t -> p kc t", p=128)), xslot, writes=[x.d])
            fw.dma(SP, lambda e: e.dma_start(out=cosT.h[:, 0:64], in_=cst["cosS"].ap()), tslot, writes=[cosT.d])
            fw.dma(SP, lambda e: e.dma_start(out=sinT.h[:, 0:64], in_=cst["sinS"].ap()), tslot2, writes=[sinT.d])
        else:
            fw.dma(SP, lambda e, ti=ti: e.dma_start(out=x.h[:], in_=xpT.ap()[:, ti * 512:(ti + 1) * 512].rearrange("(kc p) t -> p kc t", p=128)), xslot, writes=[x.d])
            fw.dma(SP, lambda e, ti=ti: e.dma_start(out=cosT.h[:], in_=cst["cosP"].ap()[:, ti * 512:(ti + 1) * 512]), tslot, writes=[cosT.d])
            fw.dma(SP, lambda e, ti=ti: e.dma_start(out=sinT.h[:], in_=cst["sinP"].ap()[:, ti * 512:(ti + 1) * 512]), tslot2, writes=[sinT.d])
            if ti == 0:
                fw.op(DVE, lambda e: e.memset(Sst.h[:], 0.0), writes=[Sst.d])
                fw.op(POOL, lambda e: e.memset(Sbf.h[:], 0.0), writes=[Sbf.d])
        for layer in range(nlayers):
            if layer % 2 == 0:
                if not samp:
                    fw.op(ACT, lambda e, layer=layer: e.activation(out=Sbf.h[:], in_=Sst.h[:, layer // 2, :, :, :], func=AF.Copy), reads=[Sst.d], writes=[Sbf.d])
                even_layer(layer // 2, kind, ti, Tn)
            else:
                odd_layer(layer // 2, kind, ti, Tn)
            ffn(layer, Tn)
            if dbg and dctr[0] < 8:
                di = dctr[0]
                out_toks.append(fw.dma(SP, lambda e, di=di: e.dma_start(out=o_dbg.ap()[di], in_=x.h[:]), oslot(("dbg", di)), reads=[x.d]))
                dctr[0] += 1
        if (not samp) and nlayers > 0:
            pass
        fw.op(ACT, lambda e: e.activation(out=cat.h[:, :, 0:Tn], in_=x.h[:, :, 0:Tn], func=AF.Square), reads=[x.d], writes=[cat.d])
        b = bank()

        def mmf(e, b=b, Tn=Tn):
            r = None
            for kc in range(KC):
                r = e.matmul(b.h[:, 0:Tn], lhsT=ones, rhs=cat.h[:, kc, 0:Tn], start=(kc == 0), stop=(kc == KC - 1))
            return r
        fw.op(PE, mmf, reads=[cat.d, identones.d], writes=[b.d])
        fw.op(ACT, lambda e, b=b, Tn=Tn: e.activation(out=rstd.h[:, 0:Tn], in_=b.h[:, 0:Tn], func=AF.Sqrt, scale=1.0 / D, bias=EPS), reads=[b.d], writes=[rstd.d])
        fw.op(DVE, lambda e, Tn=Tn: e.reciprocal(out=rstd.h[:, 0:Tn], in_=rstd.h[:, 0:Tn]), reads=[rstd.d], writes=[rstd.d])
        for kc in range(KC):
            fw.op(DVE, lambda e, kc=kc, Tn=Tn: e.scalar_tensor_tensor(out=x.h[:, kc, 0:Tn], in0=x.h[:, kc, 0:Tn], scalar=gains.h[:, 8, kc:kc + 1], in1=rstd.h[:, 0:Tn], op0=ALU.mult, op1=ALU.mult),
                  reads=[x.d, gains.d, rstd.d], writes=[x.d])
        if samp:
            out_toks.append(fw.dma(SP, lambda e: e.dma_start(out=ysT.ap().rearrange("(kc p) t -> p kc t", p=128), in_=x.h[:, :, 0:64]), oslot("ys"), reads=[x.d]))
        else:
            out_toks.append(fw.dma(SP, lambda e, ti=ti: e.dma_start(out=ypT.ap()[:, ti * 512:(ti + 1) * 512].rearrange("(kc p) t -> p kc t", p=128), in_=x.h[:]), oslot("yp"), reads=[x.d]))
    for (kind_, ti_) in tiles:
        lctr[0] = 0
        try:
            run_tile(kind_, ti_)
        except StopIteration:
            break
        passno[0] += 1
    last = {}
    for t in out_toks:
        last[id(t[0])] = t
    fw.wait_tokens(SP, list(last.values()))
    fw.emit()
    return nc


_CACHE = {}


def make_in_maps(inp, ncores=8):
    W = prep_weights(inp)
    C = host_consts()
    maps = []
    f32 = np.float32
    for c in range(ncores):
        b = c // 2
        m = dict(W)
        for k, v in C.items():
            m["c_" + k] = v
        m["xpT"] = np.ascontiguousarray(np.asarray(inp["x_prompt"][b]).T)
        m["xsT"] = np.ascontiguousarray(np.asarray(inp["x_sample"][16 * c:16 * c + 16]).reshape(64, D).T)
        m["sret"] = np.ascontiguousarray(np.asarray(inp["state_ret"][:, 16 * c:16 * c + 16]))
        m["slru"] = np.ascontiguousarray(np.asarray(inp["state_lru"][:, 16 * c:16 * c + 16]).transpose(0, 2, 1))
        m["sconv"] = np.ascontiguousarray(np.asarray(inp["state_conv"][:, 16 * c:16 * c + 16]).transpose(0, 3, 1, 2))
        for nm, key in (("sssm_re", "state_ssm_re"), ("sssm_im", "state_ssm_im")):
            a = np.asarray(inp[key][:, 16 * c:16 * c + 16]).reshape(2, 16, 2, 64, 64).transpose(0, 2, 4, 3, 1).reshape(2, 128, 64, 16)
            m[nm] = np.ascontiguousarray(a)
        maps.append(m)
    return maps


def assemble(results):
    f32 = np.float32
    y_p = np.zeros((4, 2048, D), f32)
    y_s = np.zeros((128, 4, D), f32)
    ret_p = np.zeros((2, 4, 4, 256, 256), f32)
    ret_s = np.zeros((2, 128, 4, 256, 256), f32)
    lru_p = np.zeros((2, 4, 1024), f32)
    lru_s = np.zeros((2, 128, 1024), f32)
    conv_p = np.zeros((2, 4, 3, 1024), f32)
    conv_s = np.zeros((2, 128, 3, 1024), f32)
    ssm_p = [np.zeros((2, 4, 128, 64), f32), np.zeros((2, 4, 128, 64), f32)]
    ssm_s = [np.zeros((2, 128, 128, 64), f32), np.zeros((2, 128, 128, 64), f32)]
    for c, r in enumerate(results):
        sl = slice(16 * c, 16 * c + 16)
        y_s[sl] = r["ysT"].T.reshape(16, 4, D)
        ret_s[:, sl] = r["o_rets"]
        lru_s[:, sl] = r["o_lrus"].transpose(0, 2, 1)
        conv_s[:, sl] = r["o_convs"].transpose(0, 2, 3, 1)
        for ri, nm in enumerate(("o_ssms_re", "o_ssms_im")):
            a = r[nm].reshape(2, 2, 64, 64, 16).transpose(0, 4, 1, 3, 2).reshape(2, 16, 128, 64)
            ssm_s[ri][:, sl] = a
        if c % 2 == 0:
            b = c // 2
            y_p[b] = r["ypT"].T
            ret_p[:, b] = r["o_retp"]
            lru_p[:, b] = r["o_lrup"].transpose(0, 2, 1).reshape(2, 1024)
            conv_p[:, b] = r["o_convp"].transpose(0, 2, 1)
            for ri, nm in enumerate(("o_ssmp_re", "o_ssmp_im")):
                a = r[nm].reshape(2, 2, 64, 64).transpose(0, 1, 3, 2).reshape(2, 128, 64)
                ssm_p[ri][:, b] = a
    return (y_p, y_s, ret_p, ret_s, lru_p, lru_s, conv_p, conv_s, ssm_p[0], ssm_s[0], ssm_p[1], ssm_s[1])


def kernel(**inputs):
    nc = build({})
    maps = make_in_maps(inputs)
    res = run_bass_kernel_spmd(nc, maps, core_ids=list(range(8)))
    return assemble(res.results)
```

```python
import math
import numpy as np
import concourse.bass as bass
import concourse.mybir as mybir
from concourse.bass_utils import run_bass_kernel_spmd

F32 = mybir.dt.float32
BF16 = mybir.dt.bfloat16
AF = mybir.ActivationFunctionType
ALU = mybir.AluOpType
SEM_MAX = 12000

D = 2048
KC = 16
FH = 5632
EPS = 1e-6
GAM = [1.0 - 2.0 ** (-5 - h) for h in range(4)]
GELU_C = 2.0 * math.sqrt(2.0 / math.pi)


class Dep:
    __slots__ = ("w", "r")

    def __init__(self):
        self.w = None
        self.r = []


class Eng:
    def __init__(self, fw, name, is_pe=False):
        self.fw = fw
        self.name = name
        self.is_pe = is_pe
        self.ops = []
        self.count = 0
        self.sems = []
        self.seen = {}

    def token(self):
        i, v = divmod(self.count - 1, SEM_MAX)
        while len(self.sems) <= i:
            self.sems.append(self.fw.nc.alloc_semaphore(f"s_{self.name}_{len(self.sems)}"))
        return (self.sems[i], v + 1, self)


class Slot:
    def __init__(self, fw, name):
        self.sem = fw.nc.alloc_semaphore(name)
        self.val = 0


class FW:
    def __init__(self, nc):
        self.nc = nc
        self.pe = Eng(self, "pe", True)
        self.act = Eng(self, "act")
        self.dve = Eng(self, "dve")
        self.pool = Eng(self, "pool")
        self.sp = Eng(self, "sp")
        self.nslot = 0

    def slot(self):
        self.nslot += 1
        return Slot(self, f"dq{self.nslot}")

    @staticmethod
    def _flat(lst):
        out = []
        for d in lst:
            if isinstance(d, (list, tuple)):
                out.extend(FW._flat(d))
            else:
                out.append(d)
        return out

    def _waits(self, eng, reads, writes):
        deps = {}
        for d in reads:
            if d.w is not None:
                deps[id(d.w)] = d.w
        for d in writes:
            if d.w is not None:
                deps[id(d.w)] = d.w
            for t in d.r:
                deps[id(t)] = t
        waits = []
        for t in deps.values():
            sem, val, src = t
            if src is eng and eng.is_pe:
                continue
            k = id(sem)
            if eng.seen.get(k, 0) >= val:
                continue
            eng.seen[k] = val
            waits.append((sem, val))
        return waits

    def op(self, eng, fn, reads=(), writes=()):
        reads = self._flat(reads)
        writes = self._flat(writes)
        waits = self._waits(eng, reads, writes)
        eng.count += 1
        tok = eng.token()
        for d in writes:
            d.w = tok
            d.r = []
        for d in reads:
            if d.w is not tok:
                d.r.append(tok)
        eng.ops.append((waits, fn, (tok[0], 1)))
        return tok

    def dma(self, eng, fn, slot, reads=(), writes=(), n=1):
        reads = self._flat(reads)
        writes = self._flat(writes)
        waits = self._waits(eng, reads, writes)
        slot.val += 16 * n
        tok = (slot.sem, slot.val, slot)
        for d in writes:
            d.w = tok
            d.r = []
        for d in reads:
            d.r.append(tok)
        eng.ops.append((waits, fn, (slot.sem, 16)))
        return tok

    def wait_tokens(self, eng, toks):
        waits = []
        for (sem, val, src) in toks:
            if eng.seen.get(id(sem), 0) >= val:
                continue
            eng.seen[id(sem)] = val
            waits.append((sem, val))
        eng.ops.append((waits, None, None))

    def emit(self):
        nc = self.nc

        def run(eng, e):
            for waits, fn, inc in eng.ops:
                for sem, val in waits:
                    e.wait_ge(sem, val)
                if fn is None:
                    continue
                r = fn(e)
                if isinstance(r, (list, tuple)):
                    for x in r:
                        x.then_inc(inc[0], inc[1])
                else:
                    r.then_inc(inc[0], inc[1])

        with nc.Block() as block:
            @block.tensor
            def _(e):
                run(self.pe, e)

            @block.scalar
            def _(e):
                run(self.act, e)

            @block.vector
            def _(e):
                run(self.dve, e)

            @block.gpsimd
            def _(e):
                run(self.pool, e)

            @block.sync
            def _(e):
                run(self.sp, e)


class T:
    def __init__(self, h):
        self.h = h
        self.d = Dep()

    def __getitem__(self, k):
        return self.h[k]


def sap(t, off, dims, parts=128, p0=0):
    row = 1
    for s in t.shape[1:]:
        row *= s
    return bass.AP(t, p0 * row + off, [[row, parts]] + [list(d) for d in dims])


def host_consts():
    f32 = np.float32
    c = {}
    inv = (1.0 / np.power(f32(10000.0), np.linspace(0.0, 1.0, 128, dtype=f32))).astype(f32)
    posP = np.arange(2048, dtype=f32)
    angP = (posP[None, :] * inv[:, None]).astype(f32).astype(np.float64)
    posS = (16384 + (np.arange(64) % 4)).astype(f32)
    angS = (posS[None, :] * inv[:, None]).astype(f32).astype(np.float64)
    c["cosP"] = np.cos(angP).astype(f32)
    c["sinP"] = np.sin(angP).astype(f32)
    c["cosS"] = np.cos(angS).astype(f32)
    c["sinS"] = np.sin(angS).astype(f32)
    g = np.array(GAM, dtype=np.float64)
    nP = np.arange(512) % 128
    nS = np.arange(64) % 4
    qdP = np.power(g[:, None], nP[None, :] + 1.0)
    qdS = np.power(g[:, None], nS[None, :] + 1.0)
    c["qdecP"] = np.broadcast_to(qdP[None, :, 0:128], (128, 4, 128)).astype(f32).copy()
    c["qdecS"] = np.broadcast_to(qdS[None], (128, 4, 64)).astype(f32).copy()
    m = np.arange(128)
    kv = np.zeros((128, 16), f32)
    kv[:, 0:4] = (np.power(g[None, :], -(m[:, None] + 1.0)) / 16.0)
    kv[:, 4:8] = (np.power(g[None, :], 127.0 - m[:, None]) / 16.0)
    kv[:, 8:12] = (np.power(g[None, :], -((m[:, None] % 4) + 1.0)) / 16.0)
    kv[:, 12:16] = (np.power(g[None, :], 3.0 - (m[:, None] % 4)) / 16.0)
    c["kvec"] = kv
    mk = np.zeros((128, 192), f32)
    mk[:, 0:128] = (m[None, :] >= m[:, None]).astype(f32)
    ms = np.arange(64)
    mk[0:64, 128:192] = ((ms[None, :] >= ms[:, None]) & (ms[None, :] // 4 == ms[:, None] // 4)).astype(f32)
    c["masks"] = mk
    oh = np.zeros((128, 16), f32)
    oh[0:64] = (ms[:, None] // 4 == np.arange(16)[None, :]).astype(f32)
    c["onehot"] = oh
    io = np.zeros((128, 256), f32)
    io[:, 0:128] = np.eye(128, dtype=f32)
    io[:, 128:256] = 1.0
    c["identones"] = io
    st = np.zeros((128, 8, 240), f32)
    for a in range(8):
        for j in range(16):
            st[a * 16 + j, a, 112 + j] = 1.0
    c["strips"] = st
    bm = np.zeros((128, 128), f32)
    for s_ in range(8):
        for t_ in range(s_, 8):
            bm[s_ * 16:(s_ + 1) * 16, t_ * 16:(t_ + 1) * 16] = 1.0
    c["bmask"] = bm
    return c


def prep_weights(inp):
    f32 = np.float32
    w = {}
    gains = [inp["norm_mix_even"][0], inp["norm_mix_even"][1], inp["norm_mix_odd"][0], inp["norm_mix_odd"][1],
             inp["norm_ffn"][0], inp["norm_ffn"][1], inp["norm_ffn"][2], inp["norm_ffn"][3], inp["norm_final"]]
    w["gains"] = np.ascontiguousarray(np.stack([np.asarray(g).reshape(16, 128).T for g in gains], axis=1)).astype(f32)
    lv = np.zeros((128, 2, 8, 8), f32)
    for e in range(2):
        for i in range(4):
            lv[:, e, :, i] = np.asarray(inp["lru_conv_w"][e, i]).reshape(8, 128).T
        lv[:, e, :, 4] = np.asarray(inp["lru_conv_b"][e]).reshape(8, 128).T
        lv[:, e, :, 5] = np.asarray(inp["lru_ba"][e]).reshape(8, 128).T
        lv[:, e, :, 6] = np.asarray(inp["lru_bx"][e]).reshape(8, 128).T
        lv[:, e, :, 7] = np.asarray(inp["lru_lambda"][e]).reshape(8, 128).T
    w["lruvec"] = lv
    w["ssmd"] = np.ascontiguousarray(np.stack([np.asarray(inp["ssm_d"][o]).reshape(16, 128).T for o in range(2)], axis=1)).astype(f32)
    w["bglu"] = np.ascontiguousarray(np.stack([np.asarray(inp["b_glu"][o]).reshape(32, 128).T for o in range(2)], axis=1)).astype(f32)
    for nm in ("ssm_a_re", "ssm_a_im"):
        a = np.asarray(inp[nm]).reshape(2, 2, 64, 64).transpose(0, 1, 3, 2).reshape(2, 128, 64)
        w[nm] = np.ascontiguousarray(a)
    w["ssm_log_dt"] = np.ascontiguousarray(np.asarray(inp["ssm_log_dt"]).reshape(2, 128))
    for nm in ("ssm_b_re", "ssm_b_im"):
        a = np.asarray(inp[nm]).reshape(2, 2, 64, 64, 16).transpose(0, 1, 3, 2, 4).reshape(2, 128, 64, 16)
        w[nm] = np.ascontiguousarray(a)
    for nm in ("ssm_c_re", "ssm_c_im"):
        a = np.asarray(inp[nm]).reshape(2, 2, 64, 16, 64).transpose(0, 1, 4, 2, 3).reshape(2, 128, 64, 16)
        w[nm] = np.ascontiguousarray(a)
    for nm in ("w_in_even", "w_out_even", "w_glu", "w_ffn_gu", "w_ffn_down", "lru_wa", "lru_wx"):
        w[nm] = np.ascontiguousarray(np.asarray(inp[nm], dtype=f32))
    return w


def build(cfg):
    tiles = cfg.get("tiles", [("p", 0), ("p", 1), ("p", 2), ("p", 3), ("s", 0)])
    nlayers = cfg.get("nlayers", 4)
    dbg = cfg.get("dbg", False)
    do_odd = cfg.get("do_odd", True)

    nc = bass.Bass("TRN2", target_bir_lowering=False)
    fw = FW(nc)
    PE, ACT, DVE, POOL, SP = fw.pe, fw.act, fw.dve, fw.pool, fw.sp

    def din(name, shape):
        return nc.dram_tensor(name, list(shape), F32, kind="ExternalInput")

    def dout(name, shape):
        return nc.dram_tensor(name, list(shape), F32, kind="ExternalOutput")

    xpT = din("xpT", [D, 2048])
    xsT = din("xsT", [D, 64])
    sret = din("sret", [2, 16, 4, 256, 256])
    slru = din("slru", [2, 1024, 16])
    sconv = din("sconv", [2, 1024, 16, 3])
    sssm = [din("sssm_re", [2, 128, 64, 16]), din("sssm_im", [2, 128, 64, 16])]
    w_in = din("w_in_even", [2, D, 6144])
    w_out = din("w_out_even", [2, D, D])
    w_glu = din("w_glu", [2, D, 2 * D])
    w_gu = din("w_ffn_gu", [4, D, 2 * FH])
    w_dn = din("w_ffn_down", [4, FH, D])
    lru_wa = din("lru_wa", [2, 8, 128, 128])
    lru_wx = din("lru_wx", [2, 8, 128, 128])
    gains_d = din("gains", [128, 9, 16])
    lruvec_d = din("lruvec", [128, 2, 8, 8])
    ssmd_d = din("ssmd", [128, 2, 16])
    bglu_d = din("bglu", [128, 2, 32])
    a_d = [din("ssm_a_re", [2, 128, 64]), din("ssm_a_im", [2, 128, 64])]
    ldt_d = din("ssm_log_dt", [2, 128])
    b_d = [din("ssm_b_re", [2, 128, 64, 16]), din("ssm_b_im", [2, 128, 64, 16])]
    c_d = [din("ssm_c_re", [2, 128, 64, 16]), din("ssm_c_im", [2, 128, 64, 16])]
    cst = {k: din("c_" + k, v.shape) for k, v in host_consts().items()}

    ypT = dout("ypT", [D, 2048])
    ysT = dout("ysT", [D, 64])
    o_retp = dout("o_retp", [2, 4, 256, 256])
    o_rets = dout("o_rets", [2, 16, 4, 256, 256])
    o_lrup = dout("o_lrup", [2, 128, 8])
    o_lrus = dout("o_lrus", [2, 1024, 16])
    o_convp = dout("o_convp", [2, 1024, 3])
    o_convs = dout("o_convs", [2, 1024, 16, 3])
    o_ssmp = [dout("o_ssmp_re", [2, 128, 64]), dout("o_ssmp_im", [2, 128, 64])]
    o_ssms = [dout("o_ssms_re", [2, 128, 64, 16]), dout("o_ssms_im", [2, 128, 64, 16])]
    if dbg:
        o_dbg = dout("o_dbg", [8, 128, 16, 512])

    def sb(name, shape, dt=F32):
        return T(nc.alloc_sbuf_tensor("sb_" + name, list(shape), dt))

    x = sb("x", [128, KC, 512])
    xn = sb("xn", [128, KC, 512], BF16)
    cat = sb("cat", [128, KC, 512], BF16)
    NW = 2
    wb = [sb(f"wb{i}", [128, KC, 512], BF16) for i in range(NW)]
    wslot = [fw.slot() for _ in range(NW)]
    wctr = [0]
    gains = sb("gains", [128, 9, 16])
    lruvec = sb("lruvec", [128, 2, 8, 8])
    ssmd = sb("ssmd", [128, 2, 16])
    bglu = sb("bglu", [128, 2, 32])
    wa_sb = sb("wa_sb", [128, 2, 8, 128], BF16)
    wx_sb = sb("wx_sb", [128, 2, 8, 128], BF16)
    kvec = sb("kvec", [128, 16])
    masks = sb("masks", [128, 192])
    onehot = sb("onehot", [128, 16])
    identones = sb("identones", [128, 256], BF16)
    qdecP = sb("qdecP", [128, 4, 128])
    qdecS = sb("qdecS", [128, 4, 64])
    strips = sb("strips", [128, 8, 240], BF16)
    bmask = sb("bmask", [128, 128])
    Pst = sb("Pst", [128, 2, 2, 64])
    apar = sb("apar", [128, 2, 2, 64])
    dtb = sb("dtb", [128, 2, 64])
    lsp = sb("lsp", [128, 2, 8, 2])
    Sst = sb("Sst", [128, 2, 4, 2, 256])
    Sbf = sb("Sbf", [128, 4, 2, 256], BF16)
    hst = sb("hst", [128, 2, 8])
    convst = sb("convst", [128, 2, 8, 3])
    ident = identones.h[:, 0:128]
    ones = identones.h[:, 128:256]

    cslot = fw.slot()
    cslot2 = fw.slot()
    cdeps = []
    cdeps2 = []

    def cload(t, src, cast=False):
        if cast:
            fw.dma(POOL, lambda e: e.dma_start(out=t.h[:], in_=src), cslot2, writes=[t.d])
            cdeps2.append(t.d)
        else:
            fw.dma(SP, lambda e: e.dma_start(out=t.h[:], in_=src), cslot, writes=[t.d])
            cdeps.append(t.d)

    cload(gains, gains_d.ap())
    cload(lruvec, lruvec_d.ap())
    cload(ssmd, ssmd_d.ap())
    cload(bglu, bglu_d.ap())
    cload(wa_sb, lru_wa.ap().rearrange("e n k j -> k e n j"), cast=True)
    cload(wx_sb, lru_wx.ap().rearrange("e n k j -> k e n j"), cast=True)
    cload(kvec, cst["kvec"].ap())
    cload(masks, cst["masks"].ap())
    cload(onehot, cst["onehot"].ap())
    cload(identones, cst["identones"].ap(), cast=True)
    cload(qdecP, cst["qdecP"].ap())
    cload(qdecS, cst["qdecS"].ap())
    cload(strips, cst["strips"].ap(), cast=True)
    cload(bmask, cst["bmask"].ap())
    for ri in range(2):
        fw.dma(SP, lambda e, ri=ri: e.dma_start(out=apar.h[:, ri, :, :], in_=a_d[ri].ap().rearrange("o p g -> p o g")), cslot, writes=[apar.d])
    for gh in range(2):
        for o in range(2):
            fw.dma(SP, lambda e, gh=gh, o=o: e.dma_start(out=dtb.h[gh * 64:(gh + 1) * 64, o, :], in_=bass.AP(ldt_d, o * 128 + gh * 64, [[0, 64], [1, 64]])), cslot, writes=[dtb.d])
    cdeps.append(apar.d)
    cdeps.append(dtb.d)
    final_tok = (cslot.sem, cslot.val, cslot)
    for d_ in cdeps:
        d_.w = final_tok
    final_tok2 = (cslot2.sem, cslot2.val, cslot2)
    for d_ in cdeps2:
        d_.w = final_tok2

    fw.op(ACT, lambda e: e.activation(out=dtb.h[:], in_=dtb.h[:], func=AF.Exp), reads=[dtb.d], writes=[dtb.d])
    sp_t = sb("sp_t", [128, 2, 8])
    fw.op(ACT, lambda e: e.activation(out=sp_t.h[:], in_=lruvec.h[:, :, :, 7], func=AF.Exp, scale=-1.0), reads=[lruvec.d], writes=[sp_t.d])
    fw.op(ACT, lambda e: e.activation(out=sp_t.h[:], in_=sp_t.h[:], func=AF.Ln, bias=1.0), reads=[sp_t.d], writes=[sp_t.d])
    fw.op(DVE, lambda e: e.tensor_scalar(out=lsp.h[:, :, :, 0], in0=sp_t.h[:], scalar1=-8.0, scalar2=None, op0=ALU.mult), reads=[sp_t.d], writes=[lsp.d])
    fw.op(DVE, lambda e: e.tensor_scalar(out=lsp.h[:, :, :, 1], in0=sp_t.h[:], scalar1=-16.0, scalar2=None, op0=ALU.mult), reads=[sp_t.d], writes=[lsp.d])

    banks = [T(nc.alloc_psum_tensor(f"pb{i}", [128, 512], F32)) for i in range(8)]
    bctr = [0]

    def bank():
        b = banks[2 + bctr[0] % 6]
        bctr[0] += 1
        return b

    NPG = 124
    arena = nc.alloc_sbuf_tensor("arena", [128, NPG * 128], F32)
    pdeps = [Dep() for _ in range(NPG)]

    class Ph:
        def __init__(self, p0=0):
            self.p = p0

        def al(self, shape, dt=F32):
            nel = 1
            for s_ in shape[1:]:
                nel *= s_
            nb = nel * (4 if dt == F32 else 2)
            npg = (nb + 511) // 512
            assert self.p + npg <= NPG, (self.p, npg)
            v = arena[:, self.p * 128:(self.p + npg) * 128]
            if dt != F32:
                v = v.bitcast(dt)
            v = v[:, 0:nel]
            if len(shape) == 3:
                v = v.rearrange("p (a b) -> p a b", a=shape[1])
            elif len(shape) == 4:
                v = v.rearrange("p (a b c) -> p a b c", a=shape[1], b=shape[2])
            t = T(v)
            t.d = pdeps[self.p:self.p + npg]
            self.p += npg
            return t

    ph = Ph()
    qraw = ph.al([128, 2, 512])
    kraw = ph.al([128, 2, 512])
    qr = ph.al([128, 2, 512], BF16)
    kr = ph.al([128, 2, 512], BF16)
    qdd = ph.al([128, 2, 512], BF16)
    t1 = ph.al([128, 2, 512])
    t2 = ph.al([128, 2, 512])
    gate = ph.al([128, 2, 512], BF16)
    vtok = ph.al([128, 4, 1024], BF16)
    kdtok = ph.al([128, 4, 256], BF16)
    kdm = [ph.al([128, 256], BF16) for i in range(2)]
    PT = [ph.al([128, 128], BF16) for i in range(2)]
    oT = ph.al([128, 2, 512])
    sq = ph.al([128, 2, 512], BF16)
    S0f = [ph.al([128, 2, 256]) for i in range(3)]
    S0b = [ph.al([128, 2, 256], BF16) for i in range(3)]
    Sout = [ph.al([128, 2, 256]) for i in range(3)]
    p_ret_end = ph.p
    ph = Ph()
    xp_ = ph.al([128, 3 + 512 + 1])
    xps = ph.al([128, 16, 7])
    xc = ph.al([128, 512])
    xcb = ph.al([128, 512], BF16)
    rg = ph.al([128, 512])
    ig = ph.al([128, 512])
    av = ph.al([128, 512])
    mv = ph.al([128, 512])
    hT = ph.al([128, 512])
    h0s = ph.al([128, 8, 16])
    cv0s = ph.al([128, 8, 16, 3])
    lrus_o = ph.al([128, 8, 16])
    convs_o = ph.al([128, 8, 16, 3])
    y32 = ph.al([128, 512])
    g1 = ph.al([128, 512])
    g2 = ph.al([128, 512])
    ph = Ph()
    hact = [ph.al([128, 4, 512], BF16) for i in range(2)]
    sgt = [ph.al([128, 512]) for i in range(2)]
    wf = [ph.al([128, KC, 512], BF16) for i in range(2)]
    ph = Ph()
    Wneg = ph.al([128, 8, 2, 64])
    Wpos = ph.al([128, 9, 2, 64])
    frfi = ph.al([128, 2, 64])
    wtmp = [ph.al([128, 64]) for i in range(8)]
    braw = [ph.al([128, 8, 16]) for i in range(2)]
    craw = [ph.al([128, 8, 16]) for i in range(2)]
    bbar = [ph.al([128, 8, 16]) for i in range(2)]
    tA = ph.al([128, 1024])
    tB = ph.al([128, 1024])
    Rp = [ph.al([128, 8, 8, 16], BF16) for i in range(2)]
    Op = [ph.al([128, 8, 8, 16], BF16) for i in range(2)]
    MTs = ph.al([128, 16, 128], BF16)
    QTs = [ph.al([128, 128], BF16) for i in range(2)]
    Ub2 = [ph.al([128, 16, 64], BF16) for i in range(2)]
    Yb = ph.al([128, 16, 64], BF16)
    Aar = ph.al([128, 2, 8, 65])
    Abf = ph.al([128, 2, 8, 64], BF16)
    h0s5 = ph.al([128, 2, 8, 16])
    Pp = ph.al([128, 2, 8, 16])
    Pf = ph.al([128, 2, 8, 16])
    y32b = ph.al([128, 512])
    g1b = ph.al([128, 512])
    g2b = ph.al([128, 512])
    Wd = ph.al([128, 7, 2, 64])
    rstd = sb("rstd", [128, 512])
    cosT = sb("cosT", [128, 512])
    sinT = sb("sinT", [128, 512])
    tslot = fw.slot()
    tslot2 = fw.slot()
    stslot2 = fw.slot()
    xslot = fw.slot()
    S0slot = [fw.slot() for _ in range(3)]
    S0bslot = [fw.slot() for _ in range(3)]
    Soslot = [fw.slot() for _ in range(3)]
    stslot = fw.slot()
    oslots = {}

    def oslot(k):
        if k not in oslots:
            oslots[k] = fw.slot()
        return oslots[k]

    out_toks = []

    NLOADS = 192
    wscr_l = [nc.dram_tensor(f"wscr{q}", [NLOADS // 2, 128, KC * 512], BF16, kind="Internal") for q in range(2)]
    wscr_ap = lambda idx: wscr_l[idx // (NLOADS // 2)].ap()[idx % (NLOADS // 2)]
    wsd = [Dep() for _ in range(NLOADS)]
    wslot_hw = [fw.slot() for _ in range(NW)]
    sslot = [fw.slot() for _ in range(NW)]
    lctr = [0]
    passno = [0]
    use_scr = cfg.get("use_scr", True)

    pool_main = {"bufs": wb, "sw": wslot, "hw": wslot_hw, "st": sslot, "ctr": wctr}
    pool_ffn = {"bufs": [wb[0], wb[1], wf[0], wf[1]], "sw": wslot + [fw.slot(), fw.slot()], "hw": wslot_hw + [fw.slot(), fw.slot()],
                "st": sslot + [fw.slot(), fw.slot()], "ctr": [0]}

    def load_w(dram_ap_list, pool=None):
        pool = pool or pool_main
        i = pool["ctr"][0] % len(pool["bufs"])
        pool["ctr"][0] += 1
        t = pool["bufs"][i]
        wslot_, wslot_hw_, sslot_ = pool["sw"], pool["hw"], pool["st"]
        idx = lctr[0]
        lctr[0] += 1
        flat = t.h[:].rearrange("p a b -> p (a b)")
        if use_scr and passno[0] > 0:
            fw.dma(SP, lambda e: e.dma_start(out=flat, in_=wscr_ap(idx)), wslot_hw_[i], reads=[wsd[idx]], writes=[t.d])
            return t

        def fn(e, t=t, lst=dram_ap_list):
            r = []
            for src, dst in lst:
                r.append(e.dma_start(out=dst(t.h), in_=src))
            return r
        fw.dma(POOL, fn, wslot_[i], writes=[t.d], n=len(dram_ap_list))
        if use_scr and len(tiles) > 1:
            fw.dma(SP, lambda e: e.dma_start(out=wscr_ap(idx), in_=flat), sslot_[i], reads=[t.d], writes=[wsd[idx]])
        return t

    def wsrc(wd, c0, ncol):
        return wd[:, c0:c0 + ncol].rearrange("(kc p) j -> p kc j", p=128)

    def rmsnorm(gi, Tn):
        fw.op(ACT, lambda e: e.activation(out=cat.h[:, :, 0:Tn], in_=x.h[:, :, 0:Tn], func=AF.Square), reads=[x.d], writes=[cat.d])
        b = bank()

        def mm(e):
            r = None
            for kc in range(KC):
                r = e.matmul(b.h[:, 0:Tn], lhsT=ones, rhs=cat.h[:, kc, 0:Tn], start=(kc == 0), stop=(kc == KC - 1))
            return r
        fw.op(PE, mm, reads=[cat.d, identones.d], writes=[b.d])
        fw.op(ACT, lambda e: e.activation(out=rstd.h[:, 0:Tn], in_=b.h[:, 0:Tn], func=AF.Sqrt, scale=1.0 / D, bias=EPS), reads=[b.d], writes=[rstd.d])
        fw.op(DVE, lambda e: e.reciprocal(out=rstd.h[:, 0:Tn], in_=rstd.h[:, 0:Tn]), reads=[rstd.d], writes=[rstd.d])
        for kc in range(KC):
            eng = DVE
            fw.op(eng, lambda e, kc=kc: e.scalar_tensor_tensor(out=xn.h[:, kc, 0:Tn], in0=x.h[:, kc, 0:Tn], scalar=gains.h[:, gi, kc:kc + 1],
                                                               in1=rstd.h[:, 0:Tn], op0=ALU.mult, op1=ALU.mult),
                  reads=[x.d, gains.d, rstd.d], writes=[xn.d])

    def proj_fm(wt, col0, src, Tn, nk=KC):
        b = bank()

        def mm(e):
            r = None
            for kc in range(nk):
                r = e.matmul(b.h[:, 0:Tn], lhsT=wt.h[:, kc, col0:col0 + 128], rhs=src.h[:, kc, 0:Tn], start=(kc == 0), stop=(kc == nk - 1))
            return r
        fw.op(PE, mm, reads=[wt.d, src.d], writes=[b.d])
        return b

    def add_to_x(b, oc, Tn):
        fw.op(DVE, lambda e: e.tensor_tensor(out=x.h[:, oc, 0:Tn], in0=b.h[:, 0:Tn], in1=x.h[:, oc, 0:Tn], op=ALU.add), reads=[b.d, x.d], writes=[x.d])

    def ffn(layer, Tn):
        rmsnorm(4 + layer, Tn)
        wg = w_gu.ap()[layer]
        wd = w_dn.ap()[layer]
        for hb in range(FH // 512):
            tg = load_w([(wsrc(wg, hb * 512, 512), lambda h: h[:])], pool_ffn)
            tu = load_w([(wsrc(wg, FH + hb * 512, 512), lambda h: h[:])], pool_ffn)
            ha = hact[hb % 2]
            for m in range(4):
                bg = proj_fm(tg, m * 128, xn, Tn)
                bu = proj_fm(tu, m * 128, xn, Tn)
                sg = sgt[m % 2]
                fw.op(ACT, lambda e, bg=bg, sg=sg: e.activation(out=sg.h[:, 0:Tn], in_=bg.h[:, 0:Tn], func=AF.Silu), reads=[bg.d], writes=[sg.d])
                fw.op(DVE, lambda e, bu=bu, sg=sg, ha=ha, m=m: e.tensor_tensor(out=ha.h[:, m, 0:Tn], in0=bu.h[:, 0:Tn], in1=sg.h[:, 0:Tn], op=ALU.mult),
                      reads=[bu.d, sg.d], writes=[ha.d])
            td = load_w([(wd[hb * 512:(hb + 1) * 512, :].rearrange("(kc p) j -> p kc j", p=128), lambda h: h[:].rearrange("p a b -> p (a b)").rearrange("p (kc j) -> p kc j", kc=4))], pool_ffn)
            tdv = td.h[:].rearrange("p a b -> p (a b)").rearrange("p (kc j) -> p kc j", kc=4)
            for oc in range(KC):
                b = bank()

                def mm(e, b=b, oc=oc, ha=ha, tdv=tdv):
                    r = None
                    for kc in range(4):
                        r = e.matmul(b.h[:, 0:Tn], lhsT=tdv[:, kc, oc * 128:(oc + 1) * 128], rhs=ha.h[:, kc, 0:Tn], start=(kc == 0), stop=(kc == 3))
                    return r
                fw.op(PE, mm, reads=[td.d, ha.d], writes=[b.d])
                add_to_x(b, oc, Tn)

    def even_layer(e_, kind, ti, Tn):
        samp = kind == "s"
        NTC = 1 if samp else 4
        TR = 64 if samp else 128
        rmsnorm(e_, Tn)
        wi = w_in.ap()[e_]
        qdec = qdecS if samp else qdecP
        kv0 = 8 if samp else 0
        gd = [GAM[h] ** (4 if samp else 128) for h in range(4)]
        for half in range(2):
            tv = load_w([(wsrc(wi, 2048 + half * 512, 512), lambda h: h[:])])
            for tc in range(NTC):
                b = bank()

                def mm(e, b=b, tc=tc, tv=tv):
                    r = None
                    for kc in range(KC):
                        r = e.matmul(b.h[0:TR, :], lhsT=xn.h[:, kc, tc * 128:tc * 128 + TR], rhs=tv.h[:, kc, :], start=(kc == 0), stop=(kc == KC - 1))
                    return r
                fw.op(PE, mm, reads=[tv.d, xn.d], writes=[b.d])
                fw.op(ACT, lambda e, b=b, tc=tc, half=half: e.activation(out=vtok.h[0:TR, tc, half * 512:(half + 1) * 512], in_=b.h[0:TR, :], func=AF.Copy),
                      reads=[b.d], writes=[vtok.d])
        def _head(h):
            tqk = load_w([(wsrc(wi, h * 256, 256), lambda hh: hh[:, :, 0:256]), (wsrc(wi, 1024 + h * 256, 256), lambda hh: hh[:, :, 256:512])])
            for dc in range(2):
                bq = proj_fm(tqk, dc * 128, xn, Tn)
                fw.op(ACT, lambda e, bq=bq, dc=dc: e.activation(out=qraw.h[:, dc, 0:Tn], in_=bq.h[:, 0:Tn], func=AF.Copy), reads=[bq.d], writes=[qraw.d])
                bk = proj_fm(tqk, 256 + dc * 128, xn, Tn)
                fw.op(ACT, lambda e, bk=bk, dc=dc: e.activation(out=kraw.h[:, dc, 0:Tn], in_=bk.h[:, 0:Tn], func=AF.Copy), reads=[bk.d], writes=[kraw.d])
            tg = load_w([(wsrc(wi, 3072 + h * 256, 256), lambda hh: hh[:, :, 0:256])])
            for dc in range(2):
                bg = proj_fm(tg, dc * 128, xn, Tn)
                fw.op(ACT, lambda e, bg=bg, dc=dc: e.activation(out=gate.h[:, dc, 0:Tn], in_=bg.h[:, 0:Tn], func=AF.Silu), reads=[bg.d], writes=[gate.d])
            for raw, outt, eng in ((qraw, qr, DVE), (kraw, kr, POOL)):
                for dc in range(2):
                    fw.op(eng, lambda e, raw=raw, dc=dc: e.tensor_tensor(out=t1.h[:, dc, 0:Tn], in0=raw.h[:, dc, 0:Tn], in1=cosT.h[:, 0:Tn], op=ALU.mult), reads=[raw.d, cosT.d], writes=[t1.d])
                    fw.op(eng, lambda e, raw=raw, dc=dc: e.tensor_tensor(out=t2.h[:, dc, 0:Tn], in0=raw.h[:, dc, 0:Tn], in1=sinT.h[:, 0:Tn], op=ALU.mult), reads=[raw.d, sinT.d], writes=[t2.d])
                fw.op(eng, lambda e, outt=outt: e.tensor_tensor(out=outt.h[:, 0, 0:Tn], in0=t1.h[:, 0, 0:Tn], in1=t2.h[:, 1, 0:Tn], op=ALU.subtract), reads=[t1.d, t2.d], writes=[outt.d])
                fw.op(eng, lambda e, outt=outt: e.tensor_tensor(out=outt.h[:, 1, 0:Tn], in0=t1.h[:, 1, 0:Tn], in1=t2.h[:, 0, 0:Tn], op=ALU.add), reads=[t1.d, t2.d], writes=[outt.d])
            if samp:
                qdb = sap(qdecS.h, h * 64, [[0, 2], [1, 64]])
                fw.op(DVE, lambda e, qdb=qdb: e.tensor_tensor(out=qdd.h[:, :, 0:64], in0=qr.h[:, :, 0:64], in1=qdb, op=ALU.mult), reads=[qr.d, qdecS.d], writes=[qdd.d])
            else:
                qdb = sap(qdecP.h, h * 128, [[0, 2], [0, 4], [1, 128]])
                fw.op(DVE, lambda e, qdb=qdb: e.tensor_tensor(out=qdd.h[:, :, :].rearrange("p a (c n) -> p a c n", c=4), in0=qr.h[:, :, :].rearrange("p a (c n) -> p a c n", c=4), in1=qdb, op=ALU.mult), reads=[qr.d, qdecP.d], writes=[qdd.d])
            for tc in range(NTC):
                b = bank()
                bb = b.h[:, 0:128].bitcast(BF16)

                def tr(e, tc=tc, bb=bb):
                    r = None
                    for dc in range(2):
                        r = e.transpose(out=bb[0:TR, dc * 128:(dc + 1) * 128], in_=kr.h[:, dc, tc * 128:tc * 128 + TR], identity=ident)
                    return r
                fw.op(PE, tr, reads=[kr.d, identones.d], writes=[b.d])
                fw.op(DVE, lambda e, tc=tc, bb=bb: e.tensor_scalar(out=kdtok.h[0:TR, tc, :], in0=bb[0:TR, :], scalar1=kvec.h[0:TR, kv0 + 4 + h:kv0 + 5 + h], scalar2=None, op0=ALU.mult),
                      reads=[b.d, kvec.d], writes=[kdtok.d])
            po = [banks[0], banks[1]]
            for c in range(NTC):
                bs = bank()

                def sc(e, bs=bs, c=c):
                    r = None
                    for dc in range(2):
                        r = e.matmul(bs.h[0:TR, 0:TR], lhsT=kr.h[:, dc, c * 128:c * 128 + TR], rhs=qdd.h[:, dc, c * 128:c * 128 + TR], start=(dc == 0), stop=(dc == 1))
                    return r
                fw.op(PE, sc, reads=[kr.d, qdd.d], writes=[bs.d])
                pt = PT[c % 2]
                moff = 128 if samp else 0
                fw.op(DVE, lambda e, bs=bs, pt=pt: e.scalar_tensor_tensor(out=pt.h[0:TR, 0:TR], in0=bs.h[0:TR, 0:TR], scalar=kvec.h[0:TR, kv0 + h:kv0 + h + 1],
                                                                             in1=masks.h[0:TR, moff:moff + TR], op0=ALU.mult, op1=ALU.mult),
                      reads=[bs.d, kvec.d, masks.d], writes=[pt.d])
                if not samp:
                    for ec in range(2):
                        def om(e, ec=ec, c=c, pt=pt):
                            e.matmul(po[ec].h[:, c * 128:(c + 1) * 128], lhsT=vtok.h[:, c, h * 256 + ec * 128:h * 256 + (ec + 1) * 128], rhs=pt.h[:, :], start=True, stop=False)
                            r = None
                            for dc in range(2):
                                r = e.matmul(po[ec].h[:, c * 128:(c + 1) * 128], lhsT=Sbf.h[:, h, dc, ec * 128:(ec + 1) * 128], rhs=qdd.h[:, dc, c * 128:(c + 1) * 128], start=False, stop=(dc == 1))
                            return r
                        fw.op(PE, om, reads=[vtok.d, pt.d, Sbf.d, qdd.d], writes=[po[ec].d])
                    bS = bank()

                    def su(e, bS=bS, c=c):
                        r = None
                        for dc in range(2):
                            r = e.matmul(bS.h[:, dc * 256:(dc + 1) * 256], lhsT=kdtok.h[:, c, dc * 128:(dc + 1) * 128], rhs=vtok.h[:, c, h * 256:(h + 1) * 256], start=True, stop=True)
                        return r
                    fw.op(PE, su, reads=[kdtok.d, vtok.d], writes=[bS.d])
                    fw.op(DVE, lambda e, bS=bS: e.scalar_tensor_tensor(out=Sst.h[:, e_, h, :, :], in0=Sst.h[:, e_, h, :, :], scalar=gd[h], in1=bS.h[:, :].rearrange("p (a b) -> p a b", a=2), op0=ALU.mult, op1=ALU.add),
                          reads=[bS.d, Sst.d], writes=[Sst.d])
                    fw.op(ACT, lambda e: e.activation(out=Sbf.h[:, h, :, :], in_=Sst.h[:, e_, h, :, :], func=AF.Copy), reads=[Sst.d], writes=[Sbf.d])
                else:
                    for ec in range(2):
                        fw.op(PE, lambda e, ec=ec, pt=pt: e.matmul(po[ec].h[:, 0:64], lhsT=vtok.h[0:64, 0, h * 256 + ec * 128:h * 256 + (ec + 1) * 128], rhs=pt.h[0:64, 0:64], start=True, stop=True),
                              reads=[vtok.d, pt.d], writes=[po[ec].d])
                    for j in range(16):
                        si = (h * 16 + j) % 3
                        s0f, s0b, so_ = S0f[si], S0b[si], Sout[si]
                        src = sret.ap()[e_, j, h].rearrange("(dc p) e -> p dc e", p=128)
                        fw.dma(SP, lambda e, s0f=s0f, src=src: e.dma_start(out=s0f.h[:], in_=src), S0slot[si], writes=[s0f.d])
                        fw.dma(POOL, lambda e, s0b=s0b, src=src: e.dma_start(out=s0b.h[:], in_=src), S0bslot[si], writes=[s0b.d])
                        for ec in range(2):
                            def im(e, ec=ec, j=j, s0b=s0b):
                                r = None
                                for dc in range(2):
                                    r = e.matmul(po[ec].h[:, 4 * j:4 * j + 4], lhsT=s0b.h[:, dc, ec * 128:(ec + 1) * 128], rhs=qdd.h[:, dc, 4 * j:4 * j + 4], start=False, stop=(dc == 1), skip_group_check=True)
                                return r
                            fw.op(PE, im, reads=[s0b.d, qdd.d], writes=[po[ec].d])
                        km = kdm[j % 2]
                        fw.op(DVE, lambda e, km=km, j=j: e.tensor_scalar(out=km.h[0:64, :], in0=kdtok.h[0:64, 0, :], scalar1=onehot.h[0:64, j:j + 1], scalar2=None, op0=ALU.mult),
                              reads=[kdtok.d, onehot.d], writes=[km.d])
                        bS = bank()

                        def su(e, bS=bS, km=km):
                            r = None
                            for dc in range(2):
                                r = e.matmul(bS.h[:, dc * 256:(dc + 1) * 256], lhsT=km.h[0:64, dc * 128:(dc + 1) * 128], rhs=vtok.h[0:64, 0, h * 256:(h + 1) * 256], start=True, stop=True)
                            return r
                        fw.op(PE, su, reads=[km.d, vtok.d], writes=[bS.d])
                        fw.op(DVE, lambda e, bS=bS, s0f=s0f, so_=so_: e.scalar_tensor_tensor(out=so_.h[:], in0=s0f.h[:], scalar=gd[h], in1=bS.h[:, :].rearrange("p (a b) -> p a b", a=2), op0=ALU.mult, op1=ALU.add),
                              reads=[bS.d, s0f.d], writes=[so_.d])
                        dst = o_rets.ap()[e_, j, h].rearrange("(dc p) e -> p dc e", p=128)
                        out_toks.append(fw.dma(SP, lambda e, so_=so_, dst=dst: e.dma_start(out=dst, in_=so_.h[:]), Soslot[si], reads=[so_.d]))
            for ec in range(2):
                fw.op(ACT, lambda e, ec=ec: e.activation(out=oT.h[:, ec, 0:Tn], in_=po[ec].h[:, 0:Tn], func=AF.Copy), reads=[po[ec].d], writes=[oT.d])
            fw.op(ACT, lambda e: e.activation(out=sq.h[:, :, 0:Tn], in_=oT.h[:, :, 0:Tn], func=AF.Square), reads=[oT.d], writes=[sq.d])
            bn = bank()

            def nm(e, bn=bn):
                r = None
                for ec in range(2):
                    r = e.matmul(bn.h[:, 0:Tn], lhsT=ones, rhs=sq.h[:, ec, 0:Tn], start=(ec == 0), stop=(ec == 1))
                return r
            fw.op(PE, nm, reads=[sq.d, identones.d], writes=[bn.d])
            fw.op(ACT, lambda e, bn=bn: e.activation(out=rstd.h[:, 0:Tn], in_=bn.h[:, 0:Tn], func=AF.Sqrt, scale=1.0 / 256, bias=EPS), reads=[bn.d], writes=[rstd.d])
            fw.op(DVE, lambda e: e.reciprocal(out=rstd.h[:, 0:Tn], in_=rstd.h[:, 0:Tn]), reads=[rstd.d], writes=[rstd.d])
            for ec in range(2):
                fw.op(DVE, lambda e, ec=ec: e.tensor_tensor(out=t1.h[:, ec, 0:Tn], in0=oT.h[:, ec, 0:Tn], in1=rstd.h[:, 0:Tn], op=ALU.mult), reads=[oT.d, rstd.d], writes=[t1.d])
                fw.op(DVE, lambda e, ec=ec: e.tensor_tensor(out=cat.h[:, 2 * h + ec, 0:Tn], in0=t1.h[:, ec, 0:Tn], in1=gate.h[:, ec, 0:Tn], op=ALU.mult), reads=[t1.d, gate.d], writes=[cat.d])
            if (not samp) and ti == 3:
                dst = o_retp.ap()[e_, h].rearrange("(dc p) e -> p dc e", p=128)
                out_toks.append(fw.dma(SP, lambda e, dst=dst: e.dma_start(out=dst, in_=Sst.h[:, e_, h, :, :]), oslot(("retp", e_, h)), reads=[Sst.d]))
        for h_ in range(4):
            _head(h_)
        if samp:
            fw.dma(SP, lambda e: e.dma_start(out=h0s.h[:], in_=slru.ap()[e_].rearrange("(n p) j -> p n j", p=128)), stslot, writes=[h0s.d])
            fw.dma(SP, lambda e: e.dma_start(out=cv0s.h[:], in_=sconv.ap()[e_].rearrange("(n p) j i -> p n j i", p=128)), stslot2, writes=[cv0s.d])
        def _blk(nb):
            txy = load_w([(wsrc(wi, 4096 + nb * 128, 128), lambda hh: hh[:, :, 0:128]), (wsrc(wi, 5120 + nb * 128, 128), lambda hh: hh[:, :, 128:256])])
            bx_ = proj_fm(txy, 0, xn, Tn)
            by_ = proj_fm(txy, 128, xn, Tn)
            lv = lambda k: lruvec.h[:, e_, nb, k:k + 1]
            if not samp:
                fw.op(ACT, lambda e, bx_=bx_: e.activation(out=xp_.h[:, 3:3 + Tn], in_=bx_.h[:, 0:Tn], func=AF.Copy), reads=[bx_.d], writes=[xp_.d])
                if ti == 0:
                    fw.op(DVE, lambda e: e.memset(xp_.h[:, 0:3], 0.0), writes=[xp_.d])
                else:
                    fw.op(DVE, lambda e, nb=nb: e.tensor_copy(out=xp_.h[:, 0:3], in_=convst.h[:, e_, nb, :]), reads=[convst.d], writes=[xp_.d])
                xin = lambda i: xp_.h[:, i:i + Tn]
                xco = xc.h[:, 0:Tn]
            else:
                fw.op(ACT, lambda e, bx_=bx_: e.activation(out=xps.h[:, :, 3:7], in_=bx_.h[:, 0:64].rearrange("p (j t) -> p j t", t=4), func=AF.Copy), reads=[bx_.d], writes=[xps.d])
                fw.op(DVE, lambda e, nb=nb: e.tensor_copy(out=xps.h[:, :, 0:3], in_=cv0s.h[:, nb, :, :]), reads=[cv0s.d], writes=[xps.d])
                xin = lambda i: xps.h[:, :, i:i + 4]
                xco = xc.h[:, 0:64].rearrange("p (j t) -> p j t", t=4)
            fw.op(DVE, lambda e, xin=xin, xco=xco, nb=nb: e.tensor_scalar(out=xco, in0=xin(0), scalar1=lruvec.h[:, e_, nb, 0:1], scalar2=lruvec.h[:, e_, nb, 4:5], op0=ALU.mult, op1=ALU.add),
                  reads=[xp_.d, xps.d, lruvec.d], writes=[xc.d])
            for i in range(1, 4):
                fw.op(DVE, lambda e, xin=xin, xco=xco, nb=nb, i=i: e.scalar_tensor_tensor(out=xco, in0=xin(i), scalar=lruvec.h[:, e_, nb, i:i + 1], in1=xco, op0=ALU.mult, op1=ALU.add),
                      reads=[xp_.d, xps.d, lruvec.d, xc.d], writes=[xc.d])
            if not samp:
                fw.op(POOL, lambda e, nb=nb: e.tensor_copy(out=convst.h[:, e_, nb, :], in_=xp_.h[:, Tn:Tn + 3]), reads=[xp_.d], writes=[convst.d])
            else:
                fw.op(POOL, lambda e, nb=nb: e.tensor_copy(out=convs_o.h[:, nb, :, :], in_=xps.h[:, :, 4:7]), reads=[xps.d], writes=[convs_o.d])
            fw.op(ACT, lambda e: e.activation(out=xcb.h[:, 0:Tn], in_=xc.h[:, 0:Tn], func=AF.Copy), reads=[xc.d], writes=[xcb.d])
            br = bank()
            fw.op(PE, lambda e, br=br, nb=nb: e.matmul(br.h[:, 0:Tn], lhsT=wa_sb.h[:, e_, nb, :], rhs=xcb.h[:, 0:Tn], start=True, stop=True), reads=[wa_sb.d, xcb.d], writes=[br.d])
            bi = bank()
            fw.op(PE, lambda e, bi=bi, nb=nb: e.matmul(bi.h[:, 0:Tn], lhsT=wx_sb.h[:, e_, nb, :], rhs=xcb.h[:, 0:Tn], start=True, stop=True), reads=[wx_sb.d, xcb.d], writes=[bi.d])
            fw.op(ACT, lambda e, br=br, nb=nb: e.activation(out=rg.h[:, 0:Tn], in_=br.h[:, 0:Tn], func=AF.Sigmoid, bias=lruvec.h[:, e_, nb, 5:6]), reads=[br.d, lruvec.d], writes=[rg.d])
            fw.op(ACT, lambda e, bi=bi, nb=nb: e.activation(out=ig.h[:, 0:Tn], in_=bi.h[:, 0:Tn], func=AF.Sigmoid, bias=lruvec.h[:, e_, nb, 6:7]), reads=[bi.d, lruvec.d], writes=[ig.d])
            fw.op(ACT, lambda e, nb=nb: e.activation(out=av.h[:, 0:Tn], in_=rg.h[:, 0:Tn], func=AF.Exp, scale=lsp.h[:, e_, nb, 0:1]), reads=[rg.d, lsp.d], writes=[av.d])
            fw.op(ACT, lambda e, nb=nb: e.activation(out=mv.h[:, 0:Tn], in_=rg.h[:, 0:Tn], func=AF.Exp, scale=lsp.h[:, e_, nb, 1:2]), reads=[rg.d, lsp.d], writes=[mv.d])
            fw.op(ACT, lambda e: e.activation(out=mv.h[:, 0:Tn], in_=mv.h[:, 0:Tn], func=AF.Sqrt, scale=-1.0, bias=1.0), reads=[mv.d], writes=[mv.d])
            fw.op(DVE, lambda e: e.tensor_tensor(out=mv.h[:, 0:Tn], in0=mv.h[:, 0:Tn], in1=ig.h[:, 0:Tn], op=ALU.mult), reads=[mv.d, ig.d], writes=[mv.d])
            fw.op(DVE, lambda e: e.tensor_tensor(out=mv.h[:, 0:Tn], in0=mv.h[:, 0:Tn], in1=xc.h[:, 0:Tn], op=ALU.mult), reads=[mv.d, xc.d], writes=[mv.d])
            if not samp:
                init = 0.0 if ti == 0 else hst.h[:, e_, nb:nb + 1]
                fw.op(DVE, lambda e, init=init: e.tensor_tensor_scan(out=hT.h[:, 0:Tn], data0=av.h[:, 0:Tn], data1=mv.h[:, 0:Tn], initial=init, op0=ALU.mult, op1=ALU.add),
                      reads=[av.d, mv.d, hst.d], writes=[hT.d])
                fw.op(POOL, lambda e, nb=nb: e.tensor_copy(out=hst.h[:, e_, nb:nb + 1], in_=hT.h[:, Tn - 1:Tn]), reads=[hT.d], writes=[hst.d])
            else:
                av3 = av.h[:, 0:64].rearrange("p (j t) -> p j t", t=4)
                mv3 = mv.h[:, 0:64].rearrange("p (j t) -> p j t", t=4)
                fw.op(DVE, lambda e, nb=nb: e.tensor_tensor(out=g1.h[:, 0:16], in0=av3[:, :, 0], in1=h0s.h[:, nb, :], op=ALU.mult), reads=[av.d, h0s.d], writes=[g1.d])
                fw.op(DVE, lambda e: e.tensor_tensor(out=mv3[:, :, 0], in0=mv3[:, :, 0], in1=g1.h[:, 0:16], op=ALU.add), reads=[mv.d, g1.d], writes=[mv.d])
                fw.op(DVE, lambda e: e.memset(av3[:, :, 0], 0.0), reads=[g1.d], writes=[av.d])
                fw.op(DVE, lambda e: e.tensor_tensor_scan(out=hT.h[:, 0:64], data0=av.h[:, 0:64], data1=mv.h[:, 0:64], initial=0.0, op0=ALU.mult, op1=ALU.add),
                      reads=[av.d, mv.d], writes=[hT.d])
                fw.op(POOL, lambda e, nb=nb: e.tensor_copy(out=lrus_o.h[:, nb, :], in_=hT.h[:, 0:64].rearrange("p (j t) -> p j t", t=4)[:, :, 3]), reads=[hT.d], writes=[lrus_o.d])
            fw.op(ACT, lambda e, by_=by_: e.activation(out=y32.h[:, 0:Tn], in_=by_.h[:, 0:Tn], func=AF.Copy), reads=[by_.d], writes=[y32.d])
            gelu_mul(y32, hT, cat.h[:, 8 + nb, 0:Tn], cat, Tn, g1, g2)
        for nb_ in range(8):
            _blk(nb_)
        if (not samp) and ti == 3:
            out_toks.append(fw.dma(SP, lambda e: e.dma_start(out=o_lrup.ap()[e_], in_=hst.h[:, e_, :]), oslot(("lrup", e_)), reads=[hst.d]))
            out_toks.append(fw.dma(SP, lambda e: e.dma_start(out=o_convp.ap()[e_].rearrange("(n p) i -> p n i", p=128), in_=convst.h[:, e_, :, :]), oslot(("convp", e_)), reads=[convst.d]))
        if samp:
            out_toks.append(fw.dma(SP, lambda e: e.dma_start(out=o_lrus.ap()[e_].rearrange("(n p) j -> p n j", p=128), in_=lrus_o.h[:]), oslot(("lrus", e_)), reads=[lrus_o.d]))
            out_toks.append(fw.dma(SP, lambda e: e.dma_start(out=o_convs.ap()[e_].rearrange("(n p) j i -> p n j i", p=128), in_=convs_o.h[:]), oslot(("convs", e_)), reads=[convs_o.d]))
        wo = w_out.ap()[e_]
        for og in range(4):
            two = load_w([(wsrc(wo, og * 512, 512), lambda hh: hh[:])])
            for m in range(4):
                b = proj_fm(two, m * 128, cat, Tn)
                add_to_x(b, og * 4 + m, Tn)

    def gelu_mul(src, mul, out_ap, out_t, Tn, g1, g2):
        s = src.h[:, 0:Tn]
        fw.op(DVE, lambda e: e.tensor_tensor(out=g1.h[:, 0:Tn], in0=s, in1=s, op=ALU.mult), reads=[src.d], writes=[g1.d])
        fw.op(DVE, lambda e: e.tensor_scalar(out=g1.h[:, 0:Tn], in0=g1.h[:, 0:Tn], scalar1=0.044715, scalar2=1.0, op0=ALU.mult, op1=ALU.add), reads=[g1.d], writes=[g1.d])
        fw.op(DVE, lambda e: e.tensor_tensor(out=g1.h[:, 0:Tn], in0=g1.h[:, 0:Tn], in1=s, op=ALU.mult), reads=[g1.d, src.d], writes=[g1.d])
        fw.op(ACT, lambda e: e.activation(out=g2.h[:, 0:Tn], in_=g1.h[:, 0:Tn], func=AF.Sigmoid, scale=GELU_C), reads=[g1.d], writes=[g2.d])
        if mul is not None:
            fw.op(DVE, lambda e: e.tensor_tensor(out=g2.h[:, 0:Tn], in0=g2.h[:, 0:Tn], in1=s, op=ALU.mult), reads=[g2.d, src.d], writes=[g2.d])
            fw.op(DVE, lambda e: e.tensor_tensor(out=out_ap, in0=g2.h[:, 0:Tn], in1=mul.h[:, 0:Tn], op=ALU.mult), reads=[g2.d, mul.d], writes=[out_t.d])
        else:
            fw.op(DVE, lambda e: e.tensor_tensor(out=out_ap, in0=g2.h[:, 0:Tn], in1=s, op=ALU.mult), reads=[g2.d, src.d], writes=[out_t.d])

    def odd_layer(o_, kind, ti, Tn):
        rmsnorm(2 + o_, Tn)
        if not do_odd:
            return
        s5_layer(o_, kind, ti, Tn)
        if cfg.get("stop_s5"):
            raise StopIteration
        wg = w_glu.ap()[o_]
        for og in range(4):
            t1w = load_w([(wsrc(wg, og * 512, 512), lambda hh: hh[:])])
            t2w = load_w([(wsrc(wg, D + og * 512, 512), lambda hh: hh[:])])
            for m in range(4):
                oc = og * 4 + m
                b1 = proj_fm(t1w, m * 128, cat, Tn)
                b2 = proj_fm(t2w, m * 128, cat, Tn)
                sg = sgt[m % 2]
                fw.op(ACT, lambda e, b2=b2, sg=sg, oc=oc: e.activation(out=sg.h[:, 0:Tn], in_=b2.h[:, 0:Tn], func=AF.Sigmoid, bias=bglu.h[:, o_, 16 + oc:17 + oc]), reads=[b2.d, bglu.d], writes=[sg.d])
                fw.op(DVE, lambda e, b1=b1, sg=sg, oc=oc: e.scalar_tensor_tensor(out=sg.h[:, 0:Tn], in0=b1.h[:, 0:Tn], scalar=bglu.h[:, o_, oc:oc + 1], in1=sg.h[:, 0:Tn], op0=ALU.add, op1=ALU.mult),
                      reads=[b1.d, sg.d, bglu.d], writes=[sg.d])
                fw.op(DVE, lambda e, sg=sg, oc=oc: e.tensor_tensor(out=x.h[:, oc, 0:Tn], in0=sg.h[:, 0:Tn], in1=x.h[:, oc, 0:Tn], op=ALU.add), reads=[sg.d, x.d], writes=[x.d])

    def vap(t, off, dims, parts=128, p0=0):
        v = t.h
        ps = v.ap[0][0]
        return bass.AP(arena, v.offset + p0 * ps + off, [[ps, parts]] + [list(d_) for d_ in dims])

    bslots = [fw.slot() for _ in range(4)]
    h0slots = [fw.slot() for _ in range(2)]
    TWO_PI = 2.0 * math.pi

    def s5_layer(o_, kind, ti, Tn):
        samp = kind == "s"
        n = 16 if samp else 64
        SL = 4 if samp else 8
        s_list = list(range(4, 8)) if samp else list(range(8))
        W = wtmp

        def tt(eng, out, i0, i1, op, rd, wr):
            fw.op(eng, lambda e: e.tensor_tensor(out=out, in0=i0, in1=i1, op=op), reads=rd, writes=wr)

        def ts(eng, out, i0, s1, s2, op0, op1, rd, wr):
            if op1 is None:
                fw.op(eng, lambda e: e.tensor_scalar(out=out, in0=i0, scalar1=s1, scalar2=None, op0=op0), reads=rd, writes=wr)
            else:
                fw.op(eng, lambda e: e.tensor_scalar(out=out, in0=i0, scalar1=s1, scalar2=s2, op0=op0, op1=op1), reads=rd, writes=wr)

        are = apar.h[:, 0, o_, :]
        aim = apar.h[:, 1, o_, :]
        dt_ = dtb.h[:, o_, :]
        wd = [w_.d for w_ in W]
        tt(DVE, W[0].h[:], are, dt_, ALU.mult, [apar.d, dtb.d], [W[0].d])
        tt(DVE, W[1].h[:], aim, dt_, ALU.mult, [apar.d, dtb.d], [W[1].d])
        fw.op(ACT, lambda e: e.activation(out=W[2].h[:], in_=W[0].h[:], func=AF.Exp), reads=[W[0].d], writes=[W[2].d])
        fw.op(ACT, lambda e: e.activation(out=W[3].h[:], in_=W[0].h[:], func=AF.Exp, scale=-1.0), reads=[W[0].d], writes=[W[3].d])
        for dst, shift in ((W[5], 0.0), (W[6], math.pi / 2)):
            ts(DVE, dst.h[:], W[1].h[:], shift, None, ALU.add, None, [W[1].d], [dst.d])
            ts(DVE, W[7].h[:], W[1].h[:], shift, None, ALU.add, None, [W[1].d], [W[7].d])
            for kthr in range(5):
                thr = (2 * kthr + 1) * math.pi
                ts(DVE, W[4].h[:], W[7].h[:], thr, -TWO_PI, ALU.is_gt, ALU.mult, [W[7].d], [W[4].d])
                tt(DVE, dst.h[:], dst.h[:], W[4].h[:], ALU.add, [dst.d, W[4].d], [dst.d])
        fw.op(ACT, lambda e: e.activation(out=W[5].h[:], in_=W[5].h[:], func=AF.Sin), reads=[W[5].d], writes=[W[5].d])
        fw.op(ACT, lambda e: e.activation(out=W[6].h[:], in_=W[6].h[:], func=AF.Sin), reads=[W[6].d], writes=[W[6].d])
        wp = lambda t_, ri: Wpos.h[:, t_, ri, :]
        wn = lambda s_, ri: Wneg.h[:, s_, ri, :]
        tt(DVE, wp(1, 0), W[2].h[:], W[6].h[:], ALU.mult, [W[2].d, W[6].d], [Wpos.d])
        tt(DVE, wp(1, 1), W[2].h[:], W[5].h[:], ALU.mult, [W[2].d, W[5].d], [Wpos.d])
        tt(DVE, wn(1, 0), W[3].h[:], W[6].h[:], ALU.mult, [W[3].d, W[6].d], [Wneg.d])
        fw.op(DVE, lambda e: e.scalar_tensor_tensor(out=wn(1, 1), in0=W[3].h[:], scalar=-1.0, in1=W[5].h[:], op0=ALU.mult, op1=ALU.mult), reads=[W[3].d, W[5].d], writes=[Wneg.d])
        for arr in (Wpos, Wneg):
            fw.op(DVE, lambda e, arr=arr: e.memset(arr.h[:, 0, 0, :], 1.0), writes=[arr.d])
            fw.op(DVE, lambda e, arr=arr: e.memset(arr.h[:, 0, 1, :], 0.0), writes=[arr.d])

        def cmul(outr, outi, ar, ai, br, bi, rd, wr):
            tt(DVE, W[0].h[:], ar, br, ALU.mult, rd, [W[0].d])
            tt(DVE, W[4].h[:], ai, bi, ALU.mult, rd, [W[4].d])
            tt(DVE, outr, W[0].h[:], W[4].h[:], ALU.subtract, [W[0].d, W[4].d], wr)
            tt(DVE, W[0].h[:], ar, bi, ALU.mult, rd + wr, [W[0].d])
            tt(DVE, W[4].h[:], ai, br, ALU.mult, rd + wr, [W[4].d])
            tt(DVE, outi, W[0].h[:], W[4].h[:], ALU.add, [W[0].d, W[4].d], wr)
        for t_ in range(2, 9):
            cmul(wp(t_, 0), wp(t_, 1), wp(t_ - 1, 0), wp(t_ - 1, 1), wp(1, 0), wp(1, 1), [Wpos.d], [Wpos.d])
        for s_ in range(2, 8):
            cmul(wn(s_, 0), wn(s_, 1), wn(s_ - 1, 0), wn(s_ - 1, 1), wn(1, 0), wn(1, 1), [Wneg.d], [Wneg.d])
        if not samp:
            wdv = lambda k_, ri: Wd.h[:, k_, ri, :]
            fw.op(DVE, lambda e: e.tensor_copy(out=Wd.h[:, 0, :, :], in_=Wpos.h[:, 8, :, :]), reads=[Wpos.d], writes=[Wd.d])
            for k_ in range(1, 7):
                tt(DVE, W[0].h[:], wdv(k_ - 1, 0), wdv(k_ - 1, 0), ALU.mult, [Wd.d], [W[0].d])
                tt(DVE, W[4].h[:], wdv(k_ - 1, 1), wdv(k_ - 1, 1), ALU.mult, [Wd.d], [W[4].d])
                tt(DVE, wdv(k_, 0), W[0].h[:], W[4].h[:], ALU.subtract, [W[0].d, W[4].d], [Wd.d])
                fw.op(DVE, lambda e, k_=k_: e.scalar_tensor_tensor(out=wdv(k_, 1), in0=wdv(k_ - 1, 0), scalar=2.0, in1=wdv(k_ - 1, 1), op0=ALU.mult, op1=ALU.mult), reads=[Wd.d], writes=[Wd.d])
        fr_ = frfi.h[:, 0, :]
        fi_ = frfi.h[:, 1, :]
        tt(DVE, W[2].h[:], are, are, ALU.mult, [apar.d], [W[2].d])
        tt(DVE, W[3].h[:], aim, aim, ALU.mult, [apar.d], [W[3].d])
        tt(DVE, W[2].h[:], W[2].h[:], W[3].h[:], ALU.add, [W[2].d, W[3].d], [W[2].d])
        fw.op(DVE, lambda e: e.reciprocal(out=W[2].h[:], in_=W[2].h[:]), reads=[W[2].d], writes=[W[2].d])
        ts(DVE, W[3].h[:], wp(1, 0), -1.0, None, ALU.add, None, [Wpos.d], [W[3].d])
        tt(DVE, W[5].h[:], W[3].h[:], are, ALU.mult, [W[3].d, apar.d], [W[5].d])
        tt(DVE, W[6].h[:], wp(1, 1), aim, ALU.mult, [Wpos.d, apar.d], [W[6].d])
        tt(DVE, W[5].h[:], W[5].h[:], W[6].h[:], ALU.add, [W[5].d, W[6].d], [W[5].d])
        tt(DVE, fr_, W[5].h[:], W[2].h[:], ALU.mult, [W[5].d, W[2].d], [frfi.d])
        tt(DVE, W[5].h[:], wp(1, 1), are, ALU.mult, [Wpos.d, apar.d], [W[5].d])
        tt(DVE, W[6].h[:], W[3].h[:], aim, ALU.mult, [W[3].d, apar.d], [W[6].d])
        tt(DVE, W[5].h[:], W[5].h[:], W[6].h[:], ALU.subtract, [W[5].d, W[6].d], [W[5].d])
        tt(DVE, fi_, W[5].h[:], W[2].h[:], ALU.mult, [W[5].d, W[2].d], [frfi.d])

        bHr, bHi = banks[0], banks[1]
        def grp_views(gh, g8):
            P0 = 64 * gh
            r0 = Rp[0].h[P0:P0 + 64, g8, :, :].rearrange("p s j -> p (s j)")
            r1 = Rp[1].h[P0:P0 + 64, g8, :, :].rearrange("p s j -> p (s j)")
            o0 = Op[0].h[P0:P0 + 64, g8, :, :].rearrange("p s j -> p (s j)")
            o1 = Op[1].h[P0:P0 + 64, g8, :, :].rearrange("p s j -> p (s j)")
            return P0, r0, r1, o0, o1

        def st_P1(bt):
            g0 = bt * 8
            for ri in range(2):
                fw.dma(SP, lambda e, ri=ri: e.dma_start(out=braw[ri].h[:], in_=b_d[ri].ap()[o_, :, g0:g0 + 8, :]), bslots[ri], writes=[braw[ri].d])
                fw.dma(SP, lambda e, ri=ri: e.dma_start(out=craw[ri].h[:], in_=c_d[ri].ap()[o_, :, g0:g0 + 8, :]), bslots[2 + ri], writes=[craw[ri].d])
            frb = vap(frfi, g0, [[1, 8], [0, 16]])
            fib = vap(frfi, 64 + g0, [[1, 8], [0, 16]])
            v8 = lambda t_: t_.h[:, :, :]
            tA3 = tA.h[:, 0:128].rearrange("p (a b) -> p a b", a=8)
            tB3 = tB.h[:, 0:128].rearrange("p (a b) -> p a b", a=8)
            tt(DVE, tA3, v8(braw[0]), frb, ALU.mult, [braw[0].d, frfi.d], [tA.d])
            tt(DVE, tB3, v8(braw[1]), fib, ALU.mult, [braw[1].d, frfi.d], [tB.d])
            tt(DVE, v8(bbar[0]), tA3, tB3, ALU.subtract, [tA.d, tB.d], [bbar[0].d])
            tt(DVE, tA3, v8(braw[1]), frb, ALU.mult, [braw[1].d, frfi.d], [tA.d])
            tt(DVE, tB3, v8(braw[0]), fib, ALU.mult, [braw[0].d, frfi.d], [tB.d])
            tt(DVE, v8(bbar[1]), tA3, tB3, ALU.add, [tA.d, tB.d], [bbar[1].d])
            tA4 = tA.h[:, :].rearrange("p (a b c) -> p a b c", a=8, b=8)
            tB4 = tB.h[:, :].rearrange("p (a b c) -> p a b c", a=8, b=8)
            wnr = vap(Wneg, g0, [[1, 8], [128, 8], [0, 16]])
            wni = vap(Wneg, 64 + g0, [[1, 8], [128, 8], [0, 16]])
            wpr = vap(Wpos, g0, [[1, 8], [128, 8], [0, 16]])
            wpi = vap(Wpos, 64 + g0, [[1, 8], [128, 8], [0, 16]])
            bb4 = [vap(bbar[ri], 0, [[16, 8], [0, 8], [1, 16]]) for ri in range(2)]
            cc4 = [vap(craw[ri], 0, [[16, 8], [0, 8], [1, 16]]) for ri in range(2)]
            e1, e2 = DVE, POOL
            tt(e1, tA4, bb4[0], wnr, ALU.mult, [bbar[0].d, Wneg.d], [tA.d])
            tt(e1, tB4, bb4[1], wni, ALU.mult, [bbar[1].d, Wneg.d], [tB.d])
            tt(e1, Rp[0].h[:], tA4, tB4, ALU.subtract, [tA.d, tB.d], [Rp[0].d])
            tt(e1, tA4, bb4[1], wnr, ALU.mult, [bbar[1].d, Wneg.d], [tA.d])
            tt(e1, tB4, bb4[0], wni, ALU.mult, [bbar[0].d, Wneg.d], [tB.d])
            tt(e1, Rp[1].h[:], tA4, tB4, ALU.add, [tA.d, tB.d], [Rp[1].d])

        def st_P2(bt):
            g0 = bt * 8
            tA4 = tA.h[:, :].rearrange("p (a b c) -> p a b c", a=8, b=8)
            tB4 = tB.h[:, :].rearrange("p (a b c) -> p a b c", a=8, b=8)
            wpr = vap(Wpos, g0, [[1, 8], [128, 8], [0, 16]])
            wpi = vap(Wpos, 64 + g0, [[1, 8], [128, 8], [0, 16]])
            cc4 = [vap(craw[ri], 0, [[16, 8], [0, 8], [1, 16]]) for ri in range(2)]
            e1 = DVE
            tt(e1, tA4, cc4[0], wpr, ALU.mult, [craw[0].d, Wpos.d], [tA.d])
            tt(e1, tB4, cc4[1], wpi, ALU.mult, [craw[1].d, Wpos.d], [tB.d])
            tt(e1, Op[0].h[:], tA4, tB4, ALU.subtract, [tA.d, tB.d], [Op[0].d])
            tt(e1, tA4, cc4[0], wpi, ALU.mult, [craw[0].d, Wpos.d], [tA.d])
            tt(e1, tB4, cc4[1], wpr, ALU.mult, [craw[1].d, Wpos.d], [tB.d])
            fw.op(e1, lambda e: e.scalar_tensor_tensor(out=Op[1].h[:].rearrange("p a b c -> p (a b c)"), in0=tA.h[:, :], scalar=-1.0, in1=tB.h[:, :], op0=ALU.mult, op1=ALU.subtract),
                  reads=[tA.d, tB.d], writes=[Op[1].d])

        def st_U(bt):
            Ubc = Ub2[bt % 2]
            for gh in range(2):
                ft = bt + 8 * gh
                for g8 in range(8):
                    gi = gh * 8 + g8
                    b3 = bank()

                    def um(e, b3=b3, g8=g8, ft=ft):
                        r = None
                        for k_, s_ in enumerate(s_list):
                            rhs = xn.h[:, ft, 0:Tn].rearrange("p (c s) -> p s c", s=SL)[:, s_ - (8 - SL), :]
                            r = e.matmul(b3.h[:, 0:n], lhsT=strips.h[:, g8, 112 - 16 * s_:240 - 16 * s_], rhs=rhs, start=(k_ == 0), stop=(k_ == len(s_list) - 1))
                        return r
                    fw.op(PE, um, reads=[strips.d, xn.d], writes=[b3.d])
                    fw.op(ACT, lambda e, b3=b3, gi=gi: e.activation(out=Ubc.h[:, gi, 0:n], in_=b3.h[:, 0:n], func=AF.Copy), reads=[b3.d], writes=[Ubc.d])

        def st_A(bt):
            Ubc = Ub2[bt % 2]
            for gh in range(2):
                for g8 in range(8):
                    gi = gh * 8 + g8
                    P0, r0, r1, o0, o1 = grp_views(gh, g8)
                    b = bank()

                    def mt(e, b=b, r0=r0, r1=r1, o0=o0, o1=o1):
                        e.matmul(b.h[:, 0:128], lhsT=r0, rhs=o0, start=True, stop=False)
                        return e.matmul(b.h[:, 0:128], lhsT=r1, rhs=o1, start=False, stop=True)
                    fw.op(PE, mt, reads=[Rp[0].d, Rp[1].d, Op[0].d, Op[1].d], writes=[b.d])
                    fw.op(DVE, lambda e, b=b, gi=gi: e.tensor_tensor(out=MTs.h[:, gi, :], in0=b.h[:, 0:128], in1=bmask.h[:], op=ALU.mult), reads=[b.d, bmask.d], writes=[MTs.d])
                    b2 = bank()
                    bb2 = b2.h[:, 0:64].bitcast(BF16)

                    def qt(e, bb2=bb2, r0=r0, r1=r1, P0=P0):
                        idn = ident[P0:P0 + 64, P0:P0 + 64]
                        e.transpose(out=bb2[:, 0:64], in_=r0, identity=idn)
                        return e.transpose(out=bb2[:, 64:128], in_=r1, identity=idn)
                    fw.op(PE, qt, reads=[Rp[0].d, Rp[1].d, identones.d], writes=[b2.d])
                    q_ = QTs[gi % 2]
                    fw.op(ACT, lambda e, bb2=bb2, q_=q_: e.activation(out=q_.h[:], in_=bb2, func=AF.Copy), reads=[b2.d], writes=[q_.d])
                    fw.op(PE, lambda e, Ubc=Ubc, q_=q_, gi=gi, g8=g8, P0=P0: e.matmul(bHr.h[P0:P0 + 64, g8 * 64:g8 * 64 + n], lhsT=q_.h[:, 0:64], rhs=Ubc.h[:, gi, 0:n], start=True, stop=True),
                          reads=[q_.d, Ubc.d], writes=[bHr.d])
                    fw.op(PE, lambda e, Ubc=Ubc, q_=q_, gi=gi, g8=g8, P0=P0: e.matmul(bHi.h[P0:P0 + 64, g8 * 64:g8 * 64 + n], lhsT=q_.h[:, 64:128], rhs=Ubc.h[:, gi, 0:n], start=True, stop=True),
                          reads=[q_.d, Ubc.d], writes=[bHi.d])

        def st_S(bt):
            g0 = bt * 8
            Hv = [bHr.h[:, :].rearrange("p (g c) -> p g c", g=8)[:, :, 0:n], bHi.h[:, :].rearrange("p (g c) -> p g c", g=8)[:, :, 0:n]]
            if not samp:
                for ri in range(2):
                    fw.op(ACT, lambda e, ri=ri: e.activation(out=Aar.h[:, ri, :, 1:n + 1], in_=Hv[ri], func=AF.Copy), reads=[(bHr, bHi)[ri].d], writes=[Aar.d])
                if ti == 0:
                    fw.op(DVE, lambda e: e.memset(Aar.h[:, :, :, 0], 0.0), writes=[Aar.d])
                else:
                    fw.op(DVE, lambda e: e.tensor_copy(out=Aar.h[:, :, :, 0], in_=Pst.h[:, o_, :, g0:g0 + 8]), reads=[Pst.d], writes=[Aar.d])
                T1f = tA.h[:, :].rearrange("p (r g c) -> p r g c", r=2, g=8)
                T2f = tB.h[:, :].rearrange("p (r g c) -> p r g c", r=2, g=8)
                L = n + 1
                wr0 = vap(Wd, g0, [[0, 2], [1, 8], [0, n]])
                wi0 = vap(Wd, 64 + g0, [[0, 2], [1, 8], [0, n]])
                tt(DVE, T1f[:, :, :, 0:n], Aar.h[:, :, :, 1:L], wr0, ALU.mult, [Aar.d, Wd.d], [tA.d])
                tt(DVE, T2f[:, :, :, 0:n], Aar.h[:, :, :, 1:L], wi0, ALU.mult, [Aar.d, Wd.d], [tB.d])
                tt(DVE, Aar.h[:, 0, :, 1:L], T1f[:, 0, :, 0:n], T2f[:, 1, :, 0:n], ALU.subtract, [tA.d, tB.d], [Aar.d])
                tt(DVE, Aar.h[:, 1, :, 1:L], T1f[:, 1, :, 0:n], T2f[:, 0, :, 0:n], ALU.add, [tA.d, tB.d], [Aar.d])
                for k_ in range(7):
                    d_ = 1 << k_
                    Lc = L - d_
                    wr_ = vap(Wd, k_ * 128 + g0, [[0, 2], [1, 8], [0, Lc]])
                    wi_ = vap(Wd, k_ * 128 + 64 + g0, [[0, 2], [1, 8], [0, Lc]])
                    tt(DVE, T1f[:, :, :, 0:Lc], Aar.h[:, :, :, 0:Lc], wr_, ALU.mult, [Aar.d, Wd.d], [tA.d])
                    tt(DVE, T2f[:, :, :, 0:Lc], Aar.h[:, :, :, 0:Lc], wi_, ALU.mult, [Aar.d, Wd.d], [tB.d])
                    tt(DVE, Aar.h[:, 0, :, d_:L], Aar.h[:, 0, :, d_:L], T1f[:, 0, :, 0:Lc], ALU.add, [Aar.d, tA.d], [Aar.d])
                    tt(DVE, Aar.h[:, 0, :, d_:L], Aar.h[:, 0, :, d_:L], T2f[:, 1, :, 0:Lc], ALU.subtract, [Aar.d, tB.d], [Aar.d])
                    tt(DVE, Aar.h[:, 1, :, d_:L], Aar.h[:, 1, :, d_:L], T1f[:, 1, :, 0:Lc], ALU.add, [Aar.d, tA.d], [Aar.d])
                    tt(DVE, Aar.h[:, 1, :, d_:L], Aar.h[:, 1, :, d_:L], T2f[:, 0, :, 0:Lc], ALU.add, [Aar.d, tB.d], [Aar.d])
                fw.op(DVE, lambda e: e.tensor_copy(out=Pst.h[:, o_, :, g0:g0 + 8], in_=Aar.h[:, :, :, n]), reads=[Aar.d], writes=[Pst.d])
                fw.op(ACT, lambda e: e.activation(out=Abf.h[:, :, :, 0:n], in_=Aar.h[:, :, :, 0:n], func=AF.Copy), reads=[Aar.d], writes=[Abf.d])
            else:
                for ri in range(2):
                    fw.dma(SP, lambda e, ri=ri: e.dma_start(out=h0s5.h[:, ri, :, :], in_=sssm[ri].ap()[o_, :, g0:g0 + 8, :]), h0slots[ri], writes=[h0s5.d])
                wm3r = vap(Wneg, 3 * 128 + g0, [[1, 8], [0, 16]])
                wm3i = vap(Wneg, 3 * 128 + 64 + g0, [[1, 8], [0, 16]])
                w7r = vap(Wpos, 7 * 128 + g0, [[1, 8], [0, 16]])
                w7i = vap(Wpos, 7 * 128 + 64 + g0, [[1, 8], [0, 16]])
                tA3 = tA.h[:, 0:128].rearrange("p (a b) -> p a b", a=8)
                tB3 = tB.h[:, 0:128].rearrange("p (a b) -> p a b", a=8)

                def cm3(outr, outi, xr, xi, wr_, wi_, rd, wrd):
                    tt(DVE, tA3, xr, wr_, ALU.mult, rd, [tA.d])
                    tt(DVE, tB3, xi, wi_, ALU.mult, rd, [tB.d])
                    tt(DVE, outr, tA3, tB3, ALU.subtract, [tA.d, tB.d], wrd)
                    tt(DVE, tA3, xi, wr_, ALU.mult, rd, [tA.d])
                    tt(DVE, tB3, xr, wi_, ALU.mult, rd, [tB.d])
                    tt(DVE, outi, tA3, tB3, ALU.add, [tA.d, tB.d], wrd)
                cm3(Pp.h[:, 0, :, :], Pp.h[:, 1, :, :], h0s5.h[:, 0, :, :], h0s5.h[:, 1, :, :], wm3r, wm3i, [h0s5.d, Wneg.d], [Pp.d])
                fw.op(ACT, lambda e: e.activation(out=Abf.h[:, :, :, 0:16], in_=Pp.h[:, :, :, :], func=AF.Copy), reads=[Pp.d], writes=[Abf.d])
                for ri in range(2):
                    tt(DVE, Aar.h[:, ri, :, 0:16], Hv[ri], Pp.h[:, ri, :, :], ALU.add, [(bHr, bHi)[ri].d, Pp.d], [Aar.d])
                cm3(Pf.h[:, 0, :, :], Pf.h[:, 1, :, :], Aar.h[:, 0, :, 0:16], Aar.h[:, 1, :, 0:16], w7r, w7i, [Aar.d, Wpos.d], [Pf.d])
                for ri in range(2):
                    out_toks.append(fw.dma(SP, lambda e, ri=ri: e.dma_start(out=o_ssms[ri].ap()[o_, :, g0:g0 + 8, :], in_=Pf.h[:, ri, :, :]), oslot(("ssms", ri)), reads=[Pf.d]))

        def st_Y(bt):
            Ubc = Ub2[bt % 2]
            for gh in range(2):
                for g8 in range(8):
                    gi = gh * 8 + g8
                    P0, r0, r1, o0, o1 = grp_views(gh, g8)
                    b = bank()

                    def ym(e, Ubc=Ubc, b=b, gi=gi, g8=g8, P0=P0, o0=o0, o1=o1):
                        e.matmul(b.h[:, 0:n], lhsT=MTs.h[:, gi, :], rhs=Ubc.h[:, gi, 0:n], start=True, stop=False)
                        e.matmul(b.h[:, 0:n], lhsT=o0, rhs=Abf.h[P0:P0 + 64, 0, g8, 0:n], start=False, stop=False)
                        return e.matmul(b.h[:, 0:n], lhsT=o1, rhs=Abf.h[P0:P0 + 64, 1, g8, 0:n], start=False, stop=True)
                    fw.op(PE, ym, reads=[MTs.d, Ubc.d, Op[0].d, Op[1].d, Abf.d], writes=[b.d])
                    fw.op(ACT, lambda e, b=b, gi=gi: e.activation(out=Yb.h[:, gi, 0:n], in_=b.h[:, 0:n], func=AF.Copy), reads=[b.d], writes=[Yb.d])

        def st_B(bt):
            for gh in range(2):
                ft = bt + 8 * gh
                bY = bank()

                def bc(e, bY=bY, gh=gh):
                    r = None
                    for t_ in s_list:
                        for g8 in range(8):
                            r = e.matmul(bY.h[:, t_ * 64:t_ * 64 + n], lhsT=strips.h[:, t_, 112 - 16 * g8:240 - 16 * g8], rhs=Yb.h[:, gh * 8 + g8, 0:n], start=(g8 == 0), stop=(g8 == 7))
                    return r
                fw.op(PE, bc, reads=[strips.d, Yb.d], writes=[bY.d])
                src = bY.h[:, :].rearrange("p (t c) -> p t c", t=8)[:, 8 - SL:8, 0:n]
                dst = y32b.h[:, 0:Tn].rearrange("p (c t) -> p t c", t=SL)
                fw.op(ACT, lambda e, src=src, dst=dst: e.activation(out=dst, in_=src, func=AF.Copy), reads=[bY.d], writes=[y32b.d])
                fw.op(DVE, lambda e, ft=ft: e.scalar_tensor_tensor(out=y32b.h[:, 0:Tn], in0=xn.h[:, ft, 0:Tn], scalar=ssmd.h[:, o_, ft:ft + 1], in1=y32b.h[:, 0:Tn], op0=ALU.mult, op1=ALU.add),
                      reads=[xn.d, ssmd.d, y32b.d], writes=[y32b.d])
                gelu_mul(y32b, None, cat.h[:, ft, 0:Tn], cat, Tn, g1b, g2b)

        st_P1(0)
        st_P2(0)
        st_U(0)
        for b_ in range(8):
            st_A(b_)
            if b_ < 7:
                st_U(b_ + 1)
            st_S(b_)
            if b_ < 7:
                st_P1(b_ + 1)
            st_Y(b_)
            if b_ < 7:
                st_P2(b_ + 1)
            st_B(b_)
        if (not samp) and ti == 3:
            w1r = Wneg.h[:, 1, 0, :]
            w1i = Wneg.h[:, 1, 1, :]
            pr = Pst.h[:, o_, 0, :]
            pi_ = Pst.h[:, o_, 1, :]
            tt(DVE, W[0].h[:], pr, w1r, ALU.mult, [Pst.d, Wneg.d], [W[0].d])
            tt(DVE, W[4].h[:], pi_, w1i, ALU.mult, [Pst.d, Wneg.d], [W[4].d])
            tt(DVE, W[2].h[:], W[0].h[:], W[4].h[:], ALU.subtract, [W[0].d, W[4].d], [W[2].d])
            tt(DVE, W[0].h[:], pi_, w1r, ALU.mult, [Pst.d, Wneg.d], [W[0].d])
            tt(DVE, W[4].h[:], pr, w1i, ALU.mult, [Pst.d, Wneg.d], [W[4].d])
            tt(DVE, W[3].h[:], W[0].h[:], W[4].h[:], ALU.add, [W[0].d, W[4].d], [W[3].d])
            out_toks.append(fw.dma(SP, lambda e: e.dma_start(out=o_ssmp[0].ap()[o_], in_=W[2].h[:]), oslot(("ssmp", 0, o_)), reads=[W[2].d]))
            out_toks.append(fw.dma(SP, lambda e: e.dma_start(out=o_ssmp[1].ap()[o_], in_=W[3].h[:]), oslot(("ssmp", 1, o_)), reads=[W[3].d]))

    dctr = [0]

    def run_tile(kind, ti):
        samp = kind == "s"
        Tn = 64 if samp else 512
        if samp:
            fw.dma(SP, lambda e: e.dma_start(out=x.h[:, :, 0:64], in_=xsT.ap().rearrange("(kc p) t -> p kc t", p=128)), xslot, writes=[x.d])
            fw.dma(SP, lambda e: e.dma_start(out=cosT.h[:, 0:64], in_=cst["cosS"].ap()), tslot, writes=[cosT.d])
            fw.dma(SP, lambda e: e.dma_start(out=sinT.h[:, 0:64], in_=cst["sinS"].ap()), tslot2, writes=[sinT.d])
        else:
            fw.dma(SP, lambda e, ti=ti: e.dma_start(out=x.h[:], in_=xpT.ap()[:, ti * 512:(ti + 1) * 512].rearrange("(kc p) t -> p kc t", p=128)), xslot, writes=[x.d])
            fw.dma(SP, lambda e, ti=ti: e.dma_start(out=cosT.h[:], in_=cst["cosP"].ap()[:, ti * 512:(ti + 1) * 512]), tslot, writes=[cosT.d])
            fw.dma(SP, lambda e, ti=ti: e.dma_start(out=sinT.h[:], in_=cst["sinP"].ap()[:, ti * 512:(ti + 1) * 512]), tslot2, writes=[sinT.d])
            if ti == 0:
                fw.op(DVE, lambda e: e.memset(Sst.h[:], 0.0), writes=[Sst.d])
                fw.op(POOL, lambda e: e.memset(Sbf.h[:], 0.0), writes=[Sbf.d])
        for layer in range(nlayers):
            if layer % 2 == 0:
                if not samp:
                    fw.op(ACT, lambda e, layer=layer: e.activation(out=Sbf.h[:], in_=Sst.h[:, layer // 2, :, :, :], func=AF.Copy), reads=[Sst.d], writes=[Sbf.d])
                even_layer(layer // 2, kind, ti, Tn)
            else:
                odd_layer(layer // 2, kind, ti, Tn)
            ffn(layer, Tn)
            if dbg and dctr[0] < 8:
                di = dctr[0]
                out_toks.append(fw.dma(SP, lambda e, di=di: e.dma_start(out=o_dbg.ap()[di], in_=x.h[:]), oslot(("dbg", di)), reads=[x.d]))
                dctr[0] += 1
        if (not samp) and nlayers > 0:
            pass
        fw.op(ACT, lambda e: e.activation(out=cat.h[:, :, 0:Tn], in_=x.h[:, :, 0:Tn], func=AF.Square), reads=[x.d], writes=[cat.d])
        b = bank()

        def mmf(e, b=b, Tn=Tn):
            r = None
            for kc in range(KC):
                r = e.matmul(b.h[:, 0:Tn], lhsT=ones, rhs=cat.h[:, kc, 0:Tn], start=(kc == 0), stop=(kc == KC - 1))
            return r
        fw.op(PE, mmf, reads=[cat.d, identones.d], writes=[b.d])
        fw.op(ACT, lambda e, b=b, Tn=Tn: e.activation(out=rstd.h[:, 0:Tn], in_=b.h[:, 0:Tn], func=AF.Sqrt, scale=1.0 / D, bias=EPS), reads=[b.d], writes=[rstd.d])
        fw.op(DVE, lambda e, Tn=Tn: e.reciprocal(out=rstd.h[:, 0:Tn], in_=rstd.h[:, 0:Tn]), reads=[rstd.d], writes=[rstd.d])
        for kc in range(KC):
            fw.op(DVE, lambda e, kc=kc, Tn=Tn: e.scalar_tensor_tensor(out=x.h[:, kc, 0:Tn], in0=x.h[:, kc, 0:Tn], scalar=gains.h[:, 8, kc:kc + 1], in1=rstd.h[:, 0:Tn], op0=ALU.mult, op1=ALU.mult),
                  reads=[x.d, gains.d, rstd.d], writes=[x.d])
        if samp:
            out_toks.append(fw.dma(SP, lambda e: e.dma_start(out=ysT.ap().rearrange("(kc p) t -> p kc t", p=128), in_=x.h[:, :, 0:64]), oslot("ys"), reads=[x.d]))
        else:
            out_toks.append(fw.dma(SP, lambda e, ti=ti: e.dma_start(out=ypT.ap()[:, ti * 512:(ti + 1) * 512].rearrange("(kc p) t -> p kc t", p=128), in_=x.h[:]), oslot("yp"), reads=[x.d]))
    for (kind_, ti_) in tiles:
        lctr[0] = 0
        try:
            run_tile(kind_, ti_)
        except StopIteration:
            break
        passno[0] += 1
    last = {}
    for t in out_toks:
        last[id(t[0])] = t
    fw.wait_tokens(SP, list(last.values()))
    fw.emit()
    return nc


_CACHE = {}


def make_in_maps(inp, ncores=8):
    W = prep_weights(inp)
    C = host_consts()
    maps = []
    f32 = np.float32
    for c in range(ncores):
        b = c // 2
        m = dict(W)
        for k, v in C.items():
            m["c_" + k] = v
        m["xpT"] = np.ascontiguousarray(np.asarray(inp["x_prompt"][b]).T)
        m["xsT"] = np.ascontiguousarray(np.asarray(inp["x_sample"][16 * c:16 * c + 16]).reshape(64, D).T)
        m["sret"] = np.ascontiguousarray(np.asarray(inp["state_ret"][:, 16 * c:16 * c + 16]))
        m["slru"] = np.ascontiguousarray(np.asarray(inp["state_lru"][:, 16 * c:16 * c + 16]).transpose(0, 2, 1))
        m["sconv"] = np.ascontiguousarray(np.asarray(inp["state_conv"][:, 16 * c:16 * c + 16]).transpose(0, 3, 1, 2))
        for nm, key in (("sssm_re", "state_ssm_re"), ("sssm_im", "state_ssm_im")):
            a = np.asarray(inp[key][:, 16 * c:16 * c + 16]).reshape(2, 16, 2, 64, 64).transpose(0, 2, 4, 3, 1).reshape(2, 128, 64, 16)
            m[nm] = np.ascontiguousarray(a)
        maps.append(m)
    return maps


def assemble(results):
    f32 = np.float32
    y_p = np.zeros((4, 2048, D), f32)
    y_s = np.zeros((128, 4, D), f32)
    ret_p = np.zeros((2, 4, 4, 256, 256), f32)
    ret_s = np.zeros((2, 128, 4, 256, 256), f32)
    lru_p = np.zeros((2, 4, 1024), f32)
    lru_s = np.zeros((2, 128, 1024), f32)
    conv_p = np.zeros((2, 4, 3, 1024), f32)
    conv_s = np.zeros((2, 128, 3, 1024), f32)
    ssm_p = [np.zeros((2, 4, 128, 64), f32), np.zeros((2, 4, 128, 64), f32)]
    ssm_s = [np.zeros((2, 128, 128, 64), f32), np.zeros((2, 128, 128, 64), f32)]
    for c, r in enumerate(results):
        sl = slice(16 * c, 16 * c + 16)
        y_s[sl] = r["ysT"].T.reshape(16, 4, D)
        ret_s[:, sl] = r["o_rets"]
        lru_s[:, sl] = r["o_lrus"].transpose(0, 2, 1)
        conv_s[:, sl] = r["o_convs"].transpose(0, 2, 3, 1)
        for ri, nm in enumerate(("o_ssms_re", "o_ssms_im")):
            a = r[nm].reshape(2, 2, 64, 64, 16).transpose(0, 4, 1, 3, 2).reshape(2, 16, 128, 64)
            ssm_s[ri][:, sl] = a
        if c % 2 == 0:
            b = c // 2
            y_p[b] = r["ypT"].T
            ret_p[:, b] = r["o_retp"]
            lru_p[:, b] = r["o_lrup"].transpose(0, 2, 1).reshape(2, 1024)
            conv_p[:, b] = r["o_convp"].transpose(0, 2, 1)
            for ri, nm in enumerate(("o_ssmp_re", "o_ssmp_im")):
                a = r[nm].reshape(2, 2, 64, 64).transpose(0, 1, 3, 2).reshape(2, 128, 64)
                ssm_p[ri][:, b] = a
    return (y_p, y_s, ret_p, ret_s, lru_p, lru_s, conv_p, conv_s, ssm_p[0], ssm_s[0], ssm_p[1], ssm_s[1])


def kernel(**inputs):
    nc = build({})
    maps = make_in_maps(inputs)
    res = run_bass_kernel_spmd(nc, maps, core_ids=list(range(8)))
    return assemble(res.results)
```
